# Optimizing a Trainium2 kernel written in Bass

```python
import math
import jax, jax.numpy as jnp
from jax import lax
import numpy as np

D_MODEL = 1024
BATCH = 4
SEQ = 4096
DEPTH = 2

HEAD_DIM = 64
N_HEADS_DIFF = 8
N_HEADS_MOBA = 8
DIFF_HALF = HEAD_DIM // 2
MOBA_BLOCK = 256
MOBA_TOPK = 3
MOBA_Q_CHUNK = 64
ATTN_Q_BLOCK = 128
GMLP_CHUNK = 128
GMLP_GROUPS = 8
GMLP_WIDTH = 512
CONV_CH = 512
CONV_KERNEL = 31
D_FF = -(-8 * D_MODEL // (3 * 256)) * 256
ROPE_THETA = 10000.0
EPS = 1e-6
N_EVEN = (DEPTH + 1) // 2
N_ODD = DEPTH // 2
EVEN_IN = 3 * N_HEADS_DIFF * HEAD_DIM + 3 * N_HEADS_MOBA * HEAD_DIM
EVEN_OUT = (N_HEADS_DIFF + N_HEADS_MOBA) * HEAD_DIM
ODD_IN = 2 * GMLP_WIDTH + 2 * CONV_CH
ODD_OUT = GMLP_WIDTH + CONV_CH

kernel_name = "hybrid_diffattn_moba_gmlp_conformer_block"


def rms_norm(x, g):
    xf = x.astype(jnp.float32)
    y = xf * lax.rsqrt(jnp.mean(xf * xf, axis=-1, keepdims=True) + EPS)
    return (y * g.astype(jnp.float32)).astype(x.dtype)


def layer_norm(x, g, b):
    xf = x.astype(jnp.float32)
    mu = jnp.mean(xf, axis=-1, keepdims=True)
    xc = xf - mu
    y = xc * lax.rsqrt(jnp.mean(xc * xc, axis=-1, keepdims=True) + EPS)
    return (y * g.astype(jnp.float32) + b.astype(jnp.float32)).astype(x.dtype)


def rope_tables(dim):
    inv = 1.0 / (ROPE_THETA ** (jnp.arange(0, dim, 2, dtype=jnp.float32) / dim))
    ang = jnp.arange(SEQ, dtype=jnp.float32)[:, None] * inv[None, :]
    return jnp.cos(ang), jnp.sin(ang)


def apply_rope(x, cos, sin):
    x1, x2 = jnp.split(x, 2, axis=-1)
    c = cos.astype(x.dtype)
    s = sin.astype(x.dtype)
    return jnp.concatenate([x1 * c - x2 * s, x2 * c + x1 * s], axis=-1)


def diff_attention(q, k, v, lam):
    B, H = q.shape[0], q.shape[1]
    nq = SEQ // ATTN_Q_BLOCK
    scale = DIFF_HALF ** -0.5
    qb = jnp.moveaxis(q.reshape(B, H, 2, nq, ATTN_Q_BLOCK, DIFF_HALF), 3, 0)
    kpos = jnp.arange(SEQ)

    def block(args):
        q_blk, i = args
        s = jnp.einsum('bhmqd,bhmkd->bhmqk', q_blk, k).astype(jnp.float32) * scale
        qpos = i * ATTN_Q_BLOCK + jnp.arange(ATTN_Q_BLOCK)
        s = jnp.where(kpos[None, :] <= qpos[:, None], s, -jnp.inf)
        p = jax.nn.softmax(s, axis=-1)
        w = p[:, :, 0] - lam * p[:, :, 1]
        return jnp.einsum('bhqk,bhkd->bhqd', w.astype(v.dtype), v)

    out = lax.map(block, (qb, jnp.arange(nq)))
    return jnp.moveaxis(out, 0, 2).reshape(B, H, SEQ, HEAD_DIM)


def moba_attention(q, k, v):
    B, H, S, D = q.shape
    nb = -(-S // MOBA_BLOCK)
    pad = nb * MOBA_BLOCK - S
    kp = jnp.pad(k, ((0, 0), (0, 0), (0, pad), (0, 0)))
    vp = jnp.pad(v, ((0, 0), (0, 0), (0, pad), (0, 0)))
    kb = kp.reshape(B, H, nb, MOBA_BLOCK, D)
    vb = vp.reshape(B, H, nb, MOBA_BLOCK, D)
    kbar = jnp.mean(kb.astype(jnp.float32), axis=3)
    qblk = jnp.arange(S) // MOBA_BLOCK
    gate = jnp.einsum('bhsd,bhnd->bhsn', q.astype(jnp.float32), kbar)
    past = jnp.arange(nb)[None, :] < qblk[:, None]
    gate = jnp.where(past, gate, -jnp.inf)
    kk = min(MOBA_TOPK, nb)
    _, idx = lax.top_k(gate, kk)

    nc = S // MOBA_Q_CHUNK
    qc = jnp.moveaxis(q.reshape(B, H, nc, MOBA_Q_CHUNK, D), 2, 0)
    ic = jnp.moveaxis(idx.reshape(B, H, nc, MOBA_Q_CHUNK, kk), 2, 0)
    bi = jnp.arange(B)[:, None, None, None]
    hi = jnp.arange(H)[None, :, None, None]
    scale = D ** -0.5
    offs = jnp.arange(MOBA_BLOCK)

    def chunk(args):
        q_c, idx_c, c = args
        q0 = c * MOBA_Q_CHUNK
        j = q0 // MOBA_BLOCK
        k_sel = kb[bi, hi, idx_c]
        v_sel = vb[bi, hi, idx_c]
        s_sel = jnp.einsum('bhqd,bhqnkd->bhqnk', q_c, k_sel).astype(jnp.float32) * scale
        valid = jnp.arange(kk) < j
        s_sel = jnp.where(valid[None, None, None, :, None], s_sel, -jnp.inf)
        k_own = lax.dynamic_slice_in_dim(kp, j * MOBA_BLOCK, MOBA_BLOCK, axis=2)
        v_own = lax.dynamic_slice_in_dim(vp, j * MOBA_BLOCK, MOBA_BLOCK, axis=2)
        s_own = jnp.einsum('bhqd,bhkd->bhqk', q_c, k_own).astype(jnp.float32) * scale
        qpos = q0 + jnp.arange(MOBA_Q_CHUNK)
        kpos = j * MOBA_BLOCK + offs
        s_own = jnp.where(kpos[None, :] <= qpos[:, None], s_own, -jnp.inf)
        s_all = jnp.concatenate([s_sel.reshape(B, H, MOBA_Q_CHUNK, kk * MOBA_BLOCK), s_own], axis=-1)
        p = jax.nn.softmax(s_all, axis=-1).astype(v.dtype)
        p_sel = p[..., :kk * MOBA_BLOCK].reshape(B, H, MOBA_Q_CHUNK, kk, MOBA_BLOCK)
        p_own = p[..., kk * MOBA_BLOCK:]
        return (jnp.einsum('bhqnk,bhqnkd->bhqd', p_sel, v_sel)
                + jnp.einsum('bhqk,bhkd->bhqd', p_own, v_own))

    out = lax.map(chunk, (qc, ic, jnp.arange(nc)))
    return jnp.moveaxis(out, 0, 2).reshape(B, H, S, D)


def even_mixer(h, w_in, w_out, qn_g, kn_g, lq1, lk1, lq2, lk2, subln_g, mqn_g, mkn_g, lambda_init):
    B, S, _ = h.shape
    nd = N_HEADS_DIFF * HEAD_DIM
    nm = N_HEADS_MOBA * HEAD_DIM
    proj = h @ w_in
    dq, dk, dv, mq, mk, mv = jnp.split(proj, [nd, 2 * nd, 3 * nd, 3 * nd + nm, 3 * nd + 2 * nm], axis=-1)
    cos32, sin32 = rope_tables(DIFF_HALF)
    dq = dq.reshape(B, S, N_HEADS_DIFF, 2, DIFF_HALF).transpose(0, 2, 3, 1, 4)
    dk = dk.reshape(B, S, N_HEADS_DIFF, 2, DIFF_HALF).transpose(0, 2, 3, 1, 4)
    dq = apply_rope(rms_norm(dq, qn_g), cos32, sin32)
    dk = apply_rope(rms_norm(dk, kn_g), cos32, sin32)
    dv = dv.reshape(B, S, N_HEADS_DIFF, HEAD_DIM).transpose(0, 2, 1, 3)
    f32 = jnp.float32
    lam = (jnp.exp(jnp.sum(lq1.astype(f32) * lk1.astype(f32)))
           - jnp.exp(jnp.sum(lq2.astype(f32) * lk2.astype(f32))) + lambda_init)
    a = diff_attention(dq, dk, dv, lam)
    a = rms_norm(a, subln_g) * (1.0 - lambda_init)
    cos64, sin64 = rope_tables(HEAD_DIM)
    mq = mq.reshape(B, S, N_HEADS_MOBA, HEAD_DIM).transpose(0, 2, 1, 3)
    mk = mk.reshape(B, S, N_HEADS_MOBA, HEAD_DIM).transpose(0, 2, 1, 3)
    mv = mv.reshape(B, S, N_HEADS_MOBA, HEAD_DIM).transpose(0, 2, 1, 3)
    mq = apply_rope(rms_norm(mq, mqn_g), cos64, sin64)
    mk = apply_rope(rms_norm(mk, mkn_g), cos64, sin64)
    m = moba_attention(mq, mk, mv)
    out = jnp.concatenate([a, m], axis=1)
    out = out.transpose(0, 2, 1, 3).reshape(B, S, EVEN_OUT)
    return out @ w_out


def odd_mixer(h, w_in, b_in, w_out, ln_g, ln_b, w_s, b_s, conv_w, conv_b, cln_g, cln_b):
    B, S, _ = h.shape
    proj = h @ w_in + b_in
    gz, ca, cg = jnp.split(proj, [2 * GMLP_WIDTH, 2 * GMLP_WIDTH + CONV_CH], axis=-1)
    gu, gv = jnp.split(jax.nn.gelu(gz), 2, axis=-1)
    gv = layer_norm(gv, ln_g, ln_b)
    nc = S // GMLP_CHUNK
    gvc = gv.reshape(B, nc, GMLP_CHUNK, GMLP_GROUPS, GMLP_WIDTH // GMLP_GROUPS)
    tri = jnp.tril(jnp.ones((GMLP_CHUNK, GMLP_CHUNK), dtype=bool))
    wm = jnp.where(tri[None], w_s, jnp.zeros_like(w_s))
    sg = jnp.einsum('gts,bcsgd->bctgd', wm, gvc) + b_s.T[None, None, :, :, None]
    c_out = gu * sg.reshape(B, S, GMLP_WIDTH)
    c = ca * jax.nn.sigmoid(cg)
    c = lax.conv_general_dilated(c, conv_w[:, None, :], window_strides=(1,),
                                 padding=[(CONV_KERNEL - 1, 0)],
                                 dimension_numbers=('NWC', 'WIO', 'NWC'),
                                 feature_group_count=CONV_CH) + conv_b
    d_out = jax.nn.silu(layer_norm(c, cln_g, cln_b))
    return jnp.concatenate([c_out, d_out], axis=-1) @ w_out


def swiglu(h, w_in, w_out):
    g, u = jnp.split(h @ w_in, 2, axis=-1)
    return (jax.nn.silu(g) * u) @ w_out


def setup_inputs(seed: int = 0) -> dict:
    key = jax.random.key(seed)
    ks = jax.random.split(key, 32)

    def nrm(k, shape, scale):
        return jax.random.normal(k, shape, jnp.float32) * scale

    return {
        "x": nrm(ks[0], (BATCH, SEQ, D_MODEL), 1.0),
        "attn_norm_g": 1.0 + nrm(ks[1], (DEPTH, D_MODEL), 0.02),
        "ffn_norm_g": 1.0 + nrm(ks[2], (DEPTH, D_MODEL), 0.02),
        "ffn_w_in": nrm(ks[3], (DEPTH, D_MODEL, 2 * D_FF), D_MODEL ** -0.5),
        "ffn_w_out": nrm(ks[4], (DEPTH, D_FF, D_MODEL), D_FF ** -0.5),
        "even_w_in": nrm(ks[5], (N_EVEN, D_MODEL, EVEN_IN), D_MODEL ** -0.5),
        "even_w_out": nrm(ks[6], (N_EVEN, EVEN_OUT, D_MODEL), EVEN_OUT ** -0.5),
        "diff_q_norm_g": 1.0 + nrm(ks[7], (N_EVEN, DIFF_HALF), 0.02),
        "diff_k_norm_g": 1.0 + nrm(ks[8], (N_EVEN, DIFF_HALF), 0.02),
        "diff_lambda_q1": nrm(ks[9], (N_EVEN, DIFF_HALF), 0.1),
        "diff_lambda_k1": nrm(ks[10], (N_EVEN, DIFF_HALF), 0.1),
        "diff_lambda_q2": nrm(ks[11], (N_EVEN, DIFF_HALF), 0.1),
        "diff_lambda_k2": nrm(ks[12], (N_EVEN, DIFF_HALF), 0.1),
        "diff_subln_g": 1.0 + nrm(ks[13], (N_EVEN, HEAD_DIM), 0.02),
        "moba_q_norm_g": 1.0 + nrm(ks[14], (N_EVEN, HEAD_DIM), 0.02),
        "moba_k_norm_g": 1.0 + nrm(ks[15], (N_EVEN, HEAD_DIM), 0.02),
        "odd_w_in": nrm(ks[16], (N_ODD, D_MODEL, ODD_IN), D_MODEL ** -0.5),
        "odd_b_in": nrm(ks[17], (N_ODD, ODD_IN), 0.02),
        "odd_w_out": nrm(ks[18], (N_ODD, ODD_OUT, D_MODEL), ODD_OUT ** -0.5),
        "gmlp_ln_g": 1.0 + nrm(ks[19], (N_ODD, GMLP_WIDTH), 0.02),
        "gmlp_ln_b": nrm(ks[20], (N_ODD, GMLP_WIDTH), 0.02),
        "gmlp_w_s": nrm(ks[21], (N_ODD, GMLP_GROUPS, GMLP_CHUNK, GMLP_CHUNK), GMLP_CHUNK ** -0.5),
        "gmlp_b_s": 1.0 + nrm(ks[22], (N_ODD, GMLP_GROUPS, GMLP_CHUNK), 0.02),
        "conv_w": nrm(ks[23], (N_ODD, CONV_KERNEL, CONV_CH), CONV_KERNEL ** -0.5),
        "conv_b": nrm(ks[24], (N_ODD, CONV_CH), 0.02),
        "conv_ln_g": 1.0 + nrm(ks[25], (N_ODD, CONV_CH), 0.02),
        "conv_ln_b": nrm(ks[26], (N_ODD, CONV_CH), 0.02),
    }


def reference(x, attn_norm_g, ffn_norm_g, ffn_w_in, ffn_w_out, even_w_in, even_w_out,
              diff_q_norm_g, diff_k_norm_g, diff_lambda_q1, diff_lambda_k1,
              diff_lambda_q2, diff_lambda_k2, diff_subln_g, moba_q_norm_g, moba_k_norm_g,
              odd_w_in, odd_b_in, odd_w_out, gmlp_ln_g, gmlp_ln_b, gmlp_w_s, gmlp_b_s,
              conv_w, conv_b, conv_ln_g, conv_ln_b):
    for i in range(DEPTH):
        h = rms_norm(x, attn_norm_g[i])
        if i % 2 == 0:
            e = i // 2
            lambda_init = 0.8 - 0.6 * math.exp(-0.3 * i)
            mix = even_mixer(h, even_w_in[e], even_w_out[e], diff_q_norm_g[e], diff_k_norm_g[e],
                             diff_lambda_q1[e], diff_lambda_k1[e], diff_lambda_q2[e], diff_lambda_k2[e],
                             diff_subln_g[e], moba_q_norm_g[e], moba_k_norm_g[e], lambda_init)
        else:
            o = i // 2
            mix = odd_mixer(h, odd_w_in[o], odd_b_in[o], odd_w_out[o], gmlp_ln_g[o], gmlp_ln_b[o],
                            gmlp_w_s[o], gmlp_b_s[o], conv_w[o], conv_b[o], conv_ln_g[o], conv_ln_b[o])
        x = x + mix
        x = x + swiglu(rms_norm(x, ffn_norm_g[i]), ffn_w_in[i], ffn_w_out[i])
    return x
```

```python
import numpy as np
import concourse.bass as bass
import concourse.mybir as mybir
from concourse.bass_utils import run_bass_kernel_spmd

F32 = mybir.dt.float32
BF16 = mybir.dt.bfloat16
AF = mybir.ActivationFunctionType
ALU = mybir.AluOpType
AX = mybir.AxisListType

ENGS = ["pe", "act", "dve", "pool", "sp"]
EPOCH = 30000


class Prog:
    def __init__(self, nc):
        self.nc = nc
        self.ops = {e: [] for e in ENGS}
        self.count = {e: 0 for e in ENGS}
        self.seen = {e: {} for e in ENGS}
        self.res = {}
        self.lanes = {}
        self.semkeys = []
        self.lane_waited = {}
        self.lane_sem = {}
        self.free_sems = []
        self.nsem = 0
        self._cap = None

    def _deps(self, eng, reads, writes):
        deps = []
        for r in reads:
            st = self.res.get(r)
            if st and st[0] is not None:
                deps.append(st[0])
        for w in writes:
            st = self.res.get(w)
            if st:
                if st[0] is not None:
                    deps.append(st[0])
                deps.extend(st[1].values())
        need = {}
        for (k, v) in deps:
            if eng == "pe" and k == ("pe", "raw"):
                continue
            if self.seen[eng].get(k, 0) < v:
                need[k] = max(need.get(k, 0), v)
        for k, v in need.items():
            self.ops[eng].append(("wait", k, v))
            self.seen[eng][k] = v
            if k[0] == "dma":
                self.lane_waited[k[1]] = max(self.lane_waited.get(k[1], 0), v)

    def _mark(self, tok, reads, writes, rkey):
        for r in reads:
            st = self.res.setdefault(r, [None, {}])
            st[1][rkey] = tok
        for w in writes:
            self.res[w] = [tok, {}]

    def capture(self, f):
        old = self._cap
        self._cap = []
        f()
        lst = self._cap
        self._cap = old
        return lst

    def replay(self, item):
        if item[0] == "op":
            self.op(*item[1:])
        elif item[0] == "dmag":
            self.dma_group(*item[1:])
        else:
            for it in item[1]:
                self.replay(it)

    def op(self, eng, fn, reads=(), writes=()):
        if self._cap is not None:
            self._cap.append(("op", eng, fn, list(reads), list(writes)))
            return None
        self._deps(eng, reads, writes)
        self.count[eng] += 1
        c = self.count[eng]
        key = (eng, "raw")
        tok = (key, c)
        self.ops[eng].append(("op", fn, key, 1, c))
        self._mark(tok, reads, writes, eng)
        return tok

    def dma(self, eng, lane, fn, reads=(), writes=(), multi=False):
        return self.dma_group(eng, lane, [fn], reads, writes)

    def dma_group(self, eng, lane, fns, reads=(), writes=()):
        if self._cap is not None:
            self._cap.append(("dmag", eng, lane, list(fns), list(reads), list(writes)))
            return None
        self._deps(eng, reads, writes)
        if lane not in self.lane_sem:
            self.lane_sem[lane] = (self.nsem, 0)
            self.nsem += 1
        semid, base = self.lane_sem[lane]
        n = self.lanes.get(lane, 0) + len(fns)
        self.lanes[lane] = n
        key = ("dma", semid)
        if key not in self.semkeys:
            self.semkeys.append(key)
        tok = (key, base + 16 * n)
        for fn in fns:
            self.ops[eng].append(("op", fn, key, 16, None))
        self._mark(tok, reads, writes, ("dma", lane))
        return tok

    def retire_lanes(self):
        toks = []
        for lane, n in self.lanes.items():
            semid, base = self.lane_sem[lane]
            toks.append((("dma", semid), base + 16 * n))
        return toks

    def recycle_lanes(self):
        pass

    def wait_all(self, eng, toks):
        for (k, v) in toks:
            if self.seen[eng].get(k, 0) < v:
                self.ops[eng].append(("wait", k, v))
                self.seen[eng][k] = v
                if k[0] == "dma":
                    self.lane_waited[k[1]] = max(self.lane_waited.get(k[1], 0), v)

    def emit(self):
        nc = self.nc
        from contextlib import ExitStack
        waited = {e: set() for e in ENGS}
        for name in ENGS:
            for it in self.ops[name]:
                if it[0] == "wait" and it[1][1] == "raw":
                    waited[it[1][0]].add(it[2])
        rank = {}
        for e_ in ENGS:
            for i, c in enumerate(sorted(waited[e_])):
                rank[(e_, c)] = i + 1
        semkeys = list(self.semkeys)
        for (e_, c), r in rank.items():
            k = (e_, (r - 1) // EPOCH)
            if k not in semkeys:
                semkeys.append(k)
        with ExitStack() as es:
            sems = {}
            for k in semkeys:
                nm = "s_" + "_".join(str(x) for x in k)
                sems[k] = es.enter_context(nc.semaphore(nm))
            block = es.enter_context(nc.Block())

            def run(e, name):
                for it in self.ops[name]:
                    if it[0] == "wait":
                        if it[1][1] == "raw":
                            r = rank[(it[1][0], it[2])]
                            e.wait_ge(sems[(it[1][0], (r - 1) // EPOCH)], (r - 1) % EPOCH + 1)
                        else:
                            e.wait_ge(sems[it[1]], it[2])
                    else:
                        ins = it[1](e)
                        if it[4] is None:
                            ins.then_inc(sems[it[2]], it[3])
                        else:
                            r = rank.get((name, it[4]))
                            if r is not None:
                                ins.then_inc(sems[(name, (r - 1) // EPOCH)], 1)

            @block.tensor
            def _(e):
                run(e, "pe")

            @block.scalar
            def _(e):
                run(e, "act")

            @block.vector
            def _(e):
                run(e, "dve")

            @block.gpsimd
            def _(e):
                run(e, "pool")

            @block.sync
            def _(e):
                run(e, "sp")


NQT = 17
NQ = NQT * 128
NKT = 32
NK = 4096
D = 1024
DFF = 2816
EPS = 1e-6
NEG = -30000.0
LAMBDA_INIT0 = 0.8 - 0.6 * 1.0


class Arena:
    def __init__(self, ap, nbytes):
        self.ap = ap
        self.cap = nbytes
        self.off = 0

    def reset(self):
        self.off = 0

    def alloc(self, shape_free, dt):
        n = 1
        for s in shape_free:
            n *= s
        esz = 4 if dt == F32 else 2
        self.off = (self.off + 63) // 64 * 64
        nb = n * esz
        assert self.off + nb <= self.cap, ("arena overflow", self.off, nb, self.cap)
        v = self.ap[:, self.off // 2:(self.off + nb) // 2]
        self.off += nb
        if dt == F32:
            v = v.bitcast(F32)
        if len(shape_free) == 2:
            v = v.rearrange("p (a b) -> p a b", b=shape_free[1])
        elif len(shape_free) == 3:
            v = v.rearrange("p (a b c) -> p a b c", b=shape_free[1], c=shape_free[2])
        return v


def build_program(stop_after=None, debug=False):
    nc = bass.Bass("TRN2", target_bir_lowering=False)
    dbg_kind = "ExternalOutput" if debug else "Internal"

    def din(name, shape, dt=F32):
        return nc.dram_tensor(name, list(shape), dt, kind="ExternalInput").ap()

    def dscr(name, shape, dt):
        return nc.dram_tensor(name, list(shape), dt, kind=dbg_kind).ap()

    xin = din("xin", [NK, D])
    cs64 = din("cs64", [NK, 64])
    cs32 = din("cs32", [NK, 32])
    flags = din("flags", [128, 20])
    onehot = din("onehot", [16, NK], BF16)
    e_w_in = din("even_w_in", [D, 3072])
    e_w_out = din("even_w_out", [D, D])
    f_w_in = din("ffn_w_in", [2, D, 2 * DFF])
    f_w_out = din("ffn_w_out", [2, DFF, D])
    o_w_in = din("odd_w_in", [D, 2048])
    o_w_out = din("odd_w_out", [D, D])
    attn_g = din("attn_norm_g", [2, D])
    ffn_g = din("ffn_norm_g", [2, D])
    gains = din("gains", [1, 320])
    lams = din("lams", [1, 128])
    o_b_row = din("odd_b_row", [1, 2048])
    o_b_col = din("odd_b_col", [128, 8])
    gl_gb = din("gmlp_ln_gb", [1, 1024])
    wsT_in = din("gmlp_wsT", [128, 8, 128])
    bsT_in = din("gmlp_bsT", [128, 8])
    cwT = din("conv_wT", [128, 4, 31])
    cvec = din("conv_vecs", [1, 1536])
    out = nc.dram_tensor("out", [2048, D], F32, kind="ExternalOutput").ap()

    QdT = dscr("QdT", [512, NQ], BF16)
    KdT = dscr("KdT", [512, NK], BF16)
    Vd = dscr("Vd", [8, NK, 64], BF16)
    QmT = dscr("QmT", [512, NQ], BF16)
    KmT = dscr("KmT", [512, NK], BF16)
    Vm = dscr("Vm", [8, NK, 64], BF16)
    X1 = dscr("X1", [NQ, D], F32)
    X2 = dscr("X2", [NQ, D], F32)
    X3 = dscr("X3", [2048, D], F32)

    ARENA_BYTES = 190 * 1024
    arena_t = nc.alloc_sbuf_tensor("arena", [128, ARENA_BYTES // 2], BF16)
    AR = Arena(arena_t.ap(), ARENA_BYTES)
    PSALL = nc.alloc_psum_tensor("psall", [128, 4096], F32).ap()
    PS = [PSALL[:, i * 512:(i + 1) * 512] for i in range(8)]
    PSN = ["ps%d" % i for i in range(8)]

    P = Prog(nc)
    uid = [0]

    def nm(s):
        uid[0] += 1
        return "%s#%d" % (s, uid[0])

    lane_ctr = [0]

    def newlane(s="l"):
        lane_ctr[0] += 1
        return "%s%d" % (s, lane_ctr[0])

    def barrier():
        toks = []
        for e in ENGS:
            c = P.count[e]
            if c > 0:
                toks.append(((e, "raw"), c))
        toks.extend(P.retire_lanes())
        for e in ENGS:
            P.wait_all(e, toks)
        P.res.clear()
        P.recycle_lanes()

    ident = AR.alloc([128], BF16)
    flg = AR.alloc([20], F32)
    rnd = AR.alloc([8], F32)
    PERSIST = None

    P.op("pool", lambda e: e.memset(ident, 1.0), writes=["ident"])
    P.op("pool", lambda e: e.affine_select(out=ident, in_=ident, pattern=[[-1, 128]], compare_op=ALU.is_equal,
                                           fill=0.0, base=0, channel_multiplier=1), reads=["ident"], writes=["ident"])
    P.dma("sp", "c_flg", lambda e: e.dma_start(out=flg, in_=flags), writes=["flg"])
    AR.off = (AR.off + 63) // 64 * 64
    PERSIST = AR.off

    def reset_arena():
        AR.off = PERSIST

    def rms_rstd(ssq, ssq_name, rs, rs_name, hd_eps):
        P.op("act", lambda e: e.activation(out=rs, in_=ssq, func=AF.Ln, bias=float(hd_eps), scale=1.0),
             reads=[ssq_name], writes=[rs_name])
        P.op("act", lambda e: e.activation(out=rs, in_=rs, func=AF.Exp, scale=-0.5),
             reads=[rs_name], writes=[rs_name])

    def named(ap, name):
        return ap

    def norm_tile(xt, xt_n, gbc, gbc_n, hb, hb_n, sq, sq_n, ssq, rs, pfx):
        P.op("act", lambda e: e.activation(out=sq, in_=xt, func=AF.Square, accum_out=ssq),
             reads=[xt_n], writes=[sq_n, pfx + "_ssq"])
        rms_rstd(ssq, pfx + "_ssq", rs, pfx + "_rs", D * EPS)
        P.op("dve", lambda e: e.scalar_tensor_tensor(out=hb, in0=xt, scalar=rs, in1=gbc, op0=ALU.mult, op1=ALU.mult),
             reads=[xt_n, pfx + "_rs", gbc_n], writes=[hb_n])

    def transpose8(src, src_n, dst, dst_n, psi, nblk=8, evac="act"):
        psb = PS[psi].bitcast(BF16)
        for k in range(nblk):
            P.op("pe", lambda e, k=k: e.transpose(out=psb[:, k * 128:(k + 1) * 128], in_=src[:, k * 128:(k + 1) * 128],
                                                  identity=ident), reads=[src_n, "ident"], writes=[PSN[psi]])
        if evac == "act":
            P.op("act", lambda e: e.activation(out=dst, in_=psb[:, 0:nblk * 128].rearrange("p (a b) -> p a b", b=128),
                                               func=AF.Copy), reads=[PSN[psi]], writes=[dst_n])
        else:
            P.op("dve", lambda e: e.tensor_copy(out=dst, in_=psb[:, 0:nblk * 128].rearrange("p (a b) -> p a b", b=128)),
                 reads=[PSN[psi]], writes=[dst_n])

    def load_gain_bc(dst, dst_n, src_row, lane, scale):
        P.dma("sp", lane, lambda e: e.dma_start(out=dst, in_=src_row.partition_broadcast(128)), writes=[dst_n])
        if scale != 1.0:
            P.op("dve", lambda e: e.tensor_scalar(out=dst, in0=dst, scalar1=float(scale), scalar2=None, op0=ALU.mult),
                 reads=[dst_n], writes=[dst_n])

    def load_w_bf16(dst, dst_n, w_ap, lane, nsplit=4):
        K = dst.shape[1]
        wv = w_ap.rearrange("(k p) n -> p k n", p=128)
        step = max(1, K // nsplit)
        toks = []
        for k0 in range(0, K, step):
            k1 = min(K, k0 + step)
            toks.append(P.dma("pool", "%s_%d" % (lane, k0), lambda e, k0=k0, k1=k1: e.dma_start(out=dst[:, k0:k1, :], in_=wv[:, k0:k1, :]),
                              writes=[dst_n + "_%d" % k0], multi=True))
        return toks

    def phase_A():
        reset_arena()
        Wi = AR.alloc([8, 3072], BF16)
        gbc = AR.alloc([D], F32)
        Gq = AR.alloc([4, 512], F32)
        gsm = AR.alloc([320], F32)
        NC_ = 4
        cst = [AR.alloc([96], F32) for _ in range(NC_)]
        gct = [AR.alloc([4, 2, 64], F32) for _ in range(NC_)]
        xt = [AR.alloc([D], F32) for _ in range(NC_)]
        sq = AR.alloc([D], BF16)
        hbs = [AR.alloc([D], BF16) for _ in range(2)]
        hTs = [AR.alloc([8, 128], BF16) for _ in range(2)]
        ssqs = [AR.alloc([1], F32) for _ in range(2)]
        rss = [AR.alloc([1], F32) for _ in range(2)]
        NS = 4
        sqgs = [AR.alloc([512], F32) for _ in range(NS)]
        qns = [AR.alloc([512], F32) for _ in range(NS)]
        tas = [AR.alloc([512], F32) for _ in range(NS)]
        tbs = [AR.alloc([512], F32) for _ in range(NS)]
        ssqhs = [AR.alloc([16], F32) for _ in range(NS)]
        rshs = [AR.alloc([16], F32) for _ in range(NS)]
        qbs = [AR.alloc([512], BF16) for _ in range(NS)]
        qTs = [AR.alloc([4, 128], BF16) for _ in range(NS)]
        vbs = [AR.alloc([512], BF16) for _ in range(NS)]

        load_w_bf16(Wi, "A_Wi", e_w_in, "A_w", nsplit=8)
        wnames = ["A_Wi_%d" % k for k in range(8)]
        load_gain_bc(gbc, "A_gbc", attn_g[0:1, :], "A_g", 32.0)
        P.dma("sp", "A_gs", lambda e: e.dma_start(out=gsm, in_=gains.partition_broadcast(128)), writes=["A_gsm"])
        for gi, (o, hd) in enumerate([(0, 32), (32, 32), (128, 64), (192, 64)]):
            nh = 512 // hd
            P.op("dve", lambda e, gi=gi, o=o, hd=hd, nh=nh: e.tensor_scalar(
                out=Gq[:, gi, :].rearrange("p (a b) -> p a b", b=hd),
                in0=gsm[:, o:o + hd].unsqueeze(1).to_broadcast([128, nh, hd]),
                scalar1=float(hd) ** 0.5, scalar2=None, op0=ALU.mult), reads=["A_gsm"], writes=["A_Gq%d" % gi])

        Gs = AR.alloc([4, 64], F32)
        for gi, (o, hd) in enumerate([(0, 32), (32, 32), (128, 64), (192, 64)]):
            P.op("dve", lambda e, gi=gi, o=o, hd=hd: e.tensor_scalar(out=Gs[:, gi, 0:hd], in0=gsm[:, o:o + hd], scalar1=float(hd) ** 0.5,
                                                                    scalar2=None, op0=ALU.mult), reads=["A_gsm"], writes=["A_Gs"])
        qdt_v = QdT.rearrange("(c p) n -> p c n", p=128)
        kdt_v = KdT.rearrange("(c p) n -> p c n", p=128)
        qmt_v = QmT.rearrange("(c p) n -> p c n", p=128)
        kmt_v = KmT.rearrange("(c p) n -> p c n", p=128)
        vd_v = Vd.rearrange("h k d -> k h d")
        vm_v = Vm.rearrange("h k d -> k h d")

        def prologue(t):
            b2 = t % 2
            c4 = t % NC_
            xtn = "A_xt%d" % c4
            csn = "A_cs%d" % c4
            hb, hT, hbn, hTn = hbs[b2], hTs[b2], "A_hb%d" % b2, "A_hT%d" % b2
            ssq, rs, pfx = ssqs[b2], rss[b2], "A%d" % b2

            def st0():
                P.dma_group("sp", "A_in%d" % c4, [
                    lambda e: e.dma_start(out=xt[c4], in_=xin[t * 128:(t + 1) * 128, :]),
                    lambda e: e.dma_start(out=cst[c4][:, 0:64], in_=cs64[t * 128:(t + 1) * 128, :]),
                    lambda e: e.dma_start(out=cst[c4][:, 64:96], in_=cs32[t * 128:(t + 1) * 128, :])],
                    writes=[xtn, csn + "a", csn + "b"])

            def st1():
                P.op("act", lambda e: e.activation(out=sq, in_=xt[c4], func=AF.Square, accum_out=ssq), reads=[xtn], writes=["A_sq", pfx + "_ssq"])
                rms_rstd(ssq, pfx + "_ssq", rs, pfx + "_rs", D * EPS)
                for gi_, hd_ in enumerate((32, 32, 64, 64)):
                    hf_ = hd_ // 2
                    co_ = 64 if hd_ == 32 else 0
                    for cs_ in range(2):
                        P.op("pool", lambda e, gi_=gi_, hd_=hd_, hf_=hf_, co_=co_, cs_=cs_: e.tensor_tensor(
                            out=gct[c4][:, gi_, cs_, 0:hd_].rearrange("p (b c) -> p b c", c=hf_),
                            in0=Gs[:, gi_, 0:hd_].rearrange("p (b c) -> p b c", c=hf_),
                            in1=cst[c4][:, co_ + cs_ * hf_:co_ + (cs_ + 1) * hf_].unsqueeze(1).to_broadcast([128, 2, hf_]), op=ALU.mult),
                            reads=["A_Gs", csn + "a", csn + "b"], writes=[csn + "g"])

            def st2():
                P.op("dve", lambda e: e.scalar_tensor_tensor(out=hb, in0=xt[c4], scalar=rs, in1=gbc, op0=ALU.mult, op1=ALU.mult),
                     reads=[xtn, pfx + "_rs", "A_gbc"], writes=[hbn])

            def st3():
                transpose8(hb, hbn, hT, hTn, 7)
            return [st0, st1, st2, st3]

        def group_item(t, g, gidx, qidx):
            b2 = t % 2
            c4 = t % NC_
            csn = "A_cs%d" % c4
            hT, hTn = hTs[b2], "A_hT%d" % b2
            psi = gidx % 5
            ps = PS[psi]
            z = gidx % NS

            def s_mm():
                for k in range(8):
                    P.op("pe", lambda e, k=k: e.matmul(ps, lhsT=hT[:, k, :], rhs=Wi[:, k, g * 512:(g + 1) * 512], start=(k == 0), stop=(k == 7)),
                         reads=[hTn, wnames[k]], writes=[PSN[psi]])
            if g in (2, 5):
                vb, vbn = vbs[z], "A_vb%d" % z
                dst = vd_v if g == 2 else vm_v

                def s_cp():
                    P.op("act", lambda e: e.activation(out=vb, in_=ps, func=AF.Copy), reads=[PSN[psi]], writes=[vbn])

                def s_st():
                    P.dma("sp", "A_vs%d" % z, lambda e: e.dma_start(out=dst[t * 128:(t + 1) * 128, :, :], in_=vb.rearrange("p (h d) -> p h d", d=64)),
                          reads=[vbn])
                return [s_mm, s_cp, s_st]
            isd = g in (0, 1)
            hd = 32 if isd else 64
            half = hd // 2
            nh = 512 // hd
            gi = {0: 0, 1: 1, 3: 2, 4: 3}[g]
            sqg, qn, ta, tb, ssqh, rsh, qb, qT = sqgs[z], qns[z], tas[z], tbs[z], ssqhs[z], rshs[z], qbs[z], qTs[z]
            n_sqg, n_qn, n_ta, n_tb, n_ssqh, n_rsh, n_qb, n_qT = ["A_%s%d" % (x, z) for x in ("sqg", "qn", "ta", "tb", "ssqh", "rsh", "qb", "qT")]
            ssq_v = ssqh[:, 0:nh]
            rs_v = rsh[:, 0:nh]
            co = 64 if isd else 0
            cosv = cst[c4][:, co:co + half]
            sinv = cst[c4][:, co + half:co + 2 * half]
            q4 = qn.rearrange("p (a b c) -> p a b c", b=2, c=half)
            ta4 = ta.rearrange("p (a b c) -> p a b c", b=2, c=half)
            tb4 = tb.rearrange("p (a b c) -> p a b c", b=2, c=half)
            qb4 = qb.rearrange("p (a b c) -> p a b c", b=2, c=half)
            tpi = 5 + (gidx % 2)
            psb = PS[tpi].bitcast(BF16)
            if g in (0, 3):
                dstv = qdt_v if g == 0 else qmt_v
                col = qidx * 128
            else:
                dstv = kdt_v if g == 1 else kmt_v
                col = t * 128

            def s1():
                P.op("act", lambda e: e.activation(out=sqg, in_=ps, func=AF.Square), reads=[PSN[psi]], writes=[n_sqg])

            def s2():
                P.op("dve", lambda e: e.tensor_reduce(out=ssq_v, in_=sqg.rearrange("p (a b) -> p a b", b=hd), axis=AX.X, op=ALU.add),
                     reads=[n_sqg], writes=[n_ssqh])

            def s3():
                rms_rstd(ssq_v, n_ssqh, rs_v, n_rsh, hd * EPS)

            def s4():
                P.op("dve", lambda e: e.tensor_tensor(out=qn.rearrange("p (a b) -> p a b", b=hd), in0=ps.rearrange("p (a b) -> p a b", b=hd),
                                                      in1=rs_v.unsqueeze(2).to_broadcast([128, nh, hd]), op=ALU.mult),
                     reads=[PSN[psi], n_rsh], writes=[n_qn])

            gcv = gct[c4][:, gi, 0, 0:hd].rearrange("p (b c) -> p b c", c=half)
            gsv = gct[c4][:, gi, 1, 0:hd].rearrange("p (b c) -> p b c", c=half)

            def s5():
                pass

            def s6():
                P.op("dve", lambda e: e.tensor_tensor(out=ta4, in0=q4, in1=gcv.unsqueeze(1).to_broadcast([128, nh, 2, half]),
                                                      op=ALU.mult), reads=[n_qn, csn + "g"], writes=[n_ta])
                P.op("pool", lambda e: e.tensor_tensor(out=tb4, in0=q4, in1=gsv.unsqueeze(1).to_broadcast([128, nh, 2, half]),
                                                       op=ALU.mult), reads=[n_qn, csn + "g"], writes=[n_tb])

            def s7():
                P.op("dve", lambda e: e.tensor_tensor(out=qb4[:, :, 0, :], in0=ta4[:, :, 0, :], in1=tb4[:, :, 1, :], op=ALU.subtract),
                     reads=[n_ta, n_tb], writes=[n_qb + "a"])
                P.op("pool", lambda e: e.tensor_tensor(out=qb4[:, :, 1, :], in0=ta4[:, :, 1, :], in1=tb4[:, :, 0, :], op=ALU.add),
                     reads=[n_ta, n_tb], writes=[n_qb + "b"])

            def s8():
                for k in range(4):
                    P.op("pe", lambda e, k=k: e.transpose(out=psb[:, k * 128:(k + 1) * 128], in_=qb[:, k * 128:(k + 1) * 128], identity=ident),
                         reads=[n_qb + "a", n_qb + "b", "ident"], writes=[PSN[tpi]])

            def s9():
                P.op("act", lambda e: e.activation(out=qT, in_=psb[:, 0:512].rearrange("p (a b) -> p a b", b=128), func=AF.Copy),
                     reads=[PSN[tpi]], writes=[n_qT])

            def s10():
                P.dma("sp", "A_qs%d" % z, lambda e: e.dma_start(out=dstv[:, :, col:col + 128], in_=qT), reads=[n_qT])
            return [s_mm, s1, s2, s3, s4, s5, s6, s7, s8, s9, s10]

        items = []
        extra = {}
        first_group_of_tile = {}
        for t in range(NKT):
            if t >= 16:
                qidx = t - 16
            elif t == 15:
                qidx = 16
            else:
                qidx = None
            groups = [1, 2, 4, 5] if qidx is None else [0, 1, 2, 3, 4, 5]
            first_group_of_tile[t] = len(items)
            for g in groups:
                items.append(group_item(t, g, len(items), qidx))
        for st in prologue(0):
            st()
        for t in range(NKT - 1):
            g0 = first_group_of_tile[t]
            for k, st in enumerate(prologue(t + 1)):
                extra.setdefault(g0 + k, []).append(st)
        maxs = max(len(it) for it in items)
        for step in range(len(items) + maxs):
            for sidx in range(maxs - 1, -1, -1):
                g = step - sidx
                if 0 <= g < len(items) and sidx < len(items[g]):
                    items[g][sidx]()
            for st in extra.get(step, []):
                st()

    phase_A()
    barrier()
    fin = []
    if stop_after == "A":
        P.emit()
        return nc

    def phase_BC():
        reset_arena()
        Wo = AR.alloc([8, D], BF16)
        C = AR.alloc([NQT, D], BF16)
        cm = AR.alloc([4, 512], BF16)
        kT = [AR.alloc([NK], BF16) for _ in range(2)]
        vbuf = [AR.alloc([NKT, 128], BF16) for _ in range(2)]
        qa = [AR.alloc([NQ], BF16) for _ in range(2)]
        qb_ = [AR.alloc([NQ], BF16) for _ in range(2)]
        pT = [AR.alloc([2, 512], BF16) for _ in range(2)]
        oT = [AR.alloc([512], BF16) for _ in range(2)]
        gsm = AR.alloc([320], F32)
        lamv = AR.alloc([128], F32)
        lt = AR.alloc([64], F32)
        lsm = AR.alloc([8], F32)
        Gsub = AR.alloc([4, 64], F32)
        gbj = AR.alloc([9, 16], F32)
        biaspad = AR.alloc([NQT, 128], BF16)
        gball = AR.alloc([NQT, 16], F32)
        ownm = AR.alloc([NQT, 16], F32)
        gmall = AR.alloc([NQT, 16], F32)
        cmpb = AR.alloc([NQT, 16, 16], BF16)
        rnk = AR.alloc([NQT, 16], F32)
        bia1 = AR.alloc([NQT, 16], F32)
        bia2 = AR.alloc([NQT, 16], F32)
        zt = AR.alloc([128], BF16)
        kbs = AR.alloc([16], F32)
        kbb = AR.alloc([16], BF16)
        gm = AR.alloc([16], F32)
        top8 = AR.alloc([8], F32)
        thr = AR.alloc([1], F32)
        rcp = AR.alloc([8], F32)
        oa = AR.alloc([4, 64], F32)
        ob = AR.alloc([4, 64], F32)
        dd = AR.alloc([4, 64], F32)
        sqd = AR.alloc([4, 64], F32)
        ssq4 = AR.alloc([4], F32)
        rs4 = AR.alloc([4], F32)
        CT = AR.alloc([8, 128], BF16)
        xr = [AR.alloc([D], F32) for _ in range(2)]
        x1 = [AR.alloc([D], F32) for _ in range(2)]

        load_w_bf16(Wo, "B_Wo", e_w_out, "B_w", nsplit=2)
        wo_names = ["B_Wo_0"] * 4 + ["B_Wo_4"] * 4
        P.op("pool", lambda e: e.memset(cm, 0.0), writes=["cm"])
        for i in range(4):
            P.op("pool", lambda e, i=i: e.affine_select(out=cm[:, i, :], in_=cm[:, i, :], pattern=[[1, 512]], compare_op=ALU.is_ge,
                                                        fill=NEG, base=-128 * i, channel_multiplier=-1), reads=["cm"], writes=["cm"])
        for b in range(2):
            P.op("pool", lambda e, b=b: e.memset(qa[b][32:64, :], 0.0), writes=["qa%d_z" % b])
            P.op("pool", lambda e, b=b: e.memset(qa[b][64:128, :], 0.0), writes=["qa%d_bias" % b])
            P.op("pool", lambda e, b=b: e.memset(qb_[b][0:32, :], 0.0), writes=["qb%d_z" % b])
            P.op("pool", lambda e, b=b: e.memset(qb_[b][64:128, :], 0.0), writes=["qb%d_z" % b])
            P.op("pool", lambda e, b=b: e.memset(vbuf[b][:, :, 64:128], 0.0), writes=["vb%d_1" % b])
            P.op("pool", lambda e, b=b: e.memset(vbuf[b][:, :, 64:65], 1.0), reads=["vb%d_1" % b], writes=["vb%d_1" % b])
            P.op("pool", lambda e, b=b: e.memset(kT[b][64:128, :], 0.0), writes=["kT%d_oh" % b])
            P.dma("sp", "B_oh%d" % b, lambda e, b=b: e.dma_start(out=kT[b][64:80, :], in_=onehot), writes=["kT%d_oh" % b])
        P.op("pool", lambda e: e.memset(biaspad, 0.0), writes=["biaspad"])
        P.op("pool", lambda e: e.memset(zt, 0.0), writes=["zt"])
        P.dma("sp", "B_gs", lambda e: e.dma_start(out=gsm, in_=gains.partition_broadcast(128)), writes=["B_gsm"])
        P.dma("sp", "B_lm", lambda e: e.dma_start(out=lamv, in_=lams.partition_broadcast(128)), writes=["B_lamv"])
        l4 = lamv.rearrange("p (a b c) -> p a b c", b=2, c=32)
        P.op("dve", lambda e: e.tensor_tensor(out=lt.rearrange("p (a c) -> p a c", c=32), in0=l4[:, :, 0, :], in1=l4[:, :, 1, :],
                                              op=ALU.mult), reads=["B_lamv"], writes=["B_lt"])
        P.op("dve", lambda e: e.tensor_reduce(out=lsm[:, 0:2], in_=lt.rearrange("p (a c) -> p a c", c=32), axis=AX.X, op=ALU.add),
             reads=["B_lt"], writes=["B_lsm"])
        P.op("act", lambda e: e.activation(out=lsm[:, 2:4], in_=lsm[:, 0:2], func=AF.Exp), reads=["B_lsm"], writes=["B_lsm2"])
        P.op("dve", lambda e: e.tensor_tensor(out=lsm[:, 4:5], in0=lsm[:, 3:4], in1=lsm[:, 2:3], op=ALU.subtract),
             reads=["B_lsm2"], writes=["B_lsm3"])
        neglam = lsm[:, 5:6]
        P.op("dve", lambda e: e.tensor_scalar(out=neglam, in0=lsm[:, 4:5], scalar1=-LAMBDA_INIT0, scalar2=None, op0=ALU.add),
             reads=["B_lsm3"], writes=["neglam"])
        P.op("dve", lambda e: e.tensor_scalar(out=Gsub, in0=gsm[:, 64:128].unsqueeze(1).to_broadcast([128, 4, 64]),
                                              scalar1=8.0 * (1.0 - LAMBDA_INIT0), scalar2=None, op0=ALU.mult),
             reads=["B_gsm"], writes=["Gsub"])
        for j in range(8):
            P.op("dve", lambda e, j=j: e.tensor_copy(out=gbj[:, j, :], in_=flg[:, 4:20]), reads=["flg"], writes=["gbj"])
            P.op("dve", lambda e, j=j: e.memset(gbj[:, j, 8 + j:16], -1e30), reads=["gbj"], writes=["gbj"])
        P.op("dve", lambda e: e.memset(gbj[:, 8, :], 0.0), reads=["gbj"], writes=["gbj"])
        P.op("dve", lambda e: e.memset(gbj[:, 8, 7:16], -1e30), reads=["gbj"], writes=["gbj"])

        P.op("dve", lambda e: e.memset(ownm, 1.0), writes=["ownm"])
        for qi in range(NQT):
            jj = qi // 2 if qi < 16 else 8
            ownb = 8 + qi // 2 if qi < 16 else 7
            P.op("dve", lambda e, qi=qi, jj=jj: e.tensor_copy(out=gball[:, qi, :], in_=gbj[:, jj, :]), reads=["gbj"], writes=["gball"])
            P.op("dve", lambda e, qi=qi, ownb=ownb: e.memset(ownm[:, qi, ownb:ownb + 1], 0.0), reads=["ownm"], writes=["ownm"])
        units = [("d", h) for h in range(8)] + [("m", h) for h in range(8)]
        groups = [(gi * 512, 512, 20 + 4 * gi, True, gi * 4) for gi in range(4)] + [(2048, 128, 16, False, 16)]
        gcount = [0]
        sc = [0]

        def issue_loads(u):
            kind, h = units[u]
            b = u % 2
            Ksrc, Vsrc, Qsrc = (KdT, Vd, QdT) if kind == "d" else (KmT, Vm, QmT)
            fns = [lambda e: e.dma_start(out=kT[b][0:64, :], in_=Ksrc[h * 64:(h + 1) * 64, :])]
            for pt in range(4):
                fns.append(lambda e, pt=pt: e.dma_start(out=vbuf[b][:, pt * 8:(pt + 1) * 8, 0:64],
                                                       in_=Vsrc[h].rearrange("(t p) d -> p t d", p=128)[:, pt * 8:(pt + 1) * 8, :]))
            wr = ["kT%d" % b] + ["vb%d_%d" % (b, pt) for pt in range(4)]
            if kind == "d":
                fns.append(lambda e: e.dma_start(out=qa[b][0:32, :], in_=Qsrc[h * 64:h * 64 + 32, :]))
                fns.append(lambda e: e.dma_start(out=qb_[b][32:64, :], in_=Qsrc[h * 64 + 32:h * 64 + 64, :]))
                wr += ["qa%d" % b, "qb%d" % b]
            else:
                fns.append(lambda e: e.dma_start(out=qa[b][0:64, :], in_=Qsrc[h * 64:(h + 1) * 64, :]))
                wr += ["qa%d" % b, "qa%d_z" % b]
            P.dma_group("sp", "B_ld%d" % b, fns, writes=wr)

        def gating_stages(u):
            kind, h = units[u]
            b = u % 2
            ps6b = PS[6].bitcast(BF16)

            def g1():
                P.op("dve", lambda e: e.tensor_reduce(out=kbs[0:64, :], in_=kT[b][0:64, :].rearrange("p (a c) -> p a c", c=256),
                                                      axis=AX.X, op=ALU.add), reads=["kT%d" % b], writes=["kbs"])
                P.op("dve", lambda e: e.tensor_scalar(out=kbb[0:64, :], in0=kbs[0:64, :], scalar1=1.0 / 256.0, scalar2=None, op0=ALU.mult),
                     reads=["kbs"], writes=["kbb"])
                for qi in range(NQT):
                    P.op("pe", lambda e, qi=qi: e.matmul(PS[7][:, qi * 16:(qi + 1) * 16], lhsT=qa[b][0:64, qi * 128:(qi + 1) * 128], rhs=kbb[0:64, :],
                                                        start=True, stop=True), reads=["qa%d" % b, "kbb"], writes=[PSN[7]])

            def g2():
                P.op("dve", lambda e: e.tensor_tensor(out=gmall, in0=PS[7][:, 0:NQT * 16].rearrange("p (q n) -> p q n", n=16), in1=gball, op=ALU.add),
                     reads=[PSN[7], "gball"], writes=["gmall"])
                P.op("dve", lambda e: e.tensor_tensor(out=cmpb, in0=gmall.unsqueeze(2).to_broadcast([128, NQT, 16, 16]),
                                                      in1=gmall.unsqueeze(3).to_broadcast([128, NQT, 16, 16]), op=ALU.is_gt),
                     reads=["gmall"], writes=["cmpb"])
                P.op("dve", lambda e: e.tensor_reduce(out=rnk, in_=cmpb, axis=AX.X, op=ALU.add), reads=["cmpb"], writes=["rnk"])
                P.op("dve", lambda e: e.tensor_scalar(out=bia1, in0=rnk, scalar1=2.5, scalar2=NEG, op0=ALU.is_gt, op1=ALU.mult),
                     reads=["rnk"], writes=["bia1"])
                P.op("dve", lambda e: e.tensor_scalar(out=bia2, in0=gmall, scalar1=-1e29, scalar2=NEG, op0=ALU.is_lt, op1=ALU.mult),
                     reads=["gmall"], writes=["bia2"])
                P.op("dve", lambda e: e.tensor_tensor(out=bia1, in0=bia1, in1=bia2, op=ALU.add), reads=["bia1", "bia2"], writes=["bia1"])
                P.op("dve", lambda e: e.tensor_tensor(out=biaspad[:, :, 64:80], in0=bia1, in1=ownm, op=ALU.mult),
                     reads=["bia1", "ownm"], writes=["biaspad"])

            def g3():
                for (q0, nb) in ((0, 8), (8, 8), (16, 1)):
                    for k in range(nb):
                        P.op("pe", lambda e, q0=q0, k=k: e.transpose(out=ps6b[:, k * 128:(k + 1) * 128], in_=biaspad[:, q0 + k, :], identity=ident),
                             reads=["biaspad", "ident"], writes=[PSN[6]])
                    P.op("act", lambda e, q0=q0, nb=nb: e.activation(out=qa[b][64:80, q0 * 128:(q0 + nb) * 128], in_=ps6b[64:80, 0:nb * 128], func=AF.Copy),
                         reads=[PSN[6]], writes=["qa%d_bias" % b])
            return [g1, g2, g3]

        def attention(u, hooks=()):
            kind, h = units[u]
            b = u % 2
            isd = kind == "d"
            dk = 128
            scale = (32.0 ** -0.5) if isd else 0.125
            for gi_, grp in enumerate(groups):
                if gi_ < len(hooks):
                    hooks[gi_]()
                do_group(kind, h, b, isd, dk, scale, *grp)

        def do_group(kind, h, b, isd, dk, scale, qc0, N, nkt, usepast, t0):
            R = N // 128
            gidx = gcount[0]
            gcount[0] += 1
            if isd:
                maps = [(qa[b], ["qa%d" % b, "qa%d_z" % b, "qa%d_bias" % b], 4), (qb_[b], ["qb%d" % b, "qb%d_z" % b], 5)]
                tbank = 6
            else:
                maps = [(qa[b], ["qa%d" % b, "qa%d_bias" % b], 4)]
                tbank = 5
            npair = nkt // 2
            steps = [(mi, kp) for mi in range(len(maps)) for kp in range(npair)]
            slot = {}
            if isd:
                dsteps = list(range(nkt))

                def dcol0(kt):
                    return 256 if (usepast and kt - (nkt - 4) >= 2) else 0

                def dqk(i):
                    kt = dsteps[i]
                    s_ = sc[0] % 2
                    sc[0] += 1
                    slot[i] = s_
                    c0_ = dcol0(kt)
                    if usepast:
                        di = kt - (nkt - 4)
                    else:
                        di = 0 if kt == nkt - 1 else -1
                    diag = di >= 0
                    for mi, (Q, r0) in enumerate(((qa[b], 0), (qb_[b], 32))):
                        bank = 2 * s_ + mi
                        P.op("pe", lambda e, Q=Q, r0=r0, bank=bank: e.matmul(
                            PS[bank][:, c0_:N], lhsT=kT[b][r0:r0 + 32, kt * 128:(kt + 1) * 128], rhs=Q[r0:r0 + 32, qc0 + c0_:qc0 + N],
                            start=True, stop=not diag), reads=["kT%d" % b, "qa%d" % b, "qb%d" % b], writes=[PSN[bank]])
                    if diag:
                        for mi in range(2):
                            bank = 2 * s_ + mi
                            P.op("pe", lambda e, bank=bank: e.matmul(PS[bank][:, c0_:N], lhsT=ident, rhs=cm[:, di, c0_:N], start=False, stop=True),
                                 reads=["ident", "cm"], writes=[PSN[bank]])

                def dex_pv(i):
                    kt = dsteps[i]
                    s_ = slot[i]
                    c0_ = dcol0(kt)
                    src = PSALL[:, 2 * s_ * 512:(2 * s_ + 2) * 512].rearrange("p (b n) -> p b n", n=512)[:, :, c0_:N]
                    dst = pT[s_][:, :, c0_:N]
                    rd = [PSN[2 * s_], PSN[2 * s_ + 1]]
                    if usepast and kt < 16:
                        P.op("act", lambda e: e.activation(out=dst, in_=src, func=AF.Exp, scale=scale, bias=flg[:, 0:1]),
                             reads=rd + ["flg"], writes=["pT%d" % s_])
                    else:
                        P.op("act", lambda e: e.activation(out=dst, in_=src, func=AF.Exp, scale=scale), reads=rd, writes=["pT%d" % s_])
                    for mi in range(2):
                        abank = maps[mi][2]
                        P.op("pe", lambda e, mi=mi, abank=abank: e.matmul(PS[abank][:, c0_:N], lhsT=vbuf[b][:, kt, :], rhs=pT[s_][:, mi, c0_:N],
                                                                         start=(kt == 0), stop=(kt == nkt - 1)),
                             reads=["pT%d" % s_, "vb%d_%d" % (b, kt // 8), "vb%d_1" % b], writes=[PSN[abank]])

                dqk(0)
                for i in range(len(dsteps)):
                    if i + 1 < len(dsteps):
                        dqk(i + 1)
                    dex_pv(i)
                steps = []

            def col0_of(kp):
                return 256 if (usepast and kp == npair - 1) else 0

            def qk(i):
                mi, kp = steps[i]
                Q, qn_, _ = maps[mi]
                s_ = sc[0] % 2
                sc[0] += 1
                slot[i] = s_
                c0_ = col0_of(kp)
                for hfi in range(2):
                    kt = 2 * kp + hfi
                    bank = 2 * s_ + hfi
                    if usepast:
                        di = kt - (nkt - 4)
                    else:
                        di = 0 if kt == nkt - 1 else -1
                    diag = di >= 0
                    P.op("pe", lambda e, kt=kt, bank=bank, diag=diag: e.matmul(PS[bank][:, c0_:N], lhsT=kT[b][0:dk, kt * 128:(kt + 1) * 128],
                                                                             rhs=Q[0:dk, qc0 + c0_:qc0 + N], start=True, stop=not diag),
                         reads=["kT%d" % b, "kT%d_oh" % b] + qn_, writes=[PSN[bank]])
                    if diag:
                        P.op("pe", lambda e, bank=bank, di=di: e.matmul(PS[bank][:, c0_:N], lhsT=ident, rhs=cm[:, di, c0_:N], start=False, stop=True),
                             reads=["ident", "cm"], writes=[PSN[bank]])

            def ex_pv(i):
                mi, kp = steps[i]
                _, _, abank = maps[mi]
                s_ = slot[i]
                c0_ = col0_of(kp)
                src = PSALL[:, 2 * s_ * 512:(2 * s_ + 2) * 512].rearrange("p (b n) -> p b n", n=512)[:, :, c0_:N]
                dst = pT[s_][:, :, c0_:N]
                rd = [PSN[2 * s_], PSN[2 * s_ + 1]]
                if usepast and 2 * kp < 16:
                    P.op("act", lambda e: e.activation(out=dst, in_=src, func=AF.Exp, scale=scale, bias=flg[:, 0:1]),
                         reads=rd + ["flg"], writes=["pT%d" % s_])
                else:
                    P.op("act", lambda e: e.activation(out=dst, in_=src, func=AF.Exp, scale=scale), reads=rd, writes=["pT%d" % s_])
                for hfi in range(2):
                    kt = 2 * kp + hfi
                    P.op("pe", lambda e, kt=kt, hfi=hfi: e.matmul(PS[abank][:, c0_:N], lhsT=vbuf[b][:, kt, :], rhs=pT[s_][:, hfi, c0_:N],
                                                                  start=(kt == 0), stop=(kt == nkt - 1)),
                         reads=["pT%d" % s_, "vb%d_%d" % (b, kt // 8), "vb%d_1" % b], writes=[PSN[abank]])

            if steps:
                qk(0)
            for i in range(len(steps)):
                if i + 1 < len(steps):
                    qk(i + 1)
                ex_pv(i)
            tpb = PS[tbank].bitcast(BF16)
            for mi, (_, _, abank) in enumerate(maps):
                P.op("dve", lambda e, mi=mi, abank=abank: e.tensor_copy(out=oT[mi][0:65, 0:N], in_=PS[abank][0:65, 0:N]),
                     reads=[PSN[abank]], writes=["oT%d" % mi])
                for r in range(R):
                    c0 = (mi * 4 + r) * 66
                    P.op("pe", lambda e, mi=mi, r=r, c0=c0: e.transpose(out=tpb[:, c0:c0 + 65], in_=oT[mi][0:65, r * 128:(r + 1) * 128],
                                                                      identity=ident[0:65, 0:65]),
                         reads=["oT%d" % mi, "ident"], writes=[PSN[tbank]])
            cn = "C_g%d" % t0
            nt = PSN[tbank]
            accA = tpb[:, 0:R * 66].rearrange("p (r c) -> p r c", c=66)
            if isd:
                accB = tpb[:, 4 * 66:(4 + R) * 66].rearrange("p (r c) -> p r c", c=66)
                P.op("dve", lambda e: e.reciprocal(out=rcp[:, 0:R], in_=accA[:, :, 64]), reads=[nt], writes=["rcpa"])
                P.op("dve", lambda e: e.reciprocal(out=rcp[:, 4:4 + R], in_=accB[:, :, 64]), reads=[nt], writes=["rcpb"])
                P.op("dve", lambda e: e.tensor_tensor(out=oa[:, 0:R, :], in0=accA[:, :, 0:64],
                                                      in1=rcp[:, 0:R].unsqueeze(2).to_broadcast([128, R, 64]), op=ALU.mult),
                     reads=[nt, "rcpa"], writes=["oa"])
                P.op("dve", lambda e: e.tensor_tensor(out=ob[:, 0:R, :], in0=accB[:, :, 0:64],
                                                      in1=rcp[:, 4:4 + R].unsqueeze(2).to_broadcast([128, R, 64]), op=ALU.mult),
                     reads=[nt, "rcpb"], writes=["ob"])
                P.op("dve", lambda e: e.scalar_tensor_tensor(out=dd[:, 0:R, :], in0=ob[:, 0:R, :], scalar=neglam, in1=oa[:, 0:R, :],
                                                             op0=ALU.mult, op1=ALU.add), reads=["oa", "ob", "neglam"], writes=["dd"])
                P.op("pool", lambda e: e.tensor_tensor(out=sqd[:, 0:R, :], in0=dd[:, 0:R, :], in1=dd[:, 0:R, :], op=ALU.mult),
                     reads=["dd"], writes=["sqd"])
                P.op("dve", lambda e: e.tensor_reduce(out=ssq4[:, 0:R], in_=sqd[:, 0:R, :], axis=AX.X, op=ALU.add),
                     reads=["sqd"], writes=["ssq4"])
                rms_rstd(ssq4[:, 0:R], "ssq4", rs4[:, 0:R], "rs4", 64 * EPS)
                P.op("dve", lambda e: e.tensor_tensor(out=dd[:, 0:R, :], in0=dd[:, 0:R, :],
                                                      in1=rs4[:, 0:R].unsqueeze(2).to_broadcast([128, R, 64]), op=ALU.mult),
                     reads=["dd", "rs4"], writes=["dd"])
                P.op("pool", lambda e: e.tensor_tensor(out=C[:, t0:t0 + R, h * 64:(h + 1) * 64], in0=dd[:, 0:R, :], in1=Gsub[:, 0:R, :],
                                                       op=ALU.mult), reads=["dd", "Gsub"], writes=[cn])
            else:
                P.op("dve", lambda e: e.reciprocal(out=rcp[:, 0:R], in_=accA[:, :, 64]), reads=[nt], writes=["rcpa"])
                P.op("dve", lambda e: e.tensor_tensor(out=C[:, t0:t0 + R, 512 + h * 64:512 + (h + 1) * 64], in0=accA[:, :, 0:64],
                                                      in1=rcp[:, 0:R].unsqueeze(2).to_broadcast([128, R, 64]), op=ALU.mult),
                     reads=[nt, "rcpa"], writes=[cn])

        issue_loads(0)
        for u in range(len(units)):
            if u + 1 < len(units):
                issue_loads(u + 1)
            hooks = gating_stages(u + 1) if (u + 1 < len(units) and units[u + 1][0] == "m") else []
            attention(u, hooks)

        CTs = [CT, AR.alloc([8, 128], BF16)]

        def c_pre(tl):
            b2 = tl % 2
            lt_ = 16 + tl if tl < 16 else 15
            cn = "C_g%d" % (tl // 4 * 4 if tl < 16 else 16)
            P.dma("sp", "C_x%d" % b2, lambda e: e.dma_start(out=xr[b2], in_=xin[lt_ * 128:(lt_ + 1) * 128, :]), writes=["C_xr%d" % b2])
            transpose8(C[:, tl, :], cn, CTs[b2], "C_CT%d" % b2, 6 + b2)

        def c_main(tl):
            b2 = tl % 2
            for hh in range(2):
                bank = 2 * b2 + hh
                for k in range(8):
                    P.op("pe", lambda e, k=k, hh=hh, bank=bank: e.matmul(PS[bank], lhsT=CTs[b2][:, k, :], rhs=Wo[:, k, hh * 512:(hh + 1) * 512],
                                                                        start=(k == 0), stop=(k == 7)), reads=["C_CT%d" % b2, wo_names[k]], writes=[PSN[bank]])
                P.op("dve", lambda e, hh=hh, bank=bank: e.tensor_tensor(out=x1[b2][:, hh * 512:(hh + 1) * 512], in0=PS[bank],
                                                                       in1=xr[b2][:, hh * 512:(hh + 1) * 512], op=ALU.add),
                     reads=[PSN[bank], "C_xr%d" % b2], writes=["C_x1%d_%d" % (b2, hh)])
            P.dma("sp", "C_s%d" % b2, lambda e: e.dma_start(out=X1[tl * 128:(tl + 1) * 128, :], in_=x1[b2]),
                  reads=["C_x1%d_0" % b2, "C_x1%d_1" % b2])

        c_pre(0)
        for tl in range(NQT):
            if tl + 1 < NQT:
                c_pre(tl + 1)
            c_main(tl)

    phase_BC()
    barrier()
    if stop_after == "BC":
        P.emit()
        return nc

    def phase_FFN(Xin, Xout, li, ntiles):
        reset_arena()
        gbc = AR.alloc([D], F32)
        xs = [AR.alloc([D], F32) for _ in range(3)]
        xr2 = [AR.alloc([D], F32) for _ in range(2)]
        sq = AR.alloc([D], BF16)
        hbs = [AR.alloc([D], BF16) for _ in range(2)]
        hT = AR.alloc([8, 1152], BF16)
        wo = AR.alloc([22, D], BF16)
        wi = [AR.alloc([8, 256], BF16) for _ in range(3)]
        AT = AR.alloc([22, 1152], BF16)
        sg = [AR.alloc([512], BF16) for _ in range(2)]
        yb = [AR.alloc([D], F32) for _ in range(2)]
        ssqs = [AR.alloc([1], F32) for _ in range(2)]
        rss = [AR.alloc([1], F32) for _ in range(2)]
        pf = "F%d" % li
        load_gain_bc(gbc, pf + "_gbc", ffn_g[li:li + 1, :], pf + "_g", 32.0)
        win_v = f_w_in[li].rearrange("(k p) n -> p k n", p=128)
        wout_v = f_w_out[li].rearrange("(j p) n -> p j n", p=128)
        half = (ntiles + 1) // 2
        passes = [list(range(0, half)), list(range(half, ntiles))]
        cnt = {"pro": 0, "y": 0, "g": 0}

        def load_wi(j):
            b3 = j % 3
            P.dma_group("pool", pf + "_wi%d" % b3, [
                lambda e: e.dma_start(out=wi[b3][:, :, 0:128], in_=win_v[:, :, j * 128:(j + 1) * 128]),
                lambda e: e.dma_start(out=wi[b3][:, :, 128:256], in_=win_v[:, :, DFF + j * 128:DFF + (j + 1) * 128])],
                writes=[pf + "_wi%dg" % b3, pf + "_wi%du" % b3])

        def pro_a(i, tl):
            c = cnt["pro"]
            cnt["pro"] += 1
            x3, h2 = c % 3, c % 2
            xn, hbn, pfx = pf + "_xs%d" % x3, pf + "_hb%d" % h2, pf + "n%d" % h2
            P.dma("sp", pf + "_x%d" % x3, lambda e: e.dma_start(out=xs[x3], in_=Xin[tl * 128:(tl + 1) * 128, :]), writes=[xn])
            norm_tile(xs[x3], xn, gbc, pf + "_gbc", hbs[h2], hbn, sq, pf + "_sq", ssqs[h2], rss[h2], pfx)
            return (hbs[h2], hbn)

        def pro_b(i, hbinfo):
            transpose8(hbinfo[0], hbinfo[1], hT[:, :, i * 128:(i + 1) * 128], pf + "_hT%d" % i, 0)

        for j0 in (0, 11):
            P.dma("pool", pf + "_wo%d" % j0, lambda e, j0=j0: e.dma_start(out=wo[:, j0:j0 + 11, :], in_=wout_v[:, j0:j0 + 11, :]),
                  writes=[pf + "_wo%d" % j0])
        for j in range(3):
            load_wi(j)
        for i, tl in enumerate(passes[0]):
            pro_b(i, pro_a(i, tl))
        for pi, tiles in enumerate(passes):
            ntok = len(tiles) * 128
            nsub = (ntok + 511) // 512
            sbw = ((ntok // 128 + nsub - 1) // nsub) * 128
            for j in range(22):
                b3 = j % 3
                if j >= 3:
                    load_wi(j)
                for sb, s0 in enumerate(range(0, ntok, sbw)):
                    n = min(sbw, ntok - s0)
                    gb = cnt["g"] % 2
                    cnt["g"] += 1
                    hnames = [pf + "_hT%d" % i for i in range(s0 // 128, (s0 + n) // 128)]
                    for (col, bank, wn) in ((0, gb, "g"), (128, 2 + gb, "u")):
                        for k in range(8):
                            P.op("pe", lambda e, k=k, col=col, bank=bank, b3=b3, s0=s0, n=n: e.matmul(
                                PS[bank][:, 0:n], lhsT=wi[b3][:, k, col:col + 128], rhs=hT[:, k, s0:s0 + n], start=(k == 0), stop=(k == 7)),
                                reads=hnames + [pf + "_wi%d%s" % (b3, wn)], writes=[PSN[bank]])
                    P.op("act", lambda e, gb=gb, n=n: e.activation(out=sg[gb][:, 0:n], in_=PS[gb][:, 0:n], func=AF.Silu),
                         reads=[PSN[gb]], writes=[pf + "_sg%d" % gb])
                    P.op("dve", lambda e, gb=gb, n=n, j=j, s0=s0: e.tensor_tensor(out=AT[:, j, s0:s0 + n], in0=PS[2 + gb][:, 0:n],
                                                                                in1=sg[gb][:, 0:n], op=ALU.mult),
                         reads=[PSN[2 + gb], pf + "_sg%d" % gb], writes=[pf + "_AT%d_%d" % (sb, j % 2)])
            nxt = passes[pi + 1] if pi + 1 < len(passes) else []
            if nxt:
                for j in range(3):
                    load_wi(j)
            for i, tl in enumerate(tiles):
                hbinfo = pro_a(i, nxt[i]) if i < len(nxt) else None
                yi = cnt["y"] % 2
                cnt["y"] += 1
                P.dma("sp", pf + "_xr%d" % yi, lambda e, tl=tl, yi=yi: e.dma_start(out=xr2[yi], in_=Xin[tl * 128:(tl + 1) * 128, :]),
                      writes=[pf + "_xr%d" % yi])
                for hh in range(2):
                    bank = 4 + (2 * yi + hh)
                    if bank == 7:
                        bank = 3 if False else 7
                    for j in range(22):
                        P.op("pe", lambda e, j=j, i=i, hh=hh, bank=bank: e.matmul(PS[bank], lhsT=AT[:, j, i * 128:(i + 1) * 128],
                                                                               rhs=wo[:, j, hh * 512:(hh + 1) * 512],
                                                                               start=(j == 0), stop=(j == 21)),
                             reads=[pf + "_AT%d_0" % (i * 128 // sbw), pf + "_AT%d_1" % (i * 128 // sbw), pf + "_wo%d" % (0 if j < 11 else 11)],
                             writes=[PSN[bank]])
                    P.op("dve", lambda e, hh=hh, bank=bank, yi=yi: e.tensor_tensor(out=yb[yi][:, hh * 512:(hh + 1) * 512], in0=PS[bank],
                                                                                 in1=xr2[yi][:, hh * 512:(hh + 1) * 512], op=ALU.add),
                         reads=[PSN[bank], pf + "_xr%d" % yi], writes=[pf + "_y%d_%d" % (yi, hh)])
                P.dma("sp", pf + "_ys%d" % yi, lambda e, tl=tl, yi=yi: e.dma_start(out=Xout[tl * 128:(tl + 1) * 128, :], in_=yb[yi]),
                      reads=[pf + "_y%d_0" % yi, pf + "_y%d_1" % yi])
                if hbinfo is not None:
                    pro_b(i, hbinfo)

    phase_FFN(X1, X2, 0, NQT)
    barrier()
    if stop_after == "F0":
        P.emit()
        return nc

    def layer_norm_free(src, src_n, dst, dst_n, g_bc, g_n, b_bc, b_n, tmp, tmp_n, sm, pfx, eng2="pool"):
        P.op("dve", lambda e: e.tensor_reduce(out=sm[:, 0:1], in_=src, axis=AX.X, op=ALU.add), reads=[src_n], writes=[pfx + "_s1"])
        P.op("dve", lambda e: e.tensor_scalar(out=sm[:, 1:2], in0=sm[:, 0:1], scalar1=-1.0 / 512.0, scalar2=None, op0=ALU.mult),
             reads=[pfx + "_s1"], writes=[pfx + "_nm"])
        P.op("dve", lambda e: e.tensor_scalar(out=tmp, in0=src, scalar1=sm[:, 1:2], scalar2=None, op0=ALU.add),
             reads=[src_n, pfx + "_nm"], writes=[tmp_n])
        P.op("act", lambda e: e.activation(out=src, in_=tmp, func=AF.Square, accum_out=sm[:, 2:3]), reads=[tmp_n], writes=[src_n, pfx + "_ss"])
        P.op("act", lambda e: e.activation(out=sm[:, 3:4], in_=sm[:, 2:3], func=AF.Ln, bias=float(EPS), scale=1.0 / 512.0),
             reads=[pfx + "_ss"], writes=[pfx + "_rs"])
        P.op("act", lambda e: e.activation(out=sm[:, 3:4], in_=sm[:, 3:4], func=AF.Exp, scale=-0.5), reads=[pfx + "_rs"], writes=[pfx + "_rs"])
        P.op("dve", lambda e: e.scalar_tensor_tensor(out=tmp, in0=tmp, scalar=sm[:, 3:4], in1=g_bc, op0=ALU.mult, op1=ALU.mult),
             reads=[tmp_n, pfx + "_rs", g_n], writes=[tmp_n])
        P.op(eng2, lambda e: e.tensor_tensor(out=dst, in0=tmp, in1=b_bc, op=ALU.add), reads=[tmp_n, b_n], writes=[dst_n])

    def phase_D():
        reset_arena()
        Wi1 = AR.alloc([8, 2048], BF16)
        Wo1 = AR.alloc([8, D], BF16)
        Dg = AR.alloc([31, 4, 128], BF16)
        wsf = AR.alloc([8, 128], F32)
        wsT = AR.alloc([8, 128], BF16)
        cbuf = AR.alloc([4, 2080], BF16)
        gbc = AR.alloc([D], F32)
        brow = AR.alloc([1024], F32)
        glgb = AR.alloc([1024], F32)
        cvv = AR.alloc([1536], F32)
        bsT = AR.alloc([8], F32)
        obc = AR.alloc([8], F32)
        cw = AR.alloc([4, 31], F32)
        xt = [AR.alloc([D], F32) for _ in range(3)]
        sq = AR.alloc([D], BF16)
        NZ = 2
        S_ = []
        for z in range(NZ):
            S_.append(dict(
                hb=AR.alloc([D], BF16), hT=AR.alloc([8, 128], BF16), sig=AR.alloc([4, 128], F32), ctmp=AR.alloc([128], F32),
                t0u=AR.alloc([512], F32), t0v=AR.alloc([512], F32), w1u=AR.alloc([512], F32), w1v=AR.alloc([512], F32),
                w2=AR.alloc([512], F32), w3=AR.alloc([512], F32), gvn=AR.alloc([512], BF16), CC=AR.alloc([D], BF16),
                CCT=AR.alloc([8, 128], BF16), sm=AR.alloc([8], F32), sm2=AR.alloc([8], F32), ssq=AR.alloc([1], F32), rs=AR.alloc([1], F32)))
        yb = [AR.alloc([D], F32) for _ in range(2)]

        load_w_bf16(Wi1, "D_Wi", o_w_in, "D_wi", nsplit=4)
        wi_n = ["D_Wi_%d" % (k // 2 * 2) for k in range(8)]
        load_w_bf16(Wo1, "D_Wo", o_w_out, "D_wo", nsplit=2)
        wo_n = ["D_Wo_%d" % (k // 4 * 4) for k in range(8)]
        load_gain_bc(gbc, "D_gbc", attn_g[1:2, :], "D_g", 32.0)
        P.dma_group("sp", "D_c", [
            lambda e: e.dma_start(out=brow, in_=o_b_row[:, 0:1024].partition_broadcast(128)),
            lambda e: e.dma_start(out=glgb, in_=gl_gb.partition_broadcast(128)),
            lambda e: e.dma_start(out=cvv, in_=cvec.partition_broadcast(128)),
            lambda e: e.dma_start(out=bsT, in_=bsT_in),
            lambda e: e.dma_start(out=obc, in_=o_b_col),
            lambda e: e.dma_start(out=cw, in_=cwT),
            lambda e: e.dma_start(out=wsf, in_=wsT_in)],
            writes=["D_brow", "D_glgb", "D_cvv", "D_bsT", "D_obc", "D_cw", "D_wsf"])
        P.op("pool", lambda e: e.affine_select(out=wsf, in_=wsf, pattern=[[0, 8], [1, 128]], compare_op=ALU.is_ge, fill=0.0,
                                               base=0, channel_multiplier=-1), reads=["D_wsf"], writes=["D_wsf"])
        P.op("pool", lambda e: e.tensor_copy(out=wsT, in_=wsf), reads=["D_wsf"], writes=["D_wsT"])
        for c in range(4):
            P.op("dve" if c % 2 == 0 else "pool", lambda e, c=c: e.tensor_tensor(
                out=Dg[:, :, c, :], in0=ident.unsqueeze(1).to_broadcast([128, 31, 128]),
                in1=cw[:, c, :].unsqueeze(2).to_broadcast([128, 31, 128]), op=ALU.mult), reads=["ident", "D_cw"], writes=["D_Dg%d" % c])

        def gelu(ps, psn, bias_bc, t0, t0n, w1, w1n, dst, dstn):
            P.op("dve", lambda e: e.tensor_tensor(out=t0, in0=ps, in1=bias_bc, op=ALU.add), reads=[psn, "D_brow"], writes=[t0n])
            P.op("act", lambda e: e.activation(out=w1, in_=t0, func=AF.Square), reads=[t0n], writes=[w1n])
            P.op("dve", lambda e: e.tensor_scalar(out=w1, in0=w1, scalar1=0.044715, scalar2=1.0, op0=ALU.mult, op1=ALU.add),
                 reads=[w1n], writes=[w1n])
            P.op("pool", lambda e: e.tensor_tensor(out=w1, in0=w1, in1=t0, op=ALU.mult), reads=[w1n, t0n], writes=[w1n])
            P.op("act", lambda e: e.activation(out=w1, in_=w1, func=AF.Sigmoid, scale=1.5957691216057308), reads=[w1n], writes=[w1n])
            P.op("pool", lambda e: e.tensor_tensor(out=dst, in0=w1, in1=t0, op=ALU.mult), reads=[w1n, t0n], writes=[dstn])

        def tile_stages(it, tl):
            b2 = it % 3
            z = it % NZ
            B = S_[z]
            N_ = lambda nme: "D_%s%d" % (nme, z)
            hb, hT, sig, ctmp, t0u, t0v, w1u, w1v, w2, w3, gvn, CC, CCT, sm, sm2 = (B[k] for k in (
                "hb", "hT", "sig", "ctmp", "t0u", "t0v", "w1u", "w1v", "w2", "w3", "gvn", "CC", "CCT", "sm", "sm2"))
            halo = tl == 16
            xn = "D_xt%d" % b2
            yi = it % 2

            def f1():
                P.dma("sp", "D_x%d" % b2, lambda e: e.dma_start(out=xt[b2], in_=X2[tl * 128:(tl + 1) * 128, :]), writes=[xn])

            def f2():
                P.op("act", lambda e: e.activation(out=sq, in_=xt[b2], func=AF.Square, accum_out=B["ssq"]), reads=[xn], writes=["D_sq", N_("n") + "_ssq"])
                rms_rstd(B["ssq"], N_("n") + "_ssq", B["rs"], N_("n") + "_rs", D * EPS)

            def f3():
                P.op("dve", lambda e: e.scalar_tensor_tensor(out=hb, in0=xt[b2], scalar=B["rs"], in1=gbc, op0=ALU.mult, op1=ALU.mult),
                     reads=[xn, N_("n") + "_rs", "D_gbc"], writes=[N_("hb")])

            def f4():
                transpose8(hb, N_("hb"), hT, N_("hT"), 0)

            def f5():
                for (base, bank) in ((1024, 1), (1536, 2)):
                    for c in range(4):
                        for k in range(8):
                            P.op("pe", lambda e, base=base, bank=bank, c=c, k=k: e.matmul(
                                PS[bank][:, c * 128:(c + 1) * 128], lhsT=Wi1[:, k, base + c * 128:base + (c + 1) * 128], rhs=hT[:, k, :],
                                start=(k == 0), stop=(k == 7)), reads=[N_("hT"), wi_n[k]], writes=[PSN[bank]])

            def f6():
                for (base, bank) in ((0, 3), (512, 4)):
                    for k in range(8):
                        P.op("pe", lambda e, base=base, bank=bank, k=k: e.matmul(PS[bank], lhsT=hT[:, k, :], rhs=Wi1[:, k, base:base + 512],
                                                                              start=(k == 0), stop=(k == 7)),
                             reads=[N_("hT"), wi_n[k]], writes=[PSN[bank]])

            def f7():
                for c in range(4):
                    P.op("act", lambda e, c=c: e.activation(out=sig[:, c, :], in_=PS[2][:, c * 128:(c + 1) * 128], func=AF.Sigmoid,
                                                            bias=obc[:, 4 + c:5 + c]), reads=[PSN[2], "D_obc"], writes=[N_("sig") + "_%d" % c])
                    if halo:
                        P.op("dve", lambda e, c=c: e.scalar_tensor_tensor(out=ctmp, in0=PS[1][:, c * 128:(c + 1) * 128], scalar=obc[:, c:c + 1],
                                                                          in1=sig[:, c, :], op0=ALU.add, op1=ALU.mult),
                             reads=[PSN[1], "D_obc", N_("sig") + "_%d" % c], writes=[N_("ctmp")])
                        P.op("dve", lambda e, c=c: e.tensor_scalar(out=cbuf[:, c, 0:32], in0=ctmp[:, 96:128], scalar1=flg[:, 1:2], scalar2=None,
                                                                   op0=ALU.mult), reads=[N_("ctmp"), "flg"], writes=["D_cb_h%d" % c])
                    else:
                        P.op("dve", lambda e, c=c: e.scalar_tensor_tensor(out=cbuf[:, c, 32 + tl * 128:32 + (tl + 1) * 128],
                                                                          in0=PS[1][:, c * 128:(c + 1) * 128], scalar=obc[:, c:c + 1],
                                                                          in1=sig[:, c, :], op0=ALU.add, op1=ALU.mult),
                             reads=[PSN[1], "D_obc", N_("sig") + "_%d" % c], writes=["D_cb_%d_%d" % (tl, c)])
            if halo:
                return [f1, f2, f3, f4, f5, f7], []

            def f8():
                gelu(PS[3], PSN[3], brow[:, 0:512], t0u, N_("t0u"), w1u, N_("w1u"), w2, N_("w2"))

            def f9():
                gelu(PS[4], PSN[4], brow[:, 512:1024], t0v, N_("t0v"), w1v, N_("w1v"), w3, N_("w3"))

            def f10():
                layer_norm_free(w3, N_("w3"), gvn, N_("gvn"), glgb[:, 0:512], "D_glgb", glgb[:, 512:1024], "D_glgb", t0v, N_("t0v"), sm, N_("ln1"))

            def f11():
                for g in range(8):
                    P.op("pe", lambda e, g=g: e.matmul(PS[0][:, g * 64:(g + 1) * 64], lhsT=wsT[:, g, :], rhs=gvn[:, g * 64:(g + 1) * 64],
                                                       start=True, stop=True), reads=["D_wsT", N_("gvn")], writes=[PSN[0]])
                P.op("dve", lambda e: e.tensor_tensor(out=t0u.rearrange("p (g d) -> p g d", d=64), in0=PS[0].rearrange("p (g d) -> p g d", d=64),
                                                      in1=bsT[:, 0:8].unsqueeze(2).to_broadcast([128, 8, 64]), op=ALU.add),
                     reads=[PSN[0], "D_bsT"], writes=[N_("t0u")])
                P.op("pool", lambda e: e.tensor_tensor(out=CC[:, 0:512], in0=w2, in1=t0u, op=ALU.mult), reads=[N_("w2"), N_("t0u")], writes=[N_("CCa")])

            def s1():
                for c in range(4):
                    rn = ["D_cb_%d_%d" % (tl, c), ("D_cb_%d_%d" % (tl - 1, c)) if tl > 0 else ("D_cb_h%d" % c)]
                    for j in range(31):
                        o = tl * 128 + 2 + j
                        P.op("pe", lambda e, c=c, j=j, o=o: e.matmul(PS[5][:, c * 128:(c + 1) * 128], lhsT=cbuf[:, c, o:o + 128], rhs=Dg[:, j, c, :],
                                                                  start=(j == 0), stop=(j == 30)), reads=rn + ["D_Dg%d" % c], writes=[PSN[5]])
                P.op("dve", lambda e: e.tensor_tensor(out=w3, in0=PS[5], in1=cvv[:, 0:512], op=ALU.add), reads=[PSN[5], "D_cvv"], writes=[N_("w3")])

            def s2():
                layer_norm_free(w3, N_("w3"), w2, N_("w2"), cvv[:, 512:1024], "D_cvv", cvv[:, 1024:1536], "D_cvv", t0v, N_("t0v"), sm2, N_("ln2"))
                P.op("act", lambda e: e.activation(out=CC[:, 512:1024], in_=w2, func=AF.Silu), reads=[N_("w2")], writes=[N_("CCb")])

            def s3():
                psb = PS[6].bitcast(BF16)
                for k in range(8):
                    P.op("pe", lambda e, k=k: e.transpose(out=psb[:, k * 128:(k + 1) * 128], in_=CC[:, k * 128:(k + 1) * 128], identity=ident),
                         reads=[N_("CCa"), N_("CCb"), "ident"], writes=[PSN[6]])
                P.op("act", lambda e: e.activation(out=CCT, in_=psb[:, 0:1024].rearrange("p (a b) -> p a b", b=128), func=AF.Copy),
                     reads=[PSN[6]], writes=[N_("CCT")])

            def s4():
                for hh in range(2):
                    bank = 7 if hh == 0 else 5
                    for k in range(8):
                        P.op("pe", lambda e, k=k, hh=hh, bank=bank: e.matmul(PS[bank], lhsT=CCT[:, k, :], rhs=Wo1[:, k, hh * 512:(hh + 1) * 512],
                                                                            start=(k == 0), stop=(k == 7)), reads=[N_("CCT"), wo_n[k]], writes=[PSN[bank]])
                    P.op("dve", lambda e, hh=hh, bank=bank: e.tensor_tensor(out=yb[yi][:, hh * 512:(hh + 1) * 512], in0=PS[bank],
                                                                           in1=xt[b2][:, hh * 512:(hh + 1) * 512], op=ALU.add),
                         reads=[PSN[bank], xn], writes=["D_y%d_%d" % (yi, hh)])
                P.dma("sp", "D_ys%d" % yi, lambda e: e.dma_start(out=X3[tl * 128:(tl + 1) * 128, :], in_=yb[yi]),
                      reads=["D_y%d_0" % yi, "D_y%d_1" % yi])
            return [f1, f2, f3, f4, f5, f6, f7, f8, f9, f10, f11], [s1, s2, s3, s4]

        def coalesce(lst):
            out_, run = [], []
            for it_ in lst:
                if it_[0] == "op" and it_[1] == "pe":
                    run.append(it_)
                else:
                    if run:
                        out_.append(("macro", run))
                        run = []
                    out_.append(it_)
            if run:
                out_.append(("macro", run))
            return out_

        def round_robin(chains):
            chains = [c for c in chains if c]
            idx = [0] * len(chains)
            while True:
                progressed = False
                for ci, c in enumerate(chains):
                    if idx[ci] < len(c):
                        P.replay(c[idx[ci]])
                        idx[ci] += 1
                        progressed = True
                if not progressed:
                    break

        order = [16] + list(range(16))
        stages = {}

        def get(it):
            if it not in stages and 0 <= it < len(order):
                stages[it] = tile_stages(it, order[it])
            return stages.get(it)

        F0_, _ = get(0)
        for f in F0_[0:4]:
            f()
        for it, tl in enumerate(order):
            F, Sn = get(it)
            halo = tl == 16
            F[4]()
            if not halo:
                F[5]()
            chains = []
            fc = F[5] if halo else F[6]
            chains.append(coalesce(P.capture(fc)))
            if not halo:
                chains.append(coalesce(P.capture(F[7])))
                chains.append(coalesce(P.capture(lambda: (F[8](), F[9](), F[10]()))))
            if it >= 1 and stages[it - 1][1]:
                Sp = stages[it - 1][1]
                chains.append(coalesce(P.capture(lambda: [st() for st in Sp])))
            nxt = get(it + 1)
            if nxt is not None:
                Fn = nxt[0]
                chains.append(coalesce(P.capture(lambda: [f() for f in Fn[0:4]])))
            round_robin(chains)
        lastS = stages[len(order) - 1][1]
        for st in lastS:
            st()
        print("phase D arena bytes", AR.off)

    phase_D()
    barrier()
    if stop_after == "D":
        P.emit()
        return nc

    phase_FFN(X3, out, 1, 16)
    barrier()
    P.emit()
    return nc


def _bf16(a):
    import ml_dtypes
    return np.asarray(a, np.float32).astype(ml_dtypes.bfloat16)


def _rope_tables(dim, pos):
    inv = 1.0 / (10000.0 ** (np.arange(0, dim, 2, dtype=np.float32) / dim))
    ang = pos.astype(np.float32)[:, None] * inv[None, :]
    return np.cos(ang).astype(np.float32), np.sin(ang).astype(np.float32)


def host_inputs(inp, core):
    b, hf = core // 2, core % 2
    x = np.asarray(inp["x"], np.float32)
    xin = np.concatenate([x[b, 0:2048], x[b, hf * 2048:(hf + 1) * 2048]], 0)
    pos = np.concatenate([np.arange(2048), hf * 2048 + np.arange(2048)])
    c64, s64 = _rope_tables(64, pos)
    c32, s32 = _rope_tables(32, pos)
    flags = np.zeros((128, 20), np.float32)
    flags[:, 0] = 0.0 if hf == 1 else -30000.0
    flags[:, 1] = 1.0 if hf == 1 else 0.0
    flags[:, 4:12] = 0.0 if hf == 1 else -1e30
    onehot = np.zeros((16, 4096), np.float32)
    for n in range(16):
        onehot[n, n * 256:(n + 1) * 256] = 1.0
    g = lambda k: np.asarray(inp[k], np.float32)
    gains = np.concatenate([g("diff_q_norm_g")[0], g("diff_k_norm_g")[0], g("diff_subln_g")[0], g("moba_q_norm_g")[0],
                            g("moba_k_norm_g")[0], np.zeros(64, np.float32)])[None, :]
    lams = np.concatenate([g("diff_lambda_q1")[0], g("diff_lambda_k1")[0], g("diff_lambda_q2")[0], g("diff_lambda_k2")[0]])[None, :]
    ob = g("odd_b_in")[0]
    m = {
        "xin": xin, "cs64": np.concatenate([c64, s64], 1), "cs32": np.concatenate([c32, s32], 1),
        "flags": flags, "onehot": _bf16(onehot),
        "even_w_in": g("even_w_in")[0], "even_w_out": g("even_w_out")[0],
        "ffn_w_in": g("ffn_w_in"), "ffn_w_out": g("ffn_w_out"),
        "odd_w_in": g("odd_w_in")[0], "odd_w_out": g("odd_w_out")[0],
        "attn_norm_g": g("attn_norm_g"), "ffn_norm_g": g("ffn_norm_g"),
        "gains": gains.astype(np.float32), "lams": lams.astype(np.float32),
        "odd_b_row": ob[None, :].copy(),
        "odd_b_col": np.ascontiguousarray(ob[1024:2048].reshape(8, 128).T),
        "gmlp_ln_gb": np.concatenate([g("gmlp_ln_g")[0], g("gmlp_ln_b")[0]])[None, :],
        "gmlp_wsT": np.ascontiguousarray(g("gmlp_w_s")[0].transpose(2, 0, 1)),
        "gmlp_bsT": np.ascontiguousarray(g("gmlp_b_s")[0].T),
        "conv_wT": np.ascontiguousarray(g("conv_w")[0].T.reshape(4, 128, 31).transpose(1, 0, 2)),
        "conv_vecs": np.concatenate([g("conv_b")[0], g("conv_ln_g")[0], g("conv_ln_b")[0]])[None, :],
    }
    return {k: np.ascontiguousarray(v) for k, v in m.items()}


def kernel(**inputs):
    in_maps = [host_inputs(inputs, c) for c in range(8)]
    nc = build_program()
    res = run_bass_kernel_spmd(nc, in_maps, core_ids=list(range(8)))
    out = np.zeros((4, 4096, 1024), np.float32)
    for c in range(8):
        b, hf = c // 2, c % 2
        out[b, hf * 2048:(hf + 1) * 2048] = np.asarray(res.results[c]["out"], np.float32)
    return out
```

```python
import numpy as np
import concourse.bass as bass
import concourse.mybir as mybir
from concourse.bass_utils import run_bass_kernel_spmd

F32 = mybir.dt.float32
BF16 = mybir.dt.bfloat16
AF = mybir.ActivationFunctionType
ALU = mybir.AluOpType
AX = mybir.AxisListType

ENGS = ["pe", "act", "dve", "pool", "sp"]
EPOCH = 30000


class Prog:
    def __init__(self, nc):
        self.nc = nc
        self.ops = {e: [] for e in ENGS}
        self.count = {e: 0 for e in ENGS}
        self.seen = {e: {} for e in ENGS}
        self.res = {}
        self.lanes = {}
        self.semkeys = []
        self.lane_waited = {}
        self.lane_sem = {}
        self.free_sems = []
        self.nsem = 0
        self._cap = None

    def _deps(self, eng, reads, writes):
        deps = []
        for r in reads:
            st = self.res.get(r)
            if st and st[0] is not None:
                deps.append(st[0])
        for w in writes:
            st = self.res.get(w)
            if st:
                if st[0] is not None:
                    deps.append(st[0])
                deps.extend(st[1].values())
        need = {}
        for (k, v) in deps:
            if eng == "pe" and k == ("pe", "raw"):
                continue
            if self.seen[eng].get(k, 0) < v:
                need[k] = max(need.get(k, 0), v)
        for k, v in need.items():
            self.ops[eng].append(("wait", k, v))
            self.seen[eng][k] = v
            if k[0] == "dma":
                self.lane_waited[k[1]] = max(self.lane_waited.get(k[1], 0), v)

    def _mark(self, tok, reads, writes, rkey):
        for r in reads:
            st = self.res.setdefault(r, [None, {}])
            st[1][rkey] = tok
        for w in writes:
            self.res[w] = [tok, {}]

    def capture(self, f):
        old = self._cap
        self._cap = []
        f()
        lst = self._cap
        self._cap = old
        return lst

    def replay(self, item):
        if item[0] == "op":
            self.op(*item[1:])
        elif item[0] == "dmag":
            self.dma_group(*item[1:])
        else:
            for it in item[1]:
                self.replay(it)

    def op(self, eng, fn, reads=(), writes=()):
        if self._cap is not None:
            self._cap.append(("op", eng, fn, list(reads), list(writes)))
            return None
        self._deps(eng, reads, writes)
        self.count[eng] += 1
        c = self.count[eng]
        key = (eng, "raw")
        tok = (key, c)
        self.ops[eng].append(("op", fn, key, 1, c))
        self._mark(tok, reads, writes, eng)
        return tok

    def dma(self, eng, lane, fn, reads=(), writes=(), multi=False):
        return self.dma_group(eng, lane, [fn], reads, writes)

    def dma_group(self, eng, lane, fns, reads=(), writes=()):
        if self._cap is not None:
            self._cap.append(("dmag", eng, lane, list(fns), list(reads), list(writes)))
            return None
        self._deps(eng, reads, writes)
        if lane not in self.lane_sem:
            self.lane_sem[lane] = (self.nsem, 0)
            self.nsem += 1
        semid, base = self.lane_sem[lane]
        n = self.lanes.get(lane, 0) + len(fns)
        self.lanes[lane] = n
        key = ("dma", semid)
        if key not in self.semkeys:
            self.semkeys.append(key)
        tok = (key, base + 16 * n)
        for fn in fns:
            self.ops[eng].append(("op", fn, key, 16, None))
        self._mark(tok, reads, writes, ("dma", lane))
        return tok

    def retire_lanes(self):
        toks = []
        for lane, n in self.lanes.items():
            semid, base = self.lane_sem[lane]
            toks.append((("dma", semid), base + 16 * n))
        return toks

    def recycle_lanes(self):
        pass

    def wait_all(self, eng, toks):
        for (k, v) in toks:
            if self.seen[eng].get(k, 0) < v:
                self.ops[eng].append(("wait", k, v))
                self.seen[eng][k] = v
                if k[0] == "dma":
                    self.lane_waited[k[1]] = max(self.lane_waited.get(k[1], 0), v)

    def emit(self):
        nc = self.nc
        from contextlib import ExitStack
        waited = {e: set() for e in ENGS}
        for name in ENGS:
            for it in self.ops[name]:
                if it[0] == "wait" and it[1][1] == "raw":
                    waited[it[1][0]].add(it[2])
        rank = {}
        for e_ in ENGS:
            for i, c in enumerate(sorted(waited[e_])):
                rank[(e_, c)] = i + 1
        semkeys = list(self.semkeys)
        for (e_, c), r in rank.items():
            k = (e_, (r - 1) // EPOCH)
            if k not in semkeys:
                semkeys.append(k)
        with ExitStack() as es:
            sems = {}
            for k in semkeys:
                nm = "s_" + "_".join(str(x) for x in k)
                sems[k] = es.enter_context(nc.semaphore(nm))
            block = es.enter_context(nc.Block())

            def run(e, name):
                for it in self.ops[name]:
                    if it[0] == "wait":
                        if it[1][1] == "raw":
                            r = rank[(it[1][0], it[2])]
                            e.wait_ge(sems[(it[1][0], (r - 1) // EPOCH)], (r - 1) % EPOCH + 1)
                        else:
                            e.wait_ge(sems[it[1]], it[2])
                    else:
                        ins = it[1](e)
                        if it[4] is None:
                            ins.then_inc(sems[it[2]], it[3])
                        else:
                            r = rank.get((name, it[4]))
                            if r is not None:
                                ins.then_inc(sems[(name, (r - 1) // EPOCH)], 1)

            @block.tensor
            def _(e):
                run(e, "pe")

            @block.scalar
            def _(e):
                run(e, "act")

            @block.vector
            def _(e):
                run(e, "dve")

            @block.gpsimd
            def _(e):
                run(e, "pool")

            @block.sync
            def _(e):
                run(e, "sp")


NQT = 17
NQ = NQT * 128
NKT = 32
NK = 4096
D = 1024
DFF = 2816
EPS = 1e-6
NEG = -30000.0
LAMBDA_INIT0 = 0.8 - 0.6 * 1.0


class Arena:
    def __init__(self, ap, nbytes):
        self.ap = ap
        self.cap = nbytes
        self.off = 0

    def reset(self):
        self.off = 0

    def alloc(self, shape_free, dt):
        n = 1
        for s in shape_free:
            n *= s
        esz = 4 if dt == F32 else 2
        self.off = (self.off + 63) // 64 * 64
        nb = n * esz
        assert self.off + nb <= self.cap, ("arena overflow", self.off, nb, self.cap)
        v = self.ap[:, self.off // 2:(self.off + nb) // 2]
        self.off += nb
        if dt == F32:
            v = v.bitcast(F32)
        if len(shape_free) == 2:
            v = v.rearrange("p (a b) -> p a b", b=shape_free[1])
        elif len(shape_free) == 3:
            v = v.rearrange("p (a b c) -> p a b c", b=shape_free[1], c=shape_free[2])
        return v


def build_program(stop_after=None, debug=False):
    nc = bass.Bass("TRN2", target_bir_lowering=False)
    dbg_kind = "ExternalOutput" if debug else "Internal"

    def din(name, shape, dt=F32):
        return nc.dram_tensor(name, list(shape), dt, kind="ExternalInput").ap()

    def dscr(name, shape, dt):
        return nc.dram_tensor(name, list(shape), dt, kind=dbg_kind).ap()

    xin = din("xin", [NK, D])
    cs64 = din("cs64", [NK, 64])
    cs32 = din("cs32", [NK, 32])
    flags = din("flags", [128, 20])
    onehot = din("onehot", [16, NK], BF16)
    e_w_in = din("even_w_in", [D, 3072])
    e_w_out = din("even_w_out", [D, D])
    f_w_in = din("ffn_w_in", [2, D, 2 * DFF])
    f_w_out = din("ffn_w_out", [2, DFF, D])
    o_w_in = din("odd_w_in", [D, 2048])
    o_w_out = din("odd_w_out", [D, D])
    attn_g = din("attn_norm_g", [2, D])
    ffn_g = din("ffn_norm_g", [2, D])
    gains = din("gains", [1, 320])
    lams = din("lams", [1, 128])
    o_b_row = din("odd_b_row", [1, 2048])
    o_b_col = din("odd_b_col", [128, 8])
    gl_gb = din("gmlp_ln_gb", [1, 1024])
    wsT_in = din("gmlp_wsT", [128, 8, 128])
    bsT_in = din("gmlp_bsT", [128, 8])
    cwT = din("conv_wT", [128, 4, 31])
    cvec = din("conv_vecs", [1, 1536])
    out = nc.dram_tensor("out", [2048, D], F32, kind="ExternalOutput").ap()

    QdT = dscr("QdT", [512, NQ], BF16)
    KdT = dscr("KdT", [512, NK], BF16)
    Vd = dscr("Vd", [8, NK, 64], BF16)
    QmT = dscr("QmT", [512, NQ], BF16)
    KmT = dscr("KmT", [512, NK], BF16)
    Vm = dscr("Vm", [8, NK, 64], BF16)
    X1 = dscr("X1", [NQ, D], F32)
    X2 = dscr("X2", [NQ, D], F32)
    X3 = dscr("X3", [2048, D], F32)

    ARENA_BYTES = 190 * 1024
    arena_t = nc.alloc_sbuf_tensor("arena", [128, ARENA_BYTES // 2], BF16)
    AR = Arena(arena_t.ap(), ARENA_BYTES)
    PSALL = nc.alloc_psum_tensor("psall", [128, 4096], F32).ap()
    PS = [PSALL[:, i * 512:(i + 1) * 512] for i in range(8)]
    PSN = ["ps%d" % i for i in range(8)]

    P = Prog(nc)
    uid = [0]

    def nm(s):
        uid[0] += 1
        return "%s#%d" % (s, uid[0])

    lane_ctr = [0]

    def newlane(s="l"):
        lane_ctr[0] += 1
        return "%s%d" % (s, lane_ctr[0])

    def barrier():
        toks = []
        for e in ENGS:
            c = P.count[e]
            if c > 0:
                toks.append(((e, "raw"), c))
        toks.extend(P.retire_lanes())
        for e in ENGS:
            P.wait_all(e, toks)
        P.res.clear()
        P.recycle_lanes()

    ident = AR.alloc([128], BF16)
    flg = AR.alloc([20], F32)
    rnd = AR.alloc([8], F32)
    PERSIST = None

    P.op("pool", lambda e: e.memset(ident, 1.0), writes=["ident"])
    P.op("pool", lambda e: e.affine_select(out=ident, in_=ident, pattern=[[-1, 128]], compare_op=ALU.is_equal,
                                           fill=0.0, base=0, channel_multiplier=1), reads=["ident"], writes=["ident"])
    P.dma("sp", "c_flg", lambda e: e.dma_start(out=flg, in_=flags), writes=["flg"])
    AR.off = (AR.off + 63) // 64 * 64
    PERSIST = AR.off

    def reset_arena():
        AR.off = PERSIST

    def rms_rstd(ssq, ssq_name, rs, rs_name, hd_eps):
        P.op("act", lambda e: e.activation(out=rs, in_=ssq, func=AF.Ln, bias=float(hd_eps), scale=1.0),
             reads=[ssq_name], writes=[rs_name])
        P.op("act", lambda e: e.activation(out=rs, in_=rs, func=AF.Exp, scale=-0.5),
             reads=[rs_name], writes=[rs_name])

    def named(ap, name):
        return ap

    def norm_tile(xt, xt_n, gbc, gbc_n, hb, hb_n, sq, sq_n, ssq, rs, pfx):
        P.op("act", lambda e: e.activation(out=sq, in_=xt, func=AF.Square, accum_out=ssq),
             reads=[xt_n], writes=[sq_n, pfx + "_ssq"])
        rms_rstd(ssq, pfx + "_ssq", rs, pfx + "_rs", D * EPS)
        P.op("dve", lambda e: e.scalar_tensor_tensor(out=hb, in0=xt, scalar=rs, in1=gbc, op0=ALU.mult, op1=ALU.mult),
             reads=[xt_n, pfx + "_rs", gbc_n], writes=[hb_n])

    def transpose8(src, src_n, dst, dst_n, psi, nblk=8, evac="act"):
        psb = PS[psi].bitcast(BF16)
        for k in range(nblk):
            P.op("pe", lambda e, k=k: e.transpose(out=psb[:, k * 128:(k + 1) * 128], in_=src[:, k * 128:(k + 1) * 128],
                                                  identity=ident), reads=[src_n, "ident"], writes=[PSN[psi]])
        if evac == "act":
            P.op("act", lambda e: e.activation(out=dst, in_=psb[:, 0:nblk * 128].rearrange("p (a b) -> p a b", b=128),
                                               func=AF.Copy), reads=[PSN[psi]], writes=[dst_n])
        else:
            P.op("dve", lambda e: e.tensor_copy(out=dst, in_=psb[:, 0:nblk * 128].rearrange("p (a b) -> p a b", b=128)),
                 reads=[PSN[psi]], writes=[dst_n])

    def load_gain_bc(dst, dst_n, src_row, lane, scale):
        P.dma("sp", lane, lambda e: e.dma_start(out=dst, in_=src_row.partition_broadcast(128)), writes=[dst_n])
        if scale != 1.0:
            P.op("dve", lambda e: e.tensor_scalar(out=dst, in0=dst, scalar1=float(scale), scalar2=None, op0=ALU.mult),
                 reads=[dst_n], writes=[dst_n])

    def load_w_bf16(dst, dst_n, w_ap, lane, nsplit=4):
        K = dst.shape[1]
        wv = w_ap.rearrange("(k p) n -> p k n", p=128)
        step = max(1, K // nsplit)
        toks = []
        for k0 in range(0, K, step):
            k1 = min(K, k0 + step)
            toks.append(P.dma("pool", "%s_%d" % (lane, k0), lambda e, k0=k0, k1=k1: e.dma_start(out=dst[:, k0:k1, :], in_=wv[:, k0:k1, :]),
                              writes=[dst_n + "_%d" % k0], multi=True))
        return toks

    def phase_A():
        reset_arena()
        Wi = AR.alloc([8, 3072], BF16)
        gbc = AR.alloc([D], F32)
        Gq = AR.alloc([4, 512], F32)
        gsm = AR.alloc([320], F32)
        NC_ = 4
        cst = [AR.alloc([96], F32) for _ in range(NC_)]
        gct = [AR.alloc([4, 2, 64], F32) for _ in range(NC_)]
        xt = [AR.alloc([D], F32) for _ in range(NC_)]
        sq = AR.alloc([D], BF16)
        hbs = [AR.alloc([D], BF16) for _ in range(2)]
        hTs = [AR.alloc([8, 128], BF16) for _ in range(2)]
        ssqs = [AR.alloc([1], F32) for _ in range(2)]
        rss = [AR.alloc([1], F32) for _ in range(2)]
        NS = 4
        sqgs = [AR.alloc([512], F32) for _ in range(NS)]
        qns = [AR.alloc([512], F32) for _ in range(NS)]
        tas = [AR.alloc([512], F32) for _ in range(NS)]
        tbs = [AR.alloc([512], F32) for _ in range(NS)]
        ssqhs = [AR.alloc([16], F32) for _ in range(NS)]
        rshs = [AR.alloc([16], F32) for _ in range(NS)]
        qbs = [AR.alloc([512], BF16) for _ in range(NS)]
        qTs = [AR.alloc([4, 128], BF16) for _ in range(NS)]
        vbs = [AR.alloc([512], BF16) for _ in range(NS)]

        load_w_bf16(Wi, "A_Wi", e_w_in, "A_w", nsplit=8)
        wnames = ["A_Wi_%d" % k for k in range(8)]
        load_gain_bc(gbc, "A_gbc", attn_g[0:1, :], "A_g", 32.0)
        P.dma("sp", "A_gs", lambda e: e.dma_start(out=gsm, in_=gains.partition_broadcast(128)), writes=["A_gsm"])
        for gi, (o, hd) in enumerate([(0, 32), (32, 32), (128, 64), (192, 64)]):
            nh = 512 // hd
            P.op("dve", lambda e, gi=gi, o=o, hd=hd, nh=nh: e.tensor_scalar(
                out=Gq[:, gi, :].rearrange("p (a b) -> p a b", b=hd),
                in0=gsm[:, o:o + hd].unsqueeze(1).to_broadcast([128, nh, hd]),
                scalar1=float(hd) ** 0.5, scalar2=None, op0=ALU.mult), reads=["A_gsm"], writes=["A_Gq%d" % gi])

        Gs = AR.alloc([4, 64], F32)
        for gi, (o, hd) in enumerate([(0, 32), (32, 32), (128, 64), (192, 64)]):
            P.op("dve", lambda e, gi=gi, o=o, hd=hd: e.tensor_scalar(out=Gs[:, gi, 0:hd], in0=gsm[:, o:o + hd], scalar1=float(hd) ** 0.5,
                                                                    scalar2=None, op0=ALU.mult), reads=["A_gsm"], writes=["A_Gs"])
        qdt_v = QdT.rearrange("(c p) n -> p c n", p=128)
        kdt_v = KdT.rearrange("(c p) n -> p c n", p=128)
        qmt_v = QmT.rearrange("(c p) n -> p c n", p=128)
        kmt_v = KmT.rearrange("(c p) n -> p c n", p=128)
        vd_v = Vd.rearrange("h k d -> k h d")
        vm_v = Vm.rearrange("h k d -> k h d")

        def prologue(t):
            b2 = t % 2
            c4 = t % NC_
            xtn = "A_xt%d" % c4
            csn = "A_cs%d" % c4
            hb, hT, hbn, hTn = hbs[b2], hTs[b2], "A_hb%d" % b2, "A_hT%d" % b2
            ssq, rs, pfx = ssqs[b2], rss[b2], "A%d" % b2

            def st0():
                P.dma_group("sp", "A_in%d" % c4, [
                    lambda e: e.dma_start(out=xt[c4], in_=xin[t * 128:(t + 1) * 128, :]),
                    lambda e: e.dma_start(out=cst[c4][:, 0:64], in_=cs64[t * 128:(t + 1) * 128, :]),
                    lambda e: e.dma_start(out=cst[c4][:, 64:96], in_=cs32[t * 128:(t + 1) * 128, :])],
                    writes=[xtn, csn + "a", csn + "b"])

            def st1():
                P.op("act", lambda e: e.activation(out=sq, in_=xt[c4], func=AF.Square, accum_out=ssq), reads=[xtn], writes=["A_sq", pfx + "_ssq"])
                rms_rstd(ssq, pfx + "_ssq", rs, pfx + "_rs", D * EPS)
                for gi_, hd_ in enumerate((32, 32, 64, 64)):
                    hf_ = hd_ // 2
                    co_ = 64 if hd_ == 32 else 0
                    for cs_ in range(2):
                        P.op("pool", lambda e, gi_=gi_, hd_=hd_, hf_=hf_, co_=co_, cs_=cs_: e.tensor_tensor(
                            out=gct[c4][:, gi_, cs_, 0:hd_].rearrange("p (b c) -> p b c", c=hf_),
                            in0=Gs[:, gi_, 0:hd_].rearrange("p (b c) -> p b c", c=hf_),
                            in1=cst[c4][:, co_ + cs_ * hf_:co_ + (cs_ + 1) * hf_].unsqueeze(1).to_broadcast([128, 2, hf_]), op=ALU.mult),
                            reads=["A_Gs", csn + "a", csn + "b"], writes=[csn + "g"])

            def st2():
                P.op("dve", lambda e: e.scalar_tensor_tensor(out=hb, in0=xt[c4], scalar=rs, in1=gbc, op0=ALU.mult, op1=ALU.mult),
                     reads=[xtn, pfx + "_rs", "A_gbc"], writes=[hbn])

            def st3():
                transpose8(hb, hbn, hT, hTn, 7)
            return [st0, st1, st2, st3]

        def group_item(t, g, gidx, qidx):
            b2 = t % 2
            c4 = t % NC_
            csn = "A_cs%d" % c4
            hT, hTn = hTs[b2], "A_hT%d" % b2
            psi = gidx % 5
            ps = PS[psi]
            z = gidx % NS

            def s_mm():
                for k in range(8):
                    P.op("pe", lambda e, k=k: e.matmul(ps, lhsT=hT[:, k, :], rhs=Wi[:, k, g * 512:(g + 1) * 512], start=(k == 0), stop=(k == 7)),
                         reads=[hTn, wnames[k]], writes=[PSN[psi]])
            if g in (2, 5):
                vb, vbn = vbs[z], "A_vb%d" % z
                dst = vd_v if g == 2 else vm_v

                def s_cp():
                    P.op("act", lambda e: e.activation(out=vb, in_=ps, func=AF.Copy), reads=[PSN[psi]], writes=[vbn])

                def s_st():
                    P.dma("sp", "A_vs%d" % z, lambda e: e.dma_start(out=dst[t * 128:(t + 1) * 128, :, :], in_=vb.rearrange("p (h d) -> p h d", d=64)),
                          reads=[vbn])
                return [s_mm, s_cp, s_st]
            isd = g in (0, 1)
            hd = 32 if isd else 64
            half = hd // 2
            nh = 512 // hd
            gi = {0: 0, 1: 1, 3: 2, 4: 3}[g]
            sqg, qn, ta, tb, ssqh, rsh, qb, qT = sqgs[z], qns[z], tas[z], tbs[z], ssqhs[z], rshs[z], qbs[z], qTs[z]
            n_sqg, n_qn, n_ta, n_tb, n_ssqh, n_rsh, n_qb, n_qT = ["A_%s%d" % (x, z) for x in ("sqg", "qn", "ta", "tb", "ssqh", "rsh", "qb", "qT")]
            ssq_v = ssqh[:, 0:nh]
            rs_v = rsh[:, 0:nh]
            co = 64 if isd else 0
            cosv = cst[c4][:, co:co + half]
            sinv = cst[c4][:, co + half:co + 2 * half]
            q4 = qn.rearrange("p (a b c) -> p a b c", b=2, c=half)
            ta4 = ta.rearrange("p (a b c) -> p a b c", b=2, c=half)
            tb4 = tb.rearrange("p (a b c) -> p a b c", b=2, c=half)
            qb4 = qb.rearrange("p (a b c) -> p a b c", b=2, c=half)
            tpi = 5 + (gidx % 2)
            psb = PS[tpi].bitcast(BF16)
            if g in (0, 3):
                dstv = qdt_v if g == 0 else qmt_v
                col = qidx * 128
            else:
                dstv = kdt_v if g == 1 else kmt_v
                col = t * 128

            def s1():
                P.op("act", lambda e: e.activation(out=sqg, in_=ps, func=AF.Square), reads=[PSN[psi]], writes=[n_sqg])

            def s2():
                P.op("dve", lambda e: e.tensor_reduce(out=ssq_v, in_=sqg.rearrange("p (a b) -> p a b", b=hd), axis=AX.X, op=ALU.add),
                     reads=[n_sqg], writes=[n_ssqh])

            def s3():
                rms_rstd(ssq_v, n_ssqh, rs_v, n_rsh, hd * EPS)

            def s4():
                P.op("dve", lambda e: e.tensor_tensor(out=qn.rearrange("p (a b) -> p a b", b=hd), in0=ps.rearrange("p (a b) -> p a b", b=hd),
                                                      in1=rs_v.unsqueeze(2).to_broadcast([128, nh, hd]), op=ALU.mult),
                     reads=[PSN[psi], n_rsh], writes=[n_qn])

            gcv = gct[c4][:, gi, 0, 0:hd].rearrange("p (b c) -> p b c", c=half)
            gsv = gct[c4][:, gi, 1, 0:hd].rearrange("p (b c) -> p b c", c=half)

            def s5():
                pass

            def s6():
                P.op("dve", lambda e: e.tensor_tensor(out=ta4, in0=q4, in1=gcv.unsqueeze(1).to_broadcast([128, nh, 2, half]),
                                                      op=ALU.mult), reads=[n_qn, csn + "g"], writes=[n_ta])
                P.op("pool", lambda e: e.tensor_tensor(out=tb4, in0=q4, in1=gsv.unsqueeze(1).to_broadcast([128, nh, 2, half]),
                                                       op=ALU.mult), reads=[n_qn, csn + "g"], writes=[n_tb])

            def s7():
                P.op("dve", lambda e: e.tensor_tensor(out=qb4[:, :, 0, :], in0=ta4[:, :, 0, :], in1=tb4[:, :, 1, :], op=ALU.subtract),
                     reads=[n_ta, n_tb], writes=[n_qb + "a"])
                P.op("pool", lambda e: e.tensor_tensor(out=qb4[:, :, 1, :], in0=ta4[:, :, 1, :], in1=tb4[:, :, 0, :], op=ALU.add),
                     reads=[n_ta, n_tb], writes=[n_qb + "b"])

            def s8():
                for k in range(4):
                    P.op("pe", lambda e, k=k: e.transpose(out=psb[:, k * 128:(k + 1) * 128], in_=qb[:, k * 128:(k + 1) * 128], identity=ident),
                         reads=[n_qb + "a", n_qb + "b", "ident"], writes=[PSN[tpi]])

            def s9():
                P.op("act", lambda e: e.activation(out=qT, in_=psb[:, 0:512].rearrange("p (a b) -> p a b", b=128), func=AF.Copy),
                     reads=[PSN[tpi]], writes=[n_qT])

            def s10():
                P.dma("sp", "A_qs%d" % z, lambda e: e.dma_start(out=dstv[:, :, col:col + 128], in_=qT), reads=[n_qT])
            return [s_mm, s1, s2, s3, s4, s5, s6, s7, s8, s9, s10]

        items = []
        extra = {}
        first_group_of_tile = {}
        for t in range(NKT):
            if t >= 16:
                qidx = t - 16
            elif t == 15:
                qidx = 16
            else:
                qidx = None
            groups = [1, 2, 4, 5] if qidx is None else [0, 1, 2, 3, 4, 5]
            first_group_of_tile[t] = len(items)
            for g in groups:
                items.append(group_item(t, g, len(items), qidx))
        for st in prologue(0):
            st()
        for t in range(NKT - 1):
            g0 = first_group_of_tile[t]
            for k, st in enumerate(prologue(t + 1)):
                extra.setdefault(g0 + k, []).append(st)
        maxs = max(len(it) for it in items)
        for step in range(len(items) + maxs):
            for sidx in range(maxs - 1, -1, -1):
                g = step - sidx
                if 0 <= g < len(items) and sidx < len(items[g]):
                    items[g][sidx]()
            for st in extra.get(step, []):
                st()

    phase_A()
    barrier()
    fin = []
    if stop_after == "A":
        P.emit()
        return nc

    def phase_BC():
        reset_arena()
        Wo = AR.alloc([8, D], BF16)
        C = AR.alloc([NQT, D], BF16)
        cm = AR.alloc([4, 512], BF16)
        kT = [AR.alloc([NK], BF16) for _ in range(2)]
        vbuf = [AR.alloc([NKT, 128], BF16) for _ in range(2)]
        qa = [AR.alloc([NQ], BF16) for _ in range(2)]
        qb_ = [AR.alloc([NQ], BF16) for _ in range(2)]
        pT = [AR.alloc([2, 512], BF16) for _ in range(2)]
        oT = [AR.alloc([512], BF16) for _ in range(2)]
        gsm = AR.alloc([320], F32)
        lamv = AR.alloc([128], F32)
        lt = AR.alloc([64], F32)
        lsm = AR.alloc([8], F32)
        Gsub = AR.alloc([4, 64], F32)
        gbj = AR.alloc([9, 16], F32)
        biaspad = AR.alloc([NQT, 128], BF16)
        gball = AR.alloc([NQT, 16], F32)
        ownm = AR.alloc([NQT, 16], F32)
        gmall = AR.alloc([NQT, 16], F32)
        cmpb = AR.alloc([NQT, 16, 16], BF16)
        rnk = AR.alloc([NQT, 16], F32)
        bia1 = AR.alloc([NQT, 16], F32)
        bia2 = AR.alloc([NQT, 16], F32)
        zt = AR.alloc([128], BF16)
        kbs = AR.alloc([16], F32)
        kbb = AR.alloc([16], BF16)
        gm = AR.alloc([16], F32)
        top8 = AR.alloc([8], F32)
        thr = AR.alloc([1], F32)
        rcp = AR.alloc([8], F32)
        oa = AR.alloc([4, 64], F32)
        ob = AR.alloc([4, 64], F32)
        dd = AR.alloc([4, 64], F32)
        sqd = AR.alloc([4, 64], F32)
        ssq4 = AR.alloc([4], F32)
        rs4 = AR.alloc([4], F32)
        CT = AR.alloc([8, 128], BF16)
        xr = [AR.alloc([D], F32) for _ in range(2)]
        x1 = [AR.alloc([D], F32) for _ in range(2)]

        load_w_bf16(Wo, "B_Wo", e_w_out, "B_w", nsplit=2)
        wo_names = ["B_Wo_0"] * 4 + ["B_Wo_4"] * 4
        P.op("pool", lambda e: e.memset(cm, 0.0), writes=["cm"])
        for i in range(4):
            P.op("pool", lambda e, i=i: e.affine_select(out=cm[:, i, :], in_=cm[:, i, :], pattern=[[1, 512]], compare_op=ALU.is_ge,
                                                        fill=NEG, base=-128 * i, channel_multiplier=-1), reads=["cm"], writes=["cm"])
        for b in range(2):
            P.op("pool", lambda e, b=b: e.memset(qa[b][32:64, :], 0.0), writes=["qa%d_z" % b])
            P.op("pool", lambda e, b=b: e.memset(qa[b][64:128, :], 0.0), writes=["qa%d_bias" % b])
            P.op("pool", lambda e, b=b: e.memset(qb_[b][0:32, :], 0.0), writes=["qb%d_z" % b])
            P.op("pool", lambda e, b=b: e.memset(qb_[b][64:128, :], 0.0), writes=["qb%d_z" % b])
            P.op("pool", lambda e, b=b: e.memset(vbuf[b][:, :, 64:128], 0.0), writes=["vb%d_1" % b])
            P.op("pool", lambda e, b=b: e.memset(vbuf[b][:, :, 64:65], 1.0), reads=["vb%d_1" % b], writes=["vb%d_1" % b])
            P.op("pool", lambda e, b=b: e.memset(kT[b][64:128, :], 0.0), writes=["kT%d_oh" % b])
            P.dma("sp", "B_oh%d" % b, lambda e, b=b: e.dma_start(out=kT[b][64:80, :], in_=onehot), writes=["kT%d_oh" % b])
        P.op("pool", lambda e: e.memset(biaspad, 0.0), writes=["biaspad"])
        P.op("pool", lambda e: e.memset(zt, 0.0), writes=["zt"])
        P.dma("sp", "B_gs", lambda e: e.dma_start(out=gsm, in_=gains.partition_broadcast(128)), writes=["B_gsm"])
        P.dma("sp", "B_lm", lambda e: e.dma_start(out=lamv, in_=lams.partition_broadcast(128)), writes=["B_lamv"])
        l4 = lamv.rearrange("p (a b c) -> p a b c", b=2, c=32)
        P.op("dve", lambda e: e.tensor_tensor(out=lt.rearrange("p (a c) -> p a c", c=32), in0=l4[:, :, 0, :], in1=l4[:, :, 1, :],
                                              op=ALU.mult), reads=["B_lamv"], writes=["B_lt"])
        P.op("dve", lambda e: e.tensor_reduce(out=lsm[:, 0:2], in_=lt.rearrange("p (a c) -> p a c", c=32), axis=AX.X, op=ALU.add),
             reads=["B_lt"], writes=["B_lsm"])
        P.op("act", lambda e: e.activation(out=lsm[:, 2:4], in_=lsm[:, 0:2], func=AF.Exp), reads=["B_lsm"], writes=["B_lsm2"])
        P.op("dve", lambda e: e.tensor_tensor(out=lsm[:, 4:5], in0=lsm[:, 3:4], in1=lsm[:, 2:3], op=ALU.subtract),
             reads=["B_lsm2"], writes=["B_lsm3"])
        neglam = lsm[:, 5:6]
        P.op("dve", lambda e: e.tensor_scalar(out=neglam, in0=lsm[:, 4:5], scalar1=-LAMBDA_INIT0, scalar2=None, op0=ALU.add),
             reads=["B_lsm3"], writes=["neglam"])
        P.op("dve", lambda e: e.tensor_scalar(out=Gsub, in0=gsm[:, 64:128].unsqueeze(1).to_broadcast([128, 4, 64]),
                                              scalar1=8.0 * (1.0 - LAMBDA_INIT0), scalar2=None, op0=ALU.mult),
             reads=["B_gsm"], writes=["Gsub"])
        for j in range(8):
            P.op("dve", lambda e, j=j: e.tensor_copy(out=gbj[:, j, :], in_=flg[:, 4:20]), reads=["flg"], writes=["gbj"])
            P.op("dve", lambda e, j=j: e.memset(gbj[:, j, 8 + j:16], -1e30), reads=["gbj"], writes=["gbj"])
        P.op("dve", lambda e: e.memset(gbj[:, 8, :], 0.0), reads=["gbj"], writes=["gbj"])
        P.op("dve", lambda e: e.memset(gbj[:, 8, 7:16], -1e30), reads=["gbj"], writes=["gbj"])

        P.op("dve", lambda e: e.memset(ownm, 1.0), writes=["ownm"])
        for qi in range(NQT):
            jj = qi // 2 if qi < 16 else 8
            ownb = 8 + qi // 2 if qi < 16 else 7
            P.op("dve", lambda e, qi=qi, jj=jj: e.tensor_copy(out=gball[:, qi, :], in_=gbj[:, jj, :]), reads=["gbj"], writes=["gball"])
            P.op("dve", lambda e, qi=qi, ownb=ownb: e.memset(ownm[:, qi, ownb:ownb + 1], 0.0), reads=["ownm"], writes=["ownm"])
        units = [("d", h) for h in range(8)] + [("m", h) for h in range(8)]
        groups = [(gi * 512, 512, 20 + 4 * gi, True, gi * 4) for gi in range(4)] + [(2048, 128, 16, False, 16)]
        gcount = [0]
        sc = [0]

        def issue_loads(u):
            kind, h = units[u]
            b = u % 2
            Ksrc, Vsrc, Qsrc = (KdT, Vd, QdT) if kind == "d" else (KmT, Vm, QmT)
            fns = [lambda e: e.dma_start(out=kT[b][0:64, :], in_=Ksrc[h * 64:(h + 1) * 64, :])]
            for pt in range(4):
                fns.append(lambda e, pt=pt: e.dma_start(out=vbuf[b][:, pt * 8:(pt + 1) * 8, 0:64],
                                                       in_=Vsrc[h].rearrange("(t p) d -> p t d", p=128)[:, pt * 8:(pt + 1) * 8, :]))
            wr = ["kT%d" % b] + ["vb%d_%d" % (b, pt) for pt in range(4)]
            if kind == "d":
                fns.append(lambda e: e.dma_start(out=qa[b][0:32, :], in_=Qsrc[h * 64:h * 64 + 32, :]))
                fns.append(lambda e: e.dma_start(out=qb_[b][32:64, :], in_=Qsrc[h * 64 + 32:h * 64 + 64, :]))
                wr += ["qa%d" % b, "qb%d" % b]
            else:
                fns.append(lambda e: e.dma_start(out=qa[b][0:64, :], in_=Qsrc[h * 64:(h + 1) * 64, :]))
                wr += ["qa%d" % b, "qa%d_z" % b]
            P.dma_group("sp", "B_ld%d" % b, fns, writes=wr)

        def gating_stages(u):
            kind, h = units[u]
            b = u % 2
            ps6b = PS[6].bitcast(BF16)

            def g1():
                P.op("dve", lambda e: e.tensor_reduce(out=kbs[0:64, :], in_=kT[b][0:64, :].rearrange("p (a c) -> p a c", c=256),
                                                      axis=AX.X, op=ALU.add), reads=["kT%d" % b], writes=["kbs"])
                P.op("dve", lambda e: e.tensor_scalar(out=kbb[0:64, :], in0=kbs[0:64, :], scalar1=1.0 / 256.0, scalar2=None, op0=ALU.mult),
                     reads=["kbs"], writes=["kbb"])
                for qi in range(NQT):
                    P.op("pe", lambda e, qi=qi: e.matmul(PS[7][:, qi * 16:(qi + 1) * 16], lhsT=qa[b][0:64, qi * 128:(qi + 1) * 128], rhs=kbb[0:64, :],
                                                        start=True, stop=True), reads=["qa%d" % b, "kbb"], writes=[PSN[7]])

            def g2():
                P.op("dve", lambda e: e.tensor_tensor(out=gmall, in0=PS[7][:, 0:NQT * 16].rearrange("p (q n) -> p q n", n=16), in1=gball, op=ALU.add),
                     reads=[PSN[7], "gball"], writes=["gmall"])
                P.op("dve", lambda e: e.tensor_tensor(out=cmpb, in0=gmall.unsqueeze(2).to_broadcast([128, NQT, 16, 16]),
                                                      in1=gmall.unsqueeze(3).to_broadcast([128, NQT, 16, 16]), op=ALU.is_gt),
                     reads=["gmall"], writes=["cmpb"])
                P.op("dve", lambda e: e.tensor_reduce(out=rnk, in_=cmpb, axis=AX.X, op=ALU.add), reads=["cmpb"], writes=["rnk"])
                P.op("dve", lambda e: e.tensor_scalar(out=bia1, in0=rnk, scalar1=2.5, scalar2=NEG, op0=ALU.is_gt, op1=ALU.mult),
                     reads=["rnk"], writes=["bia1"])
                P.op("dve", lambda e: e.tensor_scalar(out=bia2, in0=gmall, scalar1=-1e29, scalar2=NEG, op0=ALU.is_lt, op1=ALU.mult),
                     reads=["gmall"], writes=["bia2"])
                P.op("dve", lambda e: e.tensor_tensor(out=bia1, in0=bia1, in1=bia2, op=ALU.add), reads=["bia1", "bia2"], writes=["bia1"])
                P.op("dve", lambda e: e.tensor_tensor(out=biaspad[:, :, 64:80], in0=bia1, in1=ownm, op=ALU.mult),
                     reads=["bia1", "ownm"], writes=["biaspad"])

            def g3():
                for (q0, nb) in ((0, 8), (8, 8), (16, 1)):
                    for k in range(nb):
                        P.op("pe", lambda e, q0=q0, k=k: e.transpose(out=ps6b[:, k * 128:(k + 1) * 128], in_=biaspad[:, q0 + k, :], identity=ident),
                             reads=["biaspad", "ident"], writes=[PSN[6]])
                    P.op("act", lambda e, q0=q0, nb=nb: e.activation(out=qa[b][64:80, q0 * 128:(q0 + nb) * 128], in_=ps6b[64:80, 0:nb * 128], func=AF.Copy),
                         reads=[PSN[6]], writes=["qa%d_bias" % b])
            return [g1, g2, g3]

        def attention(u, hooks=()):
            kind, h = units[u]
            b = u % 2
            isd = kind == "d"
            dk = 128
            scale = (32.0 ** -0.5) if isd else 0.125
            for gi_, grp in enumerate(groups):
                if gi_ < len(hooks):
                    hooks[gi_]()
                do_group(kind, h, b, isd, dk, scale, *grp)

        def do_group(kind, h, b, isd, dk, scale, qc0, N, nkt, usepast, t0):
            R = N // 128
            gidx = gcount[0]
            gcount[0] += 1
            if isd:
                maps = [(qa[b], ["qa%d" % b, "qa%d_z" % b, "qa%d_bias" % b], 4), (qb_[b], ["qb%d" % b, "qb%d_z" % b], 5)]
                tbank = 6
            else:
                maps = [(qa[b], ["qa%d" % b, "qa%d_bias" % b], 4)]
                tbank = 5
            npair = nkt // 2
            steps = [(mi, kp) for mi in range(len(maps)) for kp in range(npair)]
            slot = {}

            def col0_of(kp):
                return 256 if (usepast and kp == npair - 1) else 0

            def qk(i):
                mi, kp = steps[i]
                Q, qn_, _ = maps[mi]
                s_ = sc[0] % 2
                sc[0] += 1
                slot[i] = s_
                c0_ = col0_of(kp)
                for hfi in range(2):
                    kt = 2 * kp + hfi
                    bank = 2 * s_ + hfi
                    if usepast:
                        di = kt - (nkt - 4)
                    else:
                        di = 0 if kt == nkt - 1 else -1
                    diag = di >= 0
                    P.op("pe", lambda e, kt=kt, bank=bank, diag=diag: e.matmul(PS[bank][:, c0_:N], lhsT=kT[b][0:dk, kt * 128:(kt + 1) * 128],
                                                                             rhs=Q[0:dk, qc0 + c0_:qc0 + N], start=True, stop=not diag),
                         reads=["kT%d" % b, "kT%d_oh" % b] + qn_, writes=[PSN[bank]])
                    if diag:
                        P.op("pe", lambda e, bank=bank, di=di: e.matmul(PS[bank][:, c0_:N], lhsT=ident, rhs=cm[:, di, c0_:N], start=False, stop=True),
                             reads=["ident", "cm"], writes=[PSN[bank]])

            def ex_pv(i):
                mi, kp = steps[i]
                _, _, abank = maps[mi]
                s_ = slot[i]
                c0_ = col0_of(kp)
                src = PSALL[:, 2 * s_ * 512:(2 * s_ + 2) * 512].rearrange("p (b n) -> p b n", n=512)[:, :, c0_:N]
                dst = pT[s_][:, :, c0_:N]
                rd = [PSN[2 * s_], PSN[2 * s_ + 1]]
                if usepast and 2 * kp < 16:
                    P.op("act", lambda e: e.activation(out=dst, in_=src, func=AF.Exp, scale=scale, bias=flg[:, 0:1]),
                         reads=rd + ["flg"], writes=["pT%d" % s_])
                else:
                    P.op("act", lambda e: e.activation(out=dst, in_=src, func=AF.Exp, scale=scale), reads=rd, writes=["pT%d" % s_])
                for hfi in range(2):
                    kt = 2 * kp + hfi
                    P.op("pe", lambda e, kt=kt, hfi=hfi: e.matmul(PS[abank][:, c0_:N], lhsT=vbuf[b][:, kt, :], rhs=pT[s_][:, hfi, c0_:N],
                                                                  start=(kt == 0), stop=(kt == nkt - 1)),
                         reads=["pT%d" % s_, "vb%d_%d" % (b, kt // 8), "vb%d_1" % b], writes=[PSN[abank]])

            qk(0)
            for i in range(len(steps)):
                if i + 1 < len(steps):
                    qk(i + 1)
                ex_pv(i)
            tpb = PS[tbank].bitcast(BF16)
            for mi, (_, _, abank) in enumerate(maps):
                P.op("dve", lambda e, mi=mi, abank=abank: e.tensor_copy(out=oT[mi][0:65, 0:N], in_=PS[abank][0:65, 0:N]),
                     reads=[PSN[abank]], writes=["oT%d" % mi])
                for r in range(R):
                    c0 = (mi * 4 + r) * 66
                    P.op("pe", lambda e, mi=mi, r=r, c0=c0: e.transpose(out=tpb[:, c0:c0 + 65], in_=oT[mi][0:65, r * 128:(r + 1) * 128],
                                                                      identity=ident[0:65, 0:65]),
                         reads=["oT%d" % mi, "ident"], writes=[PSN[tbank]])
            cn = "C_g%d" % t0
            nt = PSN[tbank]
            accA = tpb[:, 0:R * 66].rearrange("p (r c) -> p r c", c=66)
            if isd:
                accB = tpb[:, 4 * 66:(4 + R) * 66].rearrange("p (r c) -> p r c", c=66)
                P.op("dve", lambda e: e.reciprocal(out=rcp[:, 0:R], in_=accA[:, :, 64]), reads=[nt], writes=["rcpa"])
                P.op("dve", lambda e: e.reciprocal(out=rcp[:, 4:4 + R], in_=accB[:, :, 64]), reads=[nt], writes=["rcpb"])
                P.op("dve", lambda e: e.tensor_tensor(out=oa[:, 0:R, :], in0=accA[:, :, 0:64],
                                                      in1=rcp[:, 0:R].unsqueeze(2).to_broadcast([128, R, 64]), op=ALU.mult),
                     reads=[nt, "rcpa"], writes=["oa"])
                P.op("dve", lambda e: e.tensor_tensor(out=ob[:, 0:R, :], in0=accB[:, :, 0:64],
                                                      in1=rcp[:, 4:4 + R].unsqueeze(2).to_broadcast([128, R, 64]), op=ALU.mult),
                     reads=[nt, "rcpb"], writes=["ob"])
                P.op("dve", lambda e: e.scalar_tensor_tensor(out=dd[:, 0:R, :], in0=ob[:, 0:R, :], scalar=neglam, in1=oa[:, 0:R, :],
                                                             op0=ALU.mult, op1=ALU.add), reads=["oa", "ob", "neglam"], writes=["dd"])
                P.op("pool", lambda e: e.tensor_tensor(out=sqd[:, 0:R, :], in0=dd[:, 0:R, :], in1=dd[:, 0:R, :], op=ALU.mult),
                     reads=["dd"], writes=["sqd"])
                P.op("dve", lambda e: e.tensor_reduce(out=ssq4[:, 0:R], in_=sqd[:, 0:R, :], axis=AX.X, op=ALU.add),
                     reads=["sqd"], writes=["ssq4"])
                rms_rstd(ssq4[:, 0:R], "ssq4", rs4[:, 0:R], "rs4", 64 * EPS)
                P.op("dve", lambda e: e.tensor_tensor(out=dd[:, 0:R, :], in0=dd[:, 0:R, :],
                                                      in1=rs4[:, 0:R].unsqueeze(2).to_broadcast([128, R, 64]), op=ALU.mult),
                     reads=["dd", "rs4"], writes=["dd"])
                P.op("pool", lambda e: e.tensor_tensor(out=C[:, t0:t0 + R, h * 64:(h + 1) * 64], in0=dd[:, 0:R, :], in1=Gsub[:, 0:R, :],
                                                       op=ALU.mult), reads=["dd", "Gsub"], writes=[cn])
            else:
                P.op("dve", lambda e: e.reciprocal(out=rcp[:, 0:R], in_=accA[:, :, 64]), reads=[nt], writes=["rcpa"])
                P.op("dve", lambda e: e.tensor_tensor(out=C[:, t0:t0 + R, 512 + h * 64:512 + (h + 1) * 64], in0=accA[:, :, 0:64],
                                                      in1=rcp[:, 0:R].unsqueeze(2).to_broadcast([128, R, 64]), op=ALU.mult),
                     reads=[nt, "rcpa"], writes=[cn])

        issue_loads(0)
        for u in range(len(units)):
            if u + 1 < len(units):
                issue_loads(u + 1)
            hooks = gating_stages(u + 1) if (u + 1 < len(units) and units[u + 1][0] == "m") else []
            attention(u, hooks)

        CTs = [CT, AR.alloc([8, 128], BF16)]

        def c_pre(tl):
            b2 = tl % 2
            lt_ = 16 + tl if tl < 16 else 15
            cn = "C_g%d" % (tl // 4 * 4 if tl < 16 else 16)
            P.dma("sp", "C_x%d" % b2, lambda e: e.dma_start(out=xr[b2], in_=xin[lt_ * 128:(lt_ + 1) * 128, :]), writes=["C_xr%d" % b2])
            transpose8(C[:, tl, :], cn, CTs[b2], "C_CT%d" % b2, 6 + b2)

        def c_main(tl):
            b2 = tl % 2
            for hh in range(2):
                bank = 2 * b2 + hh
                for k in range(8):
                    P.op("pe", lambda e, k=k, hh=hh, bank=bank: e.matmul(PS[bank], lhsT=CTs[b2][:, k, :], rhs=Wo[:, k, hh * 512:(hh + 1) * 512],
                                                                        start=(k == 0), stop=(k == 7)), reads=["C_CT%d" % b2, wo_names[k]], writes=[PSN[bank]])
                P.op("dve", lambda e, hh=hh, bank=bank: e.tensor_tensor(out=x1[b2][:, hh * 512:(hh + 1) * 512], in0=PS[bank],
                                                                       in1=xr[b2][:, hh * 512:(hh + 1) * 512], op=ALU.add),
                     reads=[PSN[bank], "C_xr%d" % b2], writes=["C_x1%d_%d" % (b2, hh)])
            P.dma("sp", "C_s%d" % b2, lambda e: e.dma_start(out=X1[tl * 128:(tl + 1) * 128, :], in_=x1[b2]),
                  reads=["C_x1%d_0" % b2, "C_x1%d_1" % b2])

        c_pre(0)
        for tl in range(NQT):
            if tl + 1 < NQT:
                c_pre(tl + 1)
            c_main(tl)

    phase_BC()
    barrier()
    if stop_after == "BC":
        P.emit()
        return nc

    def phase_FFN(Xin, Xout, li, ntiles):
        reset_arena()
        gbc = AR.alloc([D], F32)
        xs = [AR.alloc([D], F32) for _ in range(3)]
        xr2 = [AR.alloc([D], F32) for _ in range(2)]
        sq = AR.alloc([D], BF16)
        hbs = [AR.alloc([D], BF16) for _ in range(2)]
        hT = AR.alloc([8, 1152], BF16)
        wo = AR.alloc([22, D], BF16)
        wi = [AR.alloc([8, 256], BF16) for _ in range(3)]
        AT = AR.alloc([22, 1152], BF16)
        sg = [AR.alloc([512], BF16) for _ in range(2)]
        yb = [AR.alloc([D], F32) for _ in range(2)]
        ssqs = [AR.alloc([1], F32) for _ in range(2)]
        rss = [AR.alloc([1], F32) for _ in range(2)]
        pf = "F%d" % li
        load_gain_bc(gbc, pf + "_gbc", ffn_g[li:li + 1, :], pf + "_g", 32.0)
        win_v = f_w_in[li].rearrange("(k p) n -> p k n", p=128)
        wout_v = f_w_out[li].rearrange("(j p) n -> p j n", p=128)
        half = (ntiles + 1) // 2
        passes = [list(range(0, half)), list(range(half, ntiles))]
        cnt = {"pro": 0, "y": 0, "g": 0}

        def load_wi(j):
            b3 = j % 3
            P.dma_group("pool", pf + "_wi%d" % b3, [
                lambda e: e.dma_start(out=wi[b3][:, :, 0:128], in_=win_v[:, :, j * 128:(j + 1) * 128]),
                lambda e: e.dma_start(out=wi[b3][:, :, 128:256], in_=win_v[:, :, DFF + j * 128:DFF + (j + 1) * 128])],
                writes=[pf + "_wi%dg" % b3, pf + "_wi%du" % b3])

        def pro_a(i, tl):
            c = cnt["pro"]
            cnt["pro"] += 1
            x3, h2 = c % 3, c % 2
            xn, hbn, pfx = pf + "_xs%d" % x3, pf + "_hb%d" % h2, pf + "n%d" % h2
            P.dma("sp", pf + "_x%d" % x3, lambda e: e.dma_start(out=xs[x3], in_=Xin[tl * 128:(tl + 1) * 128, :]), writes=[xn])
            norm_tile(xs[x3], xn, gbc, pf + "_gbc", hbs[h2], hbn, sq, pf + "_sq", ssqs[h2], rss[h2], pfx)
            return (hbs[h2], hbn)

        def pro_b(i, hbinfo):
            transpose8(hbinfo[0], hbinfo[1], hT[:, :, i * 128:(i + 1) * 128], pf + "_hT%d" % i, 0)

        for j0 in (0, 11):
            P.dma("pool", pf + "_wo%d" % j0, lambda e, j0=j0: e.dma_start(out=wo[:, j0:j0 + 11, :], in_=wout_v[:, j0:j0 + 11, :]),
                  writes=[pf + "_wo%d" % j0])
        for j in range(3):
            load_wi(j)
        for i, tl in enumerate(passes[0]):
            pro_b(i, pro_a(i, tl))
        for pi, tiles in enumerate(passes):
            ntok = len(tiles) * 128
            nsub = (ntok + 511) // 512
            sbw = ((ntok // 128 + nsub - 1) // nsub) * 128
            for j in range(22):
                b3 = j % 3
                if j >= 3:
                    load_wi(j)
                for sb, s0 in enumerate(range(0, ntok, sbw)):
                    n = min(sbw, ntok - s0)
                    gb = cnt["g"] % 2
                    cnt["g"] += 1
                    hnames = [pf + "_hT%d" % i for i in range(s0 // 128, (s0 + n) // 128)]
                    for (col, bank, wn) in ((0, gb, "g"), (128, 2 + gb, "u")):
                        for k in range(8):
                            P.op("pe", lambda e, k=k, col=col, bank=bank, b3=b3, s0=s0, n=n: e.matmul(
                                PS[bank][:, 0:n], lhsT=wi[b3][:, k, col:col + 128], rhs=hT[:, k, s0:s0 + n], start=(k == 0), stop=(k == 7)),
                                reads=hnames + [pf + "_wi%d%s" % (b3, wn)], writes=[PSN[bank]])
                    P.op("act", lambda e, gb=gb, n=n: e.activation(out=sg[gb][:, 0:n], in_=PS[gb][:, 0:n], func=AF.Silu),
                         reads=[PSN[gb]], writes=[pf + "_sg%d" % gb])
                    P.op("dve", lambda e, gb=gb, n=n, j=j, s0=s0: e.tensor_tensor(out=AT[:, j, s0:s0 + n], in0=PS[2 + gb][:, 0:n],
                                                                                in1=sg[gb][:, 0:n], op=ALU.mult),
                         reads=[PSN[2 + gb], pf + "_sg%d" % gb], writes=[pf + "_AT%d_%d" % (sb, j % 2)])
            nxt = passes[pi + 1] if pi + 1 < len(passes) else []
            if nxt:
                for j in range(3):
                    load_wi(j)
            for i, tl in enumerate(tiles):
                hbinfo = pro_a(i, nxt[i]) if i < len(nxt) else None
                yi = cnt["y"] % 2
                cnt["y"] += 1
                P.dma("sp", pf + "_xr%d" % yi, lambda e, tl=tl, yi=yi: e.dma_start(out=xr2[yi], in_=Xin[tl * 128:(tl + 1) * 128, :]),
                      writes=[pf + "_xr%d" % yi])
                for hh in range(2):
                    bank = 4 + (2 * yi + hh)
                    if bank == 7:
                        bank = 3 if False else 7
                    for j in range(22):
                        P.op("pe", lambda e, j=j, i=i, hh=hh, bank=bank: e.matmul(PS[bank], lhsT=AT[:, j, i * 128:(i + 1) * 128],
                                                                               rhs=wo[:, j, hh * 512:(hh + 1) * 512],
                                                                               start=(j == 0), stop=(j == 21)),
                             reads=[pf + "_AT%d_0" % (i * 128 // sbw), pf + "_AT%d_1" % (i * 128 // sbw), pf + "_wo%d" % (0 if j < 11 else 11)],
                             writes=[PSN[bank]])
                    P.op("dve", lambda e, hh=hh, bank=bank, yi=yi: e.tensor_tensor(out=yb[yi][:, hh * 512:(hh + 1) * 512], in0=PS[bank],
                                                                                 in1=xr2[yi][:, hh * 512:(hh + 1) * 512], op=ALU.add),
                         reads=[PSN[bank], pf + "_xr%d" % yi], writes=[pf + "_y%d_%d" % (yi, hh)])
                P.dma("sp", pf + "_ys%d" % yi, lambda e, tl=tl, yi=yi: e.dma_start(out=Xout[tl * 128:(tl + 1) * 128, :], in_=yb[yi]),
                      reads=[pf + "_y%d_0" % yi, pf + "_y%d_1" % yi])
                if hbinfo is not None:
                    pro_b(i, hbinfo)

    phase_FFN(X1, X2, 0, NQT)
    barrier()
    if stop_after == "F0":
        P.emit()
        return nc

    def layer_norm_free(src, src_n, dst, dst_n, g_bc, g_n, b_bc, b_n, tmp, tmp_n, sm, pfx, eng2="pool"):
        P.op("dve", lambda e: e.tensor_reduce(out=sm[:, 0:1], in_=src, axis=AX.X, op=ALU.add), reads=[src_n], writes=[pfx + "_s1"])
        P.op("dve", lambda e: e.tensor_scalar(out=sm[:, 1:2], in0=sm[:, 0:1], scalar1=-1.0 / 512.0, scalar2=None, op0=ALU.mult),
             reads=[pfx + "_s1"], writes=[pfx + "_nm"])
        P.op("dve", lambda e: e.tensor_scalar(out=tmp, in0=src, scalar1=sm[:, 1:2], scalar2=None, op0=ALU.add),
             reads=[src_n, pfx + "_nm"], writes=[tmp_n])
        P.op("act", lambda e: e.activation(out=src, in_=tmp, func=AF.Square, accum_out=sm[:, 2:3]), reads=[tmp_n], writes=[src_n, pfx + "_ss"])
        P.op("act", lambda e: e.activation(out=sm[:, 3:4], in_=sm[:, 2:3], func=AF.Ln, bias=float(EPS), scale=1.0 / 512.0),
             reads=[pfx + "_ss"], writes=[pfx + "_rs"])
        P.op("act", lambda e: e.activation(out=sm[:, 3:4], in_=sm[:, 3:4], func=AF.Exp, scale=-0.5), reads=[pfx + "_rs"], writes=[pfx + "_rs"])
        P.op("dve", lambda e: e.scalar_tensor_tensor(out=tmp, in0=tmp, scalar=sm[:, 3:4], in1=g_bc, op0=ALU.mult, op1=ALU.mult),
             reads=[tmp_n, pfx + "_rs", g_n], writes=[tmp_n])
        P.op(eng2, lambda e: e.tensor_tensor(out=dst, in0=tmp, in1=b_bc, op=ALU.add), reads=[tmp_n, b_n], writes=[dst_n])

    def phase_D():
        reset_arena()
        Wi1 = AR.alloc([8, 2048], BF16)
        Wo1 = AR.alloc([8, D], BF16)
        Dg = AR.alloc([31, 4, 128], BF16)
        wsf = AR.alloc([8, 128], F32)
        wsT = AR.alloc([8, 128], BF16)
        cbuf = AR.alloc([4, 2080], BF16)
        gbc = AR.alloc([D], F32)
        brow = AR.alloc([1024], F32)
        glgb = AR.alloc([1024], F32)
        cvv = AR.alloc([1536], F32)
        bsT = AR.alloc([8], F32)
        obc = AR.alloc([8], F32)
        cw = AR.alloc([4, 31], F32)
        xt = [AR.alloc([D], F32) for _ in range(3)]
        sq = AR.alloc([D], BF16)
        NZ = 2
        S_ = []
        for z in range(NZ):
            S_.append(dict(
                hb=AR.alloc([D], BF16), hT=AR.alloc([8, 128], BF16), sig=AR.alloc([4, 128], F32), ctmp=AR.alloc([128], F32),
                t0u=AR.alloc([512], F32), t0v=AR.alloc([512], F32), w1u=AR.alloc([512], F32), w1v=AR.alloc([512], F32),
                w2=AR.alloc([512], F32), w3=AR.alloc([512], F32), gvn=AR.alloc([512], BF16), CC=AR.alloc([D], BF16),
                CCT=AR.alloc([8, 128], BF16), sm=AR.alloc([8], F32), sm2=AR.alloc([8], F32), ssq=AR.alloc([1], F32), rs=AR.alloc([1], F32)))
        yb = [AR.alloc([D], F32) for _ in range(2)]

        load_w_bf16(Wi1, "D_Wi", o_w_in, "D_wi", nsplit=4)
        wi_n = ["D_Wi_%d" % (k // 2 * 2) for k in range(8)]
        load_w_bf16(Wo1, "D_Wo", o_w_out, "D_wo", nsplit=2)
        wo_n = ["D_Wo_%d" % (k // 4 * 4) for k in range(8)]
        load_gain_bc(gbc, "D_gbc", attn_g[1:2, :], "D_g", 32.0)
        P.dma_group("sp", "D_c", [
            lambda e: e.dma_start(out=brow, in_=o_b_row[:, 0:1024].partition_broadcast(128)),
            lambda e: e.dma_start(out=glgb, in_=gl_gb.partition_broadcast(128)),
            lambda e: e.dma_start(out=cvv, in_=cvec.partition_broadcast(128)),
            lambda e: e.dma_start(out=bsT, in_=bsT_in),
            lambda e: e.dma_start(out=obc, in_=o_b_col),
            lambda e: e.dma_start(out=cw, in_=cwT),
            lambda e: e.dma_start(out=wsf, in_=wsT_in)],
            writes=["D_brow", "D_glgb", "D_cvv", "D_bsT", "D_obc", "D_cw", "D_wsf"])
        P.op("pool", lambda e: e.affine_select(out=wsf, in_=wsf, pattern=[[0, 8], [1, 128]], compare_op=ALU.is_ge, fill=0.0,
                                               base=0, channel_multiplier=-1), reads=["D_wsf"], writes=["D_wsf"])
        P.op("pool", lambda e: e.tensor_copy(out=wsT, in_=wsf), reads=["D_wsf"], writes=["D_wsT"])
        for c in range(4):
            P.op("dve" if c % 2 == 0 else "pool", lambda e, c=c: e.tensor_tensor(
                out=Dg[:, :, c, :], in0=ident.unsqueeze(1).to_broadcast([128, 31, 128]),
                in1=cw[:, c, :].unsqueeze(2).to_broadcast([128, 31, 128]), op=ALU.mult), reads=["ident", "D_cw"], writes=["D_Dg%d" % c])

        def gelu(ps, psn, bias_bc, t0, t0n, w1, w1n, dst, dstn):
            P.op("dve", lambda e: e.tensor_tensor(out=t0, in0=ps, in1=bias_bc, op=ALU.add), reads=[psn, "D_brow"], writes=[t0n])
            P.op("act", lambda e: e.activation(out=w1, in_=t0, func=AF.Square), reads=[t0n], writes=[w1n])
            P.op("dve", lambda e: e.tensor_scalar(out=w1, in0=w1, scalar1=0.044715, scalar2=1.0, op0=ALU.mult, op1=ALU.add),
                 reads=[w1n], writes=[w1n])
            P.op("pool", lambda e: e.tensor_tensor(out=w1, in0=w1, in1=t0, op=ALU.mult), reads=[w1n, t0n], writes=[w1n])
            P.op("act", lambda e: e.activation(out=w1, in_=w1, func=AF.Sigmoid, scale=1.5957691216057308), reads=[w1n], writes=[w1n])
            P.op("pool", lambda e: e.tensor_tensor(out=dst, in0=w1, in1=t0, op=ALU.mult), reads=[w1n, t0n], writes=[dstn])

        def tile_stages(it, tl):
            b2 = it % 3
            z = it % NZ
            B = S_[z]
            N_ = lambda nme: "D_%s%d" % (nme, z)
            hb, hT, sig, ctmp, t0u, t0v, w1u, w1v, w2, w3, gvn, CC, CCT, sm, sm2 = (B[k] for k in (
                "hb", "hT", "sig", "ctmp", "t0u", "t0v", "w1u", "w1v", "w2", "w3", "gvn", "CC", "CCT", "sm", "sm2"))
            halo = tl == 16
            xn = "D_xt%d" % b2
            yi = it % 2

            def f1():
                P.dma("sp", "D_x%d" % b2, lambda e: e.dma_start(out=xt[b2], in_=X2[tl * 128:(tl + 1) * 128, :]), writes=[xn])

            def f2():
                P.op("act", lambda e: e.activation(out=sq, in_=xt[b2], func=AF.Square, accum_out=B["ssq"]), reads=[xn], writes=["D_sq", N_("n") + "_ssq"])
                rms_rstd(B["ssq"], N_("n") + "_ssq", B["rs"], N_("n") + "_rs", D * EPS)

            def f3():
                P.op("dve", lambda e: e.scalar_tensor_tensor(out=hb, in0=xt[b2], scalar=B["rs"], in1=gbc, op0=ALU.mult, op1=ALU.mult),
                     reads=[xn, N_("n") + "_rs", "D_gbc"], writes=[N_("hb")])

            def f4():
                transpose8(hb, N_("hb"), hT, N_("hT"), 0)

            def f5():
                for (base, bank) in ((1024, 1), (1536, 2)):
                    for c in range(4):
                        for k in range(8):
                            P.op("pe", lambda e, base=base, bank=bank, c=c, k=k: e.matmul(
                                PS[bank][:, c * 128:(c + 1) * 128], lhsT=Wi1[:, k, base + c * 128:base + (c + 1) * 128], rhs=hT[:, k, :],
                                start=(k == 0), stop=(k == 7)), reads=[N_("hT"), wi_n[k]], writes=[PSN[bank]])

            def f6():
                for (base, bank) in ((0, 3), (512, 4)):
                    for k in range(8):
                        P.op("pe", lambda e, base=base, bank=bank, k=k: e.matmul(PS[bank], lhsT=hT[:, k, :], rhs=Wi1[:, k, base:base + 512],
                                                                              start=(k == 0), stop=(k == 7)),
                             reads=[N_("hT"), wi_n[k]], writes=[PSN[bank]])

            def f7():
                for c in range(4):
                    P.op("act", lambda e, c=c: e.activation(out=sig[:, c, :], in_=PS[2][:, c * 128:(c + 1) * 128], func=AF.Sigmoid,
                                                            bias=obc[:, 4 + c:5 + c]), reads=[PSN[2], "D_obc"], writes=[N_("sig") + "_%d" % c])
                    if halo:
                        P.op("dve", lambda e, c=c: e.scalar_tensor_tensor(out=ctmp, in0=PS[1][:, c * 128:(c + 1) * 128], scalar=obc[:, c:c + 1],
                                                                          in1=sig[:, c, :], op0=ALU.add, op1=ALU.mult),
                             reads=[PSN[1], "D_obc", N_("sig") + "_%d" % c], writes=[N_("ctmp")])
                        P.op("dve", lambda e, c=c: e.tensor_scalar(out=cbuf[:, c, 0:32], in0=ctmp[:, 96:128], scalar1=flg[:, 1:2], scalar2=None,
                                                                   op0=ALU.mult), reads=[N_("ctmp"), "flg"], writes=["D_cb_h%d" % c])
                    else:
                        P.op("dve", lambda e, c=c: e.scalar_tensor_tensor(out=cbuf[:, c, 32 + tl * 128:32 + (tl + 1) * 128],
                                                                          in0=PS[1][:, c * 128:(c + 1) * 128], scalar=obc[:, c:c + 1],
                                                                          in1=sig[:, c, :], op0=ALU.add, op1=ALU.mult),
                             reads=[PSN[1], "D_obc", N_("sig") + "_%d" % c], writes=["D_cb_%d_%d" % (tl, c)])
            if halo:
                return [f1, f2, f3, f4, f5, f7], []

            def f8():
                gelu(PS[3], PSN[3], brow[:, 0:512], t0u, N_("t0u"), w1u, N_("w1u"), w2, N_("w2"))

            def f9():
                gelu(PS[4], PSN[4], brow[:, 512:1024], t0v, N_("t0v"), w1v, N_("w1v"), w3, N_("w3"))

            def f10():
                layer_norm_free(w3, N_("w3"), gvn, N_("gvn"), glgb[:, 0:512], "D_glgb", glgb[:, 512:1024], "D_glgb", t0v, N_("t0v"), sm, N_("ln1"))

            def f11():
                for g in range(8):
                    P.op("pe", lambda e, g=g: e.matmul(PS[0][:, g * 64:(g + 1) * 64], lhsT=wsT[:, g, :], rhs=gvn[:, g * 64:(g + 1) * 64],
                                                       start=True, stop=True), reads=["D_wsT", N_("gvn")], writes=[PSN[0]])
                P.op("dve", lambda e: e.tensor_tensor(out=t0u.rearrange("p (g d) -> p g d", d=64), in0=PS[0].rearrange("p (g d) -> p g d", d=64),
                                                      in1=bsT[:, 0:8].unsqueeze(2).to_broadcast([128, 8, 64]), op=ALU.add),
                     reads=[PSN[0], "D_bsT"], writes=[N_("t0u")])
                P.op("pool", lambda e: e.tensor_tensor(out=CC[:, 0:512], in0=w2, in1=t0u, op=ALU.mult), reads=[N_("w2"), N_("t0u")], writes=[N_("CCa")])

            def s1():
                for c in range(4):
                    rn = ["D_cb_%d_%d" % (tl, c), ("D_cb_%d_%d" % (tl - 1, c)) if tl > 0 else ("D_cb_h%d" % c)]
                    for j in range(31):
                        o = tl * 128 + 2 + j
                        P.op("pe", lambda e, c=c, j=j, o=o: e.matmul(PS[5][:, c * 128:(c + 1) * 128], lhsT=cbuf[:, c, o:o + 128], rhs=Dg[:, j, c, :],
                                                                  start=(j == 0), stop=(j == 30)), reads=rn + ["D_Dg%d" % c], writes=[PSN[5]])
                P.op("dve", lambda e: e.tensor_tensor(out=w3, in0=PS[5], in1=cvv[:, 0:512], op=ALU.add), reads=[PSN[5], "D_cvv"], writes=[N_("w3")])

            def s2():
                layer_norm_free(w3, N_("w3"), w2, N_("w2"), cvv[:, 512:1024], "D_cvv", cvv[:, 1024:1536], "D_cvv", t0v, N_("t0v"), sm2, N_("ln2"))
                P.op("act", lambda e: e.activation(out=CC[:, 512:1024], in_=w2, func=AF.Silu), reads=[N_("w2")], writes=[N_("CCb")])

            def s3():
                psb = PS[6].bitcast(BF16)
                for k in range(8):
                    P.op("pe", lambda e, k=k: e.transpose(out=psb[:, k * 128:(k + 1) * 128], in_=CC[:, k * 128:(k + 1) * 128], identity=ident),
                         reads=[N_("CCa"), N_("CCb"), "ident"], writes=[PSN[6]])
                P.op("act", lambda e: e.activation(out=CCT, in_=psb[:, 0:1024].rearrange("p (a b) -> p a b", b=128), func=AF.Copy),
                     reads=[PSN[6]], writes=[N_("CCT")])

            def s4():
                for hh in range(2):
                    bank = 7 if hh == 0 else 5
                    for k in range(8):
                        P.op("pe", lambda e, k=k, hh=hh, bank=bank: e.matmul(PS[bank], lhsT=CCT[:, k, :], rhs=Wo1[:, k, hh * 512:(hh + 1) * 512],
                                                                            start=(k == 0), stop=(k == 7)), reads=[N_("CCT"), wo_n[k]], writes=[PSN[bank]])
                    P.op("dve", lambda e, hh=hh, bank=bank: e.tensor_tensor(out=yb[yi][:, hh * 512:(hh + 1) * 512], in0=PS[bank],
                                                                           in1=xt[b2][:, hh * 512:(hh + 1) * 512], op=ALU.add),
                         reads=[PSN[bank], xn], writes=["D_y%d_%d" % (yi, hh)])
                P.dma("sp", "D_ys%d" % yi, lambda e: e.dma_start(out=X3[tl * 128:(tl + 1) * 128, :], in_=yb[yi]),
                      reads=["D_y%d_0" % yi, "D_y%d_1" % yi])
            return [f1, f2, f3, f4, f5, f6, f7, f8, f9, f10, f11], [s1, s2, s3, s4]

        def coalesce(lst):
            out_, run = [], []
            for it_ in lst:
                if it_[0] == "op" and it_[1] == "pe":
                    run.append(it_)
                else:
                    if run:
                        out_.append(("macro", run))
                        run = []
                    out_.append(it_)
            if run:
                out_.append(("macro", run))
            return out_

        def round_robin(chains):
            chains = [c for c in chains if c]
            idx = [0] * len(chains)
            while True:
                progressed = False
                for ci, c in enumerate(chains):
                    if idx[ci] < len(c):
                        P.replay(c[idx[ci]])
                        idx[ci] += 1
                        progressed = True
                if not progressed:
                    break

        order = [16] + list(range(16))
        stages = {}

        def get(it):
            if it not in stages and 0 <= it < len(order):
                stages[it] = tile_stages(it, order[it])
            return stages.get(it)

        F0_, _ = get(0)
        for f in F0_[0:4]:
            f()
        for it, tl in enumerate(order):
            F, Sn = get(it)
            halo = tl == 16
            F[4]()
            if not halo:
                F[5]()
            chains = []
            fc = F[5] if halo else F[6]
            chains.append(coalesce(P.capture(fc)))
            if not halo:
                chains.append(coalesce(P.capture(F[7])))
                chains.append(coalesce(P.capture(lambda: (F[8](), F[9](), F[10]()))))
            if it >= 1 and stages[it - 1][1]:
                Sp = stages[it - 1][1]
                chains.append(coalesce(P.capture(lambda: [st() for st in Sp])))
            nxt = get(it + 1)
            if nxt is not None:
                Fn = nxt[0]
                chains.append(coalesce(P.capture(lambda: [f() for f in Fn[0:4]])))
            round_robin(chains)
        lastS = stages[len(order) - 1][1]
        for st in lastS:
            st()

    phase_D()
    barrier()
    if stop_after == "D":
        P.emit()
        return nc

    phase_FFN(X3, out, 1, 16)
    barrier()
    P.emit()
    return nc


def _bf16(a):
    import ml_dtypes
    return np.asarray(a, np.float32).astype(ml_dtypes.bfloat16)


def _rope_tables(dim, pos):
    inv = 1.0 / (10000.0 ** (np.arange(0, dim, 2, dtype=np.float32) / dim))
    ang = pos.astype(np.float32)[:, None] * inv[None, :]
    return np.cos(ang).astype(np.float32), np.sin(ang).astype(np.float32)


def host_inputs(inp, core):
    b, hf = core // 2, core % 2
    x = np.asarray(inp["x"], np.float32)
    xin = np.concatenate([x[b, 0:2048], x[b, hf * 2048:(hf + 1) * 2048]], 0)
    pos = np.concatenate([np.arange(2048), hf * 2048 + np.arange(2048)])
    c64, s64 = _rope_tables(64, pos)
    c32, s32 = _rope_tables(32, pos)
    flags = np.zeros((128, 20), np.float32)
    flags[:, 0] = 0.0 if hf == 1 else -30000.0
    flags[:, 1] = 1.0 if hf == 1 else 0.0
    flags[:, 4:12] = 0.0 if hf == 1 else -1e30
    onehot = np.zeros((16, 4096), np.float32)
    for n in range(16):
        onehot[n, n * 256:(n + 1) * 256] = 1.0
    g = lambda k: np.asarray(inp[k], np.float32)
    gains = np.concatenate([g("diff_q_norm_g")[0], g("diff_k_norm_g")[0], g("diff_subln_g")[0], g("moba_q_norm_g")[0],
                            g("moba_k_norm_g")[0], np.zeros(64, np.float32)])[None, :]
    lams = np.concatenate([g("diff_lambda_q1")[0], g("diff_lambda_k1")[0], g("diff_lambda_q2")[0], g("diff_lambda_k2")[0]])[None, :]
    ob = g("odd_b_in")[0]
    m = {
        "xin": xin, "cs64": np.concatenate([c64, s64], 1), "cs32": np.concatenate([c32, s32], 1),
        "flags": flags, "onehot": _bf16(onehot),
        "even_w_in": g("even_w_in")[0], "even_w_out": g("even_w_out")[0],
        "ffn_w_in": g("ffn_w_in"), "ffn_w_out": g("ffn_w_out"),
        "odd_w_in": g("odd_w_in")[0], "odd_w_out": g("odd_w_out")[0],
        "attn_norm_g": g("attn_norm_g"), "ffn_norm_g": g("ffn_norm_g"),
        "gains": gains.astype(np.float32), "lams": lams.astype(np.float32),
        "odd_b_row": ob[None, :].copy(),
        "odd_b_col": np.ascontiguousarray(ob[1024:2048].reshape(8, 128).T),
        "gmlp_ln_gb": np.concatenate([g("gmlp_ln_g")[0], g("gmlp_ln_b")[0]])[None, :],
        "gmlp_wsT": np.ascontiguousarray(g("gmlp_w_s")[0].transpose(2, 0, 1)),
        "gmlp_bsT": np.ascontiguousarray(g("gmlp_b_s")[0].T),
        "conv_wT": np.ascontiguousarray(g("conv_w")[0].T.reshape(4, 128, 31).transpose(1, 0, 2)),
        "conv_vecs": np.concatenate([g("conv_b")[0], g("conv_ln_g")[0], g("conv_ln_b")[0]])[None, :],
    }
    return {k: np.ascontiguousarray(v) for k, v in m.items()}


def kernel(**inputs):
    in_maps = [host_inputs(inputs, c) for c in range(8)]
    nc = build_program()
    res = run_bass_kernel_spmd(nc, in_maps, core_ids=list(range(8)))
    out = np.zeros((4, 4096, 1024), np.float32)
    for c in range(8):
        b, hf = c // 2, c % 2
        out[b, hf * 2048:(hf + 1) * 2048] = np.asarray(res.results[c]["out"], np.float32)
    return out
```

```python
import numpy as np
import concourse.bass as bass
import concourse.mybir as mybir
from concourse.bass_utils import run_bass_kernel_spmd

F32 = mybir.dt.float32
BF16 = mybir.dt.bfloat16
AF = mybir.ActivationFunctionType
ALU = mybir.AluOpType
AX = mybir.AxisListType

ENGS = ["pe", "act", "dve", "pool", "sp"]
EPOCH = 30000


class Prog:
    def __init__(self, nc):
        self.nc = nc
        self.ops = {e: [] for e in ENGS}
        self.count = {e: 0 for e in ENGS}
        self.seen = {e: {} for e in ENGS}
        self.res = {}
        self.lanes = {}
        self.semkeys = []
        self.lane_waited = {}
        self.lane_sem = {}
        self.free_sems = []
        self.nsem = 0
        self._cap = None

    def _deps(self, eng, reads, writes):
        deps = []
        for r in reads:
            st = self.res.get(r)
            if st and st[0] is not None:
                deps.append(st[0])
        for w in writes:
            st = self.res.get(w)
            if st:
                if st[0] is not None:
                    deps.append(st[0])
                deps.extend(st[1].values())
        need = {}
        for (k, v) in deps:
            if eng == "pe" and k == ("pe", "raw"):
                continue
            if self.seen[eng].get(k, 0) < v:
                need[k] = max(need.get(k, 0), v)
        for k, v in need.items():
            self.ops[eng].append(("wait", k, v))
            self.seen[eng][k] = v
            if k[0] == "dma":
                self.lane_waited[k[1]] = max(self.lane_waited.get(k[1], 0), v)

    def _mark(self, tok, reads, writes, rkey):
        for r in reads:
            st = self.res.setdefault(r, [None, {}])
            st[1][rkey] = tok
        for w in writes:
            self.res[w] = [tok, {}]

    def capture(self, f):
        old = self._cap
        self._cap = []
        f()
        lst = self._cap
        self._cap = old
        return lst

    def replay(self, item):
        if item[0] == "op":
            self.op(*item[1:])
        elif item[0] == "dmag":
            self.dma_group(*item[1:])
        else:
            for it in item[1]:
                self.replay(it)

    def op(self, eng, fn, reads=(), writes=()):
        if self._cap is not None:
            self._cap.append(("op", eng, fn, list(reads), list(writes)))
            return None
        self._deps(eng, reads, writes)
        self.count[eng] += 1
        c = self.count[eng]
        key = (eng, "raw")
        tok = (key, c)
        self.ops[eng].append(("op", fn, key, 1, c))
        self._mark(tok, reads, writes, eng)
        return tok

    def dma(self, eng, lane, fn, reads=(), writes=(), multi=False):
        return self.dma_group(eng, lane, [fn], reads, writes)

    def dma_group(self, eng, lane, fns, reads=(), writes=()):
        if self._cap is not None:
            self._cap.append(("dmag", eng, lane, list(fns), list(reads), list(writes)))
            return None
        self._deps(eng, reads, writes)
        if lane not in self.lane_sem:
            self.lane_sem[lane] = (self.nsem, 0)
            self.nsem += 1
        semid, base = self.lane_sem[lane]
        n = self.lanes.get(lane, 0) + len(fns)
        self.lanes[lane] = n
        key = ("dma", semid)
        if key not in self.semkeys:
            self.semkeys.append(key)
        tok = (key, base + 16 * n)
        for fn in fns:
            self.ops[eng].append(("op", fn, key, 16, None))
        self._mark(tok, reads, writes, ("dma", lane))
        return tok

    def retire_lanes(self):
        toks = []
        for lane, n in self.lanes.items():
            semid, base = self.lane_sem[lane]
            toks.append((("dma", semid), base + 16 * n))
        return toks

    def recycle_lanes(self):
        pass

    def wait_all(self, eng, toks):
        for (k, v) in toks:
            if self.seen[eng].get(k, 0) < v:
                self.ops[eng].append(("wait", k, v))
                self.seen[eng][k] = v
                if k[0] == "dma":
                    self.lane_waited[k[1]] = max(self.lane_waited.get(k[1], 0), v)

    def emit(self):
        nc = self.nc
        from contextlib import ExitStack
        waited = {e: set() for e in ENGS}
        for name in ENGS:
            for it in self.ops[name]:
                if it[0] == "wait" and it[1][1] == "raw":
                    waited[it[1][0]].add(it[2])
        rank = {}
        for e_ in ENGS:
            for i, c in enumerate(sorted(waited[e_])):
                rank[(e_, c)] = i + 1
        semkeys = list(self.semkeys)
        for (e_, c), r in rank.items():
            k = (e_, (r - 1) // EPOCH)
            if k not in semkeys:
                semkeys.append(k)
        with ExitStack() as es:
            sems = {}
            for k in semkeys:
                nm = "s_" + "_".join(str(x) for x in k)
                sems[k] = es.enter_context(nc.semaphore(nm))
            block = es.enter_context(nc.Block())

            def run(e, name):
                for it in self.ops[name]:
                    if it[0] == "wait":
                        if it[1][1] == "raw":
                            r = rank[(it[1][0], it[2])]
                            e.wait_ge(sems[(it[1][0], (r - 1) // EPOCH)], (r - 1) % EPOCH + 1)
                        else:
                            e.wait_ge(sems[it[1]], it[2])
                    else:
                        ins = it[1](e)
                        if it[4] is None:
                            ins.then_inc(sems[it[2]], it[3])
                        else:
                            r = rank.get((name, it[4]))
                            if r is not None:
                                ins.then_inc(sems[(name, (r - 1) // EPOCH)], 1)

            @block.tensor
            def _(e):
                run(e, "pe")

            @block.scalar
            def _(e):
                run(e, "act")

            @block.vector
            def _(e):
                run(e, "dve")

            @block.gpsimd
            def _(e):
                run(e, "pool")

            @block.sync
            def _(e):
                run(e, "sp")


NQT = 17
NQ = NQT * 128
NKT = 32
NK = 4096
D = 1024
DFF = 2816
EPS = 1e-6
NEG = -30000.0
LAMBDA_INIT0 = 0.8 - 0.6 * 1.0


class Arena:
    def __init__(self, ap, nbytes):
        self.ap = ap
        self.cap = nbytes
        self.off = 0

    def reset(self):
        self.off = 0

    def alloc(self, shape_free, dt):
        n = 1
        for s in shape_free:
            n *= s
        esz = 4 if dt == F32 else 2
        self.off = (self.off + 63) // 64 * 64
        nb = n * esz
        assert self.off + nb <= self.cap, ("arena overflow", self.off, nb, self.cap)
        v = self.ap[:, self.off // 2:(self.off + nb) // 2]
        self.off += nb
        if dt == F32:
            v = v.bitcast(F32)
        if len(shape_free) == 2:
            v = v.rearrange("p (a b) -> p a b", b=shape_free[1])
        elif len(shape_free) == 3:
            v = v.rearrange("p (a b c) -> p a b c", b=shape_free[1], c=shape_free[2])
        return v


def build_program(stop_after=None, debug=False):
    nc = bass.Bass("TRN2", target_bir_lowering=False)
    dbg_kind = "ExternalOutput" if debug else "Internal"

    def din(name, shape, dt=F32):
        return nc.dram_tensor(name, list(shape), dt, kind="ExternalInput").ap()

    def dscr(name, shape, dt):
        return nc.dram_tensor(name, list(shape), dt, kind=dbg_kind).ap()

    xin = din("xin", [NK, D])
    cs64 = din("cs64", [NK, 64])
    cs32 = din("cs32", [NK, 32])
    flags = din("flags", [128, 20])
    onehot = din("onehot", [16, NK], BF16)
    e_w_in = din("even_w_in", [D, 3072])
    e_w_out = din("even_w_out", [D, D])
    f_w_in = din("ffn_w_in", [2, D, 2 * DFF])
    f_w_out = din("ffn_w_out", [2, DFF, D])
    o_w_in = din("odd_w_in", [D, 2048])
    o_w_out = din("odd_w_out", [D, D])
    attn_g = din("attn_norm_g", [2, D])
    ffn_g = din("ffn_norm_g", [2, D])
    gains = din("gains", [1, 320])
    lams = din("lams", [1, 128])
    o_b_row = din("odd_b_row", [1, 2048])
    o_b_col = din("odd_b_col", [128, 8])
    gl_gb = din("gmlp_ln_gb", [1, 1024])
    wsT_in = din("gmlp_wsT", [128, 8, 128])
    bsT_in = din("gmlp_bsT", [128, 8])
    cwT = din("conv_wT", [128, 4, 31])
    cvec = din("conv_vecs", [1, 1536])
    out = nc.dram_tensor("out", [2048, D], F32, kind="ExternalOutput").ap()

    QdT = dscr("QdT", [512, NQ], BF16)
    KdT = dscr("KdT", [512, NK], BF16)
    Vd = dscr("Vd", [8, NK, 64], BF16)
    QmT = dscr("QmT", [512, NQ], BF16)
    KmT = dscr("KmT", [512, NK], BF16)
    Vm = dscr("Vm", [8, NK, 64], BF16)
    X1 = dscr("X1", [NQ, D], F32)
    X2 = dscr("X2", [NQ, D], F32)
    X3 = dscr("X3", [2048, D], F32)

    ARENA_BYTES = 190 * 1024
    arena_t = nc.alloc_sbuf_tensor("arena", [128, ARENA_BYTES // 2], BF16)
    AR = Arena(arena_t.ap(), ARENA_BYTES)
    PSALL = nc.alloc_psum_tensor("psall", [128, 4096], F32).ap()
    PS = [PSALL[:, i * 512:(i + 1) * 512] for i in range(8)]
    PSN = ["ps%d" % i for i in range(8)]

    P = Prog(nc)
    uid = [0]

    def nm(s):
        uid[0] += 1
        return "%s#%d" % (s, uid[0])

    lane_ctr = [0]

    def newlane(s="l"):
        lane_ctr[0] += 1
        return "%s%d" % (s, lane_ctr[0])

    def barrier():
        toks = []
        for e in ENGS:
            c = P.count[e]
            if c > 0:
                toks.append(((e, "raw"), c))
        toks.extend(P.retire_lanes())
        for e in ENGS:
            P.wait_all(e, toks)
        P.res.clear()
        P.recycle_lanes()

    ident = AR.alloc([128], BF16)
    flg = AR.alloc([20], F32)
    rnd = AR.alloc([8], F32)
    PERSIST = None

    P.op("pool", lambda e: e.memset(ident, 1.0), writes=["ident"])
    P.op("pool", lambda e: e.affine_select(out=ident, in_=ident, pattern=[[-1, 128]], compare_op=ALU.is_equal,
                                           fill=0.0, base=0, channel_multiplier=1), reads=["ident"], writes=["ident"])
    P.dma("sp", "c_flg", lambda e: e.dma_start(out=flg, in_=flags), writes=["flg"])
    AR.off = (AR.off + 63) // 64 * 64
    PERSIST = AR.off

    def reset_arena():
        AR.off = PERSIST

    def rms_rstd(ssq, ssq_name, rs, rs_name, hd_eps):
        P.op("act", lambda e: e.activation(out=rs, in_=ssq, func=AF.Ln, bias=float(hd_eps), scale=1.0),
             reads=[ssq_name], writes=[rs_name])
        P.op("act", lambda e: e.activation(out=rs, in_=rs, func=AF.Exp, scale=-0.5),
             reads=[rs_name], writes=[rs_name])

    def named(ap, name):
        return ap

    def norm_tile(xt, xt_n, gbc, gbc_n, hb, hb_n, sq, sq_n, ssq, rs, pfx):
        P.op("act", lambda e: e.activation(out=sq, in_=xt, func=AF.Square, accum_out=ssq),
             reads=[xt_n], writes=[sq_n, pfx + "_ssq"])
        rms_rstd(ssq, pfx + "_ssq", rs, pfx + "_rs", D * EPS)
        P.op("dve", lambda e: e.scalar_tensor_tensor(out=hb, in0=xt, scalar=rs, in1=gbc, op0=ALU.mult, op1=ALU.mult),
             reads=[xt_n, pfx + "_rs", gbc_n], writes=[hb_n])

    def transpose8(src, src_n, dst, dst_n, psi, nblk=8, evac="act"):
        psb = PS[psi].bitcast(BF16)
        for k in range(nblk):
            P.op("pe", lambda e, k=k: e.transpose(out=psb[:, k * 128:(k + 1) * 128], in_=src[:, k * 128:(k + 1) * 128],
                                                  identity=ident), reads=[src_n, "ident"], writes=[PSN[psi]])
        if evac == "act":
            P.op("act", lambda e: e.activation(out=dst, in_=psb[:, 0:nblk * 128].rearrange("p (a b) -> p a b", b=128),
                                               func=AF.Copy), reads=[PSN[psi]], writes=[dst_n])
        else:
            P.op("dve", lambda e: e.tensor_copy(out=dst, in_=psb[:, 0:nblk * 128].rearrange("p (a b) -> p a b", b=128)),
                 reads=[PSN[psi]], writes=[dst_n])

    def load_gain_bc(dst, dst_n, src_row, lane, scale):
        P.dma("sp", lane, lambda e: e.dma_start(out=dst, in_=src_row.partition_broadcast(128)), writes=[dst_n])
        if scale != 1.0:
            P.op("dve", lambda e: e.tensor_scalar(out=dst, in0=dst, scalar1=float(scale), scalar2=None, op0=ALU.mult),
                 reads=[dst_n], writes=[dst_n])

    def load_w_bf16(dst, dst_n, w_ap, lane, nsplit=4):
        K = dst.shape[1]
        wv = w_ap.rearrange("(k p) n -> p k n", p=128)
        step = max(1, K // nsplit)
        toks = []
        for k0 in range(0, K, step):
            k1 = min(K, k0 + step)
            toks.append(P.dma("pool", "%s_%d" % (lane, k0), lambda e, k0=k0, k1=k1: e.dma_start(out=dst[:, k0:k1, :], in_=wv[:, k0:k1, :]),
                              writes=[dst_n + "_%d" % k0], multi=True))
        return toks

    def phase_A():
        reset_arena()
        Wi = AR.alloc([8, 3072], BF16)
        gbc = AR.alloc([D], F32)
        Gq = AR.alloc([4, 512], F32)
        gsm = AR.alloc([320], F32)
        NC_ = 4
        cst = [AR.alloc([96], F32) for _ in range(NC_)]
        gct = [AR.alloc([4, 2, 64], F32) for _ in range(NC_)]
        xt = [AR.alloc([D], F32) for _ in range(NC_)]
        sq = AR.alloc([D], BF16)
        hbs = [AR.alloc([D], BF16) for _ in range(2)]
        hTs = [AR.alloc([8, 128], BF16) for _ in range(2)]
        ssqs = [AR.alloc([1], F32) for _ in range(2)]
        rss = [AR.alloc([1], F32) for _ in range(2)]
        NS = 4
        sqgs = [AR.alloc([512], F32) for _ in range(NS)]
        qns = [AR.alloc([512], F32) for _ in range(NS)]
        tas = [AR.alloc([512], F32) for _ in range(NS)]
        tbs = [AR.alloc([512], F32) for _ in range(NS)]
        ssqhs = [AR.alloc([16], F32) for _ in range(NS)]
        rshs = [AR.alloc([16], F32) for _ in range(NS)]
        qbs = [AR.alloc([512], BF16) for _ in range(NS)]
        qTs = [AR.alloc([4, 128], BF16) for _ in range(NS)]
        vbs = [AR.alloc([512], BF16) for _ in range(NS)]

        load_w_bf16(Wi, "A_Wi", e_w_in, "A_w", nsplit=8)
        wnames = ["A_Wi_%d" % k for k in range(8)]
        load_gain_bc(gbc, "A_gbc", attn_g[0:1, :], "A_g", 32.0)
        P.dma("sp", "A_gs", lambda e: e.dma_start(out=gsm, in_=gains.partition_broadcast(128)), writes=["A_gsm"])
        for gi, (o, hd) in enumerate([(0, 32), (32, 32), (128, 64), (192, 64)]):
            nh = 512 // hd
            P.op("dve", lambda e, gi=gi, o=o, hd=hd, nh=nh: e.tensor_scalar(
                out=Gq[:, gi, :].rearrange("p (a b) -> p a b", b=hd),
                in0=gsm[:, o:o + hd].unsqueeze(1).to_broadcast([128, nh, hd]),
                scalar1=float(hd) ** 0.5, scalar2=None, op0=ALU.mult), reads=["A_gsm"], writes=["A_Gq%d" % gi])

        Gs = AR.alloc([4, 64], F32)
        for gi, (o, hd) in enumerate([(0, 32), (32, 32), (128, 64), (192, 64)]):
            P.op("dve", lambda e, gi=gi, o=o, hd=hd: e.tensor_scalar(out=Gs[:, gi, 0:hd], in0=gsm[:, o:o + hd], scalar1=float(hd) ** 0.5,
                                                                    scalar2=None, op0=ALU.mult), reads=["A_gsm"], writes=["A_Gs"])
        qdt_v = QdT.rearrange("(c p) n -> p c n", p=128)
        kdt_v = KdT.rearrange("(c p) n -> p c n", p=128)
        qmt_v = QmT.rearrange("(c p) n -> p c n", p=128)
        kmt_v = KmT.rearrange("(c p) n -> p c n", p=128)
        vd_v = Vd.rearrange("h k d -> k h d")
        vm_v = Vm.rearrange("h k d -> k h d")

        def prologue(t):
            b2 = t % 2
            c4 = t % NC_
            xtn = "A_xt%d" % c4
            csn = "A_cs%d" % c4
            hb, hT, hbn, hTn = hbs[b2], hTs[b2], "A_hb%d" % b2, "A_hT%d" % b2
            ssq, rs, pfx = ssqs[b2], rss[b2], "A%d" % b2

            def st0():
                P.dma_group("sp", "A_in%d" % c4, [
                    lambda e: e.dma_start(out=xt[c4], in_=xin[t * 128:(t + 1) * 128, :]),
                    lambda e: e.dma_start(out=cst[c4][:, 0:64], in_=cs64[t * 128:(t + 1) * 128, :]),
                    lambda e: e.dma_start(out=cst[c4][:, 64:96], in_=cs32[t * 128:(t + 1) * 128, :])],
                    writes=[xtn, csn + "a", csn + "b"])

            def st1():
                P.op("act", lambda e: e.activation(out=sq, in_=xt[c4], func=AF.Square, accum_out=ssq), reads=[xtn], writes=["A_sq", pfx + "_ssq"])
                rms_rstd(ssq, pfx + "_ssq", rs, pfx + "_rs", D * EPS)
                for gi_, hd_ in enumerate((32, 32, 64, 64)):
                    hf_ = hd_ // 2
                    co_ = 64 if hd_ == 32 else 0
                    for cs_ in range(2):
                        P.op("pool", lambda e, gi_=gi_, hd_=hd_, hf_=hf_, co_=co_, cs_=cs_: e.tensor_tensor(
                            out=gct[c4][:, gi_, cs_, 0:hd_].rearrange("p (b c) -> p b c", c=hf_),
                            in0=Gs[:, gi_, 0:hd_].rearrange("p (b c) -> p b c", c=hf_),
                            in1=cst[c4][:, co_ + cs_ * hf_:co_ + (cs_ + 1) * hf_].unsqueeze(1).to_broadcast([128, 2, hf_]), op=ALU.mult),
                            reads=["A_Gs", csn + "a", csn + "b"], writes=[csn + "g"])

            def st2():
                P.op("dve", lambda e: e.scalar_tensor_tensor(out=hb, in0=xt[c4], scalar=rs, in1=gbc, op0=ALU.mult, op1=ALU.mult),
                     reads=[xtn, pfx + "_rs", "A_gbc"], writes=[hbn])

            def st3():
                transpose8(hb, hbn, hT, hTn, 7)
            return [st0, st1, st2, st3]

        def group_item(t, g, gidx, qidx):
            b2 = t % 2
            c4 = t % NC_
            csn = "A_cs%d" % c4
            hT, hTn = hTs[b2], "A_hT%d" % b2
            psi = gidx % 5
            ps = PS[psi]
            z = gidx % NS

            def s_mm():
                for k in range(8):
                    P.op("pe", lambda e, k=k: e.matmul(ps, lhsT=hT[:, k, :], rhs=Wi[:, k, g * 512:(g + 1) * 512], start=(k == 0), stop=(k == 7)),
                         reads=[hTn, wnames[k]], writes=[PSN[psi]])
            if g in (2, 5):
                vb, vbn = vbs[z], "A_vb%d" % z
                dst = vd_v if g == 2 else vm_v

                def s_cp():
                    P.op("act", lambda e: e.activation(out=vb, in_=ps, func=AF.Copy), reads=[PSN[psi]], writes=[vbn])

                def s_st():
                    P.dma("sp", "A_vs%d" % z, lambda e: e.dma_start(out=dst[t * 128:(t + 1) * 128, :, :], in_=vb.rearrange("p (h d) -> p h d", d=64)),
                          reads=[vbn])
                return [s_mm, s_cp, s_st]
            isd = g in (0, 1)
            hd = 32 if isd else 64
            half = hd // 2
            nh = 512 // hd
            gi = {0: 0, 1: 1, 3: 2, 4: 3}[g]
            sqg, qn, ta, tb, ssqh, rsh, qb, qT = sqgs[z], qns[z], tas[z], tbs[z], ssqhs[z], rshs[z], qbs[z], qTs[z]
            n_sqg, n_qn, n_ta, n_tb, n_ssqh, n_rsh, n_qb, n_qT = ["A_%s%d" % (x, z) for x in ("sqg", "qn", "ta", "tb", "ssqh", "rsh", "qb", "qT")]
            ssq_v = ssqh[:, 0:nh]
            rs_v = rsh[:, 0:nh]
            co = 64 if isd else 0
            cosv = cst[c4][:, co:co + half]
            sinv = cst[c4][:, co + half:co + 2 * half]
            q4 = qn.rearrange("p (a b c) -> p a b c", b=2, c=half)
            ta4 = ta.rearrange("p (a b c) -> p a b c", b=2, c=half)
            tb4 = tb.rearrange("p (a b c) -> p a b c", b=2, c=half)
            qb4 = qb.rearrange("p (a b c) -> p a b c", b=2, c=half)
            tpi = 5 + (gidx % 2)
            psb = PS[tpi].bitcast(BF16)
            if g in (0, 3):
                dstv = qdt_v if g == 0 else qmt_v
                col = qidx * 128
            else:
                dstv = kdt_v if g == 1 else kmt_v
                col = t * 128

            def s1():
                P.op("act", lambda e: e.activation(out=sqg, in_=ps, func=AF.Square), reads=[PSN[psi]], writes=[n_sqg])

            def s2():
                P.op("dve", lambda e: e.tensor_reduce(out=ssq_v, in_=sqg.rearrange("p (a b) -> p a b", b=hd), axis=AX.X, op=ALU.add),
                     reads=[n_sqg], writes=[n_ssqh])

            def s3():
                rms_rstd(ssq_v, n_ssqh, rs_v, n_rsh, hd * EPS)

            def s4():
                P.op("dve", lambda e: e.tensor_tensor(out=qn.rearrange("p (a b) -> p a b", b=hd), in0=ps.rearrange("p (a b) -> p a b", b=hd),
                                                      in1=rs_v.unsqueeze(2).to_broadcast([128, nh, hd]), op=ALU.mult),
                     reads=[PSN[psi], n_rsh], writes=[n_qn])

            gcv = gct[c4][:, gi, 0, 0:hd].rearrange("p (b c) -> p b c", c=half)
            gsv = gct[c4][:, gi, 1, 0:hd].rearrange("p (b c) -> p b c", c=half)

            def s5():
                pass

            def s6():
                P.op("dve", lambda e: e.tensor_tensor(out=ta4, in0=q4, in1=gcv.unsqueeze(1).to_broadcast([128, nh, 2, half]),
                                                      op=ALU.mult), reads=[n_qn, csn + "g"], writes=[n_ta])
                P.op("pool", lambda e: e.tensor_tensor(out=tb4, in0=q4, in1=gsv.unsqueeze(1).to_broadcast([128, nh, 2, half]),
                                                       op=ALU.mult), reads=[n_qn, csn + "g"], writes=[n_tb])

            def s7():
                P.op("dve", lambda e: e.tensor_tensor(out=qb4[:, :, 0, :], in0=ta4[:, :, 0, :], in1=tb4[:, :, 1, :], op=ALU.subtract),
                     reads=[n_ta, n_tb], writes=[n_qb + "a"])
                P.op("pool", lambda e: e.tensor_tensor(out=qb4[:, :, 1, :], in0=ta4[:, :, 1, :], in1=tb4[:, :, 0, :], op=ALU.add),
                     reads=[n_ta, n_tb], writes=[n_qb + "b"])

            def s8():
                for k in range(4):
                    P.op("pe", lambda e, k=k: e.transpose(out=psb[:, k * 128:(k + 1) * 128], in_=qb[:, k * 128:(k + 1) * 128], identity=ident),
                         reads=[n_qb + "a", n_qb + "b", "ident"], writes=[PSN[tpi]])

            def s9():
                P.op("act", lambda e: e.activation(out=qT, in_=psb[:, 0:512].rearrange("p (a b) -> p a b", b=128), func=AF.Copy),
                     reads=[PSN[tpi]], writes=[n_qT])

            def s10():
                P.dma("sp", "A_qs%d" % z, lambda e: e.dma_start(out=dstv[:, :, col:col + 128], in_=qT), reads=[n_qT])
            return [s_mm, s1, s2, s3, s4, s5, s6, s7, s8, s9, s10]

        items = []
        extra = {}
        first_group_of_tile = {}
        for t in range(NKT):
            if t >= 16:
                qidx = t - 16
            elif t == 15:
                qidx = 16
            else:
                qidx = None
            groups = [1, 2, 4, 5] if qidx is None else [0, 1, 2, 3, 4, 5]
            first_group_of_tile[t] = len(items)
            for g in groups:
                items.append(group_item(t, g, len(items), qidx))
        for st in prologue(0):
            st()
        for t in range(NKT - 1):
            g0 = first_group_of_tile[t]
            for k, st in enumerate(prologue(t + 1)):
                extra.setdefault(g0 + k, []).append(st)
        maxs = max(len(it) for it in items)
        for step in range(len(items) + maxs):
            for sidx in range(maxs - 1, -1, -1):
                g = step - sidx
                if 0 <= g < len(items) and sidx < len(items[g]):
                    items[g][sidx]()
            for st in extra.get(step, []):
                st()

    phase_A()
    barrier()
    fin = []
    if stop_after == "A":
        P.emit()
        return nc

    def phase_BC():
        reset_arena()
        Wo = AR.alloc([8, D], BF16)
        C = AR.alloc([NQT, D], BF16)
        cm = AR.alloc([4, 512], BF16)
        kT = [AR.alloc([NK], BF16) for _ in range(2)]
        vbuf = [AR.alloc([NKT, 128], BF16) for _ in range(2)]
        qa = [AR.alloc([NQ], BF16) for _ in range(2)]
        qb_ = [AR.alloc([NQ], BF16) for _ in range(2)]
        pT = [AR.alloc([2, 512], BF16) for _ in range(2)]
        oT = [AR.alloc([512], BF16) for _ in range(2)]
        gsm = AR.alloc([320], F32)
        lamv = AR.alloc([128], F32)
        lt = AR.alloc([64], F32)
        lsm = AR.alloc([8], F32)
        Gsub = AR.alloc([4, 64], F32)
        gbj = AR.alloc([9, 16], F32)
        biaspad = AR.alloc([NQT, 128], BF16)
        gball = AR.alloc([NQT, 16], F32)
        ownm = AR.alloc([NQT, 16], F32)
        gmall = AR.alloc([NQT, 16], F32)
        cmpb = AR.alloc([NQT, 16, 16], BF16)
        rnk = AR.alloc([NQT, 16], F32)
        bia1 = AR.alloc([NQT, 16], F32)
        bia2 = AR.alloc([NQT, 16], F32)
        zt = AR.alloc([128], BF16)
        kbs = AR.alloc([16], F32)
        kbb = AR.alloc([16], BF16)
        gm = AR.alloc([16], F32)
        top8 = AR.alloc([8], F32)
        thr = AR.alloc([1], F32)
        rcp = AR.alloc([8], F32)
        oa = AR.alloc([4, 64], F32)
        ob = AR.alloc([4, 64], F32)
        dd = AR.alloc([4, 64], F32)
        sqd = AR.alloc([4, 64], F32)
        ssq4 = AR.alloc([4], F32)
        rs4 = AR.alloc([4], F32)
        CT = AR.alloc([8, 128], BF16)
        xr = [AR.alloc([D], F32) for _ in range(2)]
        x1 = [AR.alloc([D], F32) for _ in range(2)]

        load_w_bf16(Wo, "B_Wo", e_w_out, "B_w", nsplit=2)
        wo_names = ["B_Wo_0"] * 4 + ["B_Wo_4"] * 4
        P.op("pool", lambda e: e.memset(cm, 0.0), writes=["cm"])
        for i in range(4):
            P.op("pool", lambda e, i=i: e.affine_select(out=cm[:, i, :], in_=cm[:, i, :], pattern=[[1, 512]], compare_op=ALU.is_ge,
                                                        fill=NEG, base=-128 * i, channel_multiplier=-1), reads=["cm"], writes=["cm"])
        for b in range(2):
            P.op("pool", lambda e, b=b: e.memset(qa[b][32:64, :], 0.0), writes=["qa%d_z" % b])
            P.op("pool", lambda e, b=b: e.memset(qa[b][64:128, :], 0.0), writes=["qa%d_bias" % b])
            P.op("pool", lambda e, b=b: e.memset(qb_[b][0:32, :], 0.0), writes=["qb%d_z" % b])
            P.op("pool", lambda e, b=b: e.memset(qb_[b][64:128, :], 0.0), writes=["qb%d_z" % b])
            P.op("pool", lambda e, b=b: e.memset(vbuf[b][:, :, 64:128], 0.0), writes=["vb%d_1" % b])
            P.op("pool", lambda e, b=b: e.memset(vbuf[b][:, :, 64:65], 1.0), reads=["vb%d_1" % b], writes=["vb%d_1" % b])
            P.op("pool", lambda e, b=b: e.memset(kT[b][64:128, :], 0.0), writes=["kT%d_oh" % b])
            P.dma("sp", "B_oh%d" % b, lambda e, b=b: e.dma_start(out=kT[b][64:80, :], in_=onehot), writes=["kT%d_oh" % b])
        P.op("pool", lambda e: e.memset(biaspad, 0.0), writes=["biaspad"])
        P.op("pool", lambda e: e.memset(zt, 0.0), writes=["zt"])
        P.dma("sp", "B_gs", lambda e: e.dma_start(out=gsm, in_=gains.partition_broadcast(128)), writes=["B_gsm"])
        P.dma("sp", "B_lm", lambda e: e.dma_start(out=lamv, in_=lams.partition_broadcast(128)), writes=["B_lamv"])
        l4 = lamv.rearrange("p (a b c) -> p a b c", b=2, c=32)
        P.op("dve", lambda e: e.tensor_tensor(out=lt.rearrange("p (a c) -> p a c", c=32), in0=l4[:, :, 0, :], in1=l4[:, :, 1, :],
                                              op=ALU.mult), reads=["B_lamv"], writes=["B_lt"])
        P.op("dve", lambda e: e.tensor_reduce(out=lsm[:, 0:2], in_=lt.rearrange("p (a c) -> p a c", c=32), axis=AX.X, op=ALU.add),
             reads=["B_lt"], writes=["B_lsm"])
        P.op("act", lambda e: e.activation(out=lsm[:, 2:4], in_=lsm[:, 0:2], func=AF.Exp), reads=["B_lsm"], writes=["B_lsm2"])
        P.op("dve", lambda e: e.tensor_tensor(out=lsm[:, 4:5], in0=lsm[:, 3:4], in1=lsm[:, 2:3], op=ALU.subtract),
             reads=["B_lsm2"], writes=["B_lsm3"])
        neglam = lsm[:, 5:6]
        P.op("dve", lambda e: e.tensor_scalar(out=neglam, in0=lsm[:, 4:5], scalar1=-LAMBDA_INIT0, scalar2=None, op0=ALU.add),
             reads=["B_lsm3"], writes=["neglam"])
        P.op("dve", lambda e: e.tensor_scalar(out=Gsub, in0=gsm[:, 64:128].unsqueeze(1).to_broadcast([128, 4, 64]),
                                              scalar1=8.0 * (1.0 - LAMBDA_INIT0), scalar2=None, op0=ALU.mult),
             reads=["B_gsm"], writes=["Gsub"])
        for j in range(8):
            P.op("dve", lambda e, j=j: e.tensor_copy(out=gbj[:, j, :], in_=flg[:, 4:20]), reads=["flg"], writes=["gbj"])
            P.op("dve", lambda e, j=j: e.memset(gbj[:, j, 8 + j:16], -1e30), reads=["gbj"], writes=["gbj"])
        P.op("dve", lambda e: e.memset(gbj[:, 8, :], 0.0), reads=["gbj"], writes=["gbj"])
        P.op("dve", lambda e: e.memset(gbj[:, 8, 7:16], -1e30), reads=["gbj"], writes=["gbj"])

        P.op("dve", lambda e: e.memset(ownm, 1.0), writes=["ownm"])
        for qi in range(NQT):
            jj = qi // 2 if qi < 16 else 8
            ownb = 8 + qi // 2 if qi < 16 else 7
            P.op("dve", lambda e, qi=qi, jj=jj: e.tensor_copy(out=gball[:, qi, :], in_=gbj[:, jj, :]), reads=["gbj"], writes=["gball"])
            P.op("dve", lambda e, qi=qi, ownb=ownb: e.memset(ownm[:, qi, ownb:ownb + 1], 0.0), reads=["ownm"], writes=["ownm"])
        units = [("d", h) for h in range(8)] + [("m", h) for h in range(8)]
        groups = [(gi * 512, 512, 20 + 4 * gi, True, gi * 4) for gi in range(4)] + [(2048, 128, 16, False, 16)]
        gcount = [0]
        sc = [0]

        def issue_loads(u):
            kind, h = units[u]
            b = u % 2
            Ksrc, Vsrc, Qsrc = (KdT, Vd, QdT) if kind == "d" else (KmT, Vm, QmT)
            fns = [lambda e: e.dma_start(out=kT[b][0:64, :], in_=Ksrc[h * 64:(h + 1) * 64, :])]
            for pt in range(4):
                fns.append(lambda e, pt=pt: e.dma_start(out=vbuf[b][:, pt * 8:(pt + 1) * 8, 0:64],
                                                       in_=Vsrc[h].rearrange("(t p) d -> p t d", p=128)[:, pt * 8:(pt + 1) * 8, :]))
            wr = ["kT%d" % b] + ["vb%d_%d" % (b, pt) for pt in range(4)]
            if kind == "d":
                fns.append(lambda e: e.dma_start(out=qa[b][0:32, :], in_=Qsrc[h * 64:h * 64 + 32, :]))
                fns.append(lambda e: e.dma_start(out=qb_[b][32:64, :], in_=Qsrc[h * 64 + 32:h * 64 + 64, :]))
                wr += ["qa%d" % b, "qb%d" % b]
            else:
                fns.append(lambda e: e.dma_start(out=qa[b][0:64, :], in_=Qsrc[h * 64:(h + 1) * 64, :]))
                wr += ["qa%d" % b, "qa%d_z" % b]
            P.dma_group("sp", "B_ld%d" % b, fns, writes=wr)

        def gating_stages(u):
            kind, h = units[u]
            b = u % 2
            ps6b = PS[6].bitcast(BF16)

            def g1():
                P.op("dve", lambda e: e.tensor_reduce(out=kbs[0:64, :], in_=kT[b][0:64, :].rearrange("p (a c) -> p a c", c=256),
                                                      axis=AX.X, op=ALU.add), reads=["kT%d" % b], writes=["kbs"])
                P.op("dve", lambda e: e.tensor_scalar(out=kbb[0:64, :], in0=kbs[0:64, :], scalar1=1.0 / 256.0, scalar2=None, op0=ALU.mult),
                     reads=["kbs"], writes=["kbb"])
                for qi in range(NQT):
                    P.op("pe", lambda e, qi=qi: e.matmul(PS[7][:, qi * 16:(qi + 1) * 16], lhsT=qa[b][0:64, qi * 128:(qi + 1) * 128], rhs=kbb[0:64, :],
                                                        start=True, stop=True), reads=["qa%d" % b, "kbb"], writes=[PSN[7]])

            def g2():
                P.op("dve", lambda e: e.tensor_tensor(out=gmall, in0=PS[7][:, 0:NQT * 16].rearrange("p (q n) -> p q n", n=16), in1=gball, op=ALU.add),
                     reads=[PSN[7], "gball"], writes=["gmall"])
                P.op("dve", lambda e: e.tensor_tensor(out=cmpb, in0=gmall.unsqueeze(2).to_broadcast([128, NQT, 16, 16]),
                                                      in1=gmall.unsqueeze(3).to_broadcast([128, NQT, 16, 16]), op=ALU.is_gt),
                     reads=["gmall"], writes=["cmpb"])
                P.op("dve", lambda e: e.tensor_reduce(out=rnk, in_=cmpb, axis=AX.X, op=ALU.add), reads=["cmpb"], writes=["rnk"])
                P.op("dve", lambda e: e.tensor_scalar(out=bia1, in0=rnk, scalar1=2.5, scalar2=NEG, op0=ALU.is_gt, op1=ALU.mult),
                     reads=["rnk"], writes=["bia1"])
                P.op("dve", lambda e: e.tensor_scalar(out=bia2, in0=gmall, scalar1=-1e29, scalar2=NEG, op0=ALU.is_lt, op1=ALU.mult),
                     reads=["gmall"], writes=["bia2"])
                P.op("dve", lambda e: e.tensor_tensor(out=bia1, in0=bia1, in1=bia2, op=ALU.add), reads=["bia1", "bia2"], writes=["bia1"])
                P.op("dve", lambda e: e.tensor_tensor(out=biaspad[:, :, 64:80], in0=bia1, in1=ownm, op=ALU.mult),
                     reads=["bia1", "ownm"], writes=["biaspad"])

            def g3():
                for (q0, nb) in ((0, 8), (8, 8), (16, 1)):
                    for k in range(nb):
                        P.op("pe", lambda e, q0=q0, k=k: e.transpose(out=ps6b[:, k * 128:(k + 1) * 128], in_=biaspad[:, q0 + k, :], identity=ident),
                             reads=["biaspad", "ident"], writes=[PSN[6]])
                    P.op("act", lambda e, q0=q0, nb=nb: e.activation(out=qa[b][64:80, q0 * 128:(q0 + nb) * 128], in_=ps6b[64:80, 0:nb * 128], func=AF.Copy),
                         reads=[PSN[6]], writes=["qa%d_bias" % b])
            return [g1, g2, g3]

        def attention(u, hooks=()):
            kind, h = units[u]
            b = u % 2
            isd = kind == "d"
            dk = 128
            scale = (32.0 ** -0.5) if isd else 0.125
            for gi_, grp in enumerate(groups):
                if gi_ < len(hooks):
                    hooks[gi_]()
                do_group(kind, h, b, isd, dk, scale, *grp)

        def do_group(kind, h, b, isd, dk, scale, qc0, N, nkt, usepast, t0):
            R = N // 128
            gidx = gcount[0]
            gcount[0] += 1
            if isd:
                maps = [(qa[b], ["qa%d" % b, "qa%d_z" % b, "qa%d_bias" % b], 4), (qb_[b], ["qb%d" % b, "qb%d_z" % b], 5)]
                tbank = 6
            else:
                maps = [(qa[b], ["qa%d" % b, "qa%d_bias" % b], 4)]
                tbank = 5
            npair = nkt // 2
            steps = [(mi, kp) for mi in range(len(maps)) for kp in range(npair)]
            slot = {}

            def col0_of(kp):
                return 256 if (usepast and kp == npair - 1) else 0

            def qk(i):
                mi, kp = steps[i]
                Q, qn_, _ = maps[mi]
                s_ = sc[0] % 2
                sc[0] += 1
                slot[i] = s_
                c0_ = col0_of(kp)
                for hfi in range(2):
                    kt = 2 * kp + hfi
                    bank = 2 * s_ + hfi
                    if usepast:
                        di = kt - (nkt - 4)
                    else:
                        di = 0 if kt == nkt - 1 else -1
                    diag = di >= 0
                    P.op("pe", lambda e, kt=kt, bank=bank, diag=diag: e.matmul(PS[bank][:, c0_:N], lhsT=kT[b][0:dk, kt * 128:(kt + 1) * 128],
                                                                             rhs=Q[0:dk, qc0 + c0_:qc0 + N], start=True, stop=not diag),
                         reads=["kT%d" % b, "kT%d_oh" % b] + qn_, writes=[PSN[bank]])
                    if diag:
                        P.op("pe", lambda e, bank=bank, di=di: e.matmul(PS[bank][:, c0_:N], lhsT=ident, rhs=cm[:, di, c0_:N], start=False, stop=True),
                             reads=["ident", "cm"], writes=[PSN[bank]])

            def ex_pv(i):
                mi, kp = steps[i]
                _, _, abank = maps[mi]
                s_ = slot[i]
                c0_ = col0_of(kp)
                src = PSALL[:, 2 * s_ * 512:(2 * s_ + 2) * 512].rearrange("p (b n) -> p b n", n=512)[:, :, c0_:N]
                dst = pT[s_][:, :, c0_:N]
                rd = [PSN[2 * s_], PSN[2 * s_ + 1]]
                if usepast and 2 * kp < 16:
                    P.op("act", lambda e: e.activation(out=dst, in_=src, func=AF.Exp, scale=scale, bias=flg[:, 0:1]),
                         reads=rd + ["flg"], writes=["pT%d" % s_])
                else:
                    P.op("act", lambda e: e.activation(out=dst, in_=src, func=AF.Exp, scale=scale), reads=rd, writes=["pT%d" % s_])
                for hfi in range(2):
                    kt = 2 * kp + hfi
                    P.op("pe", lambda e, kt=kt, hfi=hfi: e.matmul(PS[abank][:, c0_:N], lhsT=vbuf[b][:, kt, :], rhs=pT[s_][:, hfi, c0_:N],
                                                                  start=(kt == 0), stop=(kt == nkt - 1)),
                         reads=["pT%d" % s_, "vb%d_%d" % (b, kt // 8), "vb%d_1" % b], writes=[PSN[abank]])

            qk(0)
            for i in range(len(steps)):
                if i + 1 < len(steps):
                    qk(i + 1)
                ex_pv(i)
            tpb = PS[tbank].bitcast(BF16)
            for mi, (_, _, abank) in enumerate(maps):
                P.op("dve", lambda e, mi=mi, abank=abank: e.tensor_copy(out=oT[mi][0:65, 0:N], in_=PS[abank][0:65, 0:N]),
                     reads=[PSN[abank]], writes=["oT%d" % mi])
                for r in range(R):
                    c0 = (mi * 4 + r) * 66
                    P.op("pe", lambda e, mi=mi, r=r, c0=c0: e.transpose(out=tpb[:, c0:c0 + 65], in_=oT[mi][0:65, r * 128:(r + 1) * 128],
                                                                      identity=ident[0:65, 0:65]),
                         reads=["oT%d" % mi, "ident"], writes=[PSN[tbank]])
            cn = "C_g%d" % t0
            nt = PSN[tbank]
            accA = tpb[:, 0:R * 66].rearrange("p (r c) -> p r c", c=66)
            if isd:
                accB = tpb[:, 4 * 66:(4 + R) * 66].rearrange("p (r c) -> p r c", c=66)
                P.op("dve", lambda e: e.reciprocal(out=rcp[:, 0:R], in_=accA[:, :, 64]), reads=[nt], writes=["rcpa"])
                P.op("dve", lambda e: e.reciprocal(out=rcp[:, 4:4 + R], in_=accB[:, :, 64]), reads=[nt], writes=["rcpb"])
                P.op("dve", lambda e: e.tensor_tensor(out=oa[:, 0:R, :], in0=accA[:, :, 0:64],
                                                      in1=rcp[:, 0:R].unsqueeze(2).to_broadcast([128, R, 64]), op=ALU.mult),
                     reads=[nt, "rcpa"], writes=["oa"])
                P.op("dve", lambda e: e.tensor_tensor(out=ob[:, 0:R, :], in0=accB[:, :, 0:64],
                                                      in1=rcp[:, 4:4 + R].unsqueeze(2).to_broadcast([128, R, 64]), op=ALU.mult),
                     reads=[nt, "rcpb"], writes=["ob"])
                P.op("dve", lambda e: e.scalar_tensor_tensor(out=dd[:, 0:R, :], in0=ob[:, 0:R, :], scalar=neglam, in1=oa[:, 0:R, :],
                                                             op0=ALU.mult, op1=ALU.add), reads=["oa", "ob", "neglam"], writes=["dd"])
                P.op("pool", lambda e: e.tensor_tensor(out=sqd[:, 0:R, :], in0=dd[:, 0:R, :], in1=dd[:, 0:R, :], op=ALU.mult),
                     reads=["dd"], writes=["sqd"])
                P.op("dve", lambda e: e.tensor_reduce(out=ssq4[:, 0:R], in_=sqd[:, 0:R, :], axis=AX.X, op=ALU.add),
                     reads=["sqd"], writes=["ssq4"])
                rms_rstd(ssq4[:, 0:R], "ssq4", rs4[:, 0:R], "rs4", 64 * EPS)
                P.op("dve", lambda e: e.tensor_tensor(out=dd[:, 0:R, :], in0=dd[:, 0:R, :],
                                                      in1=rs4[:, 0:R].unsqueeze(2).to_broadcast([128, R, 64]), op=ALU.mult),
                     reads=["dd", "rs4"], writes=["dd"])
                P.op("pool", lambda e: e.tensor_tensor(out=C[:, t0:t0 + R, h * 64:(h + 1) * 64], in0=dd[:, 0:R, :], in1=Gsub[:, 0:R, :],
                                                       op=ALU.mult), reads=["dd", "Gsub"], writes=[cn])
            else:
                P.op("dve", lambda e: e.reciprocal(out=rcp[:, 0:R], in_=accA[:, :, 64]), reads=[nt], writes=["rcpa"])
                P.op("dve", lambda e: e.tensor_tensor(out=C[:, t0:t0 + R, 512 + h * 64:512 + (h + 1) * 64], in0=accA[:, :, 0:64],
                                                      in1=rcp[:, 0:R].unsqueeze(2).to_broadcast([128, R, 64]), op=ALU.mult),
                     reads=[nt, "rcpa"], writes=[cn])

        issue_loads(0)
        for u in range(len(units)):
            if u + 1 < len(units):
                issue_loads(u + 1)
            hooks = gating_stages(u + 1) if (u + 1 < len(units) and units[u + 1][0] == "m") else []
            attention(u, hooks)

        CTs = [CT, AR.alloc([8, 128], BF16)]

        def c_pre(tl):
            b2 = tl % 2
            lt_ = 16 + tl if tl < 16 else 15
            cn = "C_g%d" % (tl // 4 * 4 if tl < 16 else 16)
            P.dma("sp", "C_x%d" % b2, lambda e: e.dma_start(out=xr[b2], in_=xin[lt_ * 128:(lt_ + 1) * 128, :]), writes=["C_xr%d" % b2])
            transpose8(C[:, tl, :], cn, CTs[b2], "C_CT%d" % b2, 6 + b2)

        def c_main(tl):
            b2 = tl % 2
            for hh in range(2):
                bank = 2 * b2 + hh
                for k in range(8):
                    P.op("pe", lambda e, k=k, hh=hh, bank=bank: e.matmul(PS[bank], lhsT=CTs[b2][:, k, :], rhs=Wo[:, k, hh * 512:(hh + 1) * 512],
                                                                        start=(k == 0), stop=(k == 7)), reads=["C_CT%d" % b2, wo_names[k]], writes=[PSN[bank]])
                P.op("dve", lambda e, hh=hh, bank=bank: e.tensor_tensor(out=x1[b2][:, hh * 512:(hh + 1) * 512], in0=PS[bank],
                                                                       in1=xr[b2][:, hh * 512:(hh + 1) * 512], op=ALU.add),
                     reads=[PSN[bank], "C_xr%d" % b2], writes=["C_x1%d_%d" % (b2, hh)])
            P.dma("sp", "C_s%d" % b2, lambda e: e.dma_start(out=X1[tl * 128:(tl + 1) * 128, :], in_=x1[b2]),
                  reads=["C_x1%d_0" % b2, "C_x1%d_1" % b2])

        c_pre(0)
        for tl in range(NQT):
            if tl + 1 < NQT:
                c_pre(tl + 1)
            c_main(tl)

    phase_BC()
    barrier()
    if stop_after == "BC":
        P.emit()
        return nc

    def phase_FFN(Xin, Xout, li, ntiles):
        reset_arena()
        gbc = AR.alloc([D], F32)
        xs = [AR.alloc([D], F32) for _ in range(3)]
        xr2 = [AR.alloc([D], F32) for _ in range(2)]
        sq = AR.alloc([D], BF16)
        hbs = [AR.alloc([D], BF16) for _ in range(2)]
        hT = AR.alloc([8, 1152], BF16)
        wo = AR.alloc([22, D], BF16)
        wi = [AR.alloc([8, 256], BF16) for _ in range(3)]
        AT = AR.alloc([22, 1152], BF16)
        sg = [AR.alloc([512], BF16) for _ in range(2)]
        yb = [AR.alloc([D], F32) for _ in range(2)]
        ssqs = [AR.alloc([1], F32) for _ in range(2)]
        rss = [AR.alloc([1], F32) for _ in range(2)]
        pf = "F%d" % li
        load_gain_bc(gbc, pf + "_gbc", ffn_g[li:li + 1, :], pf + "_g", 32.0)
        win_v = f_w_in[li].rearrange("(k p) n -> p k n", p=128)
        wout_v = f_w_out[li].rearrange("(j p) n -> p j n", p=128)
        half = (ntiles + 1) // 2
        passes = [list(range(0, half)), list(range(half, ntiles))]
        cnt = {"pro": 0, "y": 0, "g": 0}

        def load_wi(j):
            b3 = j % 3
            P.dma_group("pool", pf + "_wi%d" % b3, [
                lambda e: e.dma_start(out=wi[b3][:, :, 0:128], in_=win_v[:, :, j * 128:(j + 1) * 128]),
                lambda e: e.dma_start(out=wi[b3][:, :, 128:256], in_=win_v[:, :, DFF + j * 128:DFF + (j + 1) * 128])],
                writes=[pf + "_wi%dg" % b3, pf + "_wi%du" % b3])

        def pro_a(i, tl):
            c = cnt["pro"]
            cnt["pro"] += 1
            x3, h2 = c % 3, c % 2
            xn, hbn, pfx = pf + "_xs%d" % x3, pf + "_hb%d" % h2, pf + "n%d" % h2
            P.dma("sp", pf + "_x%d" % x3, lambda e: e.dma_start(out=xs[x3], in_=Xin[tl * 128:(tl + 1) * 128, :]), writes=[xn])
            norm_tile(xs[x3], xn, gbc, pf + "_gbc", hbs[h2], hbn, sq, pf + "_sq", ssqs[h2], rss[h2], pfx)
            return (hbs[h2], hbn)

        def pro_b(i, hbinfo):
            transpose8(hbinfo[0], hbinfo[1], hT[:, :, i * 128:(i + 1) * 128], pf + "_hT%d" % i, 0)

        for j0 in (0, 11):
            P.dma("pool", pf + "_wo%d" % j0, lambda e, j0=j0: e.dma_start(out=wo[:, j0:j0 + 11, :], in_=wout_v[:, j0:j0 + 11, :]),
                  writes=[pf + "_wo%d" % j0])
        for j in range(3):
            load_wi(j)
        prev_ = None
        for i, tl in enumerate(passes[0]):
            info_ = pro_a(i, tl)
            if prev_ is not None:
                pro_b(*prev_)
            prev_ = (i, info_)
        pro_b(*prev_)
        for pi, tiles in enumerate(passes):
            ntok = len(tiles) * 128
            nsub = (ntok + 511) // 512
            sbw = ((ntok // 128 + nsub - 1) // nsub) * 128
            for j in range(22):
                b3 = j % 3
                if j >= 3:
                    load_wi(j)
                for sb, s0 in enumerate(range(0, ntok, sbw)):
                    n = min(sbw, ntok - s0)
                    gb = cnt["g"] % 2
                    cnt["g"] += 1
                    hnames = [pf + "_hT%d" % i for i in range(s0 // 128, (s0 + n) // 128)]
                    for (col, bank, wn) in ((0, gb, "g"), (128, 2 + gb, "u")):
                        for k in range(8):
                            P.op("pe", lambda e, k=k, col=col, bank=bank, b3=b3, s0=s0, n=n: e.matmul(
                                PS[bank][:, 0:n], lhsT=wi[b3][:, k, col:col + 128], rhs=hT[:, k, s0:s0 + n], start=(k == 0), stop=(k == 7)),
                                reads=hnames + [pf + "_wi%d%s" % (b3, wn)], writes=[PSN[bank]])
                    P.op("act", lambda e, gb=gb, n=n: e.activation(out=sg[gb][:, 0:n], in_=PS[gb][:, 0:n], func=AF.Silu),
                         reads=[PSN[gb]], writes=[pf + "_sg%d" % gb])
                    P.op("dve", lambda e, gb=gb, n=n, j=j, s0=s0: e.tensor_tensor(out=AT[:, j, s0:s0 + n], in0=PS[2 + gb][:, 0:n],
                                                                                in1=sg[gb][:, 0:n], op=ALU.mult),
                         reads=[PSN[2 + gb], pf + "_sg%d" % gb], writes=[pf + "_AT%d_%d" % (sb, j % 2)])
            nxt = passes[pi + 1] if pi + 1 < len(passes) else []
            if nxt:
                for j in range(3):
                    load_wi(j)
            for i, tl in enumerate(tiles):
                hbinfo = pro_a(i, nxt[i]) if i < len(nxt) else None
                yi = cnt["y"] % 2
                cnt["y"] += 1
                P.dma("sp", pf + "_xr%d" % yi, lambda e, tl=tl, yi=yi: e.dma_start(out=xr2[yi], in_=Xin[tl * 128:(tl + 1) * 128, :]),
                      writes=[pf + "_xr%d" % yi])
                for hh in range(2):
                    bank = 4 + (2 * yi + hh)
                    if bank == 7:
                        bank = 3 if False else 7
                    for j in range(22):
                        P.op("pe", lambda e, j=j, i=i, hh=hh, bank=bank: e.matmul(PS[bank], lhsT=AT[:, j, i * 128:(i + 1) * 128],
                                                                               rhs=wo[:, j, hh * 512:(hh + 1) * 512],
                                                                               start=(j == 0), stop=(j == 21)),
                             reads=[pf + "_AT%d_0" % (i * 128 // sbw), pf + "_AT%d_1" % (i * 128 // sbw), pf + "_wo%d" % (0 if j < 11 else 11)],
                             writes=[PSN[bank]])
                    P.op("dve", lambda e, hh=hh, bank=bank, yi=yi: e.tensor_tensor(out=yb[yi][:, hh * 512:(hh + 1) * 512], in0=PS[bank],
                                                                                 in1=xr2[yi][:, hh * 512:(hh + 1) * 512], op=ALU.add),
                         reads=[PSN[bank], pf + "_xr%d" % yi], writes=[pf + "_y%d_%d" % (yi, hh)])
                P.dma("sp", pf + "_ys%d" % yi, lambda e, tl=tl, yi=yi: e.dma_start(out=Xout[tl * 128:(tl + 1) * 128, :], in_=yb[yi]),
                      reads=[pf + "_y%d_0" % yi, pf + "_y%d_1" % yi])
                if hbinfo is not None:
                    pro_b(i, hbinfo)

    phase_FFN(X1, X2, 0, NQT)
    barrier()
    if stop_after == "F0":
        P.emit()
        return nc

    def layer_norm_free(src, src_n, dst, dst_n, g_bc, g_n, b_bc, b_n, tmp, tmp_n, sm, pfx, eng2="pool"):
        P.op("dve", lambda e: e.tensor_reduce(out=sm[:, 0:1], in_=src, axis=AX.X, op=ALU.add), reads=[src_n], writes=[pfx + "_s1"])
        P.op("dve", lambda e: e.tensor_scalar(out=sm[:, 1:2], in0=sm[:, 0:1], scalar1=-1.0 / 512.0, scalar2=None, op0=ALU.mult),
             reads=[pfx + "_s1"], writes=[pfx + "_nm"])
        P.op("dve", lambda e: e.tensor_scalar(out=tmp, in0=src, scalar1=sm[:, 1:2], scalar2=None, op0=ALU.add),
             reads=[src_n, pfx + "_nm"], writes=[tmp_n])
        P.op("act", lambda e: e.activation(out=src, in_=tmp, func=AF.Square, accum_out=sm[:, 2:3]), reads=[tmp_n], writes=[src_n, pfx + "_ss"])
        P.op("act", lambda e: e.activation(out=sm[:, 3:4], in_=sm[:, 2:3], func=AF.Ln, bias=float(EPS), scale=1.0 / 512.0),
             reads=[pfx + "_ss"], writes=[pfx + "_rs"])
        P.op("act", lambda e: e.activation(out=sm[:, 3:4], in_=sm[:, 3:4], func=AF.Exp, scale=-0.5), reads=[pfx + "_rs"], writes=[pfx + "_rs"])
        P.op("dve", lambda e: e.scalar_tensor_tensor(out=tmp, in0=tmp, scalar=sm[:, 3:4], in1=g_bc, op0=ALU.mult, op1=ALU.mult),
             reads=[tmp_n, pfx + "_rs", g_n], writes=[tmp_n])
        P.op(eng2, lambda e: e.tensor_tensor(out=dst, in0=tmp, in1=b_bc, op=ALU.add), reads=[tmp_n, b_n], writes=[dst_n])

    def phase_D():
        reset_arena()
        Wi1 = AR.alloc([8, 2048], BF16)
        Wo1 = AR.alloc([8, D], BF16)
        Dg = AR.alloc([31, 4, 128], BF16)
        wsf = AR.alloc([8, 128], F32)
        wsT = AR.alloc([8, 128], BF16)
        cbuf = AR.alloc([4, 2080], BF16)
        gbc = AR.alloc([D], F32)
        brow = AR.alloc([1024], F32)
        glgb = AR.alloc([1024], F32)
        cvv = AR.alloc([1536], F32)
        bsT = AR.alloc([8], F32)
        obc = AR.alloc([8], F32)
        cw = AR.alloc([4, 31], F32)
        xt = [AR.alloc([D], F32) for _ in range(3)]
        sq = AR.alloc([D], BF16)
        NZ = 2
        S_ = []
        for z in range(NZ):
            S_.append(dict(
                hb=AR.alloc([D], BF16), hT=AR.alloc([8, 128], BF16), sig=AR.alloc([4, 128], F32), ctmp=AR.alloc([128], F32),
                t0u=AR.alloc([512], F32), t0v=AR.alloc([512], F32), w1u=AR.alloc([512], F32), w1v=AR.alloc([512], F32),
                w2=AR.alloc([512], F32), w3=AR.alloc([512], F32), gvn=AR.alloc([512], BF16), CC=AR.alloc([D], BF16),
                CCT=AR.alloc([8, 128], BF16), sm=AR.alloc([8], F32), sm2=AR.alloc([8], F32), ssq=AR.alloc([1], F32), rs=AR.alloc([1], F32)))
        yb = [AR.alloc([D], F32) for _ in range(2)]

        load_w_bf16(Wi1, "D_Wi", o_w_in, "D_wi", nsplit=4)
        wi_n = ["D_Wi_%d" % (k // 2 * 2) for k in range(8)]
        load_w_bf16(Wo1, "D_Wo", o_w_out, "D_wo", nsplit=2)
        wo_n = ["D_Wo_%d" % (k // 4 * 4) for k in range(8)]
        load_gain_bc(gbc, "D_gbc", attn_g[1:2, :], "D_g", 32.0)
        P.dma_group("sp", "D_c", [
            lambda e: e.dma_start(out=brow, in_=o_b_row[:, 0:1024].partition_broadcast(128)),
            lambda e: e.dma_start(out=glgb, in_=gl_gb.partition_broadcast(128)),
            lambda e: e.dma_start(out=cvv, in_=cvec.partition_broadcast(128)),
            lambda e: e.dma_start(out=bsT, in_=bsT_in),
            lambda e: e.dma_start(out=obc, in_=o_b_col),
            lambda e: e.dma_start(out=cw, in_=cwT),
            lambda e: e.dma_start(out=wsf, in_=wsT_in)],
            writes=["D_brow", "D_glgb", "D_cvv", "D_bsT", "D_obc", "D_cw", "D_wsf"])
        P.op("pool", lambda e: e.affine_select(out=wsf, in_=wsf, pattern=[[0, 8], [1, 128]], compare_op=ALU.is_ge, fill=0.0,
                                               base=0, channel_multiplier=-1), reads=["D_wsf"], writes=["D_wsf"])
        P.op("pool", lambda e: e.tensor_copy(out=wsT, in_=wsf), reads=["D_wsf"], writes=["D_wsT"])
        for c in range(4):
            P.op("dve" if c % 2 == 0 else "pool", lambda e, c=c: e.tensor_tensor(
                out=Dg[:, :, c, :], in0=ident.unsqueeze(1).to_broadcast([128, 31, 128]),
                in1=cw[:, c, :].unsqueeze(2).to_broadcast([128, 31, 128]), op=ALU.mult), reads=["ident", "D_cw"], writes=["D_Dg%d" % c])

        def gelu(ps, psn, bias_bc, t0, t0n, w1, w1n, dst, dstn):
            P.op("dve", lambda e: e.tensor_tensor(out=t0, in0=ps, in1=bias_bc, op=ALU.add), reads=[psn, "D_brow"], writes=[t0n])
            P.op("act", lambda e: e.activation(out=w1, in_=t0, func=AF.Square), reads=[t0n], writes=[w1n])
            P.op("dve", lambda e: e.tensor_scalar(out=w1, in0=w1, scalar1=0.044715, scalar2=1.0, op0=ALU.mult, op1=ALU.add),
                 reads=[w1n], writes=[w1n])
            P.op("pool", lambda e: e.tensor_tensor(out=w1, in0=w1, in1=t0, op=ALU.mult), reads=[w1n, t0n], writes=[w1n])
            P.op("act", lambda e: e.activation(out=w1, in_=w1, func=AF.Sigmoid, scale=1.5957691216057308), reads=[w1n], writes=[w1n])
            P.op("pool", lambda e: e.tensor_tensor(out=dst, in0=w1, in1=t0, op=ALU.mult), reads=[w1n, t0n], writes=[dstn])

        def tile_stages(it, tl):
            b2 = it % 3
            z = it % NZ
            B = S_[z]
            N_ = lambda nme: "D_%s%d" % (nme, z)
            hb, hT, sig, ctmp, t0u, t0v, w1u, w1v, w2, w3, gvn, CC, CCT, sm, sm2 = (B[k] for k in (
                "hb", "hT", "sig", "ctmp", "t0u", "t0v", "w1u", "w1v", "w2", "w3", "gvn", "CC", "CCT", "sm", "sm2"))
            halo = tl == 16
            xn = "D_xt%d" % b2
            yi = it % 2

            def f1():
                P.dma("sp", "D_x%d" % b2, lambda e: e.dma_start(out=xt[b2], in_=X2[tl * 128:(tl + 1) * 128, :]), writes=[xn])

            def f2():
                P.op("act", lambda e: e.activation(out=sq, in_=xt[b2], func=AF.Square, accum_out=B["ssq"]), reads=[xn], writes=["D_sq", N_("n") + "_ssq"])
                rms_rstd(B["ssq"], N_("n") + "_ssq", B["rs"], N_("n") + "_rs", D * EPS)

            def f3():
                P.op("dve", lambda e: e.scalar_tensor_tensor(out=hb, in0=xt[b2], scalar=B["rs"], in1=gbc, op0=ALU.mult, op1=ALU.mult),
                     reads=[xn, N_("n") + "_rs", "D_gbc"], writes=[N_("hb")])

            def f4():
                transpose8(hb, N_("hb"), hT, N_("hT"), 0)

            def f5():
                for (base, bank) in ((1024, 1), (1536, 2)):
                    for c in range(4):
                        for k in range(8):
                            P.op("pe", lambda e, base=base, bank=bank, c=c, k=k: e.matmul(
                                PS[bank][:, c * 128:(c + 1) * 128], lhsT=Wi1[:, k, base + c * 128:base + (c + 1) * 128], rhs=hT[:, k, :],
                                start=(k == 0), stop=(k == 7)), reads=[N_("hT"), wi_n[k]], writes=[PSN[bank]])

            def f6():
                for (base, bank) in ((0, 3), (512, 4)):
                    for k in range(8):
                        P.op("pe", lambda e, base=base, bank=bank, k=k: e.matmul(PS[bank], lhsT=hT[:, k, :], rhs=Wi1[:, k, base:base + 512],
                                                                              start=(k == 0), stop=(k == 7)),
                             reads=[N_("hT"), wi_n[k]], writes=[PSN[bank]])

            def f7():
                for c in range(4):
                    P.op("act", lambda e, c=c: e.activation(out=sig[:, c, :], in_=PS[2][:, c * 128:(c + 1) * 128], func=AF.Sigmoid,
                                                            bias=obc[:, 4 + c:5 + c]), reads=[PSN[2], "D_obc"], writes=[N_("sig") + "_%d" % c])
                    if halo:
                        P.op("dve", lambda e, c=c: e.scalar_tensor_tensor(out=ctmp, in0=PS[1][:, c * 128:(c + 1) * 128], scalar=obc[:, c:c + 1],
                                                                          in1=sig[:, c, :], op0=ALU.add, op1=ALU.mult),
                             reads=[PSN[1], "D_obc", N_("sig") + "_%d" % c], writes=[N_("ctmp")])
                        P.op("dve", lambda e, c=c: e.tensor_scalar(out=cbuf[:, c, 0:32], in0=ctmp[:, 96:128], scalar1=flg[:, 1:2], scalar2=None,
                                                                   op0=ALU.mult), reads=[N_("ctmp"), "flg"], writes=["D_cb_h%d" % c])
                    else:
                        P.op("dve", lambda e, c=c: e.scalar_tensor_tensor(out=cbuf[:, c, 32 + tl * 128:32 + (tl + 1) * 128],
                                                                          in0=PS[1][:, c * 128:(c + 1) * 128], scalar=obc[:, c:c + 1],
                                                                          in1=sig[:, c, :], op0=ALU.add, op1=ALU.mult),
                             reads=[PSN[1], "D_obc", N_("sig") + "_%d" % c], writes=["D_cb_%d_%d" % (tl, c)])
            if halo:
                return [f1, f2, f3, f4, f5, f7], []

            def f8():
                gelu(PS[3], PSN[3], brow[:, 0:512], t0u, N_("t0u"), w1u, N_("w1u"), w2, N_("w2"))

            def f9():
                gelu(PS[4], PSN[4], brow[:, 512:1024], t0v, N_("t0v"), w1v, N_("w1v"), w3, N_("w3"))

            def f10():
                layer_norm_free(w3, N_("w3"), gvn, N_("gvn"), glgb[:, 0:512], "D_glgb", glgb[:, 512:1024], "D_glgb", t0v, N_("t0v"), sm, N_("ln1"))

            def f11():
                for g in range(8):
                    P.op("pe", lambda e, g=g: e.matmul(PS[0][:, g * 64:(g + 1) * 64], lhsT=wsT[:, g, :], rhs=gvn[:, g * 64:(g + 1) * 64],
                                                       start=True, stop=True), reads=["D_wsT", N_("gvn")], writes=[PSN[0]])
                P.op("dve", lambda e: e.tensor_tensor(out=t0u.rearrange("p (g d) -> p g d", d=64), in0=PS[0].rearrange("p (g d) -> p g d", d=64),
                                                      in1=bsT[:, 0:8].unsqueeze(2).to_broadcast([128, 8, 64]), op=ALU.add),
                     reads=[PSN[0], "D_bsT"], writes=[N_("t0u")])
                P.op("pool", lambda e: e.tensor_tensor(out=CC[:, 0:512], in0=w2, in1=t0u, op=ALU.mult), reads=[N_("w2"), N_("t0u")], writes=[N_("CCa")])

            def s1():
                for c in range(4):
                    rn = ["D_cb_%d_%d" % (tl, c), ("D_cb_%d_%d" % (tl - 1, c)) if tl > 0 else ("D_cb_h%d" % c)]
                    for j in range(31):
                        o = tl * 128 + 2 + j
                        P.op("pe", lambda e, c=c, j=j, o=o: e.matmul(PS[5][:, c * 128:(c + 1) * 128], lhsT=cbuf[:, c, o:o + 128], rhs=Dg[:, j, c, :],
                                                                  start=(j == 0), stop=(j == 30)), reads=rn + ["D_Dg%d" % c], writes=[PSN[5]])
                P.op("dve", lambda e: e.tensor_tensor(out=w3, in0=PS[5], in1=cvv[:, 0:512], op=ALU.add), reads=[PSN[5], "D_cvv"], writes=[N_("w3")])

            def s2():
                layer_norm_free(w3, N_("w3"), w2, N_("w2"), cvv[:, 512:1024], "D_cvv", cvv[:, 1024:1536], "D_cvv", t0v, N_("t0v"), sm2, N_("ln2"))
                P.op("act", lambda e: e.activation(out=CC[:, 512:1024], in_=w2, func=AF.Silu), reads=[N_("w2")], writes=[N_("CCb")])

            def s3():
                psb = PS[6].bitcast(BF16)
                for k in range(8):
                    P.op("pe", lambda e, k=k: e.transpose(out=psb[:, k * 128:(k + 1) * 128], in_=CC[:, k * 128:(k + 1) * 128], identity=ident),
                         reads=[N_("CCa"), N_("CCb"), "ident"], writes=[PSN[6]])
                P.op("act", lambda e: e.activation(out=CCT, in_=psb[:, 0:1024].rearrange("p (a b) -> p a b", b=128), func=AF.Copy),
                     reads=[PSN[6]], writes=[N_("CCT")])

            def s4():
                for hh in range(2):
                    bank = 7 if hh == 0 else 5
                    for k in range(8):
                        P.op("pe", lambda e, k=k, hh=hh, bank=bank: e.matmul(PS[bank], lhsT=CCT[:, k, :], rhs=Wo1[:, k, hh * 512:(hh + 1) * 512],
                                                                            start=(k == 0), stop=(k == 7)), reads=[N_("CCT"), wo_n[k]], writes=[PSN[bank]])
                    P.op("dve", lambda e, hh=hh, bank=bank: e.tensor_tensor(out=yb[yi][:, hh * 512:(hh + 1) * 512], in0=PS[bank],
                                                                           in1=xt[b2][:, hh * 512:(hh + 1) * 512], op=ALU.add),
                         reads=[PSN[bank], xn], writes=["D_y%d_%d" % (yi, hh)])
                P.dma("sp", "D_ys%d" % yi, lambda e: e.dma_start(out=X3[tl * 128:(tl + 1) * 128, :], in_=yb[yi]),
                      reads=["D_y%d_0" % yi, "D_y%d_1" % yi])
            return [f1, f2, f3, f4, f5, f6, f7, f8, f9, f10, f11], [s1, s2, s3, s4]

        def coalesce(lst):
            out_, run = [], []
            for it_ in lst:
                if it_[0] == "op" and it_[1] == "pe":
                    run.append(it_)
                else:
                    if run:
                        out_.append(("macro", run))
                        run = []
                    out_.append(it_)
            if run:
                out_.append(("macro", run))
            return out_

        def round_robin(chains):
            chains = [c for c in chains if c]
            idx = [0] * len(chains)
            while True:
                progressed = False
                for ci, c in enumerate(chains):
                    if idx[ci] < len(c):
                        P.replay(c[idx[ci]])
                        idx[ci] += 1
                        progressed = True
                if not progressed:
                    break

        order = [16] + list(range(16))
        stages = {}

        def get(it):
            if it not in stages and 0 <= it < len(order):
                stages[it] = tile_stages(it, order[it])
            return stages.get(it)

        F0_, _ = get(0)
        for f in F0_[0:4]:
            f()
        for it, tl in enumerate(order):
            F, Sn = get(it)
            halo = tl == 16
            F[4]()
            if not halo:
                F[5]()
            chains = []
            fc = F[5] if halo else F[6]
            chains.append(coalesce(P.capture(fc)))
            if not halo:
                chains.append(coalesce(P.capture(F[7])))
                chains.append(coalesce(P.capture(lambda: (F[8](), F[9](), F[10]()))))
            if it >= 1 and stages[it - 1][1]:
                Sp = stages[it - 1][1]
                chains.append(coalesce(P.capture(lambda: [st() for st in Sp])))
            nxt = get(it + 1)
            if nxt is not None:
                Fn = nxt[0]
                chains.append(coalesce(P.capture(lambda: [f() for f in Fn[0:4]])))
            round_robin(chains)
        lastS = stages[len(order) - 1][1]
        for st in lastS:
            st()

    phase_D()
    barrier()
    if stop_after == "D":
        P.emit()
        return nc

    phase_FFN(X3, out, 1, 16)
    barrier()
    P.emit()
    return nc


def _bf16(a):
    import ml_dtypes
    return np.asarray(a, np.float32).astype(ml_dtypes.bfloat16)


def _rope_tables(dim, pos):
    inv = 1.0 / (10000.0 ** (np.arange(0, dim, 2, dtype=np.float32) / dim))
    ang = pos.astype(np.float32)[:, None] * inv[None, :]
    return np.cos(ang).astype(np.float32), np.sin(ang).astype(np.float32)


def host_inputs(inp, core):
    b, hf = core // 2, core % 2
    x = np.asarray(inp["x"], np.float32)
    xin = np.concatenate([x[b, 0:2048], x[b, hf * 2048:(hf + 1) * 2048]], 0)
    pos = np.concatenate([np.arange(2048), hf * 2048 + np.arange(2048)])
    c64, s64 = _rope_tables(64, pos)
    c32, s32 = _rope_tables(32, pos)
    flags = np.zeros((128, 20), np.float32)
    flags[:, 0] = 0.0 if hf == 1 else -30000.0
    flags[:, 1] = 1.0 if hf == 1 else 0.0
    flags[:, 4:12] = 0.0 if hf == 1 else -1e30
    onehot = np.zeros((16, 4096), np.float32)
    for n in range(16):
        onehot[n, n * 256:(n + 1) * 256] = 1.0
    g = lambda k: np.asarray(inp[k], np.float32)
    gains = np.concatenate([g("diff_q_norm_g")[0], g("diff_k_norm_g")[0], g("diff_subln_g")[0], g("moba_q_norm_g")[0],
                            g("moba_k_norm_g")[0], np.zeros(64, np.float32)])[None, :]
    lams = np.concatenate([g("diff_lambda_q1")[0], g("diff_lambda_k1")[0], g("diff_lambda_q2")[0], g("diff_lambda_k2")[0]])[None, :]
    ob = g("odd_b_in")[0]
    m = {
        "xin": xin, "cs64": np.concatenate([c64, s64], 1), "cs32": np.concatenate([c32, s32], 1),
        "flags": flags, "onehot": _bf16(onehot),
        "even_w_in": g("even_w_in")[0], "even_w_out": g("even_w_out")[0],
        "ffn_w_in": g("ffn_w_in"), "ffn_w_out": g("ffn_w_out"),
        "odd_w_in": g("odd_w_in")[0], "odd_w_out": g("odd_w_out")[0],
        "attn_norm_g": g("attn_norm_g"), "ffn_norm_g": g("ffn_norm_g"),
        "gains": gains.astype(np.float32), "lams": lams.astype(np.float32),
        "odd_b_row": ob[None, :].copy(),
        "odd_b_col": np.ascontiguousarray(ob[1024:2048].reshape(8, 128).T),
        "gmlp_ln_gb": np.concatenate([g("gmlp_ln_g")[0], g("gmlp_ln_b")[0]])[None, :],
        "gmlp_wsT": np.ascontiguousarray(g("gmlp_w_s")[0].transpose(2, 0, 1)),
        "gmlp_bsT": np.ascontiguousarray(g("gmlp_b_s")[0].T),
        "conv_wT": np.ascontiguousarray(g("conv_w")[0].T.reshape(4, 128, 31).transpose(1, 0, 2)),
        "conv_vecs": np.concatenate([g("conv_b")[0], g("conv_ln_g")[0], g("conv_ln_b")[0]])[None, :],
    }
    return {k: np.ascontiguousarray(v) for k, v in m.items()}


def kernel(**inputs):
    in_maps = [host_inputs(inputs, c) for c in range(8)]
    nc = build_program()
    res = run_bass_kernel_spmd(nc, in_maps, core_ids=list(range(8)))
    out = np.zeros((4, 4096, 1024), np.float32)
    for c in range(8):
        b, hf = c // 2, c % 2
        out[b, hf * 2048:(hf + 1) * 2048] = np.asarray(res.results[c]["out"], np.float32)
    return out
```

```python
import numpy as np
import concourse.bass as bass
import concourse.mybir as mybir
from concourse.bass_utils import run_bass_kernel_spmd

F32 = mybir.dt.float32
BF16 = mybir.dt.bfloat16
AF = mybir.ActivationFunctionType
ALU = mybir.AluOpType
AX = mybir.AxisListType

ENGS = ["pe", "act", "dve", "pool", "sp"]
EPOCH = 30000


class Prog:
    def __init__(self, nc):
        self.nc = nc
        self.ops = {e: [] for e in ENGS}
        self.count = {e: 0 for e in ENGS}
        self.seen = {e: {} for e in ENGS}
        self.res = {}
        self.lanes = {}
        self.semkeys = []
        self.lane_waited = {}
        self.lane_sem = {}
        self.free_sems = []
        self.nsem = 0
        self._cap = None

    def _deps(self, eng, reads, writes):
        deps = []
        for r in reads:
            st = self.res.get(r)
            if st and st[0] is not None:
                deps.append(st[0])
        for w in writes:
            st = self.res.get(w)
            if st:
                if st[0] is not None:
                    deps.append(st[0])
                deps.extend(st[1].values())
        need = {}
        for (k, v) in deps:
            if eng == "pe" and k == ("pe", "raw"):
                continue
            if self.seen[eng].get(k, 0) < v:
                need[k] = max(need.get(k, 0), v)
        for k, v in need.items():
            self.ops[eng].append(("wait", k, v))
            self.seen[eng][k] = v
            if k[0] == "dma":
                self.lane_waited[k[1]] = max(self.lane_waited.get(k[1], 0), v)

    def _mark(self, tok, reads, writes, rkey):
        for r in reads:
            st = self.res.setdefault(r, [None, {}])
            st[1][rkey] = tok
        for w in writes:
            self.res[w] = [tok, {}]

    def capture(self, f):
        old = self._cap
        self._cap = []
        f()
        lst = self._cap
        self._cap = old
        return lst

    def replay(self, item):
        if item[0] == "op":
            self.op(*item[1:])
        elif item[0] == "dmag":
            self.dma_group(*item[1:])
        else:
            for it in item[1]:
                self.replay(it)

    def op(self, eng, fn, reads=(), writes=()):
        if self._cap is not None:
            self._cap.append(("op", eng, fn, list(reads), list(writes)))
            return None
        self._deps(eng, reads, writes)
        self.count[eng] += 1
        c = self.count[eng]
        key = (eng, "raw")
        tok = (key, c)
        self.ops[eng].append(("op", fn, key, 1, c))
        self._mark(tok, reads, writes, eng)
        return tok

    def dma(self, eng, lane, fn, reads=(), writes=(), multi=False):
        return self.dma_group(eng, lane, [fn], reads, writes)

    def dma_group(self, eng, lane, fns, reads=(), writes=()):
        if self._cap is not None:
            self._cap.append(("dmag", eng, lane, list(fns), list(reads), list(writes)))
            return None
        self._deps(eng, reads, writes)
        if lane not in self.lane_sem:
            self.lane_sem[lane] = (self.nsem, 0)
            self.nsem += 1
        semid, base = self.lane_sem[lane]
        n = self.lanes.get(lane, 0) + len(fns)
        self.lanes[lane] = n
        key = ("dma", semid)
        if key not in self.semkeys:
            self.semkeys.append(key)
        tok = (key, base + 16 * n)
        for fn in fns:
            self.ops[eng].append(("op", fn, key, 16, None))
        self._mark(tok, reads, writes, ("dma", lane))
        return tok

    def retire_lanes(self):
        toks = []
        for lane, n in self.lanes.items():
            semid, base = self.lane_sem[lane]
            toks.append((("dma", semid), base + 16 * n))
        return toks

    def recycle_lanes(self):
        pass

    def wait_all(self, eng, toks):
        for (k, v) in toks:
            if self.seen[eng].get(k, 0) < v:
                self.ops[eng].append(("wait", k, v))
                self.seen[eng][k] = v
                if k[0] == "dma":
                    self.lane_waited[k[1]] = max(self.lane_waited.get(k[1], 0), v)

    def emit(self):
        nc = self.nc
        from contextlib import ExitStack
        waited = {e: set() for e in ENGS}
        for name in ENGS:
            for it in self.ops[name]:
                if it[0] == "wait" and it[1][1] == "raw":
                    waited[it[1][0]].add(it[2])
        rank = {}
        for e_ in ENGS:
            for i, c in enumerate(sorted(waited[e_])):
                rank[(e_, c)] = i + 1
        semkeys = list(self.semkeys)
        for (e_, c), r in rank.items():
            k = (e_, (r - 1) // EPOCH)
            if k not in semkeys:
                semkeys.append(k)
        with ExitStack() as es:
            sems = {}
            for k in semkeys:
                nm = "s_" + "_".join(str(x) for x in k)
                sems[k] = es.enter_context(nc.semaphore(nm))
            block = es.enter_context(nc.Block())

            def run(e, name):
                for it in self.ops[name]:
                    if it[0] == "wait":
                        if it[1][1] == "raw":
                            r = rank[(it[1][0], it[2])]
                            e.wait_ge(sems[(it[1][0], (r - 1) // EPOCH)], (r - 1) % EPOCH + 1)
                        else:
                            e.wait_ge(sems[it[1]], it[2])
                    else:
                        ins = it[1](e)
                        if it[4] is None:
                            ins.then_inc(sems[it[2]], it[3])
                        else:
                            r = rank.get((name, it[4]))
                            if r is not None:
                                ins.then_inc(sems[(name, (r - 1) // EPOCH)], 1)

            @block.tensor
            def _(e):
                run(e, "pe")

            @block.scalar
            def _(e):
                run(e, "act")

            @block.vector
            def _(e):
                run(e, "dve")

            @block.gpsimd
            def _(e):
                run(e, "pool")

            @block.sync
            def _(e):
                run(e, "sp")


NQT = 17
NQ = NQT * 128
NKT = 32
NK = 4096
D = 1024
DFF = 2816
EPS = 1e-6
NEG = -30000.0
LAMBDA_INIT0 = 0.8 - 0.6 * 1.0


class Arena:
    def __init__(self, ap, nbytes):
        self.ap = ap
        self.cap = nbytes
        self.off = 0

    def reset(self):
        self.off = 0

    def alloc(self, shape_free, dt):
        n = 1
        for s in shape_free:
            n *= s
        esz = 4 if dt == F32 else 2
        self.off = (self.off + 63) // 64 * 64
        nb = n * esz
        assert self.off + nb <= self.cap, ("arena overflow", self.off, nb, self.cap)
        v = self.ap[:, self.off // 2:(self.off + nb) // 2]
        self.off += nb
        if dt == F32:
            v = v.bitcast(F32)
        if len(shape_free) == 2:
            v = v.rearrange("p (a b) -> p a b", b=shape_free[1])
        elif len(shape_free) == 3:
            v = v.rearrange("p (a b c) -> p a b c", b=shape_free[1], c=shape_free[2])
        return v


def build_program(stop_after=None, debug=False):
    nc = bass.Bass("TRN2", target_bir_lowering=False)
    dbg_kind = "ExternalOutput" if debug else "Internal"

    def din(name, shape, dt=F32):
        return nc.dram_tensor(name, list(shape), dt, kind="ExternalInput").ap()

    def dscr(name, shape, dt):
        return nc.dram_tensor(name, list(shape), dt, kind=dbg_kind).ap()

    xin = din("xin", [NK, D])
    cs64 = din("cs64", [NK, 64])
    cs32 = din("cs32", [NK, 32])
    flags = din("flags", [128, 20])
    onehot = din("onehot", [16, NK], BF16)
    e_w_in = din("even_w_in", [D, 3072])
    e_w_out = din("even_w_out", [D, D])
    f_w_in = din("ffn_w_in", [2, D, 2 * DFF])
    f_w_out = din("ffn_w_out", [2, DFF, D])
    o_w_in = din("odd_w_in", [D, 2048])
    o_w_out = din("odd_w_out", [D, D])
    attn_g = din("attn_norm_g", [2, D])
    ffn_g = din("ffn_norm_g", [2, D])
    gains = din("gains", [1, 320])
    lams = din("lams", [1, 128])
    o_b_row = din("odd_b_row", [1, 2048])
    o_b_col = din("odd_b_col", [128, 8])
    gl_gb = din("gmlp_ln_gb", [1, 1024])
    wsT_in = din("gmlp_wsT", [128, 8, 128])
    bsT_in = din("gmlp_bsT", [128, 8])
    cwT = din("conv_wT", [128, 4, 31])
    cvec = din("conv_vecs", [1, 1536])
    out = nc.dram_tensor("out", [2048, D], F32, kind="ExternalOutput").ap()

    QdT = dscr("QdT", [512, NQ], BF16)
    KdT = dscr("KdT", [512, NK], BF16)
    Vd = dscr("Vd", [8, NK, 64], BF16)
    QmT = dscr("QmT", [512, NQ], BF16)
    KmT = dscr("KmT", [512, NK], BF16)
    Vm = dscr("Vm", [8, NK, 64], BF16)
    X1 = dscr("X1", [NQ, D], F32)
    X2 = dscr("X2", [NQ, D], F32)
    X3 = dscr("X3", [2048, D], F32)

    ARENA_BYTES = 190 * 1024
    arena_t = nc.alloc_sbuf_tensor("arena", [128, ARENA_BYTES // 2], BF16)
    AR = Arena(arena_t.ap(), ARENA_BYTES)
    PSALL = nc.alloc_psum_tensor("psall", [128, 4096], F32).ap()
    PS = [PSALL[:, i * 512:(i + 1) * 512] for i in range(8)]
    PSN = ["ps%d" % i for i in range(8)]

    P = Prog(nc)
    uid = [0]

    def nm(s):
        uid[0] += 1
        return "%s#%d" % (s, uid[0])

    lane_ctr = [0]

    def newlane(s="l"):
        lane_ctr[0] += 1
        return "%s%d" % (s, lane_ctr[0])

    def barrier():
        toks = []
        for e in ENGS:
            c = P.count[e]
            if c > 0:
                toks.append(((e, "raw"), c))
        toks.extend(P.retire_lanes())
        for e in ENGS:
            P.wait_all(e, toks)
        P.res.clear()
        P.recycle_lanes()

    ident = AR.alloc([128], BF16)
    flg = AR.alloc([20], F32)
    rnd = AR.alloc([8], F32)
    PERSIST = None

    P.op("pool", lambda e: e.memset(ident, 1.0), writes=["ident"])
    P.op("pool", lambda e: e.affine_select(out=ident, in_=ident, pattern=[[-1, 128]], compare_op=ALU.is_equal,
                                           fill=0.0, base=0, channel_multiplier=1), reads=["ident"], writes=["ident"])
    P.dma("sp", "c_flg", lambda e: e.dma_start(out=flg, in_=flags), writes=["flg"])
    AR.off = (AR.off + 63) // 64 * 64
    PERSIST = AR.off

    def reset_arena():
        AR.off = PERSIST

    def rms_rstd(ssq, ssq_name, rs, rs_name, hd_eps):
        P.op("act", lambda e: e.activation(out=rs, in_=ssq, func=AF.Ln, bias=float(hd_eps), scale=1.0),
             reads=[ssq_name], writes=[rs_name])
        P.op("act", lambda e: e.activation(out=rs, in_=rs, func=AF.Exp, scale=-0.5),
             reads=[rs_name], writes=[rs_name])

    def named(ap, name):
        return ap

    def norm_tile(xt, xt_n, gbc, gbc_n, hb, hb_n, sq, sq_n, ssq, rs, pfx):
        P.op("act", lambda e: e.activation(out=sq, in_=xt, func=AF.Square, accum_out=ssq),
             reads=[xt_n], writes=[sq_n, pfx + "_ssq"])
        rms_rstd(ssq, pfx + "_ssq", rs, pfx + "_rs", D * EPS)
        P.op("dve", lambda e: e.scalar_tensor_tensor(out=hb, in0=xt, scalar=rs, in1=gbc, op0=ALU.mult, op1=ALU.mult),
             reads=[xt_n, pfx + "_rs", gbc_n], writes=[hb_n])

    def transpose8(src, src_n, dst, dst_n, psi, nblk=8, evac="act"):
        psb = PS[psi].bitcast(BF16)
        for k in range(nblk):
            P.op("pe", lambda e, k=k: e.transpose(out=psb[:, k * 128:(k + 1) * 128], in_=src[:, k * 128:(k + 1) * 128],
                                                  identity=ident), reads=[src_n, "ident"], writes=[PSN[psi]])
        if evac == "act":
            P.op("act", lambda e: e.activation(out=dst, in_=psb[:, 0:nblk * 128].rearrange("p (a b) -> p a b", b=128),
                                               func=AF.Copy), reads=[PSN[psi]], writes=[dst_n])
        else:
            P.op("dve", lambda e: e.tensor_copy(out=dst, in_=psb[:, 0:nblk * 128].rearrange("p (a b) -> p a b", b=128)),
                 reads=[PSN[psi]], writes=[dst_n])

    def load_gain_bc(dst, dst_n, src_row, lane, scale):
        P.dma("sp", lane, lambda e: e.dma_start(out=dst, in_=src_row.partition_broadcast(128)), writes=[dst_n])
        if scale != 1.0:
            P.op("dve", lambda e: e.tensor_scalar(out=dst, in0=dst, scalar1=float(scale), scalar2=None, op0=ALU.mult),
                 reads=[dst_n], writes=[dst_n])

    def load_w_bf16(dst, dst_n, w_ap, lane, nsplit=4):
        K = dst.shape[1]
        wv = w_ap.rearrange("(k p) n -> p k n", p=128)
        step = max(1, K // nsplit)
        toks = []
        for k0 in range(0, K, step):
            k1 = min(K, k0 + step)
            toks.append(P.dma("pool", "%s_%d" % (lane, k0), lambda e, k0=k0, k1=k1: e.dma_start(out=dst[:, k0:k1, :], in_=wv[:, k0:k1, :]),
                              writes=[dst_n + "_%d" % k0], multi=True))
        return toks

    def phase_A():
        reset_arena()
        Wi = AR.alloc([8, 3072], BF16)
        gbc = AR.alloc([D], F32)
        Gq = AR.alloc([4, 512], F32)
        gsm = AR.alloc([320], F32)
        NC_ = 4
        cst = [AR.alloc([96], F32) for _ in range(NC_)]
        gct = [AR.alloc([4, 2, 64], F32) for _ in range(NC_)]
        xt = [AR.alloc([D], F32) for _ in range(NC_)]
        sq = AR.alloc([D], BF16)
        hbs = [AR.alloc([D], BF16) for _ in range(2)]
        hTs = [AR.alloc([8, 128], BF16) for _ in range(2)]
        ssqs = [AR.alloc([1], F32) for _ in range(2)]
        rss = [AR.alloc([1], F32) for _ in range(2)]
        NS = 4
        sqgs = [AR.alloc([512], F32) for _ in range(NS)]
        qns = [AR.alloc([512], F32) for _ in range(NS)]
        tas = [AR.alloc([512], F32) for _ in range(NS)]
        tbs = [AR.alloc([512], F32) for _ in range(NS)]
        ssqhs = [AR.alloc([16], F32) for _ in range(NS)]
        rshs = [AR.alloc([16], F32) for _ in range(NS)]
        qbs = [AR.alloc([512], BF16) for _ in range(NS)]
        qTs = [AR.alloc([4, 128], BF16) for _ in range(NS)]
        vbs = [AR.alloc([512], BF16) for _ in range(NS)]

        load_w_bf16(Wi, "A_Wi", e_w_in, "A_w", nsplit=8)
        wnames = ["A_Wi_%d" % k for k in range(8)]
        load_gain_bc(gbc, "A_gbc", attn_g[0:1, :], "A_g", 32.0)
        P.dma("sp", "A_gs", lambda e: e.dma_start(out=gsm, in_=gains.partition_broadcast(128)), writes=["A_gsm"])
        for gi, (o, hd) in enumerate([(0, 32), (32, 32), (128, 64), (192, 64)]):
            nh = 512 // hd
            P.op("dve", lambda e, gi=gi, o=o, hd=hd, nh=nh: e.tensor_scalar(
                out=Gq[:, gi, :].rearrange("p (a b) -> p a b", b=hd),
                in0=gsm[:, o:o + hd].unsqueeze(1).to_broadcast([128, nh, hd]),
                scalar1=float(hd) ** 0.5, scalar2=None, op0=ALU.mult), reads=["A_gsm"], writes=["A_Gq%d" % gi])

        Gs = AR.alloc([4, 64], F32)
        for gi, (o, hd) in enumerate([(0, 32), (32, 32), (128, 64), (192, 64)]):
            P.op("dve", lambda e, gi=gi, o=o, hd=hd: e.tensor_scalar(out=Gs[:, gi, 0:hd], in0=gsm[:, o:o + hd], scalar1=float(hd) ** 0.5,
                                                                    scalar2=None, op0=ALU.mult), reads=["A_gsm"], writes=["A_Gs"])
        qdt_v = QdT.rearrange("(c p) n -> p c n", p=128)
        kdt_v = KdT.rearrange("(c p) n -> p c n", p=128)
        qmt_v = QmT.rearrange("(c p) n -> p c n", p=128)
        kmt_v = KmT.rearrange("(c p) n -> p c n", p=128)
        vd_v = Vd.rearrange("h k d -> k h d")
        vm_v = Vm.rearrange("h k d -> k h d")

        def prologue(t):
            b2 = t % 2
            c4 = t % NC_
            xtn = "A_xt%d" % c4
            csn = "A_cs%d" % c4
            hb, hT, hbn, hTn = hbs[b2], hTs[b2], "A_hb%d" % b2, "A_hT%d" % b2
            ssq, rs, pfx = ssqs[b2], rss[b2], "A%d" % b2

            def st0():
                P.dma_group("sp", "A_in%d" % c4, [
                    lambda e: e.dma_start(out=xt[c4], in_=xin[t * 128:(t + 1) * 128, :]),
                    lambda e: e.dma_start(out=cst[c4][:, 0:64], in_=cs64[t * 128:(t + 1) * 128, :]),
                    lambda e: e.dma_start(out=cst[c4][:, 64:96], in_=cs32[t * 128:(t + 1) * 128, :])],
                    writes=[xtn, csn + "a", csn + "b"])

            def st1():
                P.op("act", lambda e: e.activation(out=sq, in_=xt[c4], func=AF.Square, accum_out=ssq), reads=[xtn], writes=["A_sq", pfx + "_ssq"])
                rms_rstd(ssq, pfx + "_ssq", rs, pfx + "_rs", D * EPS)
                for gi_, hd_ in enumerate((32, 32, 64, 64)):
                    hf_ = hd_ // 2
                    co_ = 64 if hd_ == 32 else 0
                    for cs_ in range(2):
                        P.op("pool", lambda e, gi_=gi_, hd_=hd_, hf_=hf_, co_=co_, cs_=cs_: e.tensor_tensor(
                            out=gct[c4][:, gi_, cs_, 0:hd_].rearrange("p (b c) -> p b c", c=hf_),
                            in0=Gs[:, gi_, 0:hd_].rearrange("p (b c) -> p b c", c=hf_),
                            in1=cst[c4][:, co_ + cs_ * hf_:co_ + (cs_ + 1) * hf_].unsqueeze(1).to_broadcast([128, 2, hf_]), op=ALU.mult),
                            reads=["A_Gs", csn + "a", csn + "b"], writes=[csn + "g"])

            def st2():
                P.op("dve", lambda e: e.scalar_tensor_tensor(out=hb, in0=xt[c4], scalar=rs, in1=gbc, op0=ALU.mult, op1=ALU.mult),
                     reads=[xtn, pfx + "_rs", "A_gbc"], writes=[hbn])

            def st3():
                transpose8(hb, hbn, hT, hTn, 7)
            return [st0, st1, st2, st3]

        def group_item(t, g, gidx, qidx):
            b2 = t % 2
            c4 = t % NC_
            csn = "A_cs%d" % c4
            hT, hTn = hTs[b2], "A_hT%d" % b2
            psi = gidx % 5
            ps = PS[psi]
            z = gidx % NS

            def s_mm():
                for k in range(8):
                    P.op("pe", lambda e, k=k: e.matmul(ps, lhsT=hT[:, k, :], rhs=Wi[:, k, g * 512:(g + 1) * 512], start=(k == 0), stop=(k == 7)),
                         reads=[hTn, wnames[k]], writes=[PSN[psi]])
            if g in (2, 5):
                vb, vbn = vbs[z], "A_vb%d" % z
                dst = vd_v if g == 2 else vm_v

                def s_cp():
                    P.op("act", lambda e: e.activation(out=vb, in_=ps, func=AF.Copy), reads=[PSN[psi]], writes=[vbn])

                def s_st():
                    P.dma("sp", "A_vs%d" % z, lambda e: e.dma_start(out=dst[t * 128:(t + 1) * 128, :, :], in_=vb.rearrange("p (h d) -> p h d", d=64)),
                          reads=[vbn])
                return [s_mm, s_cp, s_st]
            isd = g in (0, 1)
            hd = 32 if isd else 64
            half = hd // 2
            nh = 512 // hd
            gi = {0: 0, 1: 1, 3: 2, 4: 3}[g]
            sqg, qn, ta, tb, ssqh, rsh, qb, qT = sqgs[z], qns[z], tas[z], tbs[z], ssqhs[z], rshs[z], qbs[z], qTs[z]
            n_sqg, n_qn, n_ta, n_tb, n_ssqh, n_rsh, n_qb, n_qT = ["A_%s%d" % (x, z) for x in ("sqg", "qn", "ta", "tb", "ssqh", "rsh", "qb", "qT")]
            ssq_v = ssqh[:, 0:nh]
            rs_v = rsh[:, 0:nh]
            co = 64 if isd else 0
            cosv = cst[c4][:, co:co + half]
            sinv = cst[c4][:, co + half:co + 2 * half]
            q4 = qn.rearrange("p (a b c) -> p a b c", b=2, c=half)
            ta4 = ta.rearrange("p (a b c) -> p a b c", b=2, c=half)
            tb4 = tb.rearrange("p (a b c) -> p a b c", b=2, c=half)
            qb4 = qb.rearrange("p (a b c) -> p a b c", b=2, c=half)
            tpi = 5 + (gidx % 2)
            psb = PS[tpi].bitcast(BF16)
            if g in (0, 3):
                dstv = qdt_v if g == 0 else qmt_v
                col = qidx * 128
            else:
                dstv = kdt_v if g == 1 else kmt_v
                col = t * 128

            def s1():
                P.op("act", lambda e: e.activation(out=sqg, in_=ps, func=AF.Square), reads=[PSN[psi]], writes=[n_sqg])

            def s2():
                P.op("dve", lambda e: e.tensor_reduce(out=ssq_v, in_=sqg.rearrange("p (a b) -> p a b", b=hd), axis=AX.X, op=ALU.add),
                     reads=[n_sqg], writes=[n_ssqh])

            def s3():
                rms_rstd(ssq_v, n_ssqh, rs_v, n_rsh, hd * EPS)

            def s4():
                P.op("dve", lambda e: e.tensor_tensor(out=qn.rearrange("p (a b) -> p a b", b=hd), in0=ps.rearrange("p (a b) -> p a b", b=hd),
                                                      in1=rs_v.unsqueeze(2).to_broadcast([128, nh, hd]), op=ALU.mult),
                     reads=[PSN[psi], n_rsh], writes=[n_qn])

            gcv = gct[c4][:, gi, 0, 0:hd].rearrange("p (b c) -> p b c", c=half)
            gsv = gct[c4][:, gi, 1, 0:hd].rearrange("p (b c) -> p b c", c=half)

            def s5():
                pass

            def s6():
                P.op("dve", lambda e: e.tensor_tensor(out=ta4, in0=q4, in1=gcv.unsqueeze(1).to_broadcast([128, nh, 2, half]),
                                                      op=ALU.mult), reads=[n_qn, csn + "g"], writes=[n_ta])
                P.op("pool", lambda e: e.tensor_tensor(out=tb4, in0=q4, in1=gsv.unsqueeze(1).to_broadcast([128, nh, 2, half]),
                                                       op=ALU.mult), reads=[n_qn, csn + "g"], writes=[n_tb])

            def s7():
                P.op("dve", lambda e: e.tensor_tensor(out=qb4[:, :, 0, :], in0=ta4[:, :, 0, :], in1=tb4[:, :, 1, :], op=ALU.subtract),
                     reads=[n_ta, n_tb], writes=[n_qb + "a"])
                P.op("pool", lambda e: e.tensor_tensor(out=qb4[:, :, 1, :], in0=ta4[:, :, 1, :], in1=tb4[:, :, 0, :], op=ALU.add),
                     reads=[n_ta, n_tb], writes=[n_qb + "b"])

            def s8():
                for k in range(4):
                    P.op("pe", lambda e, k=k: e.transpose(out=psb[:, k * 128:(k + 1) * 128], in_=qb[:, k * 128:(k + 1) * 128], identity=ident),
                         reads=[n_qb + "a", n_qb + "b", "ident"], writes=[PSN[tpi]])

            def s9():
                P.op("act", lambda e: e.activation(out=qT, in_=psb[:, 0:512].rearrange("p (a b) -> p a b", b=128), func=AF.Copy),
                     reads=[PSN[tpi]], writes=[n_qT])

            def s10():
                P.dma("sp", "A_qs%d" % z, lambda e: e.dma_start(out=dstv[:, :, col:col + 128], in_=qT), reads=[n_qT])
            return [s_mm, s1, s2, s3, s4, s5, s6, s7, s8, s9, s10]

        items = []
        extra = {}
        first_group_of_tile = {}
        for t in range(NKT):
            if t >= 16:
                qidx = t - 16
            elif t == 15:
                qidx = 16
            else:
                qidx = None
            groups = [1, 2, 4, 5] if qidx is None else [0, 1, 2, 3, 4, 5]
            first_group_of_tile[t] = len(items)
            for g in groups:
                items.append(group_item(t, g, len(items), qidx))
        for st in prologue(0):
            st()
        for t in range(NKT - 1):
            g0 = first_group_of_tile[t]
            for k, st in enumerate(prologue(t + 1)):
                extra.setdefault(g0 + k, []).append(st)
        maxs = max(len(it) for it in items)
        for step in range(len(items) + maxs):
            for sidx in range(maxs - 1, -1, -1):
                g = step - sidx
                if 0 <= g < len(items) and sidx < len(items[g]):
                    items[g][sidx]()
            for st in extra.get(step, []):
                st()

    phase_A()
    barrier()
    fin = []
    if stop_after == "A":
        P.emit()
        return nc

    def phase_BC():
        reset_arena()
        Wo = AR.alloc([8, D], BF16)
        C = AR.alloc([NQT, D], BF16)
        cm = AR.alloc([4, 512], BF16)
        kT = [AR.alloc([NK], BF16) for _ in range(2)]
        vbuf = [AR.alloc([NKT, 128], BF16) for _ in range(2)]
        qa = [AR.alloc([NQ], BF16) for _ in range(2)]
        qb_ = [AR.alloc([NQ], BF16) for _ in range(2)]
        pT = [AR.alloc([2, 512], BF16) for _ in range(3)]
        oT = [AR.alloc([512], BF16) for _ in range(2)]
        tsb = AR.alloc([528], BF16)
        gsm = AR.alloc([320], F32)
        lamv = AR.alloc([128], F32)
        lt = AR.alloc([64], F32)
        lsm = AR.alloc([8], F32)
        Gsub = AR.alloc([4, 64], F32)
        gbj = AR.alloc([9, 16], F32)
        biaspad = AR.alloc([NQT, 128], BF16)
        gball = AR.alloc([NQT, 16], F32)
        ownm = AR.alloc([NQT, 16], F32)
        gmall = AR.alloc([NQT, 16], F32)
        cmpb = AR.alloc([NQT, 16, 16], BF16)
        rnk = AR.alloc([NQT, 16], F32)
        bia1 = AR.alloc([NQT, 16], F32)
        bia2 = AR.alloc([NQT, 16], F32)
        zt = AR.alloc([128], BF16)
        kbs = AR.alloc([16], F32)
        kbb = AR.alloc([16], BF16)
        gm = AR.alloc([16], F32)
        top8 = AR.alloc([8], F32)
        thr = AR.alloc([1], F32)
        rcp = AR.alloc([8], F32)
        oa = AR.alloc([4, 64], F32)
        ob = AR.alloc([4, 64], F32)
        dd = AR.alloc([4, 64], F32)
        sqd = AR.alloc([4, 64], F32)
        ssq4 = AR.alloc([4], F32)
        rs4 = AR.alloc([4], F32)
        CT = AR.alloc([8, 128], BF16)
        xr = [AR.alloc([D], F32) for _ in range(2)]
        x1 = [AR.alloc([D], F32) for _ in range(2)]

        load_w_bf16(Wo, "B_Wo", e_w_out, "B_w", nsplit=2)
        wo_names = ["B_Wo_0"] * 4 + ["B_Wo_4"] * 4
        P.op("pool", lambda e: e.memset(cm, 0.0), writes=["cm"])
        for i in range(4):
            P.op("pool", lambda e, i=i: e.affine_select(out=cm[:, i, :], in_=cm[:, i, :], pattern=[[1, 512]], compare_op=ALU.is_ge,
                                                        fill=NEG, base=-128 * i, channel_multiplier=-1), reads=["cm"], writes=["cm"])
        for b in range(2):
            P.op("pool", lambda e, b=b: e.memset(qa[b][32:64, :], 0.0), writes=["qa%d_z" % b])
            P.op("pool", lambda e, b=b: e.memset(qa[b][64:128, :], 0.0), writes=["qa%d_bias" % b])
            P.op("pool", lambda e, b=b: e.memset(qb_[b][0:32, :], 0.0), writes=["qb%d_z" % b])
            P.op("pool", lambda e, b=b: e.memset(qb_[b][64:128, :], 0.0), writes=["qb%d_z" % b])
            P.op("pool", lambda e, b=b: e.memset(vbuf[b][:, :, 64:128], 0.0), writes=["vb%d_1" % b])
            P.op("pool", lambda e, b=b: e.memset(vbuf[b][:, :, 64:65], 1.0), reads=["vb%d_1" % b], writes=["vb%d_1" % b])
            P.op("pool", lambda e, b=b: e.memset(kT[b][64:128, :], 0.0), writes=["kT%d_oh" % b])
            P.dma("sp", "B_oh%d" % b, lambda e, b=b: e.dma_start(out=kT[b][64:80, :], in_=onehot), writes=["kT%d_oh" % b])
        P.op("pool", lambda e: e.memset(biaspad, 0.0), writes=["biaspad"])
        P.op("pool", lambda e: e.memset(zt, 0.0), writes=["zt"])
        P.dma("sp", "B_gs", lambda e: e.dma_start(out=gsm, in_=gains.partition_broadcast(128)), writes=["B_gsm"])
        P.dma("sp", "B_lm", lambda e: e.dma_start(out=lamv, in_=lams.partition_broadcast(128)), writes=["B_lamv"])
        l4 = lamv.rearrange("p (a b c) -> p a b c", b=2, c=32)
        P.op("dve", lambda e: e.tensor_tensor(out=lt.rearrange("p (a c) -> p a c", c=32), in0=l4[:, :, 0, :], in1=l4[:, :, 1, :],
                                              op=ALU.mult), reads=["B_lamv"], writes=["B_lt"])
        P.op("dve", lambda e: e.tensor_reduce(out=lsm[:, 0:2], in_=lt.rearrange("p (a c) -> p a c", c=32), axis=AX.X, op=ALU.add),
             reads=["B_lt"], writes=["B_lsm"])
        P.op("act", lambda e: e.activation(out=lsm[:, 2:4], in_=lsm[:, 0:2], func=AF.Exp), reads=["B_lsm"], writes=["B_lsm2"])
        P.op("dve", lambda e: e.tensor_tensor(out=lsm[:, 4:5], in0=lsm[:, 3:4], in1=lsm[:, 2:3], op=ALU.subtract),
             reads=["B_lsm2"], writes=["B_lsm3"])
        neglam = lsm[:, 5:6]
        P.op("dve", lambda e: e.tensor_scalar(out=neglam, in0=lsm[:, 4:5], scalar1=-LAMBDA_INIT0, scalar2=None, op0=ALU.add),
             reads=["B_lsm3"], writes=["neglam"])
        P.op("dve", lambda e: e.tensor_scalar(out=Gsub, in0=gsm[:, 64:128].unsqueeze(1).to_broadcast([128, 4, 64]),
                                              scalar1=8.0 * (1.0 - LAMBDA_INIT0), scalar2=None, op0=ALU.mult),
             reads=["B_gsm"], writes=["Gsub"])
        for j in range(8):
            P.op("dve", lambda e, j=j: e.tensor_copy(out=gbj[:, j, :], in_=flg[:, 4:20]), reads=["flg"], writes=["gbj"])
            P.op("dve", lambda e, j=j: e.memset(gbj[:, j, 8 + j:16], -1e30), reads=["gbj"], writes=["gbj"])
        P.op("dve", lambda e: e.memset(gbj[:, 8, :], 0.0), reads=["gbj"], writes=["gbj"])
        P.op("dve", lambda e: e.memset(gbj[:, 8, 7:16], -1e30), reads=["gbj"], writes=["gbj"])

        P.op("dve", lambda e: e.memset(ownm, 1.0), writes=["ownm"])
        for qi in range(NQT):
            jj = qi // 2 if qi < 16 else 8
            ownb = 8 + qi // 2 if qi < 16 else 7
            P.op("dve", lambda e, qi=qi, jj=jj: e.tensor_copy(out=gball[:, qi, :], in_=gbj[:, jj, :]), reads=["gbj"], writes=["gball"])
            P.op("dve", lambda e, qi=qi, ownb=ownb: e.memset(ownm[:, qi, ownb:ownb + 1], 0.0), reads=["ownm"], writes=["ownm"])
        units = [("d", h) for h in range(8)] + [("m", h) for h in range(8)]
        groups = [(gi * 512, 512, 20 + 4 * gi, True, gi * 4) for gi in range(4)] + [(2048, 128, 16, False, 16)]
        gcount = [0]
        sc = [0]

        def issue_loads(u):
            kind, h = units[u]
            b = u % 2
            Ksrc, Vsrc, Qsrc = (KdT, Vd, QdT) if kind == "d" else (KmT, Vm, QmT)
            fns = [lambda e: e.dma_start(out=kT[b][0:64, :], in_=Ksrc[h * 64:(h + 1) * 64, :])]
            for pt in range(4):
                fns.append(lambda e, pt=pt: e.dma_start(out=vbuf[b][:, pt * 8:(pt + 1) * 8, 0:64],
                                                       in_=Vsrc[h].rearrange("(t p) d -> p t d", p=128)[:, pt * 8:(pt + 1) * 8, :]))
            wr = ["kT%d" % b] + ["vb%d_%d" % (b, pt) for pt in range(4)]
            if kind == "d":
                fns.append(lambda e: e.dma_start(out=qa[b][0:32, :], in_=Qsrc[h * 64:h * 64 + 32, :]))
                fns.append(lambda e: e.dma_start(out=qb_[b][32:64, :], in_=Qsrc[h * 64 + 32:h * 64 + 64, :]))
                wr += ["qa%d" % b, "qb%d" % b]
            else:
                fns.append(lambda e: e.dma_start(out=qa[b][0:64, :], in_=Qsrc[h * 64:(h + 1) * 64, :]))
                wr += ["qa%d" % b, "qa%d_z" % b]
            P.dma_group("sp", "B_ld%d" % b, fns, writes=wr)

        def gating_stages(u):
            kind, h = units[u]
            b = u % 2
            ps7b = PS[7].bitcast(BF16)

            def g1():
                P.op("dve", lambda e: e.tensor_reduce(out=kbs[0:64, :], in_=kT[b][0:64, :].rearrange("p (a c) -> p a c", c=256),
                                                      axis=AX.X, op=ALU.add), reads=["kT%d" % b], writes=["kbs"])
                P.op("dve", lambda e: e.tensor_scalar(out=kbb[0:64, :], in0=kbs[0:64, :], scalar1=1.0 / 256.0, scalar2=None, op0=ALU.mult),
                     reads=["kbs"], writes=["kbb"])
                for qi in range(NQT):
                    P.op("pe", lambda e, qi=qi: e.matmul(PS[7][:, qi * 16:(qi + 1) * 16], lhsT=qa[b][0:64, qi * 128:(qi + 1) * 128], rhs=kbb[0:64, :],
                                                        start=True, stop=True), reads=["qa%d" % b, "kbb"], writes=[PSN[7]])

            def g2():
                P.op("dve", lambda e: e.tensor_tensor(out=gmall, in0=PS[7][:, 0:NQT * 16].rearrange("p (q n) -> p q n", n=16), in1=gball, op=ALU.add),
                     reads=[PSN[7], "gball"], writes=["gmall"])
                P.op("dve", lambda e: e.tensor_tensor(out=cmpb, in0=gmall.unsqueeze(2).to_broadcast([128, NQT, 16, 16]),
                                                      in1=gmall.unsqueeze(3).to_broadcast([128, NQT, 16, 16]), op=ALU.is_gt),
                     reads=["gmall"], writes=["cmpb"])
                P.op("dve", lambda e: e.tensor_reduce(out=rnk, in_=cmpb, axis=AX.X, op=ALU.add), reads=["cmpb"], writes=["rnk"])
                P.op("dve", lambda e: e.tensor_scalar(out=bia1, in0=rnk, scalar1=2.5, scalar2=NEG, op0=ALU.is_gt, op1=ALU.mult),
                     reads=["rnk"], writes=["bia1"])
                P.op("dve", lambda e: e.tensor_scalar(out=bia2, in0=gmall, scalar1=-1e29, scalar2=NEG, op0=ALU.is_lt, op1=ALU.mult),
                     reads=["gmall"], writes=["bia2"])
                P.op("dve", lambda e: e.tensor_tensor(out=bia1, in0=bia1, in1=bia2, op=ALU.add), reads=["bia1", "bia2"], writes=["bia1"])
                P.op("dve", lambda e: e.tensor_tensor(out=biaspad[:, :, 64:80], in0=bia1, in1=ownm, op=ALU.mult),
                     reads=["bia1", "ownm"], writes=["biaspad"])

            def g3():
                for q0 in range(0, NQT, 3):
                    nb = min(3, NQT - q0)
                    for k in range(nb):
                        P.op("pe", lambda e, q0=q0, k=k: e.transpose(out=ps7b[:, 544 + k * 128:544 + (k + 1) * 128], in_=biaspad[:, q0 + k, :],
                                                                    identity=ident), reads=["biaspad", "ident"], writes=[PSN[7]])
                    P.op("act", lambda e, q0=q0, nb=nb: e.activation(out=qa[b][64:80, q0 * 128:(q0 + nb) * 128], in_=ps7b[64:80, 544:544 + nb * 128],
                                                                    func=AF.Copy), reads=[PSN[7]], writes=["qa%d_bias" % b])
            return [g1, g2, g3]

        def attention(u, hooks=()):
            kind, h = units[u]
            b = u % 2
            isd = kind == "d"
            dk = 128
            scale = (32.0 ** -0.5) if isd else 0.125
            for gi_, grp in enumerate(groups):
                if gi_ < len(hooks):
                    hooks[gi_]()
                do_group(kind, h, b, isd, dk, scale, (not isd) or len(hooks) == 0, *grp)

        def do_group(kind, h, b, isd, dk, scale, wide, qc0, N, nkt, usepast, t0):
            R = N // 128
            gidx = gcount[0]
            gcount[0] += 1
            nsl = 3 if wide else 2
            if isd and wide:
                maps = [(qa[b], ["qa%d" % b, "qa%d_z" % b, "qa%d_bias" % b], 6), (qb_[b], ["qb%d" % b, "qb%d_z" % b], 7)]
                tbank = 0
            elif isd:
                maps = [(qa[b], ["qa%d" % b, "qa%d_z" % b, "qa%d_bias" % b], 4), (qb_[b], ["qb%d" % b, "qb%d_z" % b], 5)]
                tbank = 6
            elif wide:
                maps = [(qa[b], ["qa%d" % b, "qa%d_bias" % b], 6)]
                tbank = 0
            else:
                maps = [(qa[b], ["qa%d" % b, "qa%d_bias" % b], 4)]
                tbank = 5
            npair = nkt // 2
            steps = [(mi, kp) for mi in range(len(maps)) for kp in range(npair)]
            slot = {}

            def col0_of(kp):
                return 256 if (usepast and kp == npair - 1) else 0

            def qk(i):
                mi, kp = steps[i]
                Q, qn_, _ = maps[mi]
                s_ = sc[0] % nsl
                sc[0] += 1
                slot[i] = s_
                c0_ = col0_of(kp)
                for hfi in range(2):
                    kt = 2 * kp + hfi
                    bank = 2 * s_ + hfi
                    if usepast:
                        di = kt - (nkt - 4)
                    else:
                        di = 0 if kt == nkt - 1 else -1
                    diag = di >= 0
                    P.op("pe", lambda e, kt=kt, bank=bank, diag=diag: e.matmul(PS[bank][:, c0_:N], lhsT=kT[b][0:dk, kt * 128:(kt + 1) * 128],
                                                                             rhs=Q[0:dk, qc0 + c0_:qc0 + N], start=True, stop=not diag),
                         reads=["kT%d" % b, "kT%d_oh" % b] + qn_, writes=[PSN[bank]])
                    if diag:
                        P.op("pe", lambda e, bank=bank, di=di: e.matmul(PS[bank][:, c0_:N], lhsT=ident, rhs=cm[:, di, c0_:N], start=False, stop=True),
                             reads=["ident", "cm"], writes=[PSN[bank]])

            def ex_pv(i):
                mi, kp = steps[i]
                _, _, abank = maps[mi]
                s_ = slot[i]
                c0_ = col0_of(kp)
                src = PSALL[:, 2 * s_ * 512:(2 * s_ + 2) * 512].rearrange("p (b n) -> p b n", n=512)[:, :, c0_:N]
                dst = pT[s_][:, :, c0_:N]
                rd = [PSN[2 * s_], PSN[2 * s_ + 1]]
                if usepast and 2 * kp < 16:
                    P.op("act", lambda e: e.activation(out=dst, in_=src, func=AF.Exp, scale=scale, bias=flg[:, 0:1]),
                         reads=rd + ["flg"], writes=["pT%d" % s_])
                else:
                    P.op("act", lambda e: e.activation(out=dst, in_=src, func=AF.Exp, scale=scale), reads=rd, writes=["pT%d" % s_])
                for hfi in range(2):
                    kt = 2 * kp + hfi
                    P.op("pe", lambda e, kt=kt, hfi=hfi: e.matmul(PS[abank][:, c0_:N], lhsT=vbuf[b][:, kt, :], rhs=pT[s_][:, hfi, c0_:N],
                                                                  start=(kt == 0), stop=(kt == nkt - 1)),
                         reads=["pT%d" % s_, "vb%d_%d" % (b, kt // 8), "vb%d_1" % b], writes=[PSN[abank]])

            lead = nsl - 1
            for i in range(min(lead, len(steps))):
                qk(i)
            for i in range(len(steps)):
                if i + lead < len(steps):
                    qk(i + lead)
                ex_pv(i)
            tpb = PS[tbank].bitcast(BF16)
            for mi, (_, _, abank) in enumerate(maps):
                P.op("dve", lambda e, mi=mi, abank=abank: e.tensor_copy(out=oT[mi][0:65, 0:N], in_=PS[abank][0:65, 0:N]),
                     reads=[PSN[abank]], writes=["oT%d" % mi])
                for r in range(R):
                    c0 = (mi * 4 + r) * 66
                    P.op("pe", lambda e, mi=mi, r=r, c0=c0: e.transpose(out=tpb[:, c0:c0 + 65], in_=oT[mi][0:65, r * 128:(r + 1) * 128],
                                                                      identity=ident[0:65, 0:65]),
                         reads=["oT%d" % mi, "ident"], writes=[PSN[tbank]])
            cn = "C_g%d" % t0
            nt = PSN[tbank]
            fin = tpb
            if wide:
                P.op("dve", lambda e: e.tensor_copy(out=tsb, in_=tpb[:, 0:528]), reads=[PSN[tbank]], writes=["tsb"])
                fin = tsb
                nt = "tsb"
            accA = fin[:, 0:R * 66].rearrange("p (r c) -> p r c", c=66)
            if isd:
                accB = fin[:, 4 * 66:(4 + R) * 66].rearrange("p (r c) -> p r c", c=66)
                P.op("dve", lambda e: e.reciprocal(out=rcp[:, 0:R], in_=accA[:, :, 64]), reads=[nt], writes=["rcpa"])
                P.op("dve", lambda e: e.reciprocal(out=rcp[:, 4:4 + R], in_=accB[:, :, 64]), reads=[nt], writes=["rcpb"])
                P.op("dve", lambda e: e.tensor_tensor(out=oa[:, 0:R, :], in0=accA[:, :, 0:64],
                                                      in1=rcp[:, 0:R].unsqueeze(2).to_broadcast([128, R, 64]), op=ALU.mult),
                     reads=[nt, "rcpa"], writes=["oa"])
                P.op("dve", lambda e: e.tensor_tensor(out=ob[:, 0:R, :], in0=accB[:, :, 0:64],
                                                      in1=rcp[:, 4:4 + R].unsqueeze(2).to_broadcast([128, R, 64]), op=ALU.mult),
                     reads=[nt, "rcpb"], writes=["ob"])
                P.op("dve", lambda e: e.scalar_tensor_tensor(out=dd[:, 0:R, :], in0=ob[:, 0:R, :], scalar=neglam, in1=oa[:, 0:R, :],
                                                             op0=ALU.mult, op1=ALU.add), reads=["oa", "ob", "neglam"], writes=["dd"])
                P.op("pool", lambda e: e.tensor_tensor(out=sqd[:, 0:R, :], in0=dd[:, 0:R, :], in1=dd[:, 0:R, :], op=ALU.mult),
                     reads=["dd"], writes=["sqd"])
                P.op("dve", lambda e: e.tensor_reduce(out=ssq4[:, 0:R], in_=sqd[:, 0:R, :], axis=AX.X, op=ALU.add),
                     reads=["sqd"], writes=["ssq4"])
                rms_rstd(ssq4[:, 0:R], "ssq4", rs4[:, 0:R], "rs4", 64 * EPS)
                P.op("dve", lambda e: e.tensor_tensor(out=dd[:, 0:R, :], in0=dd[:, 0:R, :],
                                                      in1=rs4[:, 0:R].unsqueeze(2).to_broadcast([128, R, 64]), op=ALU.mult),
                     reads=["dd", "rs4"], writes=["dd"])
                P.op("pool", lambda e: e.tensor_tensor(out=C[:, t0:t0 + R, h * 64:(h + 1) * 64], in0=dd[:, 0:R, :], in1=Gsub[:, 0:R, :],
                                                       op=ALU.mult), reads=["dd", "Gsub"], writes=[cn])
            else:
                P.op("dve", lambda e: e.reciprocal(out=rcp[:, 0:R], in_=accA[:, :, 64]), reads=[nt], writes=["rcpa"])
                P.op("dve", lambda e: e.tensor_tensor(out=C[:, t0:t0 + R, 512 + h * 64:512 + (h + 1) * 64], in0=accA[:, :, 0:64],
                                                      in1=rcp[:, 0:R].unsqueeze(2).to_broadcast([128, R, 64]), op=ALU.mult),
                     reads=[nt, "rcpa"], writes=[cn])

        issue_loads(0)
        for u in range(len(units)):
            if u + 1 < len(units):
                issue_loads(u + 1)
            hooks = gating_stages(u + 1) if (u + 1 < len(units) and units[u + 1][0] == "m") else []
            attention(u, hooks)

        CTs = [CT, AR.alloc([8, 128], BF16)]

        def c_pre(tl):
            b2 = tl % 2
            lt_ = 16 + tl if tl < 16 else 15
            cn = "C_g%d" % (tl // 4 * 4 if tl < 16 else 16)
            P.dma("sp", "C_x%d" % b2, lambda e: e.dma_start(out=xr[b2], in_=xin[lt_ * 128:(lt_ + 1) * 128, :]), writes=["C_xr%d" % b2])
            transpose8(C[:, tl, :], cn, CTs[b2], "C_CT%d" % b2, 6 + b2)

        def c_main(tl):
            b2 = tl % 2
            for hh in range(2):
                bank = 2 * b2 + hh
                for k in range(8):
                    P.op("pe", lambda e, k=k, hh=hh, bank=bank: e.matmul(PS[bank], lhsT=CTs[b2][:, k, :], rhs=Wo[:, k, hh * 512:(hh + 1) * 512],
                                                                        start=(k == 0), stop=(k == 7)), reads=["C_CT%d" % b2, wo_names[k]], writes=[PSN[bank]])
                P.op("dve", lambda e, hh=hh, bank=bank: e.tensor_tensor(out=x1[b2][:, hh * 512:(hh + 1) * 512], in0=PS[bank],
                                                                       in1=xr[b2][:, hh * 512:(hh + 1) * 512], op=ALU.add),
                     reads=[PSN[bank], "C_xr%d" % b2], writes=["C_x1%d_%d" % (b2, hh)])
            P.dma("sp", "C_s%d" % b2, lambda e: e.dma_start(out=X1[tl * 128:(tl + 1) * 128, :], in_=x1[b2]),
                  reads=["C_x1%d_0" % b2, "C_x1%d_1" % b2])

        c_pre(0)
        for tl in range(NQT):
            if tl + 1 < NQT:
                c_pre(tl + 1)
            c_main(tl)

    phase_BC()
    barrier()
    if stop_after == "BC":
        P.emit()
        return nc

    def phase_FFN(Xin, Xout, li, ntiles):
        reset_arena()
        gbc = AR.alloc([D], F32)
        xs = [AR.alloc([D], F32) for _ in range(3)]
        xr2 = [AR.alloc([D], F32) for _ in range(2)]
        sq = AR.alloc([D], BF16)
        hbs = [AR.alloc([D], BF16) for _ in range(2)]
        hT = AR.alloc([8, 1152], BF16)
        wo = AR.alloc([22, D], BF16)
        wi = [AR.alloc([8, 256], BF16) for _ in range(3)]
        AT = AR.alloc([22, 1152], BF16)
        sg = [AR.alloc([512], BF16) for _ in range(2)]
        yb = [AR.alloc([D], F32) for _ in range(2)]
        ssqs = [AR.alloc([1], F32) for _ in range(2)]
        rss = [AR.alloc([1], F32) for _ in range(2)]
        pf = "F%d" % li
        load_gain_bc(gbc, pf + "_gbc", ffn_g[li:li + 1, :], pf + "_g", 32.0)
        win_v = f_w_in[li].rearrange("(k p) n -> p k n", p=128)
        wout_v = f_w_out[li].rearrange("(j p) n -> p j n", p=128)
        half = (ntiles + 1) // 2
        passes = [list(range(0, half)), list(range(half, ntiles))]
        cnt = {"pro": 0, "y": 0, "g": 0}

        def load_wi(j):
            b3 = j % 3
            P.dma_group("pool", pf + "_wi%d" % b3, [
                lambda e: e.dma_start(out=wi[b3][:, :, 0:128], in_=win_v[:, :, j * 128:(j + 1) * 128]),
                lambda e: e.dma_start(out=wi[b3][:, :, 128:256], in_=win_v[:, :, DFF + j * 128:DFF + (j + 1) * 128])],
                writes=[pf + "_wi%dg" % b3, pf + "_wi%du" % b3])

        def pro_a(i, tl):
            c = cnt["pro"]
            cnt["pro"] += 1
            x3, h2 = c % 3, c % 2
            xn, hbn, pfx = pf + "_xs%d" % x3, pf + "_hb%d" % h2, pf + "n%d" % h2
            P.dma("sp", pf + "_x%d" % x3, lambda e: e.dma_start(out=xs[x3], in_=Xin[tl * 128:(tl + 1) * 128, :]), writes=[xn])
            norm_tile(xs[x3], xn, gbc, pf + "_gbc", hbs[h2], hbn, sq, pf + "_sq", ssqs[h2], rss[h2], pfx)
            return (hbs[h2], hbn)

        def pro_b(i, hbinfo):
            transpose8(hbinfo[0], hbinfo[1], hT[:, :, i * 128:(i + 1) * 128], pf + "_hT%d" % i, 0)

        for j0 in (0, 11):
            P.dma("pool", pf + "_wo%d" % j0, lambda e, j0=j0: e.dma_start(out=wo[:, j0:j0 + 11, :], in_=wout_v[:, j0:j0 + 11, :]),
                  writes=[pf + "_wo%d" % j0])
        for j in range(3):
            load_wi(j)
        prev_ = None
        for i, tl in enumerate(passes[0]):
            info_ = pro_a(i, tl)
            if prev_ is not None:
                pro_b(*prev_)
            prev_ = (i, info_)
        pro_b(*prev_)
        for pi, tiles in enumerate(passes):
            ntok = len(tiles) * 128
            nsub = (ntok + 511) // 512
            sbw = ((ntok // 128 + nsub - 1) // nsub) * 128
            for j in range(22):
                b3 = j % 3
                if j >= 3:
                    load_wi(j)
                for sb, s0 in enumerate(range(0, ntok, sbw)):
                    n = min(sbw, ntok - s0)
                    gb = cnt["g"] % 2
                    cnt["g"] += 1
                    hnames = [pf + "_hT%d" % i for i in range(s0 // 128, (s0 + n) // 128)]
                    for (col, bank, wn) in ((0, gb, "g"), (128, 2 + gb, "u")):
                        for k in range(8):
                            P.op("pe", lambda e, k=k, col=col, bank=bank, b3=b3, s0=s0, n=n: e.matmul(
                                PS[bank][:, 0:n], lhsT=wi[b3][:, k, col:col + 128], rhs=hT[:, k, s0:s0 + n], start=(k == 0), stop=(k == 7)),
                                reads=hnames + [pf + "_wi%d%s" % (b3, wn)], writes=[PSN[bank]])
                    P.op("act", lambda e, gb=gb, n=n: e.activation(out=sg[gb][:, 0:n], in_=PS[gb][:, 0:n], func=AF.Silu),
                         reads=[PSN[gb]], writes=[pf + "_sg%d" % gb])
                    P.op("dve", lambda e, gb=gb, n=n, j=j, s0=s0: e.tensor_tensor(out=AT[:, j, s0:s0 + n], in0=PS[2 + gb][:, 0:n],
                                                                                in1=sg[gb][:, 0:n], op=ALU.mult),
                         reads=[PSN[2 + gb], pf + "_sg%d" % gb], writes=[pf + "_AT%d_%d" % (sb, j % 2)])
            nxt = passes[pi + 1] if pi + 1 < len(passes) else []
            if nxt:
                for j in range(3):
                    load_wi(j)
            for i, tl in enumerate(tiles):
                hbinfo = pro_a(i, nxt[i]) if i < len(nxt) else None
                yi = cnt["y"] % 2
                cnt["y"] += 1
                P.dma("sp", pf + "_xr%d" % yi, lambda e, tl=tl, yi=yi: e.dma_start(out=xr2[yi], in_=Xin[tl * 128:(tl + 1) * 128, :]),
                      writes=[pf + "_xr%d" % yi])
                for hh in range(2):
                    bank = 4 + (2 * yi + hh)
                    if bank == 7:
                        bank = 3 if False else 7
                    for j in range(22):
                        P.op("pe", lambda e, j=j, i=i, hh=hh, bank=bank: e.matmul(PS[bank], lhsT=AT[:, j, i * 128:(i + 1) * 128],
                                                                               rhs=wo[:, j, hh * 512:(hh + 1) * 512],
                                                                               start=(j == 0), stop=(j == 21)),
                             reads=[pf + "_AT%d_0" % (i * 128 // sbw), pf + "_AT%d_1" % (i * 128 // sbw), pf + "_wo%d" % (0 if j < 11 else 11)],
                             writes=[PSN[bank]])
                    P.op("dve", lambda e, hh=hh, bank=bank, yi=yi: e.tensor_tensor(out=yb[yi][:, hh * 512:(hh + 1) * 512], in0=PS[bank],
                                                                                 in1=xr2[yi][:, hh * 512:(hh + 1) * 512], op=ALU.add),
                         reads=[PSN[bank], pf + "_xr%d" % yi], writes=[pf + "_y%d_%d" % (yi, hh)])
                P.dma("sp", pf + "_ys%d" % yi, lambda e, tl=tl, yi=yi: e.dma_start(out=Xout[tl * 128:(tl + 1) * 128, :], in_=yb[yi]),
                      reads=[pf + "_y%d_0" % yi, pf + "_y%d_1" % yi])
                if hbinfo is not None:
                    pro_b(i, hbinfo)

    phase_FFN(X1, X2, 0, NQT)
    barrier()
    if stop_after == "F0":
        P.emit()
        return nc

    def layer_norm_free(src, src_n, dst, dst_n, g_bc, g_n, b_bc, b_n, tmp, tmp_n, sm, pfx, eng2="pool"):
        P.op("dve", lambda e: e.tensor_reduce(out=sm[:, 0:1], in_=src, axis=AX.X, op=ALU.add), reads=[src_n], writes=[pfx + "_s1"])
        P.op("dve", lambda e: e.tensor_scalar(out=sm[:, 1:2], in0=sm[:, 0:1], scalar1=-1.0 / 512.0, scalar2=None, op0=ALU.mult),
             reads=[pfx + "_s1"], writes=[pfx + "_nm"])
        P.op("dve", lambda e: e.tensor_scalar(out=tmp, in0=src, scalar1=sm[:, 1:2], scalar2=None, op0=ALU.add),
             reads=[src_n, pfx + "_nm"], writes=[tmp_n])
        P.op("act", lambda e: e.activation(out=src, in_=tmp, func=AF.Square, accum_out=sm[:, 2:3]), reads=[tmp_n], writes=[src_n, pfx + "_ss"])
        P.op("act", lambda e: e.activation(out=sm[:, 3:4], in_=sm[:, 2:3], func=AF.Ln, bias=float(EPS), scale=1.0 / 512.0),
             reads=[pfx + "_ss"], writes=[pfx + "_rs"])
        P.op("act", lambda e: e.activation(out=sm[:, 3:4], in_=sm[:, 3:4], func=AF.Exp, scale=-0.5), reads=[pfx + "_rs"], writes=[pfx + "_rs"])
        P.op("dve", lambda e: e.scalar_tensor_tensor(out=tmp, in0=tmp, scalar=sm[:, 3:4], in1=g_bc, op0=ALU.mult, op1=ALU.mult),
             reads=[tmp_n, pfx + "_rs", g_n], writes=[tmp_n])
        P.op(eng2, lambda e: e.tensor_tensor(out=dst, in0=tmp, in1=b_bc, op=ALU.add), reads=[tmp_n, b_n], writes=[dst_n])

    def phase_D():
        reset_arena()
        Wi1 = AR.alloc([8, 2048], BF16)
        Wo1 = AR.alloc([8, D], BF16)
        Dg = AR.alloc([31, 4, 128], BF16)
        wsf = AR.alloc([8, 128], F32)
        wsT = AR.alloc([8, 128], BF16)
        cbuf = AR.alloc([4, 2080], BF16)
        gbc = AR.alloc([D], F32)
        brow = AR.alloc([1024], F32)
        glgb = AR.alloc([1024], F32)
        cvv = AR.alloc([1536], F32)
        bsT = AR.alloc([8], F32)
        obc = AR.alloc([8], F32)
        cw = AR.alloc([4, 31], F32)
        xt = [AR.alloc([D], F32) for _ in range(3)]
        sq = AR.alloc([D], BF16)
        NZ = 2
        S_ = []
        for z in range(NZ):
            S_.append(dict(
                hb=AR.alloc([D], BF16), hT=AR.alloc([8, 128], BF16), sig=AR.alloc([4, 128], F32), ctmp=AR.alloc([128], F32),
                t0u=AR.alloc([512], F32), t0v=AR.alloc([512], F32), w1u=AR.alloc([512], F32), w1v=AR.alloc([512], F32),
                w2=AR.alloc([512], F32), w3=AR.alloc([512], F32), gvn=AR.alloc([512], BF16), CC=AR.alloc([D], BF16),
                CCT=AR.alloc([8, 128], BF16), sm=AR.alloc([8], F32), sm2=AR.alloc([8], F32), ssq=AR.alloc([1], F32), rs=AR.alloc([1], F32)))
        yb = [AR.alloc([D], F32) for _ in range(2)]

        load_w_bf16(Wi1, "D_Wi", o_w_in, "D_wi", nsplit=4)
        wi_n = ["D_Wi_%d" % (k // 2 * 2) for k in range(8)]
        load_w_bf16(Wo1, "D_Wo", o_w_out, "D_wo", nsplit=2)
        wo_n = ["D_Wo_%d" % (k // 4 * 4) for k in range(8)]
        load_gain_bc(gbc, "D_gbc", attn_g[1:2, :], "D_g", 32.0)
        P.dma_group("sp", "D_c", [
            lambda e: e.dma_start(out=brow, in_=o_b_row[:, 0:1024].partition_broadcast(128)),
            lambda e: e.dma_start(out=glgb, in_=gl_gb.partition_broadcast(128)),
            lambda e: e.dma_start(out=cvv, in_=cvec.partition_broadcast(128)),
            lambda e: e.dma_start(out=bsT, in_=bsT_in),
            lambda e: e.dma_start(out=obc, in_=o_b_col),
            lambda e: e.dma_start(out=cw, in_=cwT),
            lambda e: e.dma_start(out=wsf, in_=wsT_in)],
            writes=["D_brow", "D_glgb", "D_cvv", "D_bsT", "D_obc", "D_cw", "D_wsf"])
        P.op("pool", lambda e: e.affine_select(out=wsf, in_=wsf, pattern=[[0, 8], [1, 128]], compare_op=ALU.is_ge, fill=0.0,
                                               base=0, channel_multiplier=-1), reads=["D_wsf"], writes=["D_wsf"])
        P.op("pool", lambda e: e.tensor_copy(out=wsT, in_=wsf), reads=["D_wsf"], writes=["D_wsT"])
        for c in range(4):
            P.op("dve" if c % 2 == 0 else "pool", lambda e, c=c: e.tensor_tensor(
                out=Dg[:, :, c, :], in0=ident.unsqueeze(1).to_broadcast([128, 31, 128]),
                in1=cw[:, c, :].unsqueeze(2).to_broadcast([128, 31, 128]), op=ALU.mult), reads=["ident", "D_cw"], writes=["D_Dg%d" % c])

        def gelu(ps, psn, bias_bc, t0, t0n, w1, w1n, dst, dstn):
            P.op("dve", lambda e: e.tensor_tensor(out=t0, in0=ps, in1=bias_bc, op=ALU.add), reads=[psn, "D_brow"], writes=[t0n])
            P.op("act", lambda e: e.activation(out=w1, in_=t0, func=AF.Square), reads=[t0n], writes=[w1n])
            P.op("dve", lambda e: e.tensor_scalar(out=w1, in0=w1, scalar1=0.044715, scalar2=1.0, op0=ALU.mult, op1=ALU.add),
                 reads=[w1n], writes=[w1n])
            P.op("pool", lambda e: e.tensor_tensor(out=w1, in0=w1, in1=t0, op=ALU.mult), reads=[w1n, t0n], writes=[w1n])
            P.op("act", lambda e: e.activation(out=w1, in_=w1, func=AF.Sigmoid, scale=1.5957691216057308), reads=[w1n], writes=[w1n])
            P.op("pool", lambda e: e.tensor_tensor(out=dst, in0=w1, in1=t0, op=ALU.mult), reads=[w1n, t0n], writes=[dstn])

        def tile_stages(it, tl):
            b2 = it % 3
            z = it % NZ
            B = S_[z]
            N_ = lambda nme: "D_%s%d" % (nme, z)
            hb, hT, sig, ctmp, t0u, t0v, w1u, w1v, w2, w3, gvn, CC, CCT, sm, sm2 = (B[k] for k in (
                "hb", "hT", "sig", "ctmp", "t0u", "t0v", "w1u", "w1v", "w2", "w3", "gvn", "CC", "CCT", "sm", "sm2"))
            halo = tl == 16
            xn = "D_xt%d" % b2
            yi = it % 2

            def f1():
                P.dma("sp", "D_x%d" % b2, lambda e: e.dma_start(out=xt[b2], in_=X2[tl * 128:(tl + 1) * 128, :]), writes=[xn])

            def f2():
                P.op("act", lambda e: e.activation(out=sq, in_=xt[b2], func=AF.Square, accum_out=B["ssq"]), reads=[xn], writes=["D_sq", N_("n") + "_ssq"])
                rms_rstd(B["ssq"], N_("n") + "_ssq", B["rs"], N_("n") + "_rs", D * EPS)

            def f3():
                P.op("dve", lambda e: e.scalar_tensor_tensor(out=hb, in0=xt[b2], scalar=B["rs"], in1=gbc, op0=ALU.mult, op1=ALU.mult),
                     reads=[xn, N_("n") + "_rs", "D_gbc"], writes=[N_("hb")])

            def f4():
                transpose8(hb, N_("hb"), hT, N_("hT"), 0)

            def f5():
                for (base, bank) in ((1024, 1), (1536, 2)):
                    for c in range(4):
                        for k in range(8):
                            P.op("pe", lambda e, base=base, bank=bank, c=c, k=k: e.matmul(
                                PS[bank][:, c * 128:(c + 1) * 128], lhsT=Wi1[:, k, base + c * 128:base + (c + 1) * 128], rhs=hT[:, k, :],
                                start=(k == 0), stop=(k == 7)), reads=[N_("hT"), wi_n[k]], writes=[PSN[bank]])

            def f6():
                for (base, bank) in ((0, 3), (512, 4)):
                    for k in range(8):
                        P.op("pe", lambda e, base=base, bank=bank, k=k: e.matmul(PS[bank], lhsT=hT[:, k, :], rhs=Wi1[:, k, base:base + 512],
                                                                              start=(k == 0), stop=(k == 7)),
                             reads=[N_("hT"), wi_n[k]], writes=[PSN[bank]])

            def f7():
                for c in range(4):
                    P.op("act", lambda e, c=c: e.activation(out=sig[:, c, :], in_=PS[2][:, c * 128:(c + 1) * 128], func=AF.Sigmoid,
                                                            bias=obc[:, 4 + c:5 + c]), reads=[PSN[2], "D_obc"], writes=[N_("sig") + "_%d" % c])
                    if halo:
                        P.op("dve", lambda e, c=c: e.scalar_tensor_tensor(out=ctmp, in0=PS[1][:, c * 128:(c + 1) * 128], scalar=obc[:, c:c + 1],
                                                                          in1=sig[:, c, :], op0=ALU.add, op1=ALU.mult),
                             reads=[PSN[1], "D_obc", N_("sig") + "_%d" % c], writes=[N_("ctmp")])
                        P.op("dve", lambda e, c=c: e.tensor_scalar(out=cbuf[:, c, 0:32], in0=ctmp[:, 96:128], scalar1=flg[:, 1:2], scalar2=None,
                                                                   op0=ALU.mult), reads=[N_("ctmp"), "flg"], writes=["D_cb_h%d" % c])
                    else:
                        P.op("dve", lambda e, c=c: e.scalar_tensor_tensor(out=cbuf[:, c, 32 + tl * 128:32 + (tl + 1) * 128],
                                                                          in0=PS[1][:, c * 128:(c + 1) * 128], scalar=obc[:, c:c + 1],
                                                                          in1=sig[:, c, :], op0=ALU.add, op1=ALU.mult),
                             reads=[PSN[1], "D_obc", N_("sig") + "_%d" % c], writes=["D_cb_%d_%d" % (tl, c)])
            if halo:
                return [f1, f2, f3, f4, f5, f7], []

            def f8():
                gelu(PS[3], PSN[3], brow[:, 0:512], t0u, N_("t0u"), w1u, N_("w1u"), w2, N_("w2"))

            def f9():
                gelu(PS[4], PSN[4], brow[:, 512:1024], t0v, N_("t0v"), w1v, N_("w1v"), w3, N_("w3"))

            def f10():
                layer_norm_free(w3, N_("w3"), gvn, N_("gvn"), glgb[:, 0:512], "D_glgb", glgb[:, 512:1024], "D_glgb", t0v, N_("t0v"), sm, N_("ln1"))

            def f11():
                for g in range(8):
                    P.op("pe", lambda e, g=g: e.matmul(PS[0][:, g * 64:(g + 1) * 64], lhsT=wsT[:, g, :], rhs=gvn[:, g * 64:(g + 1) * 64],
                                                       start=True, stop=True), reads=["D_wsT", N_("gvn")], writes=[PSN[0]])
                P.op("dve", lambda e: e.tensor_tensor(out=t0u.rearrange("p (g d) -> p g d", d=64), in0=PS[0].rearrange("p (g d) -> p g d", d=64),
                                                      in1=bsT[:, 0:8].unsqueeze(2).to_broadcast([128, 8, 64]), op=ALU.add),
                     reads=[PSN[0], "D_bsT"], writes=[N_("t0u")])
                P.op("pool", lambda e: e.tensor_tensor(out=CC[:, 0:512], in0=w2, in1=t0u, op=ALU.mult), reads=[N_("w2"), N_("t0u")], writes=[N_("CCa")])

            def s1():
                for c in range(4):
                    rn = ["D_cb_%d_%d" % (tl, c), ("D_cb_%d_%d" % (tl - 1, c)) if tl > 0 else ("D_cb_h%d" % c)]
                    for j in range(31):
                        o = tl * 128 + 2 + j
                        P.op("pe", lambda e, c=c, j=j, o=o: e.matmul(PS[5][:, c * 128:(c + 1) * 128], lhsT=cbuf[:, c, o:o + 128], rhs=Dg[:, j, c, :],
                                                                  start=(j == 0), stop=(j == 30)), reads=rn + ["D_Dg%d" % c], writes=[PSN[5]])
                P.op("dve", lambda e: e.tensor_tensor(out=w3, in0=PS[5], in1=cvv[:, 0:512], op=ALU.add), reads=[PSN[5], "D_cvv"], writes=[N_("w3")])

            def s2():
                layer_norm_free(w3, N_("w3"), w2, N_("w2"), cvv[:, 512:1024], "D_cvv", cvv[:, 1024:1536], "D_cvv", t0v, N_("t0v"), sm2, N_("ln2"))
                P.op("act", lambda e: e.activation(out=CC[:, 512:1024], in_=w2, func=AF.Silu), reads=[N_("w2")], writes=[N_("CCb")])

            def s3():
                psb = PS[6].bitcast(BF16)
                for k in range(8):
                    P.op("pe", lambda e, k=k: e.transpose(out=psb[:, k * 128:(k + 1) * 128], in_=CC[:, k * 128:(k + 1) * 128], identity=ident),
                         reads=[N_("CCa"), N_("CCb"), "ident"], writes=[PSN[6]])
                P.op("act", lambda e: e.activation(out=CCT, in_=psb[:, 0:1024].rearrange("p (a b) -> p a b", b=128), func=AF.Copy),
                     reads=[PSN[6]], writes=[N_("CCT")])

            def s4():
                for hh in range(2):
                    bank = 7 if hh == 0 else 5
                    for k in range(8):
                        P.op("pe", lambda e, k=k, hh=hh, bank=bank: e.matmul(PS[bank], lhsT=CCT[:, k, :], rhs=Wo1[:, k, hh * 512:(hh + 1) * 512],
                                                                            start=(k == 0), stop=(k == 7)), reads=[N_("CCT"), wo_n[k]], writes=[PSN[bank]])
                    P.op("dve", lambda e, hh=hh, bank=bank: e.tensor_tensor(out=yb[yi][:, hh * 512:(hh + 1) * 512], in0=PS[bank],
                                                                           in1=xt[b2][:, hh * 512:(hh + 1) * 512], op=ALU.add),
                         reads=[PSN[bank], xn], writes=["D_y%d_%d" % (yi, hh)])
                P.dma("sp", "D_ys%d" % yi, lambda e: e.dma_start(out=X3[tl * 128:(tl + 1) * 128, :], in_=yb[yi]),
                      reads=["D_y%d_0" % yi, "D_y%d_1" % yi])
            return [f1, f2, f3, f4, f5, f6, f7, f8, f9, f10, f11], [s1, s2, s3, s4]

        def coalesce(lst):
            out_, run = [], []
            for it_ in lst:
                if it_[0] == "op" and it_[1] == "pe":
                    run.append(it_)
                else:
                    if run:
                        out_.append(("macro", run))
                        run = []
                    out_.append(it_)
            if run:
                out_.append(("macro", run))
            return out_

        def round_robin(chains):
            chains = [c for c in chains if c]
            idx = [0] * len(chains)
            while True:
                progressed = False
                for ci, c in enumerate(chains):
                    if idx[ci] < len(c):
                        P.replay(c[idx[ci]])
                        idx[ci] += 1
                        progressed = True
                if not progressed:
                    break

        order = [16] + list(range(16))
        stages = {}

        def get(it):
            if it not in stages and 0 <= it < len(order):
                stages[it] = tile_stages(it, order[it])
            return stages.get(it)

        F0_, _ = get(0)
        for f in F0_[0:4]:
            f()
        for it, tl in enumerate(order):
            F, Sn = get(it)
            halo = tl == 16
            F[4]()
            if not halo:
                F[5]()
            chains = []
            fc = F[5] if halo else F[6]
            chains.append(coalesce(P.capture(fc)))
            if not halo:
                chains.append(coalesce(P.capture(F[7])))
                chains.append(coalesce(P.capture(lambda: (F[8](), F[9](), F[10]()))))
            if it >= 1 and stages[it - 1][1]:
                Sp = stages[it - 1][1]
                chains.append(coalesce(P.capture(lambda: [st() for st in Sp])))
            nxt = get(it + 1)
            if nxt is not None:
                Fn = nxt[0]
                chains.append(coalesce(P.capture(lambda: [f() for f in Fn[0:4]])))
            round_robin(chains)
        lastS = stages[len(order) - 1][1]
        for st in lastS:
            st()

    phase_D()
    barrier()
    if stop_after == "D":
        P.emit()
        return nc

    phase_FFN(X3, out, 1, 16)
    barrier()
    P.emit()
    return nc


def _bf16(a):
    import ml_dtypes
    return np.asarray(a, np.float32).astype(ml_dtypes.bfloat16)


def _rope_tables(dim, pos):
    inv = 1.0 / (10000.0 ** (np.arange(0, dim, 2, dtype=np.float32) / dim))
    ang = pos.astype(np.float32)[:, None] * inv[None, :]
    return np.cos(ang).astype(np.float32), np.sin(ang).astype(np.float32)


def host_inputs(inp, core):
    b, hf = core // 2, core % 2
    x = np.asarray(inp["x"], np.float32)
    xin = np.concatenate([x[b, 0:2048], x[b, hf * 2048:(hf + 1) * 2048]], 0)
    pos = np.concatenate([np.arange(2048), hf * 2048 + np.arange(2048)])
    c64, s64 = _rope_tables(64, pos)
    c32, s32 = _rope_tables(32, pos)
    flags = np.zeros((128, 20), np.float32)
    flags[:, 0] = 0.0 if hf == 1 else -30000.0
    flags[:, 1] = 1.0 if hf == 1 else 0.0
    flags[:, 4:12] = 0.0 if hf == 1 else -1e30
    onehot = np.zeros((16, 4096), np.float32)
    for n in range(16):
        onehot[n, n * 256:(n + 1) * 256] = 1.0
    g = lambda k: np.asarray(inp[k], np.float32)
    gains = np.concatenate([g("diff_q_norm_g")[0], g("diff_k_norm_g")[0], g("diff_subln_g")[0], g("moba_q_norm_g")[0],
                            g("moba_k_norm_g")[0], np.zeros(64, np.float32)])[None, :]
    lams = np.concatenate([g("diff_lambda_q1")[0], g("diff_lambda_k1")[0], g("diff_lambda_q2")[0], g("diff_lambda_k2")[0]])[None, :]
    ob = g("odd_b_in")[0]
    m = {
        "xin": xin, "cs64": np.concatenate([c64, s64], 1), "cs32": np.concatenate([c32, s32], 1),
        "flags": flags, "onehot": _bf16(onehot),
        "even_w_in": g("even_w_in")[0], "even_w_out": g("even_w_out")[0],
        "ffn_w_in": g("ffn_w_in"), "ffn_w_out": g("ffn_w_out"),
        "odd_w_in": g("odd_w_in")[0], "odd_w_out": g("odd_w_out")[0],
        "attn_norm_g": g("attn_norm_g"), "ffn_norm_g": g("ffn_norm_g"),
        "gains": gains.astype(np.float32), "lams": lams.astype(np.float32),
        "odd_b_row": ob[None, :].copy(),
        "odd_b_col": np.ascontiguousarray(ob[1024:2048].reshape(8, 128).T),
        "gmlp_ln_gb": np.concatenate([g("gmlp_ln_g")[0], g("gmlp_ln_b")[0]])[None, :],
        "gmlp_wsT": np.ascontiguousarray(g("gmlp_w_s")[0].transpose(2, 0, 1)),
        "gmlp_bsT": np.ascontiguousarray(g("gmlp_b_s")[0].T),
        "conv_wT": np.ascontiguousarray(g("conv_w")[0].T.reshape(4, 128, 31).transpose(1, 0, 2)),
        "conv_vecs": np.concatenate([g("conv_b")[0], g("conv_ln_g")[0], g("conv_ln_b")[0]])[None, :],
    }
    return {k: np.ascontiguousarray(v) for k, v in m.items()}


def kernel(**inputs):
    in_maps = [host_inputs(inputs, c) for c in range(8)]
    nc = build_program()
    res = run_bass_kernel_spmd(nc, in_maps, core_ids=list(range(8)))
    out = np.zeros((4, 4096, 1024), np.float32)
    for c in range(8):
        b, hf = c // 2, c % 2
        out[b, hf * 2048:(hf + 1) * 2048] = np.asarray(res.results[c]["out"], np.float32)
    return out
```

```python
import numpy as np
import concourse.bass as bass
import concourse.mybir as mybir
from concourse.bass_utils import run_bass_kernel_spmd

F32 = mybir.dt.float32
BF16 = mybir.dt.bfloat16
AF = mybir.ActivationFunctionType
ALU = mybir.AluOpType
AX = mybir.AxisListType

ENGS = ["pe", "act", "dve", "pool", "sp"]
EPOCH = 30000


class Prog:
    def __init__(self, nc):
        self.nc = nc
        self.ops = {e: [] for e in ENGS}
        self.count = {e: 0 for e in ENGS}
        self.seen = {e: {} for e in ENGS}
        self.res = {}
        self.lanes = {}
        self.semkeys = []
        self.lane_waited = {}
        self.lane_sem = {}
        self.free_sems = []
        self.nsem = 0
        self._cap = None

    def _deps(self, eng, reads, writes):
        deps = []
        for r in reads:
            st = self.res.get(r)
            if st and st[0] is not None:
                deps.append(st[0])
        for w in writes:
            st = self.res.get(w)
            if st:
                if st[0] is not None:
                    deps.append(st[0])
                deps.extend(st[1].values())
        need = {}
        for (k, v) in deps:
            if eng == "pe" and k == ("pe", "raw"):
                continue
            if self.seen[eng].get(k, 0) < v:
                need[k] = max(need.get(k, 0), v)
        for k, v in need.items():
            self.ops[eng].append(("wait", k, v))
            self.seen[eng][k] = v
            if k[0] == "dma":
                self.lane_waited[k[1]] = max(self.lane_waited.get(k[1], 0), v)

    def _mark(self, tok, reads, writes, rkey):
        for r in reads:
            st = self.res.setdefault(r, [None, {}])
            st[1][rkey] = tok
        for w in writes:
            self.res[w] = [tok, {}]

    def capture(self, f):
        old = self._cap
        self._cap = []
        f()
        lst = self._cap
        self._cap = old
        return lst

    def replay(self, item):
        if item[0] == "op":
            self.op(*item[1:])
        elif item[0] == "dmag":
            self.dma_group(*item[1:])
        else:
            for it in item[1]:
                self.replay(it)

    def op(self, eng, fn, reads=(), writes=()):
        if self._cap is not None:
            self._cap.append(("op", eng, fn, list(reads), list(writes)))
            return None
        self._deps(eng, reads, writes)
        self.count[eng] += 1
        c = self.count[eng]
        key = (eng, "raw")
        tok = (key, c)
        self.ops[eng].append(("op", fn, key, 1, c))
        self._mark(tok, reads, writes, eng)
        return tok

    def dma(self, eng, lane, fn, reads=(), writes=(), multi=False):
        return self.dma_group(eng, lane, [fn], reads, writes)

    def dma_group(self, eng, lane, fns, reads=(), writes=()):
        if self._cap is not None:
            self._cap.append(("dmag", eng, lane, list(fns), list(reads), list(writes)))
            return None
        self._deps(eng, reads, writes)
        if lane not in self.lane_sem:
            self.lane_sem[lane] = (self.nsem, 0)
            self.nsem += 1
        semid, base = self.lane_sem[lane]
        n = self.lanes.get(lane, 0) + len(fns)
        self.lanes[lane] = n
        key = ("dma", semid)
        if key not in self.semkeys:
            self.semkeys.append(key)
        tok = (key, base + 16 * n)
        for fn in fns:
            self.ops[eng].append(("op", fn, key, 16, None))
        self._mark(tok, reads, writes, ("dma", lane))
        return tok

    def retire_lanes(self):
        toks = []
        for lane, n in self.lanes.items():
            semid, base = self.lane_sem[lane]
            toks.append((("dma", semid), base + 16 * n))
        return toks

    def recycle_lanes(self):
        pass

    def wait_all(self, eng, toks):
        for (k, v) in toks:
            if self.seen[eng].get(k, 0) < v:
                self.ops[eng].append(("wait", k, v))
                self.seen[eng][k] = v
                if k[0] == "dma":
                    self.lane_waited[k[1]] = max(self.lane_waited.get(k[1], 0), v)

    def emit(self):
        nc = self.nc
        from contextlib import ExitStack
        waited = {e: set() for e in ENGS}
        for name in ENGS:
            for it in self.ops[name]:
                if it[0] == "wait" and it[1][1] == "raw":
                    waited[it[1][0]].add(it[2])
        rank = {}
        for e_ in ENGS:
            for i, c in enumerate(sorted(waited[e_])):
                rank[(e_, c)] = i + 1
        semkeys = list(self.semkeys)
        for (e_, c), r in rank.items():
            k = (e_, (r - 1) // EPOCH)
            if k not in semkeys:
                semkeys.append(k)
        with ExitStack() as es:
            sems = {}
            for k in semkeys:
                nm = "s_" + "_".join(str(x) for x in k)
                sems[k] = es.enter_context(nc.semaphore(nm))
            block = es.enter_context(nc.Block())

            def run(e, name):
                for it in self.ops[name]:
                    if it[0] == "wait":
                        if it[1][1] == "raw":
                            r = rank[(it[1][0], it[2])]
                            e.wait_ge(sems[(it[1][0], (r - 1) // EPOCH)], (r - 1) % EPOCH + 1)
                        else:
                            e.wait_ge(sems[it[1]], it[2])
                    else:
                        ins = it[1](e)
                        if it[4] is None:
                            ins.then_inc(sems[it[2]], it[3])
                        else:
                            r = rank.get((name, it[4]))
                            if r is not None:
                                ins.then_inc(sems[(name, (r - 1) // EPOCH)], 1)

            @block.tensor
            def _(e):
                run(e, "pe")

            @block.scalar
            def _(e):
                run(e, "act")

            @block.vector
            def _(e):
                run(e, "dve")

            @block.gpsimd
            def _(e):
                run(e, "pool")

            @block.sync
            def _(e):
                run(e, "sp")


NQT = 17
NQ = NQT * 128
NKT = 32
NK = 4096
D = 1024
DFF = 2816
EPS = 1e-6
NEG = -30000.0
LAMBDA_INIT0 = 0.8 - 0.6 * 1.0


class Arena:
    def __init__(self, ap, nbytes):
        self.ap = ap
        self.cap = nbytes
        self.off = 0

    def reset(self):
        self.off = 0

    def alloc(self, shape_free, dt):
        n = 1
        for s in shape_free:
            n *= s
        esz = 4 if dt == F32 else 2
        self.off = (self.off + 63) // 64 * 64
        nb = n * esz
        assert self.off + nb <= self.cap, ("arena overflow", self.off, nb, self.cap)
        v = self.ap[:, self.off // 2:(self.off + nb) // 2]
        self.off += nb
        if dt == F32:
            v = v.bitcast(F32)
        if len(shape_free) == 2:
            v = v.rearrange("p (a b) -> p a b", b=shape_free[1])
        elif len(shape_free) == 3:
            v = v.rearrange("p (a b c) -> p a b c", b=shape_free[1], c=shape_free[2])
        return v


def build_program(stop_after=None, debug=False):
    nc = bass.Bass("TRN2", target_bir_lowering=False)
    dbg_kind = "ExternalOutput" if debug else "Internal"

    def din(name, shape, dt=F32):
        return nc.dram_tensor(name, list(shape), dt, kind="ExternalInput").ap()

    def dscr(name, shape, dt):
        return nc.dram_tensor(name, list(shape), dt, kind=dbg_kind).ap()

    xin = din("xin", [NK, D])
    cs64 = din("cs64", [NK, 64])
    cs32 = din("cs32", [NK, 32])
    flags = din("flags", [128, 20])
    onehot = din("onehot", [16, NK], BF16)
    e_w_in = din("even_w_in", [D, 3072])
    e_w_out = din("even_w_out", [D, D])
    f_w_in = din("ffn_w_in", [2, D, 2 * DFF])
    f_w_out = din("ffn_w_out", [2, DFF, D])
    o_w_in = din("odd_w_in", [D, 2048])
    o_w_out = din("odd_w_out", [D, D])
    attn_g = din("attn_norm_g", [2, D])
    ffn_g = din("ffn_norm_g", [2, D])
    gains = din("gains", [1, 320])
    lams = din("lams", [1, 128])
    o_b_row = din("odd_b_row", [1, 2048])
    o_b_col = din("odd_b_col", [128, 8])
    gl_gb = din("gmlp_ln_gb", [1, 1024])
    wsT_in = din("gmlp_wsT", [128, 8, 128])
    bsT_in = din("gmlp_bsT", [128, 8])
    cwT = din("conv_wT", [128, 4, 31])
    cvec = din("conv_vecs", [1, 1536])
    out = nc.dram_tensor("out", [2048, D], F32, kind="ExternalOutput").ap()

    QdT = dscr("QdT", [512, NQ], BF16)
    KdT = dscr("KdT", [512, NK], BF16)
    Vd = dscr("Vd", [8, NK, 64], BF16)
    QmT = dscr("QmT", [512, NQ], BF16)
    KmT = dscr("KmT", [512, NK], BF16)
    Vm = dscr("Vm", [8, NK, 64], BF16)
    X1 = dscr("X1", [NQ, D], F32)
    X2 = dscr("X2", [NQ, D], F32)
    X3 = dscr("X3", [2048, D], F32)

    ARENA_BYTES = 190 * 1024
    arena_t = nc.alloc_sbuf_tensor("arena", [128, ARENA_BYTES // 2], BF16)
    AR = Arena(arena_t.ap(), ARENA_BYTES)
    PSALL = nc.alloc_psum_tensor("psall", [128, 4096], F32).ap()
    PS = [PSALL[:, i * 512:(i + 1) * 512] for i in range(8)]
    PSN = ["ps%d" % i for i in range(8)]

    P = Prog(nc)
    uid = [0]

    def nm(s):
        uid[0] += 1
        return "%s#%d" % (s, uid[0])

    lane_ctr = [0]

    def newlane(s="l"):
        lane_ctr[0] += 1
        return "%s%d" % (s, lane_ctr[0])

    def barrier():
        toks = []
        for e in ENGS:
            c = P.count[e]
            if c > 0:
                toks.append(((e, "raw"), c))
        toks.extend(P.retire_lanes())
        for e in ENGS:
            P.wait_all(e, toks)
        P.res.clear()
        P.recycle_lanes()

    ident = AR.alloc([128], BF16)
    flg = AR.alloc([20], F32)
    rnd = AR.alloc([8], F32)
    PERSIST = None

    P.op("pool", lambda e: e.memset(ident, 1.0), writes=["ident"])
    P.op("pool", lambda e: e.affine_select(out=ident, in_=ident, pattern=[[-1, 128]], compare_op=ALU.is_equal,
                                           fill=0.0, base=0, channel_multiplier=1), reads=["ident"], writes=["ident"])
    P.dma("sp", "c_flg", lambda e: e.dma_start(out=flg, in_=flags), writes=["flg"])
    AR.off = (AR.off + 63) // 64 * 64
    PERSIST = AR.off

    def reset_arena():
        AR.off = PERSIST

    def rms_rstd(ssq, ssq_name, rs, rs_name, hd_eps):
        P.op("act", lambda e: e.activation(out=rs, in_=ssq, func=AF.Ln, bias=float(hd_eps), scale=1.0),
             reads=[ssq_name], writes=[rs_name])
        P.op("act", lambda e: e.activation(out=rs, in_=rs, func=AF.Exp, scale=-0.5),
             reads=[rs_name], writes=[rs_name])

    def named(ap, name):
        return ap

    def norm_tile(xt, xt_n, gbc, gbc_n, hb, hb_n, sq, sq_n, ssq, rs, pfx):
        P.op("act", lambda e: e.activation(out=sq, in_=xt, func=AF.Square, accum_out=ssq),
             reads=[xt_n], writes=[sq_n, pfx + "_ssq"])
        rms_rstd(ssq, pfx + "_ssq", rs, pfx + "_rs", D * EPS)
        P.op("dve", lambda e: e.scalar_tensor_tensor(out=hb, in0=xt, scalar=rs, in1=gbc, op0=ALU.mult, op1=ALU.mult),
             reads=[xt_n, pfx + "_rs", gbc_n], writes=[hb_n])

    def transpose8(src, src_n, dst, dst_n, psi, nblk=8, evac="act"):
        psb = PS[psi].bitcast(BF16)
        for k in range(nblk):
            P.op("pe", lambda e, k=k: e.transpose(out=psb[:, k * 128:(k + 1) * 128], in_=src[:, k * 128:(k + 1) * 128],
                                                  identity=ident), reads=[src_n, "ident"], writes=[PSN[psi]])
        if evac == "act":
            P.op("act", lambda e: e.activation(out=dst, in_=psb[:, 0:nblk * 128].rearrange("p (a b) -> p a b", b=128),
                                               func=AF.Copy), reads=[PSN[psi]], writes=[dst_n])
        else:
            P.op("dve", lambda e: e.tensor_copy(out=dst, in_=psb[:, 0:nblk * 128].rearrange("p (a b) -> p a b", b=128)),
                 reads=[PSN[psi]], writes=[dst_n])

    def load_gain_bc(dst, dst_n, src_row, lane, scale):
        P.dma("sp", lane, lambda e: e.dma_start(out=dst, in_=src_row.partition_broadcast(128)), writes=[dst_n])
        if scale != 1.0:
            P.op("dve", lambda e: e.tensor_scalar(out=dst, in0=dst, scalar1=float(scale), scalar2=None, op0=ALU.mult),
                 reads=[dst_n], writes=[dst_n])

    def load_w_bf16(dst, dst_n, w_ap, lane, nsplit=4):
        K = dst.shape[1]
        wv = w_ap.rearrange("(k p) n -> p k n", p=128)
        step = max(1, K // nsplit)
        toks = []
        for k0 in range(0, K, step):
            k1 = min(K, k0 + step)
            toks.append(P.dma("pool", "%s_%d" % (lane, k0), lambda e, k0=k0, k1=k1: e.dma_start(out=dst[:, k0:k1, :], in_=wv[:, k0:k1, :]),
                              writes=[dst_n + "_%d" % k0], multi=True))
        return toks

    def phase_A():
        reset_arena()
        Wi = AR.alloc([8, 3072], BF16)
        gbc = AR.alloc([D], F32)
        Gq = AR.alloc([4, 512], F32)
        gsm = AR.alloc([320], F32)
        NC_ = 4
        cst = [AR.alloc([96], F32) for _ in range(NC_)]
        gct = [AR.alloc([4, 2, 64], F32) for _ in range(NC_)]
        xt = [AR.alloc([D], F32) for _ in range(NC_)]
        sq = AR.alloc([D], BF16)
        hbs = [AR.alloc([D], BF16) for _ in range(2)]
        hTs = [AR.alloc([8, 128], BF16) for _ in range(2)]
        ssqs = [AR.alloc([1], F32) for _ in range(2)]
        rss = [AR.alloc([1], F32) for _ in range(2)]
        NS = 4
        sqgs = [AR.alloc([512], F32) for _ in range(NS)]
        qns = [AR.alloc([512], F32) for _ in range(NS)]
        tas = [AR.alloc([512], F32) for _ in range(NS)]
        tbs = [AR.alloc([512], F32) for _ in range(NS)]
        ssqhs = [AR.alloc([16], F32) for _ in range(NS)]
        rshs = [AR.alloc([16], F32) for _ in range(NS)]
        qbs = [AR.alloc([512], BF16) for _ in range(NS)]
        qTs = [AR.alloc([4, 128], BF16) for _ in range(NS)]
        vbs = [AR.alloc([512], BF16) for _ in range(NS)]

        wiv = e_w_in.rearrange("(k p) n -> p k n", p=128)
        for g_ in (1, 2, 4, 5, 0, 3):
            P.dma("pool", "A_w_g%d" % g_, lambda e, g_=g_: e.dma_start(out=Wi[:, :, g_ * 512:(g_ + 1) * 512], in_=wiv[:, :, g_ * 512:(g_ + 1) * 512]),
                  writes=["A_Wi_g%d" % g_])
        load_gain_bc(gbc, "A_gbc", attn_g[0:1, :], "A_g", 32.0)
        P.dma("sp", "A_gs", lambda e: e.dma_start(out=gsm, in_=gains.partition_broadcast(128)), writes=["A_gsm"])
        for gi, (o, hd) in enumerate([(0, 32), (32, 32), (128, 64), (192, 64)]):
            nh = 512 // hd
            P.op("dve", lambda e, gi=gi, o=o, hd=hd, nh=nh: e.tensor_scalar(
                out=Gq[:, gi, :].rearrange("p (a b) -> p a b", b=hd),
                in0=gsm[:, o:o + hd].unsqueeze(1).to_broadcast([128, nh, hd]),
                scalar1=float(hd) ** 0.5, scalar2=None, op0=ALU.mult), reads=["A_gsm"], writes=["A_Gq%d" % gi])

        Gs = AR.alloc([4, 64], F32)
        for gi, (o, hd) in enumerate([(0, 32), (32, 32), (128, 64), (192, 64)]):
            P.op("dve", lambda e, gi=gi, o=o, hd=hd: e.tensor_scalar(out=Gs[:, gi, 0:hd], in0=gsm[:, o:o + hd], scalar1=float(hd) ** 0.5,
                                                                    scalar2=None, op0=ALU.mult), reads=["A_gsm"], writes=["A_Gs"])
        qdt_v = QdT.rearrange("(c p) n -> p c n", p=128)
        kdt_v = KdT.rearrange("(c p) n -> p c n", p=128)
        qmt_v = QmT.rearrange("(c p) n -> p c n", p=128)
        kmt_v = KmT.rearrange("(c p) n -> p c n", p=128)
        vd_v = Vd.rearrange("h k d -> k h d")
        vm_v = Vm.rearrange("h k d -> k h d")

        def prologue(t):
            b2 = t % 2
            c4 = t % NC_
            xtn = "A_xt%d" % c4
            csn = "A_cs%d" % c4
            hb, hT, hbn, hTn = hbs[b2], hTs[b2], "A_hb%d" % b2, "A_hT%d" % b2
            ssq, rs, pfx = ssqs[b2], rss[b2], "A%d" % b2

            def st0():
                P.dma_group("sp", "A_in%d" % c4, [
                    lambda e: e.dma_start(out=xt[c4], in_=xin[t * 128:(t + 1) * 128, :]),
                    lambda e: e.dma_start(out=cst[c4][:, 0:64], in_=cs64[t * 128:(t + 1) * 128, :]),
                    lambda e: e.dma_start(out=cst[c4][:, 64:96], in_=cs32[t * 128:(t + 1) * 128, :])],
                    writes=[xtn, csn + "a", csn + "b"])

            def st1():
                P.op("act", lambda e: e.activation(out=sq, in_=xt[c4], func=AF.Square, accum_out=ssq), reads=[xtn], writes=["A_sq", pfx + "_ssq"])
                rms_rstd(ssq, pfx + "_ssq", rs, pfx + "_rs", D * EPS)
                for gi_, hd_ in enumerate((32, 32, 64, 64)):
                    hf_ = hd_ // 2
                    co_ = 64 if hd_ == 32 else 0
                    for cs_ in range(2):
                        P.op("pool", lambda e, gi_=gi_, hd_=hd_, hf_=hf_, co_=co_, cs_=cs_: e.tensor_tensor(
                            out=gct[c4][:, gi_, cs_, 0:hd_].rearrange("p (b c) -> p b c", c=hf_),
                            in0=Gs[:, gi_, 0:hd_].rearrange("p (b c) -> p b c", c=hf_),
                            in1=cst[c4][:, co_ + cs_ * hf_:co_ + (cs_ + 1) * hf_].unsqueeze(1).to_broadcast([128, 2, hf_]), op=ALU.mult),
                            reads=["A_Gs", csn + "a", csn + "b"], writes=[csn + "g"])

            def st2():
                P.op("dve", lambda e: e.scalar_tensor_tensor(out=hb, in0=xt[c4], scalar=rs, in1=gbc, op0=ALU.mult, op1=ALU.mult),
                     reads=[xtn, pfx + "_rs", "A_gbc"], writes=[hbn])

            def st3():
                transpose8(hb, hbn, hT, hTn, 7)
            return [st0, st1, st2, st3]

        def group_item(t, g, gidx, qidx):
            b2 = t % 2
            c4 = t % NC_
            csn = "A_cs%d" % c4
            hT, hTn = hTs[b2], "A_hT%d" % b2
            psi = gidx % 5
            ps = PS[psi]
            z = gidx % NS

            def s_mm():
                for k in range(8):
                    P.op("pe", lambda e, k=k: e.matmul(ps, lhsT=hT[:, k, :], rhs=Wi[:, k, g * 512:(g + 1) * 512], start=(k == 0), stop=(k == 7)),
                         reads=[hTn, "A_Wi_g%d" % g], writes=[PSN[psi]])
            if g in (2, 5):
                vb, vbn = vbs[z], "A_vb%d" % z
                dst = vd_v if g == 2 else vm_v

                def s_cp():
                    P.op("act", lambda e: e.activation(out=vb, in_=ps, func=AF.Copy), reads=[PSN[psi]], writes=[vbn])

                def s_st():
                    P.dma("sp", "A_vs%d" % z, lambda e: e.dma_start(out=dst[t * 128:(t + 1) * 128, :, :], in_=vb.rearrange("p (h d) -> p h d", d=64)),
                          reads=[vbn])
                return [s_mm, s_cp, s_st]
            isd = g in (0, 1)
            hd = 32 if isd else 64
            half = hd // 2
            nh = 512 // hd
            gi = {0: 0, 1: 1, 3: 2, 4: 3}[g]
            sqg, qn, ta, tb, ssqh, rsh, qb, qT = sqgs[z], qns[z], tas[z], tbs[z], ssqhs[z], rshs[z], qbs[z], qTs[z]
            n_sqg, n_qn, n_ta, n_tb, n_ssqh, n_rsh, n_qb, n_qT = ["A_%s%d" % (x, z) for x in ("sqg", "qn", "ta", "tb", "ssqh", "rsh", "qb", "qT")]
            ssq_v = ssqh[:, 0:nh]
            rs_v = rsh[:, 0:nh]
            co = 64 if isd else 0
            cosv = cst[c4][:, co:co + half]
            sinv = cst[c4][:, co + half:co + 2 * half]
            q4 = qn.rearrange("p (a b c) -> p a b c", b=2, c=half)
            ta4 = ta.rearrange("p (a b c) -> p a b c", b=2, c=half)
            tb4 = tb.rearrange("p (a b c) -> p a b c", b=2, c=half)
            qb4 = qb.rearrange("p (a b c) -> p a b c", b=2, c=half)
            tpi = 5 + (gidx % 2)
            psb = PS[tpi].bitcast(BF16)
            if g in (0, 3):
                dstv = qdt_v if g == 0 else qmt_v
                col = qidx * 128
            else:
                dstv = kdt_v if g == 1 else kmt_v
                col = t * 128

            def s1():
                P.op("act", lambda e: e.activation(out=sqg, in_=ps, func=AF.Square), reads=[PSN[psi]], writes=[n_sqg])

            def s2():
                P.op("dve", lambda e: e.tensor_reduce(out=ssq_v, in_=sqg.rearrange("p (a b) -> p a b", b=hd), axis=AX.X, op=ALU.add),
                     reads=[n_sqg], writes=[n_ssqh])

            def s3():
                rms_rstd(ssq_v, n_ssqh, rs_v, n_rsh, hd * EPS)

            def s4():
                P.op("dve", lambda e: e.tensor_tensor(out=qn.rearrange("p (a b) -> p a b", b=hd), in0=ps.rearrange("p (a b) -> p a b", b=hd),
                                                      in1=rs_v.unsqueeze(2).to_broadcast([128, nh, hd]), op=ALU.mult),
                     reads=[PSN[psi], n_rsh], writes=[n_qn])

            gcv = gct[c4][:, gi, 0, 0:hd].rearrange("p (b c) -> p b c", c=half)
            gsv = gct[c4][:, gi, 1, 0:hd].rearrange("p (b c) -> p b c", c=half)

            def s5():
                pass

            def s6():
                P.op("dve", lambda e: e.tensor_tensor(out=ta4, in0=q4, in1=gcv.unsqueeze(1).to_broadcast([128, nh, 2, half]),
                                                      op=ALU.mult), reads=[n_qn, csn + "g"], writes=[n_ta])
                P.op("pool", lambda e: e.tensor_tensor(out=tb4, in0=q4, in1=gsv.unsqueeze(1).to_broadcast([128, nh, 2, half]),
                                                       op=ALU.mult), reads=[n_qn, csn + "g"], writes=[n_tb])

            def s7():
                P.op("dve", lambda e: e.tensor_tensor(out=qb4[:, :, 0, :], in0=ta4[:, :, 0, :], in1=tb4[:, :, 1, :], op=ALU.subtract),
                     reads=[n_ta, n_tb], writes=[n_qb + "a"])
                P.op("pool", lambda e: e.tensor_tensor(out=qb4[:, :, 1, :], in0=ta4[:, :, 1, :], in1=tb4[:, :, 0, :], op=ALU.add),
                     reads=[n_ta, n_tb], writes=[n_qb + "b"])

            def s8():
                for k in range(4):
                    P.op("pe", lambda e, k=k: e.transpose(out=psb[:, k * 128:(k + 1) * 128], in_=qb[:, k * 128:(k + 1) * 128], identity=ident),
                         reads=[n_qb + "a", n_qb + "b", "ident"], writes=[PSN[tpi]])

            def s9():
                P.op("act", lambda e: e.activation(out=qT, in_=psb[:, 0:512].rearrange("p (a b) -> p a b", b=128), func=AF.Copy),
                     reads=[PSN[tpi]], writes=[n_qT])

            def s10():
                P.dma("sp", "A_qs%d" % z, lambda e: e.dma_start(out=dstv[:, :, col:col + 128], in_=qT), reads=[n_qT])
            return [s_mm, s1, s2, s3, s4, s5, s6, s7, s8, s9, s10]

        items = []
        extra = {}
        first_group_of_tile = {}
        for t in range(NKT):
            if t >= 16:
                qidx = t - 16
            elif t == 15:
                qidx = 16
            else:
                qidx = None
            groups = [1, 2, 4, 5] if qidx is None else [0, 1, 2, 3, 4, 5]
            first_group_of_tile[t] = len(items)
            for g in groups:
                items.append(group_item(t, g, len(items), qidx))
        for st in prologue(0):
            st()
        for t in range(NKT - 1):
            g0 = first_group_of_tile[t]
            for k, st in enumerate(prologue(t + 1)):
                extra.setdefault(g0 + k, []).append(st)
        maxs = max(len(it) for it in items)
        for step in range(len(items) + maxs):
            for sidx in range(maxs - 1, -1, -1):
                g = step - sidx
                if 0 <= g < len(items) and sidx < len(items[g]):
                    items[g][sidx]()
            for st in extra.get(step, []):
                st()

    phase_A()
    barrier()
    fin = []
    if stop_after == "A":
        P.emit()
        return nc

    def phase_BC():
        reset_arena()
        Wo = AR.alloc([8, D], BF16)
        C = AR.alloc([NQT, D], BF16)
        cm = AR.alloc([4, 512], BF16)
        kT = [AR.alloc([NK], BF16) for _ in range(2)]
        vbuf = [AR.alloc([NKT, 128], BF16) for _ in range(2)]
        qa = [AR.alloc([NQ], BF16) for _ in range(2)]
        qb_ = [AR.alloc([NQ], BF16) for _ in range(2)]
        pT = [AR.alloc([2, 512], BF16) for _ in range(3)]
        oT = [AR.alloc([512], BF16) for _ in range(2)]
        tsb = AR.alloc([528], BF16)
        gsm = AR.alloc([320], F32)
        lamv = AR.alloc([128], F32)
        lt = AR.alloc([64], F32)
        lsm = AR.alloc([8], F32)
        Gsub = AR.alloc([4, 64], F32)
        gbj = AR.alloc([9, 16], F32)
        biaspad = AR.alloc([NQT, 128], BF16)
        gball = AR.alloc([NQT, 16], F32)
        ownm = AR.alloc([NQT, 16], F32)
        gmall = AR.alloc([NQT, 16], F32)
        cmpb = AR.alloc([NQT, 16, 16], BF16)
        rnk = AR.alloc([NQT, 16], F32)
        bia1 = AR.alloc([NQT, 16], F32)
        bia2 = AR.alloc([NQT, 16], F32)
        zt = AR.alloc([128], BF16)
        kbs = AR.alloc([16], F32)
        kbb = AR.alloc([16], BF16)
        gm = AR.alloc([16], F32)
        top8 = AR.alloc([8], F32)
        thr = AR.alloc([1], F32)
        rcp = AR.alloc([8], F32)
        oa = AR.alloc([4, 64], F32)
        ob = AR.alloc([4, 64], F32)
        dd = AR.alloc([4, 64], F32)
        sqd = AR.alloc([4, 64], F32)
        ssq4 = AR.alloc([4], F32)
        rs4 = AR.alloc([4], F32)
        CT = AR.alloc([8, 128], BF16)
        xr = [AR.alloc([D], F32) for _ in range(2)]
        x1 = [AR.alloc([D], F32) for _ in range(2)]

        load_w_bf16(Wo, "B_Wo", e_w_out, "B_w", nsplit=2)
        wo_names = ["B_Wo_0"] * 4 + ["B_Wo_4"] * 4
        P.op("pool", lambda e: e.memset(cm, 0.0), writes=["cm"])
        for i in range(4):
            P.op("pool", lambda e, i=i: e.affine_select(out=cm[:, i, :], in_=cm[:, i, :], pattern=[[1, 512]], compare_op=ALU.is_ge,
                                                        fill=NEG, base=-128 * i, channel_multiplier=-1), reads=["cm"], writes=["cm"])
        for b in range(2):
            P.op("pool", lambda e, b=b: e.memset(qa[b][32:64, :], 0.0), writes=["qa%d_z" % b])
            P.op("pool", lambda e, b=b: e.memset(qa[b][64:128, :], 0.0), writes=["qa%d_bias" % b])
            P.op("pool", lambda e, b=b: e.memset(qb_[b][0:32, :], 0.0), writes=["qb%d_z" % b])
            P.op("pool", lambda e, b=b: e.memset(qb_[b][64:128, :], 0.0), writes=["qb%d_z" % b])
            P.op("pool", lambda e, b=b: e.memset(vbuf[b][:, :, 64:128], 0.0), writes=["vb%d_1" % b])
            P.op("pool", lambda e, b=b: e.memset(vbuf[b][:, :, 64:65], 1.0), reads=["vb%d_1" % b], writes=["vb%d_1" % b])
            P.op("pool", lambda e, b=b: e.memset(kT[b][64:128, :], 0.0), writes=["kT%d_oh" % b])
            P.dma("sp", "B_oh%d" % b, lambda e, b=b: e.dma_start(out=kT[b][64:80, :], in_=onehot), writes=["kT%d_oh" % b])
        P.op("pool", lambda e: e.memset(biaspad, 0.0), writes=["biaspad"])
        P.op("pool", lambda e: e.memset(zt, 0.0), writes=["zt"])
        P.dma("sp", "B_gs", lambda e: e.dma_start(out=gsm, in_=gains.partition_broadcast(128)), writes=["B_gsm"])
        P.dma("sp", "B_lm", lambda e: e.dma_start(out=lamv, in_=lams.partition_broadcast(128)), writes=["B_lamv"])
        l4 = lamv.rearrange("p (a b c) -> p a b c", b=2, c=32)
        P.op("dve", lambda e: e.tensor_tensor(out=lt.rearrange("p (a c) -> p a c", c=32), in0=l4[:, :, 0, :], in1=l4[:, :, 1, :],
                                              op=ALU.mult), reads=["B_lamv"], writes=["B_lt"])
        P.op("dve", lambda e: e.tensor_reduce(out=lsm[:, 0:2], in_=lt.rearrange("p (a c) -> p a c", c=32), axis=AX.X, op=ALU.add),
             reads=["B_lt"], writes=["B_lsm"])
        P.op("act", lambda e: e.activation(out=lsm[:, 2:4], in_=lsm[:, 0:2], func=AF.Exp), reads=["B_lsm"], writes=["B_lsm2"])
        P.op("dve", lambda e: e.tensor_tensor(out=lsm[:, 4:5], in0=lsm[:, 3:4], in1=lsm[:, 2:3], op=ALU.subtract),
             reads=["B_lsm2"], writes=["B_lsm3"])
        neglam = lsm[:, 5:6]
        P.op("dve", lambda e: e.tensor_scalar(out=neglam, in0=lsm[:, 4:5], scalar1=-LAMBDA_INIT0, scalar2=None, op0=ALU.add),
             reads=["B_lsm3"], writes=["neglam"])
        P.op("dve", lambda e: e.tensor_scalar(out=Gsub, in0=gsm[:, 64:128].unsqueeze(1).to_broadcast([128, 4, 64]),
                                              scalar1=8.0 * (1.0 - LAMBDA_INIT0), scalar2=None, op0=ALU.mult),
             reads=["B_gsm"], writes=["Gsub"])
        for j in range(8):
            P.op("dve", lambda e, j=j: e.tensor_copy(out=gbj[:, j, :], in_=flg[:, 4:20]), reads=["flg"], writes=["gbj"])
            P.op("dve", lambda e, j=j: e.memset(gbj[:, j, 8 + j:16], -1e30), reads=["gbj"], writes=["gbj"])
        P.op("dve", lambda e: e.memset(gbj[:, 8, :], 0.0), reads=["gbj"], writes=["gbj"])
        P.op("dve", lambda e: e.memset(gbj[:, 8, 7:16], -1e30), reads=["gbj"], writes=["gbj"])

        P.op("dve", lambda e: e.memset(ownm, 1.0), writes=["ownm"])
        for qi in range(NQT):
            jj = qi // 2 if qi < 16 else 8
            ownb = 8 + qi // 2 if qi < 16 else 7
            P.op("dve", lambda e, qi=qi, jj=jj: e.tensor_copy(out=gball[:, qi, :], in_=gbj[:, jj, :]), reads=["gbj"], writes=["gball"])
            P.op("dve", lambda e, qi=qi, ownb=ownb: e.memset(ownm[:, qi, ownb:ownb + 1], 0.0), reads=["ownm"], writes=["ownm"])
        units = [("d", h) for h in range(8)] + [("m", h) for h in range(8)]
        groups = [(gi * 512, 512, 20 + 4 * gi, True, gi * 4) for gi in range(4)] + [(2048, 128, 16, False, 16)]
        gcount = [0]
        sc = [0]

        def issue_loads(u):
            kind, h = units[u]
            b = u % 2
            Ksrc, Vsrc, Qsrc = (KdT, Vd, QdT) if kind == "d" else (KmT, Vm, QmT)
            fns = [lambda e: e.dma_start(out=kT[b][0:64, :], in_=Ksrc[h * 64:(h + 1) * 64, :])]
            for pt in range(4):
                fns.append(lambda e, pt=pt: e.dma_start(out=vbuf[b][:, pt * 8:(pt + 1) * 8, 0:64],
                                                       in_=Vsrc[h].rearrange("(t p) d -> p t d", p=128)[:, pt * 8:(pt + 1) * 8, :]))
            wr = ["kT%d" % b] + ["vb%d_%d" % (b, pt) for pt in range(4)]
            if kind == "d":
                fns.append(lambda e: e.dma_start(out=qa[b][0:32, :], in_=Qsrc[h * 64:h * 64 + 32, :]))
                fns.append(lambda e: e.dma_start(out=qb_[b][32:64, :], in_=Qsrc[h * 64 + 32:h * 64 + 64, :]))
                wr += ["qa%d" % b, "qb%d" % b]
            else:
                fns.append(lambda e: e.dma_start(out=qa[b][0:64, :], in_=Qsrc[h * 64:(h + 1) * 64, :]))
                wr += ["qa%d" % b, "qa%d_z" % b]
            P.dma_group("sp", "B_ld%d" % b, fns, writes=wr)

        def gating_stages(u):
            kind, h = units[u]
            b = u % 2
            ps7b = PS[7].bitcast(BF16)

            def g1():
                P.op("dve", lambda e: e.tensor_reduce(out=kbs[0:64, :], in_=kT[b][0:64, :].rearrange("p (a c) -> p a c", c=256),
                                                      axis=AX.X, op=ALU.add), reads=["kT%d" % b], writes=["kbs"])
                P.op("dve", lambda e: e.tensor_scalar(out=kbb[0:64, :], in0=kbs[0:64, :], scalar1=1.0 / 256.0, scalar2=None, op0=ALU.mult),
                     reads=["kbs"], writes=["kbb"])
                for qi in range(NQT):
                    P.op("pe", lambda e, qi=qi: e.matmul(PS[7][:, qi * 16:(qi + 1) * 16], lhsT=qa[b][0:64, qi * 128:(qi + 1) * 128], rhs=kbb[0:64, :],
                                                        start=True, stop=True), reads=["qa%d" % b, "kbb"], writes=[PSN[7]])

            def g2():
                P.op("dve", lambda e: e.tensor_tensor(out=gmall, in0=PS[7][:, 0:NQT * 16].rearrange("p (q n) -> p q n", n=16), in1=gball, op=ALU.add),
                     reads=[PSN[7], "gball"], writes=["gmall"])
                P.op("dve", lambda e: e.tensor_tensor(out=cmpb, in0=gmall.unsqueeze(2).to_broadcast([128, NQT, 16, 16]),
                                                      in1=gmall.unsqueeze(3).to_broadcast([128, NQT, 16, 16]), op=ALU.is_gt),
                     reads=["gmall"], writes=["cmpb"])
                P.op("dve", lambda e: e.tensor_reduce(out=rnk, in_=cmpb, axis=AX.X, op=ALU.add), reads=["cmpb"], writes=["rnk"])
                P.op("dve", lambda e: e.tensor_scalar(out=bia1, in0=rnk, scalar1=2.5, scalar2=NEG, op0=ALU.is_gt, op1=ALU.mult),
                     reads=["rnk"], writes=["bia1"])
                P.op("dve", lambda e: e.tensor_scalar(out=bia2, in0=gmall, scalar1=-1e29, scalar2=NEG, op0=ALU.is_lt, op1=ALU.mult),
                     reads=["gmall"], writes=["bia2"])
                P.op("dve", lambda e: e.tensor_tensor(out=bia1, in0=bia1, in1=bia2, op=ALU.add), reads=["bia1", "bia2"], writes=["bia1"])
                P.op("dve", lambda e: e.tensor_tensor(out=biaspad[:, :, 64:80], in0=bia1, in1=ownm, op=ALU.mult),
                     reads=["bia1", "ownm"], writes=["biaspad"])

            def g3():
                for q0 in range(0, NQT, 3):
                    nb = min(3, NQT - q0)
                    for k in range(nb):
                        P.op("pe", lambda e, q0=q0, k=k: e.transpose(out=ps7b[:, 544 + k * 128:544 + (k + 1) * 128], in_=biaspad[:, q0 + k, :],
                                                                    identity=ident), reads=["biaspad", "ident"], writes=[PSN[7]])
                    P.op("act", lambda e, q0=q0, nb=nb: e.activation(out=qa[b][64:80, q0 * 128:(q0 + nb) * 128], in_=ps7b[64:80, 544:544 + nb * 128],
                                                                    func=AF.Copy), reads=[PSN[7]], writes=["qa%d_bias" % b])
            return [g1, g2, g3]

        def attention(u, hooks=()):
            kind, h = units[u]
            b = u % 2
            isd = kind == "d"
            dk = 128
            scale = (32.0 ** -0.5) if isd else 0.125
            for gi_, grp in enumerate(groups):
                if gi_ < len(hooks):
                    hooks[gi_]()
                do_group(kind, h, b, isd, dk, scale, (not isd) or len(hooks) == 0, *grp)

        def do_group(kind, h, b, isd, dk, scale, wide, qc0, N, nkt, usepast, t0):
            R = N // 128
            gidx = gcount[0]
            gcount[0] += 1
            nsl = 3 if wide else 2
            if isd and wide:
                maps = [(qa[b], ["qa%d" % b, "qa%d_z" % b, "qa%d_bias" % b], 6), (qb_[b], ["qb%d" % b, "qb%d_z" % b], 7)]
                tbank = 0
            elif isd:
                maps = [(qa[b], ["qa%d" % b, "qa%d_z" % b, "qa%d_bias" % b], 4), (qb_[b], ["qb%d" % b, "qb%d_z" % b], 5)]
                tbank = 6
            elif wide:
                maps = [(qa[b], ["qa%d" % b, "qa%d_bias" % b], 6)]
                tbank = 0
            else:
                maps = [(qa[b], ["qa%d" % b, "qa%d_bias" % b], 4)]
                tbank = 5
            npair = nkt // 2
            steps = [(mi, kp) for mi in range(len(maps)) for kp in range(npair)]
            slot = {}

            def col0_of(kp):
                return 256 if (usepast and kp == npair - 1) else 0

            def qk(i):
                mi, kp = steps[i]
                Q, qn_, _ = maps[mi]
                s_ = sc[0] % nsl
                sc[0] += 1
                slot[i] = s_
                c0_ = col0_of(kp)
                for hfi in range(2):
                    kt = 2 * kp + hfi
                    bank = 2 * s_ + hfi
                    if usepast:
                        di = kt - (nkt - 4)
                    else:
                        di = 0 if kt == nkt - 1 else -1
                    diag = di >= 0
                    P.op("pe", lambda e, kt=kt, bank=bank, diag=diag: e.matmul(PS[bank][:, c0_:N], lhsT=kT[b][0:dk, kt * 128:(kt + 1) * 128],
                                                                             rhs=Q[0:dk, qc0 + c0_:qc0 + N], start=True, stop=not diag),
                         reads=["kT%d" % b, "kT%d_oh" % b] + qn_, writes=[PSN[bank]])
                    if diag:
                        P.op("pe", lambda e, bank=bank, di=di: e.matmul(PS[bank][:, c0_:N], lhsT=ident, rhs=cm[:, di, c0_:N], start=False, stop=True),
                             reads=["ident", "cm"], writes=[PSN[bank]])

            def ex_pv(i):
                mi, kp = steps[i]
                _, _, abank = maps[mi]
                s_ = slot[i]
                c0_ = col0_of(kp)
                src = PSALL[:, 2 * s_ * 512:(2 * s_ + 2) * 512].rearrange("p (b n) -> p b n", n=512)[:, :, c0_:N]
                dst = pT[s_][:, :, c0_:N]
                rd = [PSN[2 * s_], PSN[2 * s_ + 1]]
                if usepast and 2 * kp < 16:
                    P.op("act", lambda e: e.activation(out=dst, in_=src, func=AF.Exp, scale=scale, bias=flg[:, 0:1]),
                         reads=rd + ["flg"], writes=["pT%d" % s_])
                else:
                    P.op("act", lambda e: e.activation(out=dst, in_=src, func=AF.Exp, scale=scale), reads=rd, writes=["pT%d" % s_])
                for hfi in range(2):
                    kt = 2 * kp + hfi
                    P.op("pe", lambda e, kt=kt, hfi=hfi: e.matmul(PS[abank][:, c0_:N], lhsT=vbuf[b][:, kt, :], rhs=pT[s_][:, hfi, c0_:N],
                                                                  start=(kt == 0), stop=(kt == nkt - 1)),
                         reads=["pT%d" % s_, "vb%d_%d" % (b, kt // 8), "vb%d_1" % b], writes=[PSN[abank]])

            lead = nsl - 1
            for i in range(min(lead, len(steps))):
                qk(i)
            for i in range(len(steps)):
                if i + lead < len(steps):
                    qk(i + lead)
                ex_pv(i)
            tpb = PS[tbank].bitcast(BF16)
            for mi, (_, _, abank) in enumerate(maps):
                P.op("dve", lambda e, mi=mi, abank=abank: e.tensor_copy(out=oT[mi][0:65, 0:N], in_=PS[abank][0:65, 0:N]),
                     reads=[PSN[abank]], writes=["oT%d" % mi])
                for r in range(R):
                    c0 = (mi * 4 + r) * 66
                    P.op("pe", lambda e, mi=mi, r=r, c0=c0: e.transpose(out=tpb[:, c0:c0 + 65], in_=oT[mi][0:65, r * 128:(r + 1) * 128],
                                                                      identity=ident[0:65, 0:65]),
                         reads=["oT%d" % mi, "ident"], writes=[PSN[tbank]])
            cn = "C_g%d" % t0
            nt = PSN[tbank]
            fin = tpb
            if wide:
                P.op("dve", lambda e: e.tensor_copy(out=tsb, in_=tpb[:, 0:528]), reads=[PSN[tbank]], writes=["tsb"])
                fin = tsb
                nt = "tsb"
            accA = fin[:, 0:R * 66].rearrange("p (r c) -> p r c", c=66)
            if isd:
                accB = fin[:, 4 * 66:(4 + R) * 66].rearrange("p (r c) -> p r c", c=66)
                P.op("dve", lambda e: e.reciprocal(out=rcp[:, 0:R], in_=accA[:, :, 64]), reads=[nt], writes=["rcpa"])
                P.op("dve", lambda e: e.reciprocal(out=rcp[:, 4:4 + R], in_=accB[:, :, 64]), reads=[nt], writes=["rcpb"])
                P.op("dve", lambda e: e.tensor_tensor(out=oa[:, 0:R, :], in0=accA[:, :, 0:64],
                                                      in1=rcp[:, 0:R].unsqueeze(2).to_broadcast([128, R, 64]), op=ALU.mult),
                     reads=[nt, "rcpa"], writes=["oa"])
                P.op("dve", lambda e: e.tensor_tensor(out=ob[:, 0:R, :], in0=accB[:, :, 0:64],
                                                      in1=rcp[:, 4:4 + R].unsqueeze(2).to_broadcast([128, R, 64]), op=ALU.mult),
                     reads=[nt, "rcpb"], writes=["ob"])
                P.op("dve", lambda e: e.scalar_tensor_tensor(out=dd[:, 0:R, :], in0=ob[:, 0:R, :], scalar=neglam, in1=oa[:, 0:R, :],
                                                             op0=ALU.mult, op1=ALU.add), reads=["oa", "ob", "neglam"], writes=["dd"])
                P.op("pool", lambda e: e.tensor_tensor(out=sqd[:, 0:R, :], in0=dd[:, 0:R, :], in1=dd[:, 0:R, :], op=ALU.mult),
                     reads=["dd"], writes=["sqd"])
                P.op("dve", lambda e: e.tensor_reduce(out=ssq4[:, 0:R], in_=sqd[:, 0:R, :], axis=AX.X, op=ALU.add),
                     reads=["sqd"], writes=["ssq4"])
                rms_rstd(ssq4[:, 0:R], "ssq4", rs4[:, 0:R], "rs4", 64 * EPS)
                P.op("dve", lambda e: e.tensor_tensor(out=dd[:, 0:R, :], in0=dd[:, 0:R, :],
                                                      in1=rs4[:, 0:R].unsqueeze(2).to_broadcast([128, R, 64]), op=ALU.mult),
                     reads=["dd", "rs4"], writes=["dd"])
                P.op("pool", lambda e: e.tensor_tensor(out=C[:, t0:t0 + R, h * 64:(h + 1) * 64], in0=dd[:, 0:R, :], in1=Gsub[:, 0:R, :],
                                                       op=ALU.mult), reads=["dd", "Gsub"], writes=[cn])
            else:
                P.op("dve", lambda e: e.reciprocal(out=rcp[:, 0:R], in_=accA[:, :, 64]), reads=[nt], writes=["rcpa"])
                P.op("dve", lambda e: e.tensor_tensor(out=C[:, t0:t0 + R, 512 + h * 64:512 + (h + 1) * 64], in0=accA[:, :, 0:64],
                                                      in1=rcp[:, 0:R].unsqueeze(2).to_broadcast([128, R, 64]), op=ALU.mult),
                     reads=[nt, "rcpa"], writes=[cn])

        issue_loads(0)
        for u in range(len(units)):
            if u + 1 < len(units):
                issue_loads(u + 1)
            hooks = gating_stages(u + 1) if (u + 1 < len(units) and units[u + 1][0] == "m") else []
            attention(u, hooks)

        CTs = [CT, AR.alloc([8, 128], BF16)]

        def c_pre(tl):
            b2 = tl % 2
            lt_ = 16 + tl if tl < 16 else 15
            cn = "C_g%d" % (tl // 4 * 4 if tl < 16 else 16)
            P.dma("sp", "C_x%d" % b2, lambda e: e.dma_start(out=xr[b2], in_=xin[lt_ * 128:(lt_ + 1) * 128, :]), writes=["C_xr%d" % b2])
            transpose8(C[:, tl, :], cn, CTs[b2], "C_CT%d" % b2, 6 + b2)

        def c_main(tl):
            b2 = tl % 2
            for hh in range(2):
                bank = 2 * b2 + hh
                for k in range(8):
                    P.op("pe", lambda e, k=k, hh=hh, bank=bank: e.matmul(PS[bank], lhsT=CTs[b2][:, k, :], rhs=Wo[:, k, hh * 512:(hh + 1) * 512],
                                                                        start=(k == 0), stop=(k == 7)), reads=["C_CT%d" % b2, wo_names[k]], writes=[PSN[bank]])
                P.op("dve", lambda e, hh=hh, bank=bank: e.tensor_tensor(out=x1[b2][:, hh * 512:(hh + 1) * 512], in0=PS[bank],
                                                                       in1=xr[b2][:, hh * 512:(hh + 1) * 512], op=ALU.add),
                     reads=[PSN[bank], "C_xr%d" % b2], writes=["C_x1%d_%d" % (b2, hh)])
            P.dma("sp", "C_s%d" % b2, lambda e: e.dma_start(out=X1[tl * 128:(tl + 1) * 128, :], in_=x1[b2]),
                  reads=["C_x1%d_0" % b2, "C_x1%d_1" % b2])

        c_pre(0)
        for tl in range(NQT):
            if tl + 1 < NQT:
                c_pre(tl + 1)
            c_main(tl)

    phase_BC()
    barrier()
    if stop_after == "BC":
        P.emit()
        return nc

    def phase_FFN(Xin, Xout, li, ntiles):
        reset_arena()
        gbc = AR.alloc([D], F32)
        xs = [AR.alloc([D], F32) for _ in range(3)]
        xr2 = [AR.alloc([D], F32) for _ in range(2)]
        sq = AR.alloc([D], BF16)
        hbs = [AR.alloc([D], BF16) for _ in range(2)]
        hT = AR.alloc([8, 1152], BF16)
        wo = AR.alloc([22, D], BF16)
        wi = [AR.alloc([8, 256], BF16) for _ in range(3)]
        AT = AR.alloc([22, 1152], BF16)
        sg = [AR.alloc([512], BF16) for _ in range(2)]
        yb = [AR.alloc([D], F32) for _ in range(2)]
        ssqs = [AR.alloc([1], F32) for _ in range(2)]
        rss = [AR.alloc([1], F32) for _ in range(2)]
        pf = "F%d" % li
        load_gain_bc(gbc, pf + "_gbc", ffn_g[li:li + 1, :], pf + "_g", 32.0)
        win_v = f_w_in[li].rearrange("(k p) n -> p k n", p=128)
        wout_v = f_w_out[li].rearrange("(j p) n -> p j n", p=128)
        half = (ntiles + 1) // 2
        passes = [list(range(0, half)), list(range(half, ntiles))]
        cnt = {"pro": 0, "y": 0, "g": 0}

        def load_wi(j):
            b3 = j % 3
            P.dma_group("pool", pf + "_wi%d" % b3, [
                lambda e: e.dma_start(out=wi[b3][:, :, 0:128], in_=win_v[:, :, j * 128:(j + 1) * 128]),
                lambda e: e.dma_start(out=wi[b3][:, :, 128:256], in_=win_v[:, :, DFF + j * 128:DFF + (j + 1) * 128])],
                writes=[pf + "_wi%dg" % b3, pf + "_wi%du" % b3])

        def pro_a(i, tl):
            c = cnt["pro"]
            cnt["pro"] += 1
            x3, h2 = c % 3, c % 2
            xn, hbn, pfx = pf + "_xs%d" % x3, pf + "_hb%d" % h2, pf + "n%d" % h2
            P.dma("sp", pf + "_x%d" % x3, lambda e: e.dma_start(out=xs[x3], in_=Xin[tl * 128:(tl + 1) * 128, :]), writes=[xn])
            norm_tile(xs[x3], xn, gbc, pf + "_gbc", hbs[h2], hbn, sq, pf + "_sq", ssqs[h2], rss[h2], pfx)
            return (hbs[h2], hbn)

        def pro_b(i, hbinfo):
            transpose8(hbinfo[0], hbinfo[1], hT[:, :, i * 128:(i + 1) * 128], pf + "_hT%d" % i, 0)

        for j0 in (0, 11):
            P.dma("pool", pf + "_wo%d" % j0, lambda e, j0=j0: e.dma_start(out=wo[:, j0:j0 + 11, :], in_=wout_v[:, j0:j0 + 11, :]),
                  writes=[pf + "_wo%d" % j0])
        for j in range(3):
            load_wi(j)
        prev_ = None
        for i, tl in enumerate(passes[0]):
            info_ = pro_a(i, tl)
            if prev_ is not None:
                pro_b(*prev_)
            prev_ = (i, info_)
        pro_b(*prev_)
        for pi, tiles in enumerate(passes):
            ntok = len(tiles) * 128
            nsub = (ntok + 511) // 512
            sbw = ((ntok // 128 + nsub - 1) // nsub) * 128
            for j in range(22):
                b3 = j % 3
                if j >= 3:
                    load_wi(j)
                for sb, s0 in enumerate(range(0, ntok, sbw)):
                    n = min(sbw, ntok - s0)
                    gb = cnt["g"] % 2
                    cnt["g"] += 1
                    hnames = [pf + "_hT%d" % i for i in range(s0 // 128, (s0 + n) // 128)]
                    for (col, bank, wn) in ((0, gb, "g"), (128, 2 + gb, "u")):
                        for k in range(8):
                            P.op("pe", lambda e, k=k, col=col, bank=bank, b3=b3, s0=s0, n=n: e.matmul(
                                PS[bank][:, 0:n], lhsT=wi[b3][:, k, col:col + 128], rhs=hT[:, k, s0:s0 + n], start=(k == 0), stop=(k == 7)),
                                reads=hnames + [pf + "_wi%d%s" % (b3, wn)], writes=[PSN[bank]])
                    P.op("act", lambda e, gb=gb, n=n: e.activation(out=sg[gb][:, 0:n], in_=PS[gb][:, 0:n], func=AF.Silu),
                         reads=[PSN[gb]], writes=[pf + "_sg%d" % gb])
                    P.op("dve", lambda e, gb=gb, n=n, j=j, s0=s0: e.tensor_tensor(out=AT[:, j, s0:s0 + n], in0=PS[2 + gb][:, 0:n],
                                                                                in1=sg[gb][:, 0:n], op=ALU.mult),
                         reads=[PSN[2 + gb], pf + "_sg%d" % gb], writes=[pf + "_AT%d_%d" % (sb, j % 2)])
            nxt = passes[pi + 1] if pi + 1 < len(passes) else []
            if nxt:
                for j in range(3):
                    load_wi(j)
            for i, tl in enumerate(tiles):
                hbinfo = pro_a(i, nxt[i]) if i < len(nxt) else None
                yi = cnt["y"] % 2
                cnt["y"] += 1
                P.dma("sp", pf + "_xr%d" % yi, lambda e, tl=tl, yi=yi: e.dma_start(out=xr2[yi], in_=Xin[tl * 128:(tl + 1) * 128, :]),
                      writes=[pf + "_xr%d" % yi])
                for hh in range(2):
                    bank = 4 + (2 * yi + hh)
                    if bank == 7:
                        bank = 3 if False else 7
                    for j in range(22):
                        P.op("pe", lambda e, j=j, i=i, hh=hh, bank=bank: e.matmul(PS[bank], lhsT=AT[:, j, i * 128:(i + 1) * 128],
                                                                               rhs=wo[:, j, hh * 512:(hh + 1) * 512],
                                                                               start=(j == 0), stop=(j == 21)),
                             reads=[pf + "_AT%d_0" % (i * 128 // sbw), pf + "_AT%d_1" % (i * 128 // sbw), pf + "_wo%d" % (0 if j < 11 else 11)],
                             writes=[PSN[bank]])
                    P.op("dve", lambda e, hh=hh, bank=bank, yi=yi: e.tensor_tensor(out=yb[yi][:, hh * 512:(hh + 1) * 512], in0=PS[bank],
                                                                                 in1=xr2[yi][:, hh * 512:(hh + 1) * 512], op=ALU.add),
                         reads=[PSN[bank], pf + "_xr%d" % yi], writes=[pf + "_y%d_%d" % (yi, hh)])
                P.dma("sp", pf + "_ys%d" % yi, lambda e, tl=tl, yi=yi: e.dma_start(out=Xout[tl * 128:(tl + 1) * 128, :], in_=yb[yi]),
                      reads=[pf + "_y%d_0" % yi, pf + "_y%d_1" % yi])
                if hbinfo is not None:
                    pro_b(i, hbinfo)

    phase_FFN(X1, X2, 0, NQT)
    barrier()
    if stop_after == "F0":
        P.emit()
        return nc

    def layer_norm_free(src, src_n, dst, dst_n, g_bc, g_n, b_bc, b_n, tmp, tmp_n, sm, pfx, eng2="pool"):
        P.op("dve", lambda e: e.tensor_reduce(out=sm[:, 0:1], in_=src, axis=AX.X, op=ALU.add), reads=[src_n], writes=[pfx + "_s1"])
        P.op("dve", lambda e: e.tensor_scalar(out=sm[:, 1:2], in0=sm[:, 0:1], scalar1=-1.0 / 512.0, scalar2=None, op0=ALU.mult),
             reads=[pfx + "_s1"], writes=[pfx + "_nm"])
        P.op("dve", lambda e: e.tensor_scalar(out=tmp, in0=src, scalar1=sm[:, 1:2], scalar2=None, op0=ALU.add),
             reads=[src_n, pfx + "_nm"], writes=[tmp_n])
        P.op("act", lambda e: e.activation(out=src, in_=tmp, func=AF.Square, accum_out=sm[:, 2:3]), reads=[tmp_n], writes=[src_n, pfx + "_ss"])
        P.op("act", lambda e: e.activation(out=sm[:, 3:4], in_=sm[:, 2:3], func=AF.Ln, bias=float(EPS), scale=1.0 / 512.0),
             reads=[pfx + "_ss"], writes=[pfx + "_rs"])
        P.op("act", lambda e: e.activation(out=sm[:, 3:4], in_=sm[:, 3:4], func=AF.Exp, scale=-0.5), reads=[pfx + "_rs"], writes=[pfx + "_rs"])
        P.op("dve", lambda e: e.scalar_tensor_tensor(out=tmp, in0=tmp, scalar=sm[:, 3:4], in1=g_bc, op0=ALU.mult, op1=ALU.mult),
             reads=[tmp_n, pfx + "_rs", g_n], writes=[tmp_n])
        P.op(eng2, lambda e: e.tensor_tensor(out=dst, in0=tmp, in1=b_bc, op=ALU.add), reads=[tmp_n, b_n], writes=[dst_n])

    def phase_D():
        reset_arena()
        Wi1 = AR.alloc([8, 2048], BF16)
        Wo1 = AR.alloc([8, D], BF16)
        Dg = AR.alloc([31, 4, 128], BF16)
        wsf = AR.alloc([8, 128], F32)
        wsT = AR.alloc([8, 128], BF16)
        cbuf = AR.alloc([4, 2080], BF16)
        gbc = AR.alloc([D], F32)
        brow = AR.alloc([1024], F32)
        glgb = AR.alloc([1024], F32)
        cvv = AR.alloc([1536], F32)
        bsT = AR.alloc([8], F32)
        obc = AR.alloc([8], F32)
        cw = AR.alloc([4, 31], F32)
        xt = [AR.alloc([D], F32) for _ in range(3)]
        sq = AR.alloc([D], BF16)
        NZ = 2
        S_ = []
        for z in range(NZ):
            S_.append(dict(
                hb=AR.alloc([D], BF16), hT=AR.alloc([8, 128], BF16), sig=AR.alloc([4, 128], F32), ctmp=AR.alloc([128], F32),
                t0u=AR.alloc([512], F32), t0v=AR.alloc([512], F32), w1u=AR.alloc([512], F32), w1v=AR.alloc([512], F32),
                w2=AR.alloc([512], F32), w3=AR.alloc([512], F32), gvn=AR.alloc([512], BF16), CC=AR.alloc([D], BF16),
                CCT=AR.alloc([8, 128], BF16), sm=AR.alloc([8], F32), sm2=AR.alloc([8], F32), ssq=AR.alloc([1], F32), rs=AR.alloc([1], F32)))
        yb = [AR.alloc([D], F32) for _ in range(2)]

        wi1v = o_w_in.rearrange("(k p) n -> p k n", p=128)
        for cb_ in (2, 3, 0, 1):
            P.dma("pool", "D_wi_c%d" % cb_, lambda e, cb_=cb_: e.dma_start(out=Wi1[:, :, cb_ * 512:(cb_ + 1) * 512],
                                                                          in_=wi1v[:, :, cb_ * 512:(cb_ + 1) * 512]), writes=["D_Wi_c%d" % cb_])
        load_w_bf16(Wo1, "D_Wo", o_w_out, "D_wo", nsplit=2)
        wo_n = ["D_Wo_%d" % (k // 4 * 4) for k in range(8)]
        load_gain_bc(gbc, "D_gbc", attn_g[1:2, :], "D_g", 32.0)
        P.dma_group("sp", "D_c", [
            lambda e: e.dma_start(out=brow, in_=o_b_row[:, 0:1024].partition_broadcast(128)),
            lambda e: e.dma_start(out=glgb, in_=gl_gb.partition_broadcast(128)),
            lambda e: e.dma_start(out=cvv, in_=cvec.partition_broadcast(128)),
            lambda e: e.dma_start(out=bsT, in_=bsT_in),
            lambda e: e.dma_start(out=obc, in_=o_b_col),
            lambda e: e.dma_start(out=cw, in_=cwT),
            lambda e: e.dma_start(out=wsf, in_=wsT_in)],
            writes=["D_brow", "D_glgb", "D_cvv", "D_bsT", "D_obc", "D_cw", "D_wsf"])
        P.op("pool", lambda e: e.affine_select(out=wsf, in_=wsf, pattern=[[0, 8], [1, 128]], compare_op=ALU.is_ge, fill=0.0,
                                               base=0, channel_multiplier=-1), reads=["D_wsf"], writes=["D_wsf"])
        P.op("pool", lambda e: e.tensor_copy(out=wsT, in_=wsf), reads=["D_wsf"], writes=["D_wsT"])
        for c in range(4):
            P.op("dve" if c % 2 == 0 else "pool", lambda e, c=c: e.tensor_tensor(
                out=Dg[:, :, c, :], in0=ident.unsqueeze(1).to_broadcast([128, 31, 128]),
                in1=cw[:, c, :].unsqueeze(2).to_broadcast([128, 31, 128]), op=ALU.mult), reads=["ident", "D_cw"], writes=["D_Dg%d" % c])

        def gelu(ps, psn, bias_bc, t0, t0n, w1, w1n, dst, dstn):
            P.op("dve", lambda e: e.tensor_tensor(out=t0, in0=ps, in1=bias_bc, op=ALU.add), reads=[psn, "D_brow"], writes=[t0n])
            P.op("act", lambda e: e.activation(out=w1, in_=t0, func=AF.Square), reads=[t0n], writes=[w1n])
            P.op("dve", lambda e: e.tensor_scalar(out=w1, in0=w1, scalar1=0.044715, scalar2=1.0, op0=ALU.mult, op1=ALU.add),
                 reads=[w1n], writes=[w1n])
            P.op("pool", lambda e: e.tensor_tensor(out=w1, in0=w1, in1=t0, op=ALU.mult), reads=[w1n, t0n], writes=[w1n])
            P.op("act", lambda e: e.activation(out=w1, in_=w1, func=AF.Sigmoid, scale=1.5957691216057308), reads=[w1n], writes=[w1n])
            P.op("pool", lambda e: e.tensor_tensor(out=dst, in0=w1, in1=t0, op=ALU.mult), reads=[w1n, t0n], writes=[dstn])

        def tile_stages(it, tl):
            b2 = it % 3
            z = it % NZ
            B = S_[z]
            N_ = lambda nme: "D_%s%d" % (nme, z)
            hb, hT, sig, ctmp, t0u, t0v, w1u, w1v, w2, w3, gvn, CC, CCT, sm, sm2 = (B[k] for k in (
                "hb", "hT", "sig", "ctmp", "t0u", "t0v", "w1u", "w1v", "w2", "w3", "gvn", "CC", "CCT", "sm", "sm2"))
            halo = tl == 16
            xn = "D_xt%d" % b2
            yi = it % 2

            def f1():
                P.dma("sp", "D_x%d" % b2, lambda e: e.dma_start(out=xt[b2], in_=X2[tl * 128:(tl + 1) * 128, :]), writes=[xn])

            def f2():
                P.op("act", lambda e: e.activation(out=sq, in_=xt[b2], func=AF.Square, accum_out=B["ssq"]), reads=[xn], writes=["D_sq", N_("n") + "_ssq"])
                rms_rstd(B["ssq"], N_("n") + "_ssq", B["rs"], N_("n") + "_rs", D * EPS)

            def f3():
                P.op("dve", lambda e: e.scalar_tensor_tensor(out=hb, in0=xt[b2], scalar=B["rs"], in1=gbc, op0=ALU.mult, op1=ALU.mult),
                     reads=[xn, N_("n") + "_rs", "D_gbc"], writes=[N_("hb")])

            def f4():
                transpose8(hb, N_("hb"), hT, N_("hT"), 0)

            def f5():
                for (base, bank) in ((1024, 1), (1536, 2)):
                    for c in range(4):
                        for k in range(8):
                            P.op("pe", lambda e, base=base, bank=bank, c=c, k=k: e.matmul(
                                PS[bank][:, c * 128:(c + 1) * 128], lhsT=Wi1[:, k, base + c * 128:base + (c + 1) * 128], rhs=hT[:, k, :],
                                start=(k == 0), stop=(k == 7)), reads=[N_("hT"), "D_Wi_c%d" % (base // 512)], writes=[PSN[bank]])

            def f6():
                for (base, bank) in ((0, 3), (512, 4)):
                    for k in range(8):
                        P.op("pe", lambda e, base=base, bank=bank, k=k: e.matmul(PS[bank], lhsT=hT[:, k, :], rhs=Wi1[:, k, base:base + 512],
                                                                              start=(k == 0), stop=(k == 7)),
                             reads=[N_("hT"), "D_Wi_c%d" % (base // 512)], writes=[PSN[bank]])

            def f7():
                for c in range(4):
                    P.op("act", lambda e, c=c: e.activation(out=sig[:, c, :], in_=PS[2][:, c * 128:(c + 1) * 128], func=AF.Sigmoid,
                                                            bias=obc[:, 4 + c:5 + c]), reads=[PSN[2], "D_obc"], writes=[N_("sig") + "_%d" % c])
                    if halo:
                        P.op("dve", lambda e, c=c: e.scalar_tensor_tensor(out=ctmp, in0=PS[1][:, c * 128:(c + 1) * 128], scalar=obc[:, c:c + 1],
                                                                          in1=sig[:, c, :], op0=ALU.add, op1=ALU.mult),
                             reads=[PSN[1], "D_obc", N_("sig") + "_%d" % c], writes=[N_("ctmp")])
                        P.op("dve", lambda e, c=c: e.tensor_scalar(out=cbuf[:, c, 0:32], in0=ctmp[:, 96:128], scalar1=flg[:, 1:2], scalar2=None,
                                                                   op0=ALU.mult), reads=[N_("ctmp"), "flg"], writes=["D_cb_h%d" % c])
                    else:
                        P.op("dve", lambda e, c=c: e.scalar_tensor_tensor(out=cbuf[:, c, 32 + tl * 128:32 + (tl + 1) * 128],
                                                                          in0=PS[1][:, c * 128:(c + 1) * 128], scalar=obc[:, c:c + 1],
                                                                          in1=sig[:, c, :], op0=ALU.add, op1=ALU.mult),
                             reads=[PSN[1], "D_obc", N_("sig") + "_%d" % c], writes=["D_cb_%d_%d" % (tl, c)])
            if halo:
                return [f1, f2, f3, f4, f5, f7], []

            def f8():
                gelu(PS[3], PSN[3], brow[:, 0:512], t0u, N_("t0u"), w1u, N_("w1u"), w2, N_("w2"))

            def f9():
                gelu(PS[4], PSN[4], brow[:, 512:1024], t0v, N_("t0v"), w1v, N_("w1v"), w3, N_("w3"))

            def f10():
                layer_norm_free(w3, N_("w3"), gvn, N_("gvn"), glgb[:, 0:512], "D_glgb", glgb[:, 512:1024], "D_glgb", t0v, N_("t0v"), sm, N_("ln1"))

            def f11():
                for g in range(8):
                    P.op("pe", lambda e, g=g: e.matmul(PS[0][:, g * 64:(g + 1) * 64], lhsT=wsT[:, g, :], rhs=gvn[:, g * 64:(g + 1) * 64],
                                                       start=True, stop=True), reads=["D_wsT", N_("gvn")], writes=[PSN[0]])
                P.op("dve", lambda e: e.tensor_tensor(out=t0u.rearrange("p (g d) -> p g d", d=64), in0=PS[0].rearrange("p (g d) -> p g d", d=64),
                                                      in1=bsT[:, 0:8].unsqueeze(2).to_broadcast([128, 8, 64]), op=ALU.add),
                     reads=[PSN[0], "D_bsT"], writes=[N_("t0u")])
                P.op("pool", lambda e: e.tensor_tensor(out=CC[:, 0:512], in0=w2, in1=t0u, op=ALU.mult), reads=[N_("w2"), N_("t0u")], writes=[N_("CCa")])

            def s1():
                for c in range(4):
                    rn = ["D_cb_%d_%d" % (tl, c), ("D_cb_%d_%d" % (tl - 1, c)) if tl > 0 else ("D_cb_h%d" % c)]
                    for j in range(31):
                        o = tl * 128 + 2 + j
                        P.op("pe", lambda e, c=c, j=j, o=o: e.matmul(PS[5][:, c * 128:(c + 1) * 128], lhsT=cbuf[:, c, o:o + 128], rhs=Dg[:, j, c, :],
                                                                  start=(j == 0), stop=(j == 30)), reads=rn + ["D_Dg%d" % c], writes=[PSN[5]])
                P.op("dve", lambda e: e.tensor_tensor(out=w3, in0=PS[5], in1=cvv[:, 0:512], op=ALU.add), reads=[PSN[5], "D_cvv"], writes=[N_("w3")])

            def s2():
                layer_norm_free(w3, N_("w3"), w2, N_("w2"), cvv[:, 512:1024], "D_cvv", cvv[:, 1024:1536], "D_cvv", t0v, N_("t0v"), sm2, N_("ln2"))
                P.op("act", lambda e: e.activation(out=CC[:, 512:1024], in_=w2, func=AF.Silu), reads=[N_("w2")], writes=[N_("CCb")])

            def s3():
                psb = PS[6].bitcast(BF16)
                for k in range(8):
                    P.op("pe", lambda e, k=k: e.transpose(out=psb[:, k * 128:(k + 1) * 128], in_=CC[:, k * 128:(k + 1) * 128], identity=ident),
                         reads=[N_("CCa"), N_("CCb"), "ident"], writes=[PSN[6]])
                P.op("act", lambda e: e.activation(out=CCT, in_=psb[:, 0:1024].rearrange("p (a b) -> p a b", b=128), func=AF.Copy),
                     reads=[PSN[6]], writes=[N_("CCT")])

            def s4():
                for hh in range(2):
                    bank = 7 if hh == 0 else 5
                    for k in range(8):
                        P.op("pe", lambda e, k=k, hh=hh, bank=bank: e.matmul(PS[bank], lhsT=CCT[:, k, :], rhs=Wo1[:, k, hh * 512:(hh + 1) * 512],
                                                                            start=(k == 0), stop=(k == 7)), reads=[N_("CCT"), wo_n[k]], writes=[PSN[bank]])
                    P.op("dve", lambda e, hh=hh, bank=bank: e.tensor_tensor(out=yb[yi][:, hh * 512:(hh + 1) * 512], in0=PS[bank],
                                                                           in1=xt[b2][:, hh * 512:(hh + 1) * 512], op=ALU.add),
                         reads=[PSN[bank], xn], writes=["D_y%d_%d" % (yi, hh)])
                P.dma("sp", "D_ys%d" % yi, lambda e: e.dma_start(out=X3[tl * 128:(tl + 1) * 128, :], in_=yb[yi]),
                      reads=["D_y%d_0" % yi, "D_y%d_1" % yi])
            return [f1, f2, f3, f4, f5, f6, f7, f8, f9, f10, f11], [s1, s2, s3, s4]

        def coalesce(lst):
            out_, run = [], []
            for it_ in lst:
                if it_[0] == "op" and it_[1] == "pe":
                    run.append(it_)
                else:
                    if run:
                        out_.append(("macro", run))
                        run = []
                    out_.append(it_)
            if run:
                out_.append(("macro", run))
            return out_

        def round_robin(chains):
            chains = [c for c in chains if c]
            idx = [0] * len(chains)
            while True:
                progressed = False
                for ci, c in enumerate(chains):
                    if idx[ci] < len(c):
                        P.replay(c[idx[ci]])
                        idx[ci] += 1
                        progressed = True
                if not progressed:
                    break

        order = [16] + list(range(16))
        stages = {}

        def get(it):
            if it not in stages and 0 <= it < len(order):
                stages[it] = tile_stages(it, order[it])
            return stages.get(it)

        F0_, _ = get(0)
        for f in F0_[0:4]:
            f()
        for it, tl in enumerate(order):
            F, Sn = get(it)
            halo = tl == 16
            F[4]()
            if not halo:
                F[5]()
            chains = []
            fc = F[5] if halo else F[6]
            chains.append(coalesce(P.capture(fc)))
            if not halo:
                chains.append(coalesce(P.capture(F[7])))
                chains.append(coalesce(P.capture(lambda: (F[8](), F[9](), F[10]()))))
            if it >= 1 and stages[it - 1][1]:
                Sp = stages[it - 1][1]
                chains.append(coalesce(P.capture(lambda: [st() for st in Sp])))
            nxt = get(it + 1)
            if nxt is not None:
                Fn = nxt[0]
                chains.append(coalesce(P.capture(lambda: [f() for f in Fn[0:4]])))
            round_robin(chains)
        lastS = stages[len(order) - 1][1]
        for st in lastS:
            st()

    phase_D()
    barrier()
    if stop_after == "D":
        P.emit()
        return nc

    phase_FFN(X3, out, 1, 16)
    barrier()
    P.emit()
    return nc


def _bf16(a):
    import ml_dtypes
    return np.asarray(a, np.float32).astype(ml_dtypes.bfloat16)


def _rope_tables(dim, pos):
    inv = 1.0 / (10000.0 ** (np.arange(0, dim, 2, dtype=np.float32) / dim))
    ang = pos.astype(np.float32)[:, None] * inv[None, :]
    return np.cos(ang).astype(np.float32), np.sin(ang).astype(np.float32)


def host_inputs(inp, core):
    b, hf = core // 2, core % 2
    x = np.asarray(inp["x"], np.float32)
    xin = np.concatenate([x[b, 0:2048], x[b, hf * 2048:(hf + 1) * 2048]], 0)
    pos = np.concatenate([np.arange(2048), hf * 2048 + np.arange(2048)])
    c64, s64 = _rope_tables(64, pos)
    c32, s32 = _rope_tables(32, pos)
    flags = np.zeros((128, 20), np.float32)
    flags[:, 0] = 0.0 if hf == 1 else -30000.0
    flags[:, 1] = 1.0 if hf == 1 else 0.0
    flags[:, 4:12] = 0.0 if hf == 1 else -1e30
    onehot = np.zeros((16, 4096), np.float32)
    for n in range(16):
        onehot[n, n * 256:(n + 1) * 256] = 1.0
    g = lambda k: np.asarray(inp[k], np.float32)
    gains = np.concatenate([g("diff_q_norm_g")[0], g("diff_k_norm_g")[0], g("diff_subln_g")[0], g("moba_q_norm_g")[0],
                            g("moba_k_norm_g")[0], np.zeros(64, np.float32)])[None, :]
    lams = np.concatenate([g("diff_lambda_q1")[0], g("diff_lambda_k1")[0], g("diff_lambda_q2")[0], g("diff_lambda_k2")[0]])[None, :]
    ob = g("odd_b_in")[0]
    m = {
        "xin": xin, "cs64": np.concatenate([c64, s64], 1), "cs32": np.concatenate([c32, s32], 1),
        "flags": flags, "onehot": _bf16(onehot),
        "even_w_in": g("even_w_in")[0], "even_w_out": g("even_w_out")[0],
        "ffn_w_in": g("ffn_w_in"), "ffn_w_out": g("ffn_w_out"),
        "odd_w_in": g("odd_w_in")[0], "odd_w_out": g("odd_w_out")[0],
        "attn_norm_g": g("attn_norm_g"), "ffn_norm_g": g("ffn_norm_g"),
        "gains": gains.astype(np.float32), "lams": lams.astype(np.float32),
        "odd_b_row": ob[None, :].copy(),
        "odd_b_col": np.ascontiguousarray(ob[1024:2048].reshape(8, 128).T),
        "gmlp_ln_gb": np.concatenate([g("gmlp_ln_g")[0], g("gmlp_ln_b")[0]])[None, :],
        "gmlp_wsT": np.ascontiguousarray(g("gmlp_w_s")[0].transpose(2, 0, 1)),
        "gmlp_bsT": np.ascontiguousarray(g("gmlp_b_s")[0].T),
        "conv_wT": np.ascontiguousarray(g("conv_w")[0].T.reshape(4, 128, 31).transpose(1, 0, 2)),
        "conv_vecs": np.concatenate([g("conv_b")[0], g("conv_ln_g")[0], g("conv_ln_b")[0]])[None, :],
    }
    return {k: np.ascontiguousarray(v) for k, v in m.items()}


def kernel(**inputs):
    in_maps = [host_inputs(inputs, c) for c in range(8)]
    nc = build_program()
    res = run_bass_kernel_spmd(nc, in_maps, core_ids=list(range(8)))
    out = np.zeros((4, 4096, 1024), np.float32)
    for c in range(8):
        b, hf = c // 2, c % 2
        out[b, hf * 2048:(hf + 1) * 2048] = np.asarray(res.results[c]["out"], np.float32)
    return out
```

```python
import numpy as np
import concourse.bass as bass
import concourse.mybir as mybir
from concourse.bass_utils import run_bass_kernel_spmd

F32 = mybir.dt.float32
BF16 = mybir.dt.bfloat16
AF = mybir.ActivationFunctionType
ALU = mybir.AluOpType
AX = mybir.AxisListType

ENGS = ["pe", "act", "dve", "pool", "sp"]
EPOCH = 30000


class Prog:
    def __init__(self, nc):
        self.nc = nc
        self.ops = {e: [] for e in ENGS}
        self.count = {e: 0 for e in ENGS}
        self.seen = {e: {} for e in ENGS}
        self.res = {}
        self.lanes = {}
        self.semkeys = []
        self.lane_waited = {}
        self.lane_sem = {}
        self.free_sems = []
        self.nsem = 0
        self._cap = None

    def _deps(self, eng, reads, writes):
        deps = []
        for r in reads:
            st = self.res.get(r)
            if st and st[0] is not None:
                deps.append(st[0])
        for w in writes:
            st = self.res.get(w)
            if st:
                if st[0] is not None:
                    deps.append(st[0])
                deps.extend(st[1].values())
        need = {}
        for (k, v) in deps:
            if eng == "pe" and k == ("pe", "raw"):
                continue
            if self.seen[eng].get(k, 0) < v:
                need[k] = max(need.get(k, 0), v)
        for k, v in need.items():
            self.ops[eng].append(("wait", k, v))
            self.seen[eng][k] = v
            if k[0] == "dma":
                self.lane_waited[k[1]] = max(self.lane_waited.get(k[1], 0), v)

    def _mark(self, tok, reads, writes, rkey):
        for r in reads:
            st = self.res.setdefault(r, [None, {}])
            st[1][rkey] = tok
        for w in writes:
            self.res[w] = [tok, {}]

    def capture(self, f):
        old = self._cap
        self._cap = []
        f()
        lst = self._cap
        self._cap = old
        return lst

    def replay(self, item):
        if item[0] == "op":
            self.op(*item[1:])
        elif item[0] == "dmag":
            self.dma_group(*item[1:])
        else:
            for it in item[1]:
                self.replay(it)

    def op(self, eng, fn, reads=(), writes=()):
        if self._cap is not None:
            self._cap.append(("op", eng, fn, list(reads), list(writes)))
            return None
        self._deps(eng, reads, writes)
        self.count[eng] += 1
        c = self.count[eng]
        key = (eng, "raw")
        tok = (key, c)
        self.ops[eng].append(("op", fn, key, 1, c))
        self._mark(tok, reads, writes, eng)
        return tok

    def dma(self, eng, lane, fn, reads=(), writes=(), multi=False):
        return self.dma_group(eng, lane, [fn], reads, writes)

    def dma_group(self, eng, lane, fns, reads=(), writes=()):
        if self._cap is not None:
            self._cap.append(("dmag", eng, lane, list(fns), list(reads), list(writes)))
            return None
        self._deps(eng, reads, writes)
        if lane not in self.lane_sem:
            self.lane_sem[lane] = (self.nsem, 0)
            self.nsem += 1
        semid, base = self.lane_sem[lane]
        n = self.lanes.get(lane, 0) + len(fns)
        self.lanes[lane] = n
        key = ("dma", semid)
        if key not in self.semkeys:
            self.semkeys.append(key)
        tok = (key, base + 16 * n)
        for fn in fns:
            self.ops[eng].append(("op", fn, key, 16, None))
        self._mark(tok, reads, writes, ("dma", lane))
        return tok

    def retire_lanes(self):
        toks = []
        for lane, n in self.lanes.items():
            semid, base = self.lane_sem[lane]
            toks.append((("dma", semid), base + 16 * n))
        return toks

    def recycle_lanes(self):
        pass

    def wait_all(self, eng, toks):
        for (k, v) in toks:
            if self.seen[eng].get(k, 0) < v:
                self.ops[eng].append(("wait", k, v))
                self.seen[eng][k] = v
                if k[0] == "dma":
                    self.lane_waited[k[1]] = max(self.lane_waited.get(k[1], 0), v)

    def emit(self):
        nc = self.nc
        from contextlib import ExitStack
        waited = {e: set() for e in ENGS}
        for name in ENGS:
            for it in self.ops[name]:
                if it[0] == "wait" and it[1][1] == "raw":
                    waited[it[1][0]].add(it[2])
        rank = {}
        for e_ in ENGS:
            for i, c in enumerate(sorted(waited[e_])):
                rank[(e_, c)] = i + 1
        semkeys = list(self.semkeys)
        for (e_, c), r in rank.items():
            k = (e_, (r - 1) // EPOCH)
            if k not in semkeys:
                semkeys.append(k)
        with ExitStack() as es:
            sems = {}
            for k in semkeys:
                nm = "s_" + "_".join(str(x) for x in k)
                sems[k] = es.enter_context(nc.semaphore(nm))
            block = es.enter_context(nc.Block())

            def run(e, name):
                for it in self.ops[name]:
                    if it[0] == "wait":
                        if it[1][1] == "raw":
                            r = rank[(it[1][0], it[2])]
                            e.wait_ge(sems[(it[1][0], (r - 1) // EPOCH)], (r - 1) % EPOCH + 1)
                        else:
                            e.wait_ge(sems[it[1]], it[2])
                    else:
                        ins = it[1](e)
                        if it[4] is None:
                            ins.then_inc(sems[it[2]], it[3])
                        else:
                            r = rank.get((name, it[4]))
                            if r is not None:
                                ins.then_inc(sems[(name, (r - 1) // EPOCH)], 1)

            @block.tensor
            def _(e):
                run(e, "pe")

            @block.scalar
            def _(e):
                run(e, "act")

            @block.vector
            def _(e):
                run(e, "dve")

            @block.gpsimd
            def _(e):
                run(e, "pool")

            @block.sync
            def _(e):
                run(e, "sp")


NQT = 17
NQ = NQT * 128
NKT = 32
NK = 4096
D = 1024
DFF = 2816
EPS = 1e-6
NEG = -30000.0
LAMBDA_INIT0 = 0.8 - 0.6 * 1.0


class Arena:
    def __init__(self, ap, nbytes):
        self.ap = ap
        self.cap = nbytes
        self.off = 0

    def reset(self):
        self.off = 0

    def alloc(self, shape_free, dt):
        n = 1
        for s in shape_free:
            n *= s
        esz = 4 if dt == F32 else 2
        self.off = (self.off + 63) // 64 * 64
        nb = n * esz
        assert self.off + nb <= self.cap, ("arena overflow", self.off, nb, self.cap)
        v = self.ap[:, self.off // 2:(self.off + nb) // 2]
        self.off += nb
        if dt == F32:
            v = v.bitcast(F32)
        if len(shape_free) == 2:
            v = v.rearrange("p (a b) -> p a b", b=shape_free[1])
        elif len(shape_free) == 3:
            v = v.rearrange("p (a b c) -> p a b c", b=shape_free[1], c=shape_free[2])
        return v


def build_program(stop_after=None, debug=False):
    nc = bass.Bass("TRN2", target_bir_lowering=False)
    dbg_kind = "ExternalOutput" if debug else "Internal"

    def din(name, shape, dt=F32):
        return nc.dram_tensor(name, list(shape), dt, kind="ExternalInput").ap()

    def dscr(name, shape, dt):
        return nc.dram_tensor(name, list(shape), dt, kind=dbg_kind).ap()

    xin = din("xin", [NK, D])
    cs64 = din("cs64", [NK, 64])
    cs32 = din("cs32", [NK, 32])
    flags = din("flags", [128, 20])
    onehot = din("onehot", [16, NK], BF16)
    e_w_in = din("even_w_in", [D, 3072])
    e_w_out = din("even_w_out", [D, D])
    f_w_in = din("ffn_w_in", [2, D, 2 * DFF])
    f_w_out = din("ffn_w_out", [2, DFF, D])
    o_w_in = din("odd_w_in", [D, 2048])
    o_w_out = din("odd_w_out", [D, D])
    attn_g = din("attn_norm_g", [2, D])
    ffn_g = din("ffn_norm_g", [2, D])
    gains = din("gains", [1, 320])
    lams = din("lams", [1, 128])
    o_b_row = din("odd_b_row", [1, 2048])
    o_b_col = din("odd_b_col", [128, 8])
    gl_gb = din("gmlp_ln_gb", [1, 1024])
    wsT_in = din("gmlp_wsT", [128, 8, 128])
    bsT_in = din("gmlp_bsT", [128, 8])
    cwT = din("conv_wT", [128, 4, 31])
    cvec = din("conv_vecs", [1, 1536])
    out = nc.dram_tensor("out", [2048, D], F32, kind="ExternalOutput").ap()

    QdT = dscr("QdT", [512, NQ], BF16)
    KdT = dscr("KdT", [512, NK], BF16)
    Vd = dscr("Vd", [8, NK, 64], BF16)
    QmT = dscr("QmT", [512, NQ], BF16)
    KmT = dscr("KmT", [512, NK], BF16)
    Vm = dscr("Vm", [8, NK, 64], BF16)
    X1 = dscr("X1", [NQ, D], F32)
    X2 = dscr("X2", [NQ, D], F32)
    X3 = dscr("X3", [2048, D], F32)

    ARENA_BYTES = 190 * 1024
    arena_t = nc.alloc_sbuf_tensor("arena", [128, ARENA_BYTES // 2], BF16)
    AR = Arena(arena_t.ap(), ARENA_BYTES)
    PSALL = nc.alloc_psum_tensor("psall", [128, 4096], F32).ap()
    PS = [PSALL[:, i * 512:(i + 1) * 512] for i in range(8)]
    PSN = ["ps%d" % i for i in range(8)]

    P = Prog(nc)
    uid = [0]

    def nm(s):
        uid[0] += 1
        return "%s#%d" % (s, uid[0])

    lane_ctr = [0]

    def newlane(s="l"):
        lane_ctr[0] += 1
        return "%s%d" % (s, lane_ctr[0])

    def barrier():
        toks = []
        for e in ENGS:
            c = P.count[e]
            if c > 0:
                toks.append(((e, "raw"), c))
        toks.extend(P.retire_lanes())
        for e in ENGS:
            P.wait_all(e, toks)
        P.res.clear()
        P.recycle_lanes()

    ident = AR.alloc([128], BF16)
    flg = AR.alloc([20], F32)
    rnd = AR.alloc([8], F32)
    PERSIST = None

    P.op("pool", lambda e: e.memset(ident, 1.0), writes=["ident"])
    P.op("pool", lambda e: e.affine_select(out=ident, in_=ident, pattern=[[-1, 128]], compare_op=ALU.is_equal,
                                           fill=0.0, base=0, channel_multiplier=1), reads=["ident"], writes=["ident"])
    P.dma("sp", "c_flg", lambda e: e.dma_start(out=flg, in_=flags), writes=["flg"])
    AR.off = (AR.off + 63) // 64 * 64
    PERSIST = AR.off

    def reset_arena():
        AR.off = PERSIST

    def rms_rstd(ssq, ssq_name, rs, rs_name, hd_eps):
        P.op("act", lambda e: e.activation(out=rs, in_=ssq, func=AF.Ln, bias=float(hd_eps), scale=1.0),
             reads=[ssq_name], writes=[rs_name])
        P.op("act", lambda e: e.activation(out=rs, in_=rs, func=AF.Exp, scale=-0.5),
             reads=[rs_name], writes=[rs_name])

    def named(ap, name):
        return ap

    def norm_tile(xt, xt_n, gbc, gbc_n, hb, hb_n, sq, sq_n, ssq, rs, pfx):
        P.op("act", lambda e: e.activation(out=sq, in_=xt, func=AF.Square, accum_out=ssq),
             reads=[xt_n], writes=[sq_n, pfx + "_ssq"])
        rms_rstd(ssq, pfx + "_ssq", rs, pfx + "_rs", D * EPS)
        P.op("dve", lambda e: e.scalar_tensor_tensor(out=hb, in0=xt, scalar=rs, in1=gbc, op0=ALU.mult, op1=ALU.mult),
             reads=[xt_n, pfx + "_rs", gbc_n], writes=[hb_n])

    def transpose8(src, src_n, dst, dst_n, psi, nblk=8, evac="act"):
        psb = PS[psi].bitcast(BF16)
        for k in range(nblk):
            P.op("pe", lambda e, k=k: e.transpose(out=psb[:, k * 128:(k + 1) * 128], in_=src[:, k * 128:(k + 1) * 128],
                                                  identity=ident), reads=[src_n, "ident"], writes=[PSN[psi]])
        if evac == "act":
            P.op("act", lambda e: e.activation(out=dst, in_=psb[:, 0:nblk * 128].rearrange("p (a b) -> p a b", b=128),
                                               func=AF.Copy), reads=[PSN[psi]], writes=[dst_n])
        else:
            P.op("dve", lambda e: e.tensor_copy(out=dst, in_=psb[:, 0:nblk * 128].rearrange("p (a b) -> p a b", b=128)),
                 reads=[PSN[psi]], writes=[dst_n])

    def load_gain_bc(dst, dst_n, src_row, lane, scale):
        P.dma("sp", lane, lambda e: e.dma_start(out=dst, in_=src_row.partition_broadcast(128)), writes=[dst_n])
        if scale != 1.0:
            P.op("dve", lambda e: e.tensor_scalar(out=dst, in0=dst, scalar1=float(scale), scalar2=None, op0=ALU.mult),
                 reads=[dst_n], writes=[dst_n])

    def load_w_bf16(dst, dst_n, w_ap, lane, nsplit=4):
        K = dst.shape[1]
        wv = w_ap.rearrange("(k p) n -> p k n", p=128)
        step = max(1, K // nsplit)
        toks = []
        for k0 in range(0, K, step):
            k1 = min(K, k0 + step)
            toks.append(P.dma("pool", "%s_%d" % (lane, k0), lambda e, k0=k0, k1=k1: e.dma_start(out=dst[:, k0:k1, :], in_=wv[:, k0:k1, :]),
                              writes=[dst_n + "_%d" % k0], multi=True))
        return toks

    def phase_A():
        reset_arena()
        Wi = AR.alloc([8, 3072], BF16)
        gbc = AR.alloc([D], F32)
        Gq = AR.alloc([4, 512], F32)
        gsm = AR.alloc([320], F32)
        NC_ = 4
        cst = [AR.alloc([96], F32) for _ in range(NC_)]
        gct = [AR.alloc([4, 2, 64], F32) for _ in range(NC_)]
        xt = [AR.alloc([D], F32) for _ in range(NC_)]
        sq = AR.alloc([D], BF16)
        hbs = [AR.alloc([D], BF16) for _ in range(2)]
        hTs = [AR.alloc([8, 128], BF16) for _ in range(2)]
        ssqs = [AR.alloc([1], F32) for _ in range(2)]
        rss = [AR.alloc([1], F32) for _ in range(2)]
        NS = 4
        sqgs = [AR.alloc([512], F32) for _ in range(NS)]
        qns = [AR.alloc([512], F32) for _ in range(NS)]
        tas = [AR.alloc([512], F32) for _ in range(NS)]
        tbs = [AR.alloc([512], F32) for _ in range(NS)]
        ssqhs = [AR.alloc([16], F32) for _ in range(NS)]
        rshs = [AR.alloc([16], F32) for _ in range(NS)]
        qbs = [AR.alloc([512], BF16) for _ in range(NS)]
        qTs = [AR.alloc([4, 128], BF16) for _ in range(NS)]
        vbs = [AR.alloc([512], BF16) for _ in range(NS)]

        wiv = e_w_in.rearrange("(k p) n -> p k n", p=128)
        for g_ in (1, 2, 4, 5, 0, 3):
            P.dma("pool", "A_w_g%d" % g_, lambda e, g_=g_: e.dma_start(out=Wi[:, :, g_ * 512:(g_ + 1) * 512], in_=wiv[:, :, g_ * 512:(g_ + 1) * 512]),
                  writes=["A_Wi_g%d" % g_])
        load_gain_bc(gbc, "A_gbc", attn_g[0:1, :], "A_g", 32.0)
        P.dma("sp", "A_gs", lambda e: e.dma_start(out=gsm, in_=gains.partition_broadcast(128)), writes=["A_gsm"])
        for gi, (o, hd) in enumerate([(0, 32), (32, 32), (128, 64), (192, 64)]):
            nh = 512 // hd
            P.op("dve", lambda e, gi=gi, o=o, hd=hd, nh=nh: e.tensor_scalar(
                out=Gq[:, gi, :].rearrange("p (a b) -> p a b", b=hd),
                in0=gsm[:, o:o + hd].unsqueeze(1).to_broadcast([128, nh, hd]),
                scalar1=float(hd) ** 0.5, scalar2=None, op0=ALU.mult), reads=["A_gsm"], writes=["A_Gq%d" % gi])

        Gs = AR.alloc([4, 64], F32)
        for gi, (o, hd) in enumerate([(0, 32), (32, 32), (128, 64), (192, 64)]):
            P.op("dve", lambda e, gi=gi, o=o, hd=hd: e.tensor_scalar(out=Gs[:, gi, 0:hd], in0=gsm[:, o:o + hd], scalar1=float(hd) ** 0.5,
                                                                    scalar2=None, op0=ALU.mult), reads=["A_gsm"], writes=["A_Gs"])
        qdt_v = QdT.rearrange("(c p) n -> p c n", p=128)
        kdt_v = KdT.rearrange("(c p) n -> p c n", p=128)
        qmt_v = QmT.rearrange("(c p) n -> p c n", p=128)
        kmt_v = KmT.rearrange("(c p) n -> p c n", p=128)
        vd_v = Vd.rearrange("h k d -> k h d")
        vm_v = Vm.rearrange("h k d -> k h d")

        def prologue(t):
            b2 = t % 2
            c4 = t % NC_
            xtn = "A_xt%d" % c4
            csn = "A_cs%d" % c4
            hb, hT, hbn, hTn = hbs[b2], hTs[b2], "A_hb%d" % b2, "A_hT%d" % b2
            ssq, rs, pfx = ssqs[b2], rss[b2], "A%d" % b2

            def st0():
                P.dma_group("sp", "A_in%d" % c4, [
                    lambda e: e.dma_start(out=xt[c4], in_=xin[t * 128:(t + 1) * 128, :]),
                    lambda e: e.dma_start(out=cst[c4][:, 0:64], in_=cs64[t * 128:(t + 1) * 128, :]),
                    lambda e: e.dma_start(out=cst[c4][:, 64:96], in_=cs32[t * 128:(t + 1) * 128, :])],
                    writes=[xtn, csn + "a", csn + "b"])

            def st1():
                P.op("act", lambda e: e.activation(out=sq, in_=xt[c4], func=AF.Square, accum_out=ssq), reads=[xtn], writes=["A_sq", pfx + "_ssq"])
                rms_rstd(ssq, pfx + "_ssq", rs, pfx + "_rs", D * EPS)
                for gi_, hd_ in enumerate((32, 32, 64, 64)):
                    hf_ = hd_ // 2
                    co_ = 64 if hd_ == 32 else 0
                    for cs_ in range(2):
                        P.op("pool", lambda e, gi_=gi_, hd_=hd_, hf_=hf_, co_=co_, cs_=cs_: e.tensor_tensor(
                            out=gct[c4][:, gi_, cs_, 0:hd_].rearrange("p (b c) -> p b c", c=hf_),
                            in0=Gs[:, gi_, 0:hd_].rearrange("p (b c) -> p b c", c=hf_),
                            in1=cst[c4][:, co_ + cs_ * hf_:co_ + (cs_ + 1) * hf_].unsqueeze(1).to_broadcast([128, 2, hf_]), op=ALU.mult),
                            reads=["A_Gs", csn + "a", csn + "b"], writes=[csn + "g"])

            def st2():
                P.op("dve", lambda e: e.scalar_tensor_tensor(out=hb, in0=xt[c4], scalar=rs, in1=gbc, op0=ALU.mult, op1=ALU.mult),
                     reads=[xtn, pfx + "_rs", "A_gbc"], writes=[hbn])

            def st3():
                transpose8(hb, hbn, hT, hTn, 7)
            return [st0, st1, st2, st3]

        def group_item(t, g, gidx, qidx):
            b2 = t % 2
            c4 = t % NC_
            csn = "A_cs%d" % c4
            hT, hTn = hTs[b2], "A_hT%d" % b2
            psi = gidx % 5
            ps = PS[psi]
            z = gidx % NS

            def s_mm():
                for k in range(8):
                    P.op("pe", lambda e, k=k: e.matmul(ps, lhsT=hT[:, k, :], rhs=Wi[:, k, g * 512:(g + 1) * 512], start=(k == 0), stop=(k == 7)),
                         reads=[hTn, "A_Wi_g%d" % g], writes=[PSN[psi]])
            if g in (2, 5):
                vb, vbn = vbs[z], "A_vb%d" % z
                dst = vd_v if g == 2 else vm_v

                def s_cp():
                    P.op("act", lambda e: e.activation(out=vb, in_=ps, func=AF.Copy), reads=[PSN[psi]], writes=[vbn])

                def s_st():
                    P.dma("sp", "A_vs%d" % z, lambda e: e.dma_start(out=dst[t * 128:(t + 1) * 128, :, :], in_=vb.rearrange("p (h d) -> p h d", d=64)),
                          reads=[vbn])
                return [s_mm, s_cp, s_st]
            isd = g in (0, 1)
            hd = 32 if isd else 64
            half = hd // 2
            nh = 512 // hd
            gi = {0: 0, 1: 1, 3: 2, 4: 3}[g]
            sqg, qn, ta, tb, ssqh, rsh, qb, qT = sqgs[z], qns[z], tas[z], tbs[z], ssqhs[z], rshs[z], qbs[z], qTs[z]
            n_sqg, n_qn, n_ta, n_tb, n_ssqh, n_rsh, n_qb, n_qT = ["A_%s%d" % (x, z) for x in ("sqg", "qn", "ta", "tb", "ssqh", "rsh", "qb", "qT")]
            ssq_v = ssqh[:, 0:nh]
            rs_v = rsh[:, 0:nh]
            co = 64 if isd else 0
            cosv = cst[c4][:, co:co + half]
            sinv = cst[c4][:, co + half:co + 2 * half]
            q4 = qn.rearrange("p (a b c) -> p a b c", b=2, c=half)
            ta4 = ta.rearrange("p (a b c) -> p a b c", b=2, c=half)
            tb4 = tb.rearrange("p (a b c) -> p a b c", b=2, c=half)
            qb4 = qb.rearrange("p (a b c) -> p a b c", b=2, c=half)
            tpi = 5 + (gidx % 2)
            psb = PS[tpi].bitcast(BF16)
            if g in (0, 3):
                dstv = qdt_v if g == 0 else qmt_v
                col = qidx * 128
            else:
                dstv = kdt_v if g == 1 else kmt_v
                col = t * 128

            def s1():
                P.op("act", lambda e: e.activation(out=sqg, in_=ps, func=AF.Square), reads=[PSN[psi]], writes=[n_sqg])

            def s2():
                P.op("dve", lambda e: e.tensor_reduce(out=ssq_v, in_=sqg.rearrange("p (a b) -> p a b", b=hd), axis=AX.X, op=ALU.add),
                     reads=[n_sqg], writes=[n_ssqh])

            def s3():
                rms_rstd(ssq_v, n_ssqh, rs_v, n_rsh, hd * EPS)

            def s4():
                P.op("dve", lambda e: e.tensor_tensor(out=qn.rearrange("p (a b) -> p a b", b=hd), in0=ps.rearrange("p (a b) -> p a b", b=hd),
                                                      in1=rs_v.unsqueeze(2).to_broadcast([128, nh, hd]), op=ALU.mult),
                     reads=[PSN[psi], n_rsh], writes=[n_qn])

            gcv = gct[c4][:, gi, 0, 0:hd].rearrange("p (b c) -> p b c", c=half)
            gsv = gct[c4][:, gi, 1, 0:hd].rearrange("p (b c) -> p b c", c=half)

            def s5():
                pass

            def s6():
                P.op("dve", lambda e: e.tensor_tensor(out=ta4, in0=q4, in1=gcv.unsqueeze(1).to_broadcast([128, nh, 2, half]),
                                                      op=ALU.mult), reads=[n_qn, csn + "g"], writes=[n_ta])
                P.op("pool", lambda e: e.tensor_tensor(out=tb4, in0=q4, in1=gsv.unsqueeze(1).to_broadcast([128, nh, 2, half]),
                                                       op=ALU.mult), reads=[n_qn, csn + "g"], writes=[n_tb])

            def s7():
                P.op("dve", lambda e: e.tensor_tensor(out=qb4[:, :, 0, :], in0=ta4[:, :, 0, :], in1=tb4[:, :, 1, :], op=ALU.subtract),
                     reads=[n_ta, n_tb], writes=[n_qb + "a"])
                P.op("pool", lambda e: e.tensor_tensor(out=qb4[:, :, 1, :], in0=ta4[:, :, 1, :], in1=tb4[:, :, 0, :], op=ALU.add),
                     reads=[n_ta, n_tb], writes=[n_qb + "b"])

            def s8():
                for k in range(4):
                    P.op("pe", lambda e, k=k: e.transpose(out=psb[:, k * 128:(k + 1) * 128], in_=qb[:, k * 128:(k + 1) * 128], identity=ident),
                         reads=[n_qb + "a", n_qb + "b", "ident"], writes=[PSN[tpi]])

            def s9():
                P.op("act", lambda e: e.activation(out=qT, in_=psb[:, 0:512].rearrange("p (a b) -> p a b", b=128), func=AF.Copy),
                     reads=[PSN[tpi]], writes=[n_qT])

            def s10():
                P.dma("sp", "A_qs%d" % z, lambda e: e.dma_start(out=dstv[:, :, col:col + 128], in_=qT), reads=[n_qT])
            return [s_mm, s1, s2, s3, s4, s5, s6, s7, s8, s9, s10]

        items = []
        extra = {}
        first_group_of_tile = {}
        for t in range(NKT):
            if t >= 16:
                qidx = t - 16
            elif t == 15:
                qidx = 16
            else:
                qidx = None
            groups = [1, 2, 4, 5] if qidx is None else [0, 1, 2, 3, 4, 5]
            first_group_of_tile[t] = len(items)
            for g in groups:
                items.append(group_item(t, g, len(items), qidx))
        for st in prologue(0):
            st()
        for t in range(NKT - 1):
            g0 = first_group_of_tile[t]
            for k, st in enumerate(prologue(t + 1)):
                extra.setdefault(g0 + k, []).append(st)
        maxs = max(len(it) for it in items)
        for step in range(len(items) + maxs):
            for sidx in range(maxs - 1, -1, -1):
                g = step - sidx
                if 0 <= g < len(items) and sidx < len(items[g]):
                    items[g][sidx]()
            for st in extra.get(step, []):
                st()

    phase_A()
    barrier()
    fin = []
    if stop_after == "A":
        P.emit()
        return nc

    def phase_BC():
        reset_arena()
        Wo = AR.alloc([8, D], BF16)
        C = AR.alloc([NQT, D], BF16)
        cm = AR.alloc([4, 512], BF16)
        kT = [AR.alloc([NK], BF16) for _ in range(2)]
        vbuf = [AR.alloc([NKT, 128], BF16) for _ in range(2)]
        qa = [AR.alloc([NQ], BF16) for _ in range(2)]
        qb_ = [AR.alloc([NQ], BF16) for _ in range(2)]
        pT = [AR.alloc([2, 512], BF16) for _ in range(3)]
        oT = [AR.alloc([512], BF16) for _ in range(2)]
        tsb = AR.alloc([528], BF16)
        gsm = AR.alloc([320], F32)
        lamv = AR.alloc([128], F32)
        lt = AR.alloc([64], F32)
        lsm = AR.alloc([8], F32)
        Gsub = AR.alloc([4, 64], F32)
        gbj = AR.alloc([9, 16], F32)
        biaspad = AR.alloc([NQT, 128], BF16)
        gball = AR.alloc([NQT, 16], F32)
        ownm = AR.alloc([NQT, 16], F32)
        gmall = AR.alloc([NQT, 16], F32)
        cmpb = AR.alloc([NQT, 16, 16], BF16)
        rnk = AR.alloc([NQT, 16], F32)
        bia1 = AR.alloc([NQT, 16], F32)
        bia2 = AR.alloc([NQT, 16], F32)
        zt = AR.alloc([128], BF16)
        kbs = AR.alloc([16], F32)
        kbb = AR.alloc([16], BF16)
        gm = AR.alloc([16], F32)
        top8 = AR.alloc([8], F32)
        thr = AR.alloc([1], F32)
        rcp = AR.alloc([8], F32)
        oa = AR.alloc([4, 64], F32)
        ob = AR.alloc([4, 64], F32)
        dd = AR.alloc([4, 64], F32)
        sqd = AR.alloc([4, 64], F32)
        ssq4 = AR.alloc([4], F32)
        rs4 = AR.alloc([4], F32)
        CT = AR.alloc([8, 128], BF16)
        xr = [AR.alloc([D], F32) for _ in range(2)]
        x1 = [AR.alloc([D], F32) for _ in range(2)]

        load_w_bf16(Wo, "B_Wo", e_w_out, "B_w", nsplit=2)
        wo_names = ["B_Wo_0"] * 4 + ["B_Wo_4"] * 4
        P.op("pool", lambda e: e.memset(cm, 0.0), writes=["cm"])
        for i in range(4):
            P.op("pool", lambda e, i=i: e.affine_select(out=cm[:, i, :], in_=cm[:, i, :], pattern=[[1, 512]], compare_op=ALU.is_ge,
                                                        fill=NEG, base=-128 * i, channel_multiplier=-1), reads=["cm"], writes=["cm"])
        for b in range(2):
            P.op("pool", lambda e, b=b: e.memset(qa[b][32:64, :], 0.0), writes=["qa%d_z" % b])
            P.op("pool", lambda e, b=b: e.memset(qa[b][64:128, :], 0.0), writes=["qa%d_bias" % b])
            P.op("pool", lambda e, b=b: e.memset(qb_[b][0:32, :], 0.0), writes=["qb%d_z" % b])
            P.op("pool", lambda e, b=b: e.memset(qb_[b][64:128, :], 0.0), writes=["qb%d_z" % b])
            P.op("pool", lambda e, b=b: e.memset(vbuf[b][:, :, 64:128], 0.0), writes=["vb%d_1" % b])
            P.op("pool", lambda e, b=b: e.memset(vbuf[b][:, :, 64:65], 1.0), reads=["vb%d_1" % b], writes=["vb%d_1" % b])
            P.op("pool", lambda e, b=b: e.memset(kT[b][64:128, :], 0.0), writes=["kT%d_oh" % b])
            P.dma("sp", "B_oh%d" % b, lambda e, b=b: e.dma_start(out=kT[b][64:80, :], in_=onehot), writes=["kT%d_oh" % b])
        P.op("pool", lambda e: e.memset(biaspad, 0.0), writes=["biaspad"])
        P.op("pool", lambda e: e.memset(zt, 0.0), writes=["zt"])
        P.dma("sp", "B_gs", lambda e: e.dma_start(out=gsm, in_=gains.partition_broadcast(128)), writes=["B_gsm"])
        P.dma("sp", "B_lm", lambda e: e.dma_start(out=lamv, in_=lams.partition_broadcast(128)), writes=["B_lamv"])
        l4 = lamv.rearrange("p (a b c) -> p a b c", b=2, c=32)
        P.op("dve", lambda e: e.tensor_tensor(out=lt.rearrange("p (a c) -> p a c", c=32), in0=l4[:, :, 0, :], in1=l4[:, :, 1, :],
                                              op=ALU.mult), reads=["B_lamv"], writes=["B_lt"])
        P.op("dve", lambda e: e.tensor_reduce(out=lsm[:, 0:2], in_=lt.rearrange("p (a c) -> p a c", c=32), axis=AX.X, op=ALU.add),
             reads=["B_lt"], writes=["B_lsm"])
        P.op("act", lambda e: e.activation(out=lsm[:, 2:4], in_=lsm[:, 0:2], func=AF.Exp), reads=["B_lsm"], writes=["B_lsm2"])
        P.op("dve", lambda e: e.tensor_tensor(out=lsm[:, 4:5], in0=lsm[:, 3:4], in1=lsm[:, 2:3], op=ALU.subtract),
             reads=["B_lsm2"], writes=["B_lsm3"])
        neglam = lsm[:, 5:6]
        P.op("dve", lambda e: e.tensor_scalar(out=neglam, in0=lsm[:, 4:5], scalar1=-LAMBDA_INIT0, scalar2=None, op0=ALU.add),
             reads=["B_lsm3"], writes=["neglam"])
        P.op("dve", lambda e: e.tensor_scalar(out=Gsub, in0=gsm[:, 64:128].unsqueeze(1).to_broadcast([128, 4, 64]),
                                              scalar1=8.0 * (1.0 - LAMBDA_INIT0), scalar2=None, op0=ALU.mult),
             reads=["B_gsm"], writes=["Gsub"])
        for j in range(8):
            P.op("dve", lambda e, j=j: e.tensor_copy(out=gbj[:, j, :], in_=flg[:, 4:20]), reads=["flg"], writes=["gbj"])
            P.op("dve", lambda e, j=j: e.memset(gbj[:, j, 8 + j:16], -1e30), reads=["gbj"], writes=["gbj"])
        P.op("dve", lambda e: e.memset(gbj[:, 8, :], 0.0), reads=["gbj"], writes=["gbj"])
        P.op("dve", lambda e: e.memset(gbj[:, 8, 7:16], -1e30), reads=["gbj"], writes=["gbj"])

        P.op("dve", lambda e: e.memset(ownm, 1.0), writes=["ownm"])
        for qi in range(NQT):
            jj = qi // 2 if qi < 16 else 8
            ownb = 8 + qi // 2 if qi < 16 else 7
            P.op("dve", lambda e, qi=qi, jj=jj: e.tensor_copy(out=gball[:, qi, :], in_=gbj[:, jj, :]), reads=["gbj"], writes=["gball"])
            P.op("dve", lambda e, qi=qi, ownb=ownb: e.memset(ownm[:, qi, ownb:ownb + 1], 0.0), reads=["ownm"], writes=["ownm"])
        units = [("d", h) for h in range(8)] + [("m", h) for h in range(8)]
        groups = [(gi * 512, 512, 20 + 4 * gi, True, gi * 4) for gi in range(4)] + [(2048, 128, 16, False, 16)]
        gcount = [0]
        sc = [0]

        def issue_loads(u):
            kind, h = units[u]
            b = u % 2
            Ksrc, Vsrc, Qsrc = (KdT, Vd, QdT) if kind == "d" else (KmT, Vm, QmT)
            fns = [lambda e: e.dma_start(out=kT[b][0:64, :], in_=Ksrc[h * 64:(h + 1) * 64, :])]
            for pt in range(4):
                fns.append(lambda e, pt=pt: e.dma_start(out=vbuf[b][:, pt * 8:(pt + 1) * 8, 0:64],
                                                       in_=Vsrc[h].rearrange("(t p) d -> p t d", p=128)[:, pt * 8:(pt + 1) * 8, :]))
            wr = ["kT%d" % b] + ["vb%d_%d" % (b, pt) for pt in range(4)]
            if kind == "d":
                fns.append(lambda e: e.dma_start(out=qa[b][0:32, :], in_=Qsrc[h * 64:h * 64 + 32, :]))
                fns.append(lambda e: e.dma_start(out=qb_[b][32:64, :], in_=Qsrc[h * 64 + 32:h * 64 + 64, :]))
                wr += ["qa%d" % b, "qb%d" % b]
            else:
                fns.append(lambda e: e.dma_start(out=qa[b][0:64, :], in_=Qsrc[h * 64:(h + 1) * 64, :]))
                wr += ["qa%d" % b, "qa%d_z" % b]
            P.dma_group("sp", "B_ld%d" % b, fns, writes=wr)

        def gating_stages(u):
            kind, h = units[u]
            b = u % 2
            ps7b = PS[7].bitcast(BF16)

            def g1():
                P.op("dve", lambda e: e.tensor_reduce(out=kbs[0:64, :], in_=kT[b][0:64, :].rearrange("p (a c) -> p a c", c=256),
                                                      axis=AX.X, op=ALU.add), reads=["kT%d" % b], writes=["kbs"])
                P.op("dve", lambda e: e.tensor_scalar(out=kbb[0:64, :], in0=kbs[0:64, :], scalar1=1.0 / 256.0, scalar2=None, op0=ALU.mult),
                     reads=["kbs"], writes=["kbb"])
                for qi in range(NQT):
                    P.op("pe", lambda e, qi=qi: e.matmul(PS[7][:, qi * 16:(qi + 1) * 16], lhsT=qa[b][0:64, qi * 128:(qi + 1) * 128], rhs=kbb[0:64, :],
                                                        start=True, stop=True), reads=["qa%d" % b, "kbb"], writes=[PSN[7]])

            def g2():
                P.op("dve", lambda e: e.tensor_tensor(out=gmall, in0=PS[7][:, 0:NQT * 16].rearrange("p (q n) -> p q n", n=16), in1=gball, op=ALU.add),
                     reads=[PSN[7], "gball"], writes=["gmall"])
                P.op("dve", lambda e: e.tensor_tensor(out=cmpb, in0=gmall.unsqueeze(2).to_broadcast([128, NQT, 16, 16]),
                                                      in1=gmall.unsqueeze(3).to_broadcast([128, NQT, 16, 16]), op=ALU.is_gt),
                     reads=["gmall"], writes=["cmpb"])
                P.op("dve", lambda e: e.tensor_reduce(out=rnk, in_=cmpb, axis=AX.X, op=ALU.add), reads=["cmpb"], writes=["rnk"])
                P.op("dve", lambda e: e.tensor_scalar(out=bia1, in0=rnk, scalar1=2.5, scalar2=NEG, op0=ALU.is_gt, op1=ALU.mult),
                     reads=["rnk"], writes=["bia1"])
                P.op("dve", lambda e: e.tensor_scalar(out=bia2, in0=gmall, scalar1=-1e29, scalar2=NEG, op0=ALU.is_lt, op1=ALU.mult),
                     reads=["gmall"], writes=["bia2"])
                P.op("dve", lambda e: e.tensor_tensor(out=bia1, in0=bia1, in1=bia2, op=ALU.add), reads=["bia1", "bia2"], writes=["bia1"])
                P.op("dve", lambda e: e.tensor_tensor(out=biaspad[:, :, 64:80], in0=bia1, in1=ownm, op=ALU.mult),
                     reads=["bia1", "ownm"], writes=["biaspad"])

            def g3():
                for q0 in range(0, NQT, 3):
                    nb = min(3, NQT - q0)
                    for k in range(nb):
                        P.op("pe", lambda e, q0=q0, k=k: e.transpose(out=ps7b[:, 544 + k * 128:544 + (k + 1) * 128], in_=biaspad[:, q0 + k, :],
                                                                    identity=ident), reads=["biaspad", "ident"], writes=[PSN[7]])
                    P.op("act", lambda e, q0=q0, nb=nb: e.activation(out=qa[b][64:80, q0 * 128:(q0 + nb) * 128], in_=ps7b[64:80, 544:544 + nb * 128],
                                                                    func=AF.Copy), reads=[PSN[7]], writes=["qa%d_bias" % b])
            return [g1, g2, g3]

        def attention(u, hooks=()):
            kind, h = units[u]
            b = u % 2
            isd = kind == "d"
            dk = 128
            scale = (32.0 ** -0.5) if isd else 0.125
            for gi_, grp in enumerate(groups):
                if gi_ < len(hooks):
                    hooks[gi_]()
                do_group(kind, h, b, isd, dk, scale, (not isd) or len(hooks) == 0, *grp)

        def do_group(kind, h, b, isd, dk, scale, wide, qc0, N, nkt, usepast, t0):
            R = N // 128
            gidx = gcount[0]
            gcount[0] += 1
            nsl = 3 if wide else 2
            if isd and wide:
                maps = [(qa[b], ["qa%d" % b, "qa%d_z" % b, "qa%d_bias" % b], 6), (qb_[b], ["qb%d" % b, "qb%d_z" % b], 7)]
                tbank = 0
            elif isd:
                maps = [(qa[b], ["qa%d" % b, "qa%d_z" % b, "qa%d_bias" % b], 4), (qb_[b], ["qb%d" % b, "qb%d_z" % b], 5)]
                tbank = 6
            elif wide:
                maps = [(qa[b], ["qa%d" % b, "qa%d_bias" % b], 6)]
                tbank = 0
            else:
                maps = [(qa[b], ["qa%d" % b, "qa%d_bias" % b], 4)]
                tbank = 5
            npair = nkt // 2
            steps = [(mi, kp) for mi in range(len(maps)) for kp in range(npair)]
            slot = {}

            def col0_of(kp):
                return 256 if (usepast and kp == npair - 1) else 0

            def qk(i):
                mi, kp = steps[i]
                Q, qn_, _ = maps[mi]
                s_ = sc[0] % nsl
                sc[0] += 1
                slot[i] = s_
                c0_ = col0_of(kp)
                for hfi in range(2):
                    kt = 2 * kp + hfi
                    bank = 2 * s_ + hfi
                    if usepast:
                        di = kt - (nkt - 4)
                    else:
                        di = 0 if kt == nkt - 1 else -1
                    diag = di >= 0
                    P.op("pe", lambda e, kt=kt, bank=bank, diag=diag: e.matmul(PS[bank][:, c0_:N], lhsT=kT[b][0:dk, kt * 128:(kt + 1) * 128],
                                                                             rhs=Q[0:dk, qc0 + c0_:qc0 + N], start=True, stop=not diag),
                         reads=["kT%d" % b, "kT%d_oh" % b] + qn_, writes=[PSN[bank]])
                    if diag:
                        P.op("pe", lambda e, bank=bank, di=di: e.matmul(PS[bank][:, c0_:N], lhsT=ident, rhs=cm[:, di, c0_:N], start=False, stop=True),
                             reads=["ident", "cm"], writes=[PSN[bank]])

            def ex_pv(i):
                mi, kp = steps[i]
                _, _, abank = maps[mi]
                s_ = slot[i]
                c0_ = col0_of(kp)
                src = PSALL[:, 2 * s_ * 512:(2 * s_ + 2) * 512].rearrange("p (b n) -> p b n", n=512)[:, :, c0_:N]
                dst = pT[s_][:, :, c0_:N]
                rd = [PSN[2 * s_], PSN[2 * s_ + 1]]
                if usepast and 2 * kp < 16:
                    P.op("act", lambda e: e.activation(out=dst, in_=src, func=AF.Exp, scale=scale, bias=flg[:, 0:1]),
                         reads=rd + ["flg"], writes=["pT%d" % s_])
                else:
                    P.op("act", lambda e: e.activation(out=dst, in_=src, func=AF.Exp, scale=scale), reads=rd, writes=["pT%d" % s_])
                for hfi in range(2):
                    kt = 2 * kp + hfi
                    P.op("pe", lambda e, kt=kt, hfi=hfi: e.matmul(PS[abank][:, c0_:N], lhsT=vbuf[b][:, kt, :], rhs=pT[s_][:, hfi, c0_:N],
                                                                  start=(kt == 0), stop=(kt == nkt - 1)),
                         reads=["pT%d" % s_, "vb%d_%d" % (b, kt // 8), "vb%d_1" % b], writes=[PSN[abank]])

            lead = nsl - 1
            for i in range(min(lead, len(steps))):
                qk(i)
            for i in range(len(steps)):
                if i + lead < len(steps):
                    qk(i + lead)
                ex_pv(i)
            tpb = PS[tbank].bitcast(BF16)
            for mi, (_, _, abank) in enumerate(maps):
                P.op("dve", lambda e, mi=mi, abank=abank: e.tensor_copy(out=oT[mi][0:65, 0:N], in_=PS[abank][0:65, 0:N]),
                     reads=[PSN[abank]], writes=["oT%d" % mi])
                for r in range(R):
                    c0 = (mi * 4 + r) * 66
                    P.op("pe", lambda e, mi=mi, r=r, c0=c0: e.transpose(out=tpb[:, c0:c0 + 65], in_=oT[mi][0:65, r * 128:(r + 1) * 128],
                                                                      identity=ident[0:65, 0:65]),
                         reads=["oT%d" % mi, "ident"], writes=[PSN[tbank]])
            cn = "C_g%d" % t0
            nt = PSN[tbank]
            fin = tpb
            if wide:
                P.op("dve", lambda e: e.tensor_copy(out=tsb, in_=tpb[:, 0:528]), reads=[PSN[tbank]], writes=["tsb"])
                fin = tsb
                nt = "tsb"
            accA = fin[:, 0:R * 66].rearrange("p (r c) -> p r c", c=66)
            if isd:
                accB = fin[:, 4 * 66:(4 + R) * 66].rearrange("p (r c) -> p r c", c=66)
                P.op("dve", lambda e: e.reciprocal(out=rcp[:, 0:R], in_=accA[:, :, 64]), reads=[nt], writes=["rcpa"])
                P.op("dve", lambda e: e.reciprocal(out=rcp[:, 4:4 + R], in_=accB[:, :, 64]), reads=[nt], writes=["rcpb"])
                P.op("dve", lambda e: e.tensor_tensor(out=oa[:, 0:R, :], in0=accA[:, :, 0:64],
                                                      in1=rcp[:, 0:R].unsqueeze(2).to_broadcast([128, R, 64]), op=ALU.mult),
                     reads=[nt, "rcpa"], writes=["oa"])
                P.op("dve", lambda e: e.tensor_tensor(out=ob[:, 0:R, :], in0=accB[:, :, 0:64],
                                                      in1=rcp[:, 4:4 + R].unsqueeze(2).to_broadcast([128, R, 64]), op=ALU.mult),
                     reads=[nt, "rcpb"], writes=["ob"])
                P.op("dve", lambda e: e.scalar_tensor_tensor(out=dd[:, 0:R, :], in0=ob[:, 0:R, :], scalar=neglam, in1=oa[:, 0:R, :],
                                                             op0=ALU.mult, op1=ALU.add), reads=["oa", "ob", "neglam"], writes=["dd"])
                P.op("pool", lambda e: e.tensor_tensor(out=sqd[:, 0:R, :], in0=dd[:, 0:R, :], in1=dd[:, 0:R, :], op=ALU.mult),
                     reads=["dd"], writes=["sqd"])
                P.op("dve", lambda e: e.tensor_reduce(out=ssq4[:, 0:R], in_=sqd[:, 0:R, :], axis=AX.X, op=ALU.add),
                     reads=["sqd"], writes=["ssq4"])
                rms_rstd(ssq4[:, 0:R], "ssq4", rs4[:, 0:R], "rs4", 64 * EPS)
                P.op("dve", lambda e: e.tensor_tensor(out=dd[:, 0:R, :], in0=dd[:, 0:R, :],
                                                      in1=rs4[:, 0:R].unsqueeze(2).to_broadcast([128, R, 64]), op=ALU.mult),
                     reads=["dd", "rs4"], writes=["dd"])
                P.op("pool", lambda e: e.tensor_tensor(out=C[:, t0:t0 + R, h * 64:(h + 1) * 64], in0=dd[:, 0:R, :], in1=Gsub[:, 0:R, :],
                                                       op=ALU.mult), reads=["dd", "Gsub"], writes=[cn])
            else:
                P.op("dve", lambda e: e.reciprocal(out=rcp[:, 0:R], in_=accA[:, :, 64]), reads=[nt], writes=["rcpa"])
                P.op("dve", lambda e: e.tensor_tensor(out=C[:, t0:t0 + R, 512 + h * 64:512 + (h + 1) * 64], in0=accA[:, :, 0:64],
                                                      in1=rcp[:, 0:R].unsqueeze(2).to_broadcast([128, R, 64]), op=ALU.mult),
                     reads=[nt, "rcpa"], writes=[cn])

        issue_loads(0)
        for u in range(len(units)):
            if u + 1 < len(units):
                issue_loads(u + 1)
            hooks = gating_stages(u + 1) if (u + 1 < len(units) and units[u + 1][0] == "m") else []
            attention(u, hooks)

        CTs = [CT, AR.alloc([8, 128], BF16)]

        def c_pre(tl):
            b2 = tl % 2
            lt_ = 16 + tl if tl < 16 else 15
            cn = "C_g%d" % (tl // 4 * 4 if tl < 16 else 16)
            P.dma("sp", "C_x%d" % b2, lambda e: e.dma_start(out=xr[b2], in_=xin[lt_ * 128:(lt_ + 1) * 128, :]), writes=["C_xr%d" % b2])
            transpose8(C[:, tl, :], cn, CTs[b2], "C_CT%d" % b2, 6 + b2)

        def c_main(tl):
            b2 = tl % 2
            for hh in range(2):
                bank = 2 * b2 + hh
                for k in range(8):
                    P.op("pe", lambda e, k=k, hh=hh, bank=bank: e.matmul(PS[bank], lhsT=CTs[b2][:, k, :], rhs=Wo[:, k, hh * 512:(hh + 1) * 512],
                                                                        start=(k == 0), stop=(k == 7)), reads=["C_CT%d" % b2, wo_names[k]], writes=[PSN[bank]])
                P.op("dve", lambda e, hh=hh, bank=bank: e.tensor_tensor(out=x1[b2][:, hh * 512:(hh + 1) * 512], in0=PS[bank],
                                                                       in1=xr[b2][:, hh * 512:(hh + 1) * 512], op=ALU.add),
                     reads=[PSN[bank], "C_xr%d" % b2], writes=["C_x1%d_%d" % (b2, hh)])
            P.dma("sp", "C_s%d" % b2, lambda e: e.dma_start(out=X1[tl * 128:(tl + 1) * 128, :], in_=x1[b2]),
                  reads=["C_x1%d_0" % b2, "C_x1%d_1" % b2])

        c_pre(0)
        for tl in range(NQT):
            if tl + 1 < NQT:
                c_pre(tl + 1)
            c_main(tl)

    phase_BC()
    barrier()
    if stop_after == "BC":
        P.emit()
        return nc

    def phase_FFN(Xin, Xout, li, ntiles):
        reset_arena()
        gbc = AR.alloc([D], F32)
        xs = [AR.alloc([D], F32) for _ in range(3)]
        xr2 = [AR.alloc([D], F32) for _ in range(2)]
        sq = AR.alloc([D], BF16)
        hbs = [AR.alloc([D], BF16) for _ in range(2)]
        hT = AR.alloc([8, 1152], BF16)
        wo = AR.alloc([22, D], BF16)
        wi = [AR.alloc([8, 256], BF16) for _ in range(3)]
        AT = AR.alloc([22, 1152], BF16)
        sg = [AR.alloc([512], BF16) for _ in range(2)]
        yb = [AR.alloc([D], F32) for _ in range(2)]
        ssqs = [AR.alloc([1], F32) for _ in range(2)]
        rss = [AR.alloc([1], F32) for _ in range(2)]
        pf = "F%d" % li
        load_gain_bc(gbc, pf + "_gbc", ffn_g[li:li + 1, :], pf + "_g", 32.0)
        win_v = f_w_in[li].rearrange("(k p) n -> p k n", p=128)
        wout_v = f_w_out[li].rearrange("(j p) n -> p j n", p=128)
        half = (ntiles + 1) // 2
        passes = [list(range(0, half)), list(range(half, ntiles))]
        cnt = {"pro": 0, "y": 0, "g": 0}

        def load_wi(j):
            b3 = j % 3
            P.dma_group("pool", pf + "_wi%d" % b3, [
                lambda e: e.dma_start(out=wi[b3][:, :, 0:128], in_=win_v[:, :, j * 128:(j + 1) * 128]),
                lambda e: e.dma_start(out=wi[b3][:, :, 128:256], in_=win_v[:, :, DFF + j * 128:DFF + (j + 1) * 128])],
                writes=[pf + "_wi%dg" % b3, pf + "_wi%du" % b3])

        def pro_a(i, tl):
            c = cnt["pro"]
            cnt["pro"] += 1
            x3, h2 = c % 3, c % 2
            xn, hbn, pfx = pf + "_xs%d" % x3, pf + "_hb%d" % h2, pf + "n%d" % h2
            P.dma("sp", pf + "_x%d" % x3, lambda e: e.dma_start(out=xs[x3], in_=Xin[tl * 128:(tl + 1) * 128, :]), writes=[xn])
            norm_tile(xs[x3], xn, gbc, pf + "_gbc", hbs[h2], hbn, sq, pf + "_sq", ssqs[h2], rss[h2], pfx)
            return (hbs[h2], hbn)

        def pro_b(i, hbinfo):
            transpose8(hbinfo[0], hbinfo[1], hT[:, :, i * 128:(i + 1) * 128], pf + "_hT%d" % i, 0)

        for j in range(3):
            load_wi(j)
        for j0 in (0, 11):
            P.dma("pool", pf + "_wo%d" % j0, lambda e, j0=j0: e.dma_start(out=wo[:, j0:j0 + 11, :], in_=wout_v[:, j0:j0 + 11, :]),
                  writes=[pf + "_wo%d" % j0])
        prev_ = None
        for i, tl in enumerate(passes[0]):
            info_ = pro_a(i, tl)
            if prev_ is not None:
                pro_b(*prev_)
            prev_ = (i, info_)
        pro_b(*prev_)
        for pi, tiles in enumerate(passes):
            ntok = len(tiles) * 128
            nsub = (ntok + 511) // 512
            sbw = ((ntok // 128 + nsub - 1) // nsub) * 128
            for j in range(22):
                b3 = j % 3
                if j >= 3:
                    load_wi(j)
                for sb, s0 in enumerate(range(0, ntok, sbw)):
                    n = min(sbw, ntok - s0)
                    gb = cnt["g"] % 2
                    cnt["g"] += 1
                    hnames = [pf + "_hT%d" % i for i in range(s0 // 128, (s0 + n) // 128)]
                    for (col, bank, wn) in ((0, gb, "g"), (128, 2 + gb, "u")):
                        for k in range(8):
                            P.op("pe", lambda e, k=k, col=col, bank=bank, b3=b3, s0=s0, n=n: e.matmul(
                                PS[bank][:, 0:n], lhsT=wi[b3][:, k, col:col + 128], rhs=hT[:, k, s0:s0 + n], start=(k == 0), stop=(k == 7)),
                                reads=hnames + [pf + "_wi%d%s" % (b3, wn)], writes=[PSN[bank]])
                    P.op("act", lambda e, gb=gb, n=n: e.activation(out=sg[gb][:, 0:n], in_=PS[gb][:, 0:n], func=AF.Silu),
                         reads=[PSN[gb]], writes=[pf + "_sg%d" % gb])
                    P.op("dve", lambda e, gb=gb, n=n, j=j, s0=s0: e.tensor_tensor(out=AT[:, j, s0:s0 + n], in0=PS[2 + gb][:, 0:n],
                                                                                in1=sg[gb][:, 0:n], op=ALU.mult),
                         reads=[PSN[2 + gb], pf + "_sg%d" % gb], writes=[pf + "_AT%d_%d" % (sb, j % 2)])
            nxt = passes[pi + 1] if pi + 1 < len(passes) else []
            if nxt:
                for j in range(3):
                    load_wi(j)
            for i, tl in enumerate(tiles):
                hbinfo = pro_a(i, nxt[i]) if i < len(nxt) else None
                yi = cnt["y"] % 2
                cnt["y"] += 1
                P.dma("sp", pf + "_xr%d" % yi, lambda e, tl=tl, yi=yi: e.dma_start(out=xr2[yi], in_=Xin[tl * 128:(tl + 1) * 128, :]),
                      writes=[pf + "_xr%d" % yi])
                for hh in range(2):
                    bank = 4 + (2 * yi + hh)
                    if bank == 7:
                        bank = 3 if False else 7
                    for j in range(22):
                        P.op("pe", lambda e, j=j, i=i, hh=hh, bank=bank: e.matmul(PS[bank], lhsT=AT[:, j, i * 128:(i + 1) * 128],
                                                                               rhs=wo[:, j, hh * 512:(hh + 1) * 512],
                                                                               start=(j == 0), stop=(j == 21)),
                             reads=[pf + "_AT%d_0" % (i * 128 // sbw), pf + "_AT%d_1" % (i * 128 // sbw), pf + "_wo%d" % (0 if j < 11 else 11)],
                             writes=[PSN[bank]])
                    P.op("dve", lambda e, hh=hh, bank=bank, yi=yi: e.tensor_tensor(out=yb[yi][:, hh * 512:(hh + 1) * 512], in0=PS[bank],
                                                                                 in1=xr2[yi][:, hh * 512:(hh + 1) * 512], op=ALU.add),
                         reads=[PSN[bank], pf + "_xr%d" % yi], writes=[pf + "_y%d_%d" % (yi, hh)])
                P.dma("sp", pf + "_ys%d" % yi, lambda e, tl=tl, yi=yi: e.dma_start(out=Xout[tl * 128:(tl + 1) * 128, :], in_=yb[yi]),
                      reads=[pf + "_y%d_0" % yi, pf + "_y%d_1" % yi])
                if hbinfo is not None:
                    pro_b(i, hbinfo)

    phase_FFN(X1, X2, 0, NQT)
    barrier()
    if stop_after == "F0":
        P.emit()
        return nc

    def layer_norm_free(src, src_n, dst, dst_n, g_bc, g_n, b_bc, b_n, tmp, tmp_n, sm, pfx, eng2="pool"):
        P.op("dve", lambda e: e.tensor_reduce(out=sm[:, 0:1], in_=src, axis=AX.X, op=ALU.add), reads=[src_n], writes=[pfx + "_s1"])
        P.op("dve", lambda e: e.tensor_scalar(out=sm[:, 1:2], in0=sm[:, 0:1], scalar1=-1.0 / 512.0, scalar2=None, op0=ALU.mult),
             reads=[pfx + "_s1"], writes=[pfx + "_nm"])
        P.op("dve", lambda e: e.tensor_scalar(out=tmp, in0=src, scalar1=sm[:, 1:2], scalar2=None, op0=ALU.add),
             reads=[src_n, pfx + "_nm"], writes=[tmp_n])
        P.op("act", lambda e: e.activation(out=src, in_=tmp, func=AF.Square, accum_out=sm[:, 2:3]), reads=[tmp_n], writes=[src_n, pfx + "_ss"])
        P.op("act", lambda e: e.activation(out=sm[:, 3:4], in_=sm[:, 2:3], func=AF.Ln, bias=float(EPS), scale=1.0 / 512.0),
             reads=[pfx + "_ss"], writes=[pfx + "_rs"])
        P.op("act", lambda e: e.activation(out=sm[:, 3:4], in_=sm[:, 3:4], func=AF.Exp, scale=-0.5), reads=[pfx + "_rs"], writes=[pfx + "_rs"])
        P.op("dve", lambda e: e.scalar_tensor_tensor(out=tmp, in0=tmp, scalar=sm[:, 3:4], in1=g_bc, op0=ALU.mult, op1=ALU.mult),
             reads=[tmp_n, pfx + "_rs", g_n], writes=[tmp_n])
        P.op(eng2, lambda e: e.tensor_tensor(out=dst, in0=tmp, in1=b_bc, op=ALU.add), reads=[tmp_n, b_n], writes=[dst_n])

    def phase_D():
        reset_arena()
        Wi1 = AR.alloc([8, 2048], BF16)
        Wo1 = AR.alloc([8, D], BF16)
        Dg = AR.alloc([31, 4, 128], BF16)
        wsf = AR.alloc([8, 128], F32)
        wsT = AR.alloc([8, 128], BF16)
        cbuf = AR.alloc([4, 2080], BF16)
        gbc = AR.alloc([D], F32)
        brow = AR.alloc([1024], F32)
        glgb = AR.alloc([1024], F32)
        cvv = AR.alloc([1536], F32)
        bsT = AR.alloc([8], F32)
        obc = AR.alloc([8], F32)
        cw = AR.alloc([4, 31], F32)
        xt = [AR.alloc([D], F32) for _ in range(3)]
        sq = AR.alloc([D], BF16)
        NZ = 2
        S_ = []
        for z in range(NZ):
            S_.append(dict(
                hb=AR.alloc([D], BF16), hT=AR.alloc([8, 128], BF16), sig=AR.alloc([4, 128], F32), ctmp=AR.alloc([128], F32),
                t0u=AR.alloc([512], F32), t0v=AR.alloc([512], F32), w1u=AR.alloc([512], F32), w1v=AR.alloc([512], F32),
                w2=AR.alloc([512], F32), w3=AR.alloc([512], F32), gvn=AR.alloc([512], BF16), CC=AR.alloc([D], BF16),
                CCT=AR.alloc([8, 128], BF16), sm=AR.alloc([8], F32), sm2=AR.alloc([8], F32), ssq=AR.alloc([1], F32), rs=AR.alloc([1], F32)))
        yb = [AR.alloc([D], F32) for _ in range(2)]

        wi1v = o_w_in.rearrange("(k p) n -> p k n", p=128)
        for cb_ in (2, 3, 0, 1):
            P.dma("pool", "D_wi_c%d" % cb_, lambda e, cb_=cb_: e.dma_start(out=Wi1[:, :, cb_ * 512:(cb_ + 1) * 512],
                                                                          in_=wi1v[:, :, cb_ * 512:(cb_ + 1) * 512]), writes=["D_Wi_c%d" % cb_])
        load_w_bf16(Wo1, "D_Wo", o_w_out, "D_wo", nsplit=2)
        wo_n = ["D_Wo_%d" % (k // 4 * 4) for k in range(8)]
        load_gain_bc(gbc, "D_gbc", attn_g[1:2, :], "D_g", 32.0)
        P.dma_group("sp", "D_c", [
            lambda e: e.dma_start(out=brow, in_=o_b_row[:, 0:1024].partition_broadcast(128)),
            lambda e: e.dma_start(out=glgb, in_=gl_gb.partition_broadcast(128)),
            lambda e: e.dma_start(out=cvv, in_=cvec.partition_broadcast(128)),
            lambda e: e.dma_start(out=bsT, in_=bsT_in),
            lambda e: e.dma_start(out=obc, in_=o_b_col),
            lambda e: e.dma_start(out=cw, in_=cwT),
            lambda e: e.dma_start(out=wsf, in_=wsT_in)],
            writes=["D_brow", "D_glgb", "D_cvv", "D_bsT", "D_obc", "D_cw", "D_wsf"])
        P.op("pool", lambda e: e.affine_select(out=wsf, in_=wsf, pattern=[[0, 8], [1, 128]], compare_op=ALU.is_ge, fill=0.0,
                                               base=0, channel_multiplier=-1), reads=["D_wsf"], writes=["D_wsf"])
        P.op("pool", lambda e: e.tensor_copy(out=wsT, in_=wsf), reads=["D_wsf"], writes=["D_wsT"])
        for c in range(4):
            P.op("dve" if c % 2 == 0 else "pool", lambda e, c=c: e.tensor_tensor(
                out=Dg[:, :, c, :], in0=ident.unsqueeze(1).to_broadcast([128, 31, 128]),
                in1=cw[:, c, :].unsqueeze(2).to_broadcast([128, 31, 128]), op=ALU.mult), reads=["ident", "D_cw"], writes=["D_Dg%d" % c])

        def gelu(ps, psn, bias_bc, t0, t0n, w1, w1n, dst, dstn):
            P.op("dve", lambda e: e.tensor_tensor(out=t0, in0=ps, in1=bias_bc, op=ALU.add), reads=[psn, "D_brow"], writes=[t0n])
            P.op("act", lambda e: e.activation(out=w1, in_=t0, func=AF.Square), reads=[t0n], writes=[w1n])
            P.op("dve", lambda e: e.tensor_scalar(out=w1, in0=w1, scalar1=0.044715, scalar2=1.0, op0=ALU.mult, op1=ALU.add),
                 reads=[w1n], writes=[w1n])
            P.op("pool", lambda e: e.tensor_tensor(out=w1, in0=w1, in1=t0, op=ALU.mult), reads=[w1n, t0n], writes=[w1n])
            P.op("act", lambda e: e.activation(out=w1, in_=w1, func=AF.Sigmoid, scale=1.5957691216057308), reads=[w1n], writes=[w1n])
            P.op("pool", lambda e: e.tensor_tensor(out=dst, in0=w1, in1=t0, op=ALU.mult), reads=[w1n, t0n], writes=[dstn])

        def tile_stages(it, tl):
            b2 = it % 3
            z = it % NZ
            B = S_[z]
            N_ = lambda nme: "D_%s%d" % (nme, z)
            hb, hT, sig, ctmp, t0u, t0v, w1u, w1v, w2, w3, gvn, CC, CCT, sm, sm2 = (B[k] for k in (
                "hb", "hT", "sig", "ctmp", "t0u", "t0v", "w1u", "w1v", "w2", "w3", "gvn", "CC", "CCT", "sm", "sm2"))
            halo = tl == 16
            xn = "D_xt%d" % b2
            yi = it % 2

            def f1():
                P.dma("sp", "D_x%d" % b2, lambda e: e.dma_start(out=xt[b2], in_=X2[tl * 128:(tl + 1) * 128, :]), writes=[xn])

            def f2():
                P.op("act", lambda e: e.activation(out=sq, in_=xt[b2], func=AF.Square, accum_out=B["ssq"]), reads=[xn], writes=["D_sq", N_("n") + "_ssq"])
                rms_rstd(B["ssq"], N_("n") + "_ssq", B["rs"], N_("n") + "_rs", D * EPS)

            def f3():
                P.op("dve", lambda e: e.scalar_tensor_tensor(out=hb, in0=xt[b2], scalar=B["rs"], in1=gbc, op0=ALU.mult, op1=ALU.mult),
                     reads=[xn, N_("n") + "_rs", "D_gbc"], writes=[N_("hb")])

            def f4():
                transpose8(hb, N_("hb"), hT, N_("hT"), 0)

            def f5():
                for (base, bank) in ((1024, 1), (1536, 2)):
                    for c in range(4):
                        for k in range(8):
                            P.op("pe", lambda e, base=base, bank=bank, c=c, k=k: e.matmul(
                                PS[bank][:, c * 128:(c + 1) * 128], lhsT=Wi1[:, k, base + c * 128:base + (c + 1) * 128], rhs=hT[:, k, :],
                                start=(k == 0), stop=(k == 7)), reads=[N_("hT"), "D_Wi_c%d" % (base // 512)], writes=[PSN[bank]])

            def f6():
                for (base, bank) in ((0, 3), (512, 4)):
                    for k in range(8):
                        P.op("pe", lambda e, base=base, bank=bank, k=k: e.matmul(PS[bank], lhsT=hT[:, k, :], rhs=Wi1[:, k, base:base + 512],
                                                                              start=(k == 0), stop=(k == 7)),
                             reads=[N_("hT"), "D_Wi_c%d" % (base // 512)], writes=[PSN[bank]])

            def f7():
                for c in range(4):
                    P.op("act", lambda e, c=c: e.activation(out=sig[:, c, :], in_=PS[2][:, c * 128:(c + 1) * 128], func=AF.Sigmoid,
                                                            bias=obc[:, 4 + c:5 + c]), reads=[PSN[2], "D_obc"], writes=[N_("sig") + "_%d" % c])
                    if halo:
                        P.op("dve", lambda e, c=c: e.scalar_tensor_tensor(out=ctmp, in0=PS[1][:, c * 128:(c + 1) * 128], scalar=obc[:, c:c + 1],
                                                                          in1=sig[:, c, :], op0=ALU.add, op1=ALU.mult),
                             reads=[PSN[1], "D_obc", N_("sig") + "_%d" % c], writes=[N_("ctmp")])
                        P.op("dve", lambda e, c=c: e.tensor_scalar(out=cbuf[:, c, 0:32], in0=ctmp[:, 96:128], scalar1=flg[:, 1:2], scalar2=None,
                                                                   op0=ALU.mult), reads=[N_("ctmp"), "flg"], writes=["D_cb_h%d" % c])
                    else:
                        P.op("dve", lambda e, c=c: e.scalar_tensor_tensor(out=cbuf[:, c, 32 + tl * 128:32 + (tl + 1) * 128],
                                                                          in0=PS[1][:, c * 128:(c + 1) * 128], scalar=obc[:, c:c + 1],
                                                                          in1=sig[:, c, :], op0=ALU.add, op1=ALU.mult),
                             reads=[PSN[1], "D_obc", N_("sig") + "_%d" % c], writes=["D_cb_%d_%d" % (tl, c)])
            if halo:
                return [f1, f2, f3, f4, f5, f7], []

            def f8():
                gelu(PS[3], PSN[3], brow[:, 0:512], t0u, N_("t0u"), w1u, N_("w1u"), w2, N_("w2"))

            def f9():
                gelu(PS[4], PSN[4], brow[:, 512:1024], t0v, N_("t0v"), w1v, N_("w1v"), w3, N_("w3"))

            def f10():
                layer_norm_free(w3, N_("w3"), gvn, N_("gvn"), glgb[:, 0:512], "D_glgb", glgb[:, 512:1024], "D_glgb", t0v, N_("t0v"), sm, N_("ln1"))

            def f11():
                for g in range(8):
                    P.op("pe", lambda e, g=g: e.matmul(PS[0][:, g * 64:(g + 1) * 64], lhsT=wsT[:, g, :], rhs=gvn[:, g * 64:(g + 1) * 64],
                                                       start=True, stop=True), reads=["D_wsT", N_("gvn")], writes=[PSN[0]])
                P.op("dve", lambda e: e.tensor_tensor(out=t0u.rearrange("p (g d) -> p g d", d=64), in0=PS[0].rearrange("p (g d) -> p g d", d=64),
                                                      in1=bsT[:, 0:8].unsqueeze(2).to_broadcast([128, 8, 64]), op=ALU.add),
                     reads=[PSN[0], "D_bsT"], writes=[N_("t0u")])
                P.op("pool", lambda e: e.tensor_tensor(out=CC[:, 0:512], in0=w2, in1=t0u, op=ALU.mult), reads=[N_("w2"), N_("t0u")], writes=[N_("CCa")])

            def s1():
                for c in range(4):
                    rn = ["D_cb_%d_%d" % (tl, c), ("D_cb_%d_%d" % (tl - 1, c)) if tl > 0 else ("D_cb_h%d" % c)]
                    for j in range(31):
                        o = tl * 128 + 2 + j
                        P.op("pe", lambda e, c=c, j=j, o=o: e.matmul(PS[5][:, c * 128:(c + 1) * 128], lhsT=cbuf[:, c, o:o + 128], rhs=Dg[:, j, c, :],
                                                                  start=(j == 0), stop=(j == 30)), reads=rn + ["D_Dg%d" % c], writes=[PSN[5]])
                P.op("dve", lambda e: e.tensor_tensor(out=w3, in0=PS[5], in1=cvv[:, 0:512], op=ALU.add), reads=[PSN[5], "D_cvv"], writes=[N_("w3")])

            def s2():
                layer_norm_free(w3, N_("w3"), w2, N_("w2"), cvv[:, 512:1024], "D_cvv", cvv[:, 1024:1536], "D_cvv", t0v, N_("t0v"), sm2, N_("ln2"))
                P.op("act", lambda e: e.activation(out=CC[:, 512:1024], in_=w2, func=AF.Silu), reads=[N_("w2")], writes=[N_("CCb")])

            def s3():
                psb = PS[6].bitcast(BF16)
                for k in range(8):
                    P.op("pe", lambda e, k=k: e.transpose(out=psb[:, k * 128:(k + 1) * 128], in_=CC[:, k * 128:(k + 1) * 128], identity=ident),
                         reads=[N_("CCa"), N_("CCb"), "ident"], writes=[PSN[6]])
                P.op("act", lambda e: e.activation(out=CCT, in_=psb[:, 0:1024].rearrange("p (a b) -> p a b", b=128), func=AF.Copy),
                     reads=[PSN[6]], writes=[N_("CCT")])

            def s4():
                for hh in range(2):
                    bank = 7 if hh == 0 else 5
                    for k in range(8):
                        P.op("pe", lambda e, k=k, hh=hh, bank=bank: e.matmul(PS[bank], lhsT=CCT[:, k, :], rhs=Wo1[:, k, hh * 512:(hh + 1) * 512],
                                                                            start=(k == 0), stop=(k == 7)), reads=[N_("CCT"), wo_n[k]], writes=[PSN[bank]])
                    P.op("dve", lambda e, hh=hh, bank=bank: e.tensor_tensor(out=yb[yi][:, hh * 512:(hh + 1) * 512], in0=PS[bank],
                                                                           in1=xt[b2][:, hh * 512:(hh + 1) * 512], op=ALU.add),
                         reads=[PSN[bank], xn], writes=["D_y%d_%d" % (yi, hh)])
                P.dma("sp", "D_ys%d" % yi, lambda e: e.dma_start(out=X3[tl * 128:(tl + 1) * 128, :], in_=yb[yi]),
                      reads=["D_y%d_0" % yi, "D_y%d_1" % yi])
            return [f1, f2, f3, f4, f5, f6, f7, f8, f9, f10, f11], [s1, s2, s3, s4]

        def coalesce(lst):
            out_, run = [], []
            for it_ in lst:
                if it_[0] == "op" and it_[1] == "pe":
                    run.append(it_)
                else:
                    if run:
                        out_.append(("macro", run))
                        run = []
                    out_.append(it_)
            if run:
                out_.append(("macro", run))
            return out_

        def round_robin(chains):
            chains = [c for c in chains if c]
            idx = [0] * len(chains)
            while True:
                progressed = False
                for ci, c in enumerate(chains):
                    if idx[ci] < len(c):
                        P.replay(c[idx[ci]])
                        idx[ci] += 1
                        progressed = True
                if not progressed:
                    break

        order = [16] + list(range(16))
        stages = {}

        def get(it):
            if it not in stages and 0 <= it < len(order):
                stages[it] = tile_stages(it, order[it])
            return stages.get(it)

        F0_, _ = get(0)
        for f in F0_[0:4]:
            f()
        for it, tl in enumerate(order):
            F, Sn = get(it)
            halo = tl == 16
            F[4]()
            if not halo:
                F[5]()
            chains = []
            fc = F[5] if halo else F[6]
            chains.append(coalesce(P.capture(fc)))
            if not halo:
                chains.append(coalesce(P.capture(F[7])))
                chains.append(coalesce(P.capture(lambda: (F[8](), F[9](), F[10]()))))
            if it >= 1 and stages[it - 1][1]:
                Sp = stages[it - 1][1]
                chains.append(coalesce(P.capture(lambda: [st() for st in Sp])))
            nxt = get(it + 1)
            if nxt is not None:
                Fn = nxt[0]
                chains.append(coalesce(P.capture(lambda: [f() for f in Fn[0:4]])))
            round_robin(chains)
        lastS = stages[len(order) - 1][1]
        for st in lastS:
            st()

    phase_D()
    barrier()
    if stop_after == "D":
        P.emit()
        return nc

    phase_FFN(X3, out, 1, 16)
    barrier()
    P.emit()
    return nc


def _bf16(a):
    import ml_dtypes
    return np.asarray(a, np.float32).astype(ml_dtypes.bfloat16)


def _rope_tables(dim, pos):
    inv = 1.0 / (10000.0 ** (np.arange(0, dim, 2, dtype=np.float32) / dim))
    ang = pos.astype(np.float32)[:, None] * inv[None, :]
    return np.cos(ang).astype(np.float32), np.sin(ang).astype(np.float32)


def host_inputs(inp, core):
    b, hf = core // 2, core % 2
    x = np.asarray(inp["x"], np.float32)
    xin = np.concatenate([x[b, 0:2048], x[b, hf * 2048:(hf + 1) * 2048]], 0)
    pos = np.concatenate([np.arange(2048), hf * 2048 + np.arange(2048)])
    c64, s64 = _rope_tables(64, pos)
    c32, s32 = _rope_tables(32, pos)
    flags = np.zeros((128, 20), np.float32)
    flags[:, 0] = 0.0 if hf == 1 else -30000.0
    flags[:, 1] = 1.0 if hf == 1 else 0.0
    flags[:, 4:12] = 0.0 if hf == 1 else -1e30
    onehot = np.zeros((16, 4096), np.float32)
    for n in range(16):
        onehot[n, n * 256:(n + 1) * 256] = 1.0
    g = lambda k: np.asarray(inp[k], np.float32)
    gains = np.concatenate([g("diff_q_norm_g")[0], g("diff_k_norm_g")[0], g("diff_subln_g")[0], g("moba_q_norm_g")[0],
                            g("moba_k_norm_g")[0], np.zeros(64, np.float32)])[None, :]
    lams = np.concatenate([g("diff_lambda_q1")[0], g("diff_lambda_k1")[0], g("diff_lambda_q2")[0], g("diff_lambda_k2")[0]])[None, :]
    ob = g("odd_b_in")[0]
    m = {
        "xin": xin, "cs64": np.concatenate([c64, s64], 1), "cs32": np.concatenate([c32, s32], 1),
        "flags": flags, "onehot": _bf16(onehot),
        "even_w_in": g("even_w_in")[0], "even_w_out": g("even_w_out")[0],
        "ffn_w_in": g("ffn_w_in"), "ffn_w_out": g("ffn_w_out"),
        "odd_w_in": g("odd_w_in")[0], "odd_w_out": g("odd_w_out")[0],
        "attn_norm_g": g("attn_norm_g"), "ffn_norm_g": g("ffn_norm_g"),
        "gains": gains.astype(np.float32), "lams": lams.astype(np.float32),
        "odd_b_row": ob[None, :].copy(),
        "odd_b_col": np.ascontiguousarray(ob[1024:2048].reshape(8, 128).T),
        "gmlp_ln_gb": np.concatenate([g("gmlp_ln_g")[0], g("gmlp_ln_b")[0]])[None, :],
        "gmlp_wsT": np.ascontiguousarray(g("gmlp_w_s")[0].transpose(2, 0, 1)),
        "gmlp_bsT": np.ascontiguousarray(g("gmlp_b_s")[0].T),
        "conv_wT": np.ascontiguousarray(g("conv_w")[0].T.reshape(4, 128, 31).transpose(1, 0, 2)),
        "conv_vecs": np.concatenate([g("conv_b")[0], g("conv_ln_g")[0], g("conv_ln_b")[0]])[None, :],
    }
    return {k: np.ascontiguousarray(v) for k, v in m.items()}


def kernel(**inputs):
    in_maps = [host_inputs(inputs, c) for c in range(8)]
    nc = build_program()
    res = run_bass_kernel_spmd(nc, in_maps, core_ids=list(range(8)))
    out = np.zeros((4, 4096, 1024), np.float32)
    for c in range(8):
        b, hf = c // 2, c % 2
        out[b, hf * 2048:(hf + 1) * 2048] = np.asarray(res.results[c]["out"], np.float32)
    return out
```

```python
import numpy as np
import concourse.bass as bass
import concourse.mybir as mybir
from concourse.bass_utils import run_bass_kernel_spmd

F32 = mybir.dt.float32
BF16 = mybir.dt.bfloat16
AF = mybir.ActivationFunctionType
ALU = mybir.AluOpType
AX = mybir.AxisListType

ENGS = ["pe", "act", "dve", "pool", "sp"]
EPOCH = 30000


class Prog:
    def __init__(self, nc):
        self.nc = nc
        self.ops = {e: [] for e in ENGS}
        self.count = {e: 0 for e in ENGS}
        self.seen = {e: {} for e in ENGS}
        self.res = {}
        self.lanes = {}
        self.semkeys = []
        self.lane_waited = {}
        self.lane_sem = {}
        self.free_sems = []
        self.nsem = 0
        self._cap = None

    def _deps(self, eng, reads, writes):
        deps = []
        for r in reads:
            st = self.res.get(r)
            if st and st[0] is not None:
                deps.append(st[0])
        for w in writes:
            st = self.res.get(w)
            if st:
                if st[0] is not None:
                    deps.append(st[0])
                deps.extend(st[1].values())
        need = {}
        for (k, v) in deps:
            if eng == "pe" and k == ("pe", "raw"):
                continue
            if self.seen[eng].get(k, 0) < v:
                need[k] = max(need.get(k, 0), v)
        for k, v in need.items():
            self.ops[eng].append(("wait", k, v))
            self.seen[eng][k] = v
            if k[0] == "dma":
                self.lane_waited[k[1]] = max(self.lane_waited.get(k[1], 0), v)

    def _mark(self, tok, reads, writes, rkey):
        for r in reads:
            st = self.res.setdefault(r, [None, {}])
            st[1][rkey] = tok
        for w in writes:
            self.res[w] = [tok, {}]

    def capture(self, f):
        old = self._cap
        self._cap = []
        f()
        lst = self._cap
        self._cap = old
        return lst

    def replay(self, item):
        if item[0] == "op":
            self.op(*item[1:])
        elif item[0] == "dmag":
            self.dma_group(*item[1:])
        else:
            for it in item[1]:
                self.replay(it)

    def op(self, eng, fn, reads=(), writes=()):
        if self._cap is not None:
            self._cap.append(("op", eng, fn, list(reads), list(writes)))
            return None
        self._deps(eng, reads, writes)
        self.count[eng] += 1
        c = self.count[eng]
        key = (eng, "raw")
        tok = (key, c)
        self.ops[eng].append(("op", fn, key, 1, c))
        self._mark(tok, reads, writes, eng)
        return tok

    def dma(self, eng, lane, fn, reads=(), writes=(), multi=False):
        return self.dma_group(eng, lane, [fn], reads, writes)

    def dma_group(self, eng, lane, fns, reads=(), writes=()):
        if self._cap is not None:
            self._cap.append(("dmag", eng, lane, list(fns), list(reads), list(writes)))
            return None
        self._deps(eng, reads, writes)
        if lane not in self.lane_sem:
            self.lane_sem[lane] = (self.nsem, 0)
            self.nsem += 1
        semid, base = self.lane_sem[lane]
        n = self.lanes.get(lane, 0) + len(fns)
        self.lanes[lane] = n
        key = ("dma", semid)
        if key not in self.semkeys:
            self.semkeys.append(key)
        tok = (key, base + 16 * n)
        for fn in fns:
            self.ops[eng].append(("op", fn, key, 16, None))
        self._mark(tok, reads, writes, ("dma", lane))
        return tok

    def retire_lanes(self):
        toks = []
        for lane, n in self.lanes.items():
            semid, base = self.lane_sem[lane]
            toks.append((("dma", semid), base + 16 * n))
        return toks

    def recycle_lanes(self):
        pass

    def wait_all(self, eng, toks):
        for (k, v) in toks:
            if self.seen[eng].get(k, 0) < v:
                self.ops[eng].append(("wait", k, v))
                self.seen[eng][k] = v
                if k[0] == "dma":
                    self.lane_waited[k[1]] = max(self.lane_waited.get(k[1], 0), v)

    def emit(self):
        nc = self.nc
        from contextlib import ExitStack
        waited = {e: set() for e in ENGS}
        for name in ENGS:
            for it in self.ops[name]:
                if it[0] == "wait" and it[1][1] == "raw":
                    waited[it[1][0]].add(it[2])
        rank = {}
        for e_ in ENGS:
            for i, c in enumerate(sorted(waited[e_])):
                rank[(e_, c)] = i + 1
        semkeys = list(self.semkeys)
        for (e_, c), r in rank.items():
            k = (e_, (r - 1) // EPOCH)
            if k not in semkeys:
                semkeys.append(k)
        with ExitStack() as es:
            sems = {}
            for k in semkeys:
                nm = "s_" + "_".join(str(x) for x in k)
                sems[k] = es.enter_context(nc.semaphore(nm))
            block = es.enter_context(nc.Block())

            def run(e, name):
                for it in self.ops[name]:
                    if it[0] == "wait":
                        if it[1][1] == "raw":
                            r = rank[(it[1][0], it[2])]
                            e.wait_ge(sems[(it[1][0], (r - 1) // EPOCH)], (r - 1) % EPOCH + 1)
                        else:
                            e.wait_ge(sems[it[1]], it[2])
                    else:
                        ins = it[1](e)
                        if it[4] is None:
                            ins.then_inc(sems[it[2]], it[3])
                        else:
                            r = rank.get((name, it[4]))
                            if r is not None:
                                ins.then_inc(sems[(name, (r - 1) // EPOCH)], 1)

            @block.tensor
            def _(e):
                run(e, "pe")

            @block.scalar
            def _(e):
                run(e, "act")

            @block.vector
            def _(e):
                run(e, "dve")

            @block.gpsimd
            def _(e):
                run(e, "pool")

            @block.sync
            def _(e):
                run(e, "sp")


NQT = 17
NQ = NQT * 128
NKT = 32
NK = 4096
D = 1024
DFF = 2816
EPS = 1e-6
NEG = -30000.0
LAMBDA_INIT0 = 0.8 - 0.6 * 1.0


class Arena:
    def __init__(self, ap, nbytes):
        self.ap = ap
        self.cap = nbytes
        self.off = 0

    def reset(self):
        self.off = 0

    def alloc(self, shape_free, dt):
        n = 1
        for s in shape_free:
            n *= s
        esz = 4 if dt == F32 else 2
        self.off = (self.off + 63) // 64 * 64
        nb = n * esz
        assert self.off + nb <= self.cap, ("arena overflow", self.off, nb, self.cap)
        v = self.ap[:, self.off // 2:(self.off + nb) // 2]
        self.off += nb
        if dt == F32:
            v = v.bitcast(F32)
        if len(shape_free) == 2:
            v = v.rearrange("p (a b) -> p a b", b=shape_free[1])
        elif len(shape_free) == 3:
            v = v.rearrange("p (a b c) -> p a b c", b=shape_free[1], c=shape_free[2])
        return v


def build_program(stop_after=None, debug=False):
    nc = bass.Bass("TRN2", target_bir_lowering=False)
    dbg_kind = "ExternalOutput" if debug else "Internal"

    def din(name, shape, dt=F32):
        return nc.dram_tensor(name, list(shape), dt, kind="ExternalInput").ap()

    def dscr(name, shape, dt):
        return nc.dram_tensor(name, list(shape), dt, kind=dbg_kind).ap()

    xin = din("xin", [NK, D])
    cs64 = din("cs64", [NK, 64])
    cs32 = din("cs32", [NK, 32])
    flags = din("flags", [128, 20])
    onehot = din("onehot", [16, NK], BF16)
    e_w_in = din("even_w_in", [D, 3072])
    e_w_out = din("even_w_out", [D, D])
    f_w_in = din("ffn_w_in", [2, D, 2 * DFF])
    f_w_out = din("ffn_w_out", [2, DFF, D])
    o_w_in = din("odd_w_in", [D, 2048])
    o_w_out = din("odd_w_out", [D, D])
    attn_g = din("attn_norm_g", [2, D])
    ffn_g = din("ffn_norm_g", [2, D])
    gains = din("gains", [1, 320])
    lams = din("lams", [1, 128])
    o_b_row = din("odd_b_row", [1, 2048])
    o_b_col = din("odd_b_col", [128, 8])
    gl_gb = din("gmlp_ln_gb", [1, 1024])
    wsT_in = din("gmlp_wsT", [128, 8, 128])
    bsT_in = din("gmlp_bsT", [128, 8])
    cwT = din("conv_wT", [128, 4, 31])
    cvec = din("conv_vecs", [1, 1536])
    out = nc.dram_tensor("out", [2048, D], F32, kind="ExternalOutput").ap()

    QdT = dscr("QdT", [512, NQ], BF16)
    KdT = dscr("KdT", [512, NK], BF16)
    Vd = dscr("Vd", [8, NK, 64], BF16)
    QmT = dscr("QmT", [512, NQ], BF16)
    KmT = dscr("KmT", [512, NK], BF16)
    Vm = dscr("Vm", [8, NK, 64], BF16)
    X1 = dscr("X1", [NQ, D], F32)
    X2 = dscr("X2", [NQ, D], F32)
    X3 = dscr("X3", [2048, D], F32)

    ARENA_BYTES = 190 * 1024
    arena_t = nc.alloc_sbuf_tensor("arena", [128, ARENA_BYTES // 2], BF16)
    AR = Arena(arena_t.ap(), ARENA_BYTES)
    PSALL = nc.alloc_psum_tensor("psall", [128, 4096], F32).ap()
    PS = [PSALL[:, i * 512:(i + 1) * 512] for i in range(8)]
    PSN = ["ps%d" % i for i in range(8)]

    P = Prog(nc)
    uid = [0]

    def nm(s):
        uid[0] += 1
        return "%s#%d" % (s, uid[0])

    lane_ctr = [0]

    def newlane(s="l"):
        lane_ctr[0] += 1
        return "%s%d" % (s, lane_ctr[0])

    def barrier():
        toks = []
        for e in ENGS:
            c = P.count[e]
            if c > 0:
                toks.append(((e, "raw"), c))
        toks.extend(P.retire_lanes())
        for e in ENGS:
            P.wait_all(e, toks)
        P.res.clear()
        P.recycle_lanes()

    ident = AR.alloc([128], BF16)
    flg = AR.alloc([20], F32)
    rnd = AR.alloc([8], F32)
    PERSIST = None

    P.op("pool", lambda e: e.memset(ident, 1.0), writes=["ident"])
    P.op("pool", lambda e: e.affine_select(out=ident, in_=ident, pattern=[[-1, 128]], compare_op=ALU.is_equal,
                                           fill=0.0, base=0, channel_multiplier=1), reads=["ident"], writes=["ident"])
    P.dma("sp", "c_flg", lambda e: e.dma_start(out=flg, in_=flags), writes=["flg"])
    AR.off = (AR.off + 63) // 64 * 64
    PERSIST = AR.off

    def reset_arena():
        AR.off = PERSIST

    def rms_rstd(ssq, ssq_name, rs, rs_name, hd_eps):
        P.op("act", lambda e: e.activation(out=rs, in_=ssq, func=AF.Ln, bias=float(hd_eps), scale=1.0),
             reads=[ssq_name], writes=[rs_name])
        P.op("act", lambda e: e.activation(out=rs, in_=rs, func=AF.Exp, scale=-0.5),
             reads=[rs_name], writes=[rs_name])

    def named(ap, name):
        return ap

    def norm_tile(xt, xt_n, gbc, gbc_n, hb, hb_n, sq, sq_n, ssq, rs, pfx):
        P.op("act", lambda e: e.activation(out=sq, in_=xt, func=AF.Square, accum_out=ssq),
             reads=[xt_n], writes=[sq_n, pfx + "_ssq"])
        rms_rstd(ssq, pfx + "_ssq", rs, pfx + "_rs", D * EPS)
        P.op("dve", lambda e: e.scalar_tensor_tensor(out=hb, in0=xt, scalar=rs, in1=gbc, op0=ALU.mult, op1=ALU.mult),
             reads=[xt_n, pfx + "_rs", gbc_n], writes=[hb_n])

    def transpose8(src, src_n, dst, dst_n, psi, nblk=8, evac="act"):
        psb = PS[psi].bitcast(BF16)
        for k in range(nblk):
            P.op("pe", lambda e, k=k: e.transpose(out=psb[:, k * 128:(k + 1) * 128], in_=src[:, k * 128:(k + 1) * 128],
                                                  identity=ident), reads=[src_n, "ident"], writes=[PSN[psi]])
        if evac == "act":
            P.op("act", lambda e: e.activation(out=dst, in_=psb[:, 0:nblk * 128].rearrange("p (a b) -> p a b", b=128),
                                               func=AF.Copy), reads=[PSN[psi]], writes=[dst_n])
        else:
            P.op("dve", lambda e: e.tensor_copy(out=dst, in_=psb[:, 0:nblk * 128].rearrange("p (a b) -> p a b", b=128)),
                 reads=[PSN[psi]], writes=[dst_n])

    def load_gain_bc(dst, dst_n, src_row, lane, scale):
        P.dma("sp", lane, lambda e: e.dma_start(out=dst, in_=src_row.partition_broadcast(128)), writes=[dst_n])
        if scale != 1.0:
            P.op("dve", lambda e: e.tensor_scalar(out=dst, in0=dst, scalar1=float(scale), scalar2=None, op0=ALU.mult),
                 reads=[dst_n], writes=[dst_n])

    def load_w_bf16(dst, dst_n, w_ap, lane, nsplit=4):
        K = dst.shape[1]
        wv = w_ap.rearrange("(k p) n -> p k n", p=128)
        step = max(1, K // nsplit)
        toks = []
        for k0 in range(0, K, step):
            k1 = min(K, k0 + step)
            toks.append(P.dma("pool", "%s_%d" % (lane, k0), lambda e, k0=k0, k1=k1: e.dma_start(out=dst[:, k0:k1, :], in_=wv[:, k0:k1, :]),
                              writes=[dst_n + "_%d" % k0], multi=True))
        return toks

    def phase_A():
        reset_arena()
        Wi = AR.alloc([8, 3072], BF16)
        gbc = AR.alloc([D], F32)
        Gq = AR.alloc([4, 512], F32)
        gsm = AR.alloc([320], F32)
        NC_ = 4
        cst = [AR.alloc([96], F32) for _ in range(NC_)]
        gct = [AR.alloc([4, 2, 64], F32) for _ in range(NC_)]
        xt = [AR.alloc([D], F32) for _ in range(NC_)]
        sq = AR.alloc([D], BF16)
        hbs = [AR.alloc([D], BF16) for _ in range(2)]
        hTs = [AR.alloc([8, 128], BF16) for _ in range(2)]
        ssqs = [AR.alloc([1], F32) for _ in range(2)]
        rss = [AR.alloc([1], F32) for _ in range(2)]
        NS = 4
        sqgs = [AR.alloc([512], F32) for _ in range(NS)]
        qns = [AR.alloc([512], F32) for _ in range(NS)]
        tas = [AR.alloc([512], F32) for _ in range(NS)]
        tbs = [AR.alloc([512], F32) for _ in range(NS)]
        ssqhs = [AR.alloc([16], F32) for _ in range(NS)]
        rshs = [AR.alloc([16], F32) for _ in range(NS)]
        qbs = [AR.alloc([512], BF16) for _ in range(NS)]
        qTs = [AR.alloc([4, 128], BF16) for _ in range(NS)]
        vbs = [AR.alloc([512], BF16) for _ in range(NS)]

        wiv = e_w_in.rearrange("(k p) n -> p k n", p=128)
        for g_ in (1, 2, 4, 5, 0, 3):
            P.dma("pool", "A_w_g%d" % g_, lambda e, g_=g_: e.dma_start(out=Wi[:, :, g_ * 512:(g_ + 1) * 512], in_=wiv[:, :, g_ * 512:(g_ + 1) * 512]),
                  writes=["A_Wi_g%d" % g_])
        load_gain_bc(gbc, "A_gbc", attn_g[0:1, :], "A_g", 32.0)
        P.dma("sp", "A_gs", lambda e: e.dma_start(out=gsm, in_=gains.partition_broadcast(128)), writes=["A_gsm"])
        for gi, (o, hd) in enumerate([(0, 32), (32, 32), (128, 64), (192, 64)]):
            nh = 512 // hd
            P.op("dve", lambda e, gi=gi, o=o, hd=hd, nh=nh: e.tensor_scalar(
                out=Gq[:, gi, :].rearrange("p (a b) -> p a b", b=hd),
                in0=gsm[:, o:o + hd].unsqueeze(1).to_broadcast([128, nh, hd]),
                scalar1=float(hd) ** 0.5, scalar2=None, op0=ALU.mult), reads=["A_gsm"], writes=["A_Gq%d" % gi])

        Gs = AR.alloc([4, 64], F32)
        for gi, (o, hd) in enumerate([(0, 32), (32, 32), (128, 64), (192, 64)]):
            P.op("dve", lambda e, gi=gi, o=o, hd=hd: e.tensor_scalar(out=Gs[:, gi, 0:hd], in0=gsm[:, o:o + hd], scalar1=float(hd) ** 0.5,
                                                                    scalar2=None, op0=ALU.mult), reads=["A_gsm"], writes=["A_Gs"])
        qdt_v = QdT.rearrange("(c p) n -> p c n", p=128)
        kdt_v = KdT.rearrange("(c p) n -> p c n", p=128)
        qmt_v = QmT.rearrange("(c p) n -> p c n", p=128)
        kmt_v = KmT.rearrange("(c p) n -> p c n", p=128)
        vd_v = Vd.rearrange("h k d -> k h d")
        vm_v = Vm.rearrange("h k d -> k h d")

        def prologue(t):
            b2 = t % 2
            c4 = t % NC_
            xtn = "A_xt%d" % c4
            csn = "A_cs%d" % c4
            hb, hT, hbn, hTn = hbs[b2], hTs[b2], "A_hb%d" % b2, "A_hT%d" % b2
            ssq, rs, pfx = ssqs[b2], rss[b2], "A%d" % b2

            def st0():
                P.dma_group("sp", "A_in%d" % c4, [
                    lambda e: e.dma_start(out=xt[c4], in_=xin[t * 128:(t + 1) * 128, :]),
                    lambda e: e.dma_start(out=cst[c4][:, 0:64], in_=cs64[t * 128:(t + 1) * 128, :]),
                    lambda e: e.dma_start(out=cst[c4][:, 64:96], in_=cs32[t * 128:(t + 1) * 128, :])],
                    writes=[xtn, csn + "a", csn + "b"])

            def st1():
                P.op("act", lambda e: e.activation(out=sq, in_=xt[c4], func=AF.Square, accum_out=ssq), reads=[xtn], writes=["A_sq", pfx + "_ssq"])
                rms_rstd(ssq, pfx + "_ssq", rs, pfx + "_rs", D * EPS)
                for gi_, hd_ in enumerate((32, 32, 64, 64)):
                    hf_ = hd_ // 2
                    co_ = 64 if hd_ == 32 else 0
                    for cs_ in range(2):
                        P.op("pool", lambda e, gi_=gi_, hd_=hd_, hf_=hf_, co_=co_, cs_=cs_: e.tensor_tensor(
                            out=gct[c4][:, gi_, cs_, 0:hd_].rearrange("p (b c) -> p b c", c=hf_),
                            in0=Gs[:, gi_, 0:hd_].rearrange("p (b c) -> p b c", c=hf_),
                            in1=cst[c4][:, co_ + cs_ * hf_:co_ + (cs_ + 1) * hf_].unsqueeze(1).to_broadcast([128, 2, hf_]), op=ALU.mult),
                            reads=["A_Gs", csn + "a", csn + "b"], writes=[csn + "g"])

            def st2():
                P.op("dve", lambda e: e.scalar_tensor_tensor(out=hb, in0=xt[c4], scalar=rs, in1=gbc, op0=ALU.mult, op1=ALU.mult),
                     reads=[xtn, pfx + "_rs", "A_gbc"], writes=[hbn])

            def st3():
                transpose8(hb, hbn, hT, hTn, 7)
            return [st0, st1, st2, st3]

        def group_item(t, g, gidx, qidx):
            b2 = t % 2
            c4 = t % NC_
            csn = "A_cs%d" % c4
            hT, hTn = hTs[b2], "A_hT%d" % b2
            psi = gidx % 5
            ps = PS[psi]
            z = gidx % NS

            def s_mm():
                for k in range(8):
                    P.op("pe", lambda e, k=k: e.matmul(ps, lhsT=hT[:, k, :], rhs=Wi[:, k, g * 512:(g + 1) * 512], start=(k == 0), stop=(k == 7)),
                         reads=[hTn, "A_Wi_g%d" % g], writes=[PSN[psi]])
            if g in (2, 5):
                vb, vbn = vbs[z], "A_vb%d" % z
                dst = vd_v if g == 2 else vm_v

                def s_cp():
                    P.op("act", lambda e: e.activation(out=vb, in_=ps, func=AF.Copy), reads=[PSN[psi]], writes=[vbn])

                def s_st():
                    P.dma("sp", "A_vs%d" % z, lambda e: e.dma_start(out=dst[t * 128:(t + 1) * 128, :, :], in_=vb.rearrange("p (h d) -> p h d", d=64)),
                          reads=[vbn])
                return [s_mm, s_cp, s_st]
            isd = g in (0, 1)
            hd = 32 if isd else 64
            half = hd // 2
            nh = 512 // hd
            gi = {0: 0, 1: 1, 3: 2, 4: 3}[g]
            sqg, qn, ta, tb, ssqh, rsh, qb, qT = sqgs[z], qns[z], tas[z], tbs[z], ssqhs[z], rshs[z], qbs[z], qTs[z]
            n_sqg, n_qn, n_ta, n_tb, n_ssqh, n_rsh, n_qb, n_qT = ["A_%s%d" % (x, z) for x in ("sqg", "qn", "ta", "tb", "ssqh", "rsh", "qb", "qT")]
            ssq_v = ssqh[:, 0:nh]
            rs_v = rsh[:, 0:nh]
            co = 64 if isd else 0
            cosv = cst[c4][:, co:co + half]
            sinv = cst[c4][:, co + half:co + 2 * half]
            q4 = qn.rearrange("p (a b c) -> p a b c", b=2, c=half)
            ta4 = ta.rearrange("p (a b c) -> p a b c", b=2, c=half)
            tb4 = tb.rearrange("p (a b c) -> p a b c", b=2, c=half)
            qb4 = qb.rearrange("p (a b c) -> p a b c", b=2, c=half)
            tpi = 5 + (gidx % 2)
            psb = PS[tpi].bitcast(BF16)
            if g in (0, 3):
                dstv = qdt_v if g == 0 else qmt_v
                col = qidx * 128
            else:
                dstv = kdt_v if g == 1 else kmt_v
                col = t * 128

            def s1():
                P.op("act", lambda e: e.activation(out=sqg, in_=ps, func=AF.Square), reads=[PSN[psi]], writes=[n_sqg])

            def s2():
                P.op("dve", lambda e: e.tensor_reduce(out=ssq_v, in_=sqg.rearrange("p (a b) -> p a b", b=hd), axis=AX.X, op=ALU.add),
                     reads=[n_sqg], writes=[n_ssqh])

            def s3():
                rms_rstd(ssq_v, n_ssqh, rs_v, n_rsh, hd * EPS)

            def s4():
                P.op("dve", lambda e: e.tensor_tensor(out=qn.rearrange("p (a b) -> p a b", b=hd), in0=ps.rearrange("p (a b) -> p a b", b=hd),
                                                      in1=rs_v.unsqueeze(2).to_broadcast([128, nh, hd]), op=ALU.mult),
                     reads=[PSN[psi], n_rsh], writes=[n_qn])

            gcv = gct[c4][:, gi, 0, 0:hd].rearrange("p (b c) -> p b c", c=half)
            gsv = gct[c4][:, gi, 1, 0:hd].rearrange("p (b c) -> p b c", c=half)

            def s5():
                pass

            def s6():
                P.op("dve", lambda e: e.tensor_tensor(out=ta4, in0=q4, in1=gcv.unsqueeze(1).to_broadcast([128, nh, 2, half]),
                                                      op=ALU.mult), reads=[n_qn, csn + "g"], writes=[n_ta])
                P.op("pool", lambda e: e.tensor_tensor(out=tb4, in0=q4, in1=gsv.unsqueeze(1).to_broadcast([128, nh, 2, half]),
                                                       op=ALU.mult), reads=[n_qn, csn + "g"], writes=[n_tb])

            def s7():
                P.op("dve", lambda e: e.tensor_tensor(out=qb4[:, :, 0, :], in0=ta4[:, :, 0, :], in1=tb4[:, :, 1, :], op=ALU.subtract),
                     reads=[n_ta, n_tb], writes=[n_qb + "a"])
                P.op("pool", lambda e: e.tensor_tensor(out=qb4[:, :, 1, :], in0=ta4[:, :, 1, :], in1=tb4[:, :, 0, :], op=ALU.add),
                     reads=[n_ta, n_tb], writes=[n_qb + "b"])

            def s8():
                for k in range(4):
                    P.op("pe", lambda e, k=k: e.transpose(out=psb[:, k * 128:(k + 1) * 128], in_=qb[:, k * 128:(k + 1) * 128], identity=ident),
                         reads=[n_qb + "a", n_qb + "b", "ident"], writes=[PSN[tpi]])

            def s9():
                P.op("act", lambda e: e.activation(out=qT, in_=psb[:, 0:512].rearrange("p (a b) -> p a b", b=128), func=AF.Copy),
                     reads=[PSN[tpi]], writes=[n_qT])

            def s10():
                P.dma("sp", "A_qs%d" % z, lambda e: e.dma_start(out=dstv[:, :, col:col + 128], in_=qT), reads=[n_qT])
            return [s_mm, s1, s2, s3, s4, s5, s6, s7, s8, s9, s10]

        items = []
        extra = {}
        first_group_of_tile = {}
        for t in range(NKT):
            if t >= 16:
                qidx = t - 16
            elif t == 15:
                qidx = 16
            else:
                qidx = None
            groups = [1, 2, 4, 5] if qidx is None else [0, 1, 2, 3, 4, 5]
            first_group_of_tile[t] = len(items)
            for g in groups:
                items.append(group_item(t, g, len(items), qidx))
        for st in prologue(0):
            st()
        for t in range(NKT - 1):
            g0 = first_group_of_tile[t]
            for k, st in enumerate(prologue(t + 1)):
                extra.setdefault(g0 + k, []).append(st)
        maxs = max(len(it) for it in items)
        for step in range(len(items) + maxs):
            for sidx in range(maxs - 1, -1, -1):
                g = step - sidx
                if 0 <= g < len(items) and sidx < len(items[g]):
                    items[g][sidx]()
            for st in extra.get(step, []):
                st()

    phase_A()
    barrier()
    fin = []
    if stop_after == "A":
        P.emit()
        return nc

    def phase_BC():
        reset_arena()
        Wo = AR.alloc([8, D], BF16)
        C = AR.alloc([NQT, D], BF16)
        cm = AR.alloc([4, 512], BF16)
        kT = [AR.alloc([NK], BF16) for _ in range(2)]
        vbuf = [AR.alloc([NKT, 128], BF16) for _ in range(2)]
        qa = [AR.alloc([NQ], BF16) for _ in range(2)]
        qb_ = [AR.alloc([NQ], BF16) for _ in range(2)]
        pT = [AR.alloc([2, 512], BF16) for _ in range(3)]
        oT = [AR.alloc([512], BF16) for _ in range(2)]
        tsb = AR.alloc([528], BF16)
        gsm = AR.alloc([320], F32)
        lamv = AR.alloc([128], F32)
        lt = AR.alloc([64], F32)
        lsm = AR.alloc([8], F32)
        Gsub = AR.alloc([4, 64], F32)
        gbj = AR.alloc([9, 16], F32)
        biaspad = AR.alloc([NQT, 128], BF16)
        gball = AR.alloc([NQT, 16], F32)
        ownm = AR.alloc([NQT, 16], F32)
        gmall = AR.alloc([NQT, 16], F32)
        cmpb = AR.alloc([NQT, 16, 16], BF16)
        rnk = AR.alloc([NQT, 16], F32)
        bia1 = AR.alloc([NQT, 16], F32)
        bia2 = AR.alloc([NQT, 16], F32)
        zt = AR.alloc([128], BF16)
        kbs = AR.alloc([16], F32)
        kbb = AR.alloc([16], BF16)
        gm = AR.alloc([16], F32)
        top8 = AR.alloc([8], F32)
        thr = AR.alloc([1], F32)
        rcp = AR.alloc([8], F32)
        oa = AR.alloc([4, 64], F32)
        ob = AR.alloc([4, 64], F32)
        dd = AR.alloc([4, 64], F32)
        sqd = AR.alloc([4, 64], F32)
        ssq4 = AR.alloc([4], F32)
        rs4 = AR.alloc([4], F32)
        CT = AR.alloc([8, 128], BF16)
        xr = [AR.alloc([D], F32) for _ in range(2)]
        x1 = [AR.alloc([D], F32) for _ in range(2)]

        load_w_bf16(Wo, "B_Wo", e_w_out, "B_w", nsplit=2)
        wo_names = ["B_Wo_0"] * 4 + ["B_Wo_4"] * 4
        P.op("pool", lambda e: e.memset(cm, 0.0), writes=["cm"])
        for i in range(4):
            P.op("pool", lambda e, i=i: e.affine_select(out=cm[:, i, :], in_=cm[:, i, :], pattern=[[1, 512]], compare_op=ALU.is_ge,
                                                        fill=NEG, base=-128 * i, channel_multiplier=-1), reads=["cm"], writes=["cm"])
        for b in range(2):
            P.op("pool", lambda e, b=b: e.memset(qa[b][32:64, :], 0.0), writes=["qa%d_z" % b])
            P.op("pool", lambda e, b=b: e.memset(qa[b][64:128, :], 0.0), writes=["qa%d_bias" % b])
            P.op("pool", lambda e, b=b: e.memset(qb_[b][0:32, :], 0.0), writes=["qb%d_z" % b])
            P.op("pool", lambda e, b=b: e.memset(qb_[b][64:128, :], 0.0), writes=["qb%d_z" % b])
            P.op("pool", lambda e, b=b: e.memset(vbuf[b][:, :, 64:128], 0.0), writes=["vb%d_1" % b])
            P.op("pool", lambda e, b=b: e.memset(vbuf[b][:, :, 64:65], 1.0), reads=["vb%d_1" % b], writes=["vb%d_1" % b])
            P.op("pool", lambda e, b=b: e.memset(kT[b][64:128, :], 0.0), writes=["kT%d_oh" % b])
            P.dma("sp", "B_oh%d" % b, lambda e, b=b: e.dma_start(out=kT[b][64:80, :], in_=onehot), writes=["kT%d_oh" % b])
        P.op("pool", lambda e: e.memset(biaspad, 0.0), writes=["biaspad"])
        P.op("pool", lambda e: e.memset(zt, 0.0), writes=["zt"])
        P.op("pool", lambda e: e.memset(kbb[64:128, :], 0.0), writes=["kbb_z"])
        P.dma("sp", "B_gs", lambda e: e.dma_start(out=gsm, in_=gains.partition_broadcast(128)), writes=["B_gsm"])
        P.dma("sp", "B_lm", lambda e: e.dma_start(out=lamv, in_=lams.partition_broadcast(128)), writes=["B_lamv"])
        l4 = lamv.rearrange("p (a b c) -> p a b c", b=2, c=32)
        P.op("dve", lambda e: e.tensor_tensor(out=lt.rearrange("p (a c) -> p a c", c=32), in0=l4[:, :, 0, :], in1=l4[:, :, 1, :],
                                              op=ALU.mult), reads=["B_lamv"], writes=["B_lt"])
        P.op("dve", lambda e: e.tensor_reduce(out=lsm[:, 0:2], in_=lt.rearrange("p (a c) -> p a c", c=32), axis=AX.X, op=ALU.add),
             reads=["B_lt"], writes=["B_lsm"])
        P.op("act", lambda e: e.activation(out=lsm[:, 2:4], in_=lsm[:, 0:2], func=AF.Exp), reads=["B_lsm"], writes=["B_lsm2"])
        P.op("dve", lambda e: e.tensor_tensor(out=lsm[:, 4:5], in0=lsm[:, 3:4], in1=lsm[:, 2:3], op=ALU.subtract),
             reads=["B_lsm2"], writes=["B_lsm3"])
        neglam = lsm[:, 5:6]
        P.op("dve", lambda e: e.tensor_scalar(out=neglam, in0=lsm[:, 4:5], scalar1=-LAMBDA_INIT0, scalar2=None, op0=ALU.add),
             reads=["B_lsm3"], writes=["neglam"])
        P.op("dve", lambda e: e.tensor_scalar(out=Gsub, in0=gsm[:, 64:128].unsqueeze(1).to_broadcast([128, 4, 64]),
                                              scalar1=8.0 * (1.0 - LAMBDA_INIT0), scalar2=None, op0=ALU.mult),
             reads=["B_gsm"], writes=["Gsub"])
        for j in range(8):
            P.op("dve", lambda e, j=j: e.tensor_copy(out=gbj[:, j, :], in_=flg[:, 4:20]), reads=["flg"], writes=["gbj"])
            P.op("dve", lambda e, j=j: e.memset(gbj[:, j, 8 + j:16], -1e30), reads=["gbj"], writes=["gbj"])
        P.op("dve", lambda e: e.memset(gbj[:, 8, :], 0.0), reads=["gbj"], writes=["gbj"])
        P.op("dve", lambda e: e.memset(gbj[:, 8, 7:16], -1e30), reads=["gbj"], writes=["gbj"])

        P.op("dve", lambda e: e.memset(ownm, 1.0), writes=["ownm"])
        for qi in range(NQT):
            jj = qi // 2 if qi < 16 else 8
            ownb = 8 + qi // 2 if qi < 16 else 7
            P.op("dve", lambda e, qi=qi, jj=jj: e.tensor_copy(out=gball[:, qi, :], in_=gbj[:, jj, :]), reads=["gbj"], writes=["gball"])
            P.op("dve", lambda e, qi=qi, ownb=ownb: e.memset(ownm[:, qi, ownb:ownb + 1], 0.0), reads=["ownm"], writes=["ownm"])
        units = [("d", h) for h in range(8)] + [("m", h) for h in range(8)]
        groups = [(gi * 512, 512, 20 + 4 * gi, True, gi * 4) for gi in range(4)] + [(2048, 128, 16, False, 16)]
        gcount = [0]
        sc = [0]

        def issue_loads(u):
            kind, h = units[u]
            b = u % 2
            Ksrc, Vsrc, Qsrc = (KdT, Vd, QdT) if kind == "d" else (KmT, Vm, QmT)
            fns = [lambda e: e.dma_start(out=kT[b][0:64, :], in_=Ksrc[h * 64:(h + 1) * 64, :])]
            for pt in range(4):
                fns.append(lambda e, pt=pt: e.dma_start(out=vbuf[b][:, pt * 8:(pt + 1) * 8, 0:64],
                                                       in_=Vsrc[h].rearrange("(t p) d -> p t d", p=128)[:, pt * 8:(pt + 1) * 8, :]))
            wr = ["kT%d" % b] + ["vb%d_%d" % (b, pt) for pt in range(4)]
            if kind == "d":
                fns.append(lambda e: e.dma_start(out=qa[b][0:32, :], in_=Qsrc[h * 64:h * 64 + 32, :]))
                fns.append(lambda e: e.dma_start(out=qb_[b][32:64, :], in_=Qsrc[h * 64 + 32:h * 64 + 64, :]))
                wr += ["qa%d" % b, "qb%d" % b]
            else:
                fns.append(lambda e: e.dma_start(out=qa[b][0:64, :], in_=Qsrc[h * 64:(h + 1) * 64, :]))
                wr += ["qa%d" % b, "qa%d_z" % b]
            P.dma_group("sp", "B_ld%d" % b, fns, writes=wr)

        def gating_stages(u):
            kind, h = units[u]
            b = u % 2
            ps7b = PS[7].bitcast(BF16)

            def g1():
                P.op("dve", lambda e: e.tensor_reduce(out=kbs[0:64, :], in_=kT[b][0:64, :].rearrange("p (a c) -> p a c", c=256),
                                                      axis=AX.X, op=ALU.add), reads=["kT%d" % b], writes=["kbs"])
                P.op("dve", lambda e: e.tensor_scalar(out=kbb[0:64, :], in0=kbs[0:64, :], scalar1=1.0 / 256.0, scalar2=None, op0=ALU.mult),
                     reads=["kbs"], writes=["kbb"])
                for qi in range(NQT):
                    P.op("pe", lambda e, qi=qi: e.matmul(PS[7][:, qi * 16:(qi + 1) * 16], lhsT=qa[b][:, qi * 128:(qi + 1) * 128], rhs=kbb,
                                                        start=True, stop=True), reads=["qa%d" % b, "qa%d_bias" % b, "kbb", "kbb_z"], writes=[PSN[7]])

            def g2():
                P.op("dve", lambda e: e.tensor_tensor(out=gmall, in0=PS[7][:, 0:NQT * 16].rearrange("p (q n) -> p q n", n=16), in1=gball, op=ALU.add),
                     reads=[PSN[7], "gball"], writes=["gmall"])
                P.op("dve", lambda e: e.tensor_tensor(out=cmpb, in0=gmall.unsqueeze(2).to_broadcast([128, NQT, 16, 16]),
                                                      in1=gmall.unsqueeze(3).to_broadcast([128, NQT, 16, 16]), op=ALU.is_gt),
                     reads=["gmall"], writes=["cmpb"])
                P.op("dve", lambda e: e.tensor_reduce(out=rnk, in_=cmpb, axis=AX.X, op=ALU.add), reads=["cmpb"], writes=["rnk"])
                P.op("dve", lambda e: e.tensor_scalar(out=bia1, in0=rnk, scalar1=2.5, scalar2=NEG, op0=ALU.is_gt, op1=ALU.mult),
                     reads=["rnk"], writes=["bia1"])
                P.op("dve", lambda e: e.tensor_scalar(out=bia2, in0=gmall, scalar1=-1e29, scalar2=NEG, op0=ALU.is_lt, op1=ALU.mult),
                     reads=["gmall"], writes=["bia2"])
                P.op("dve", lambda e: e.tensor_tensor(out=bia1, in0=bia1, in1=bia2, op=ALU.add), reads=["bia1", "bia2"], writes=["bia1"])
                P.op("dve", lambda e: e.tensor_tensor(out=biaspad[:, :, 64:80], in0=bia1, in1=ownm, op=ALU.mult),
                     reads=["bia1", "ownm"], writes=["biaspad"])

            def g3():
                for q0 in range(0, NQT, 3):
                    nb = min(3, NQT - q0)
                    for k in range(nb):
                        P.op("pe", lambda e, q0=q0, k=k: e.transpose(out=ps7b[:, 544 + k * 128:544 + (k + 1) * 128], in_=biaspad[:, q0 + k, :],
                                                                    identity=ident), reads=["biaspad", "ident"], writes=[PSN[7]])
                    P.op("act", lambda e, q0=q0, nb=nb: e.activation(out=qa[b][64:80, q0 * 128:(q0 + nb) * 128], in_=ps7b[64:80, 544:544 + nb * 128],
                                                                    func=AF.Copy), reads=[PSN[7]], writes=["qa%d_bias" % b])
            return [g1, g2, g3]

        def attention(u, hooks=()):
            kind, h = units[u]
            b = u % 2
            isd = kind == "d"
            dk = 128
            scale = (32.0 ** -0.5) if isd else 0.125
            for gi_, grp in enumerate(groups):
                if gi_ < len(hooks):
                    hooks[gi_]()
                do_group(kind, h, b, isd, dk, scale, (not isd) or len(hooks) == 0, *grp)

        def do_group(kind, h, b, isd, dk, scale, wide, qc0, N, nkt, usepast, t0):
            R = N // 128
            gidx = gcount[0]
            gcount[0] += 1
            nsl = 3 if wide else 2
            if isd and wide:
                maps = [(qa[b], ["qa%d" % b, "qa%d_z" % b, "qa%d_bias" % b], 6), (qb_[b], ["qb%d" % b, "qb%d_z" % b], 7)]
                tbank = 0
            elif isd:
                maps = [(qa[b], ["qa%d" % b, "qa%d_z" % b, "qa%d_bias" % b], 4), (qb_[b], ["qb%d" % b, "qb%d_z" % b], 5)]
                tbank = 6
            elif wide:
                maps = [(qa[b], ["qa%d" % b, "qa%d_bias" % b], 6)]
                tbank = 0
            else:
                maps = [(qa[b], ["qa%d" % b, "qa%d_bias" % b], 4)]
                tbank = 5
            npair = nkt // 2
            steps = [(mi, kp) for mi in range(len(maps)) for kp in range(npair)]
            slot = {}

            def col0_of(kp):
                return 256 if (usepast and kp == npair - 1) else 0

            def qk(i):
                mi, kp = steps[i]
                Q, qn_, _ = maps[mi]
                s_ = sc[0] % nsl
                sc[0] += 1
                slot[i] = s_
                c0_ = col0_of(kp)
                for hfi in range(2):
                    kt = 2 * kp + hfi
                    bank = 2 * s_ + hfi
                    if usepast:
                        di = kt - (nkt - 4)
                    else:
                        di = 0 if kt == nkt - 1 else -1
                    diag = di >= 0
                    P.op("pe", lambda e, kt=kt, bank=bank, diag=diag: e.matmul(PS[bank][:, c0_:N], lhsT=kT[b][0:dk, kt * 128:(kt + 1) * 128],
                                                                             rhs=Q[0:dk, qc0 + c0_:qc0 + N], start=True, stop=not diag),
                         reads=["kT%d" % b, "kT%d_oh" % b] + qn_, writes=[PSN[bank]])
                    if diag:
                        P.op("pe", lambda e, bank=bank, di=di: e.matmul(PS[bank][:, c0_:N], lhsT=ident, rhs=cm[:, di, c0_:N], start=False, stop=True),
                             reads=["ident", "cm"], writes=[PSN[bank]])

            def ex_pv(i):
                mi, kp = steps[i]
                _, _, abank = maps[mi]
                s_ = slot[i]
                c0_ = col0_of(kp)
                src = PSALL[:, 2 * s_ * 512:(2 * s_ + 2) * 512].rearrange("p (b n) -> p b n", n=512)[:, :, c0_:N]
                dst = pT[s_][:, :, c0_:N]
                rd = [PSN[2 * s_], PSN[2 * s_ + 1]]
                if usepast and 2 * kp < 16:
                    P.op("act", lambda e: e.activation(out=dst, in_=src, func=AF.Exp, scale=scale, bias=flg[:, 0:1]),
                         reads=rd + ["flg"], writes=["pT%d" % s_])
                else:
                    P.op("act", lambda e: e.activation(out=dst, in_=src, func=AF.Exp, scale=scale), reads=rd, writes=["pT%d" % s_])
                for hfi in range(2):
                    kt = 2 * kp + hfi
                    P.op("pe", lambda e, kt=kt, hfi=hfi: e.matmul(PS[abank][:, c0_:N], lhsT=vbuf[b][:, kt, :], rhs=pT[s_][:, hfi, c0_:N],
                                                                  start=(kt == 0), stop=(kt == nkt - 1)),
                         reads=["pT%d" % s_, "vb%d_%d" % (b, kt // 8), "vb%d_1" % b], writes=[PSN[abank]])

            lead = nsl - 1
            for i in range(min(lead, len(steps))):
                qk(i)
            for i in range(len(steps)):
                if i + lead < len(steps):
                    qk(i + lead)
                ex_pv(i)
            tpb = PS[tbank].bitcast(BF16)
            for mi, (_, _, abank) in enumerate(maps):
                P.op("dve", lambda e, mi=mi, abank=abank: e.tensor_copy(out=oT[mi][0:65, 0:N], in_=PS[abank][0:65, 0:N]),
                     reads=[PSN[abank]], writes=["oT%d" % mi])
                for r in range(R):
                    c0 = (mi * 4 + r) * 66
                    P.op("pe", lambda e, mi=mi, r=r, c0=c0: e.transpose(out=tpb[:, c0:c0 + 65], in_=oT[mi][0:65, r * 128:(r + 1) * 128],
                                                                      identity=ident[0:65, 0:65]),
                         reads=["oT%d" % mi, "ident"], writes=[PSN[tbank]])
            cn = "C_g%d" % t0
            nt = PSN[tbank]
            fin = tpb
            if wide:
                P.op("dve", lambda e: e.tensor_copy(out=tsb, in_=tpb[:, 0:528]), reads=[PSN[tbank]], writes=["tsb"])
                fin = tsb
                nt = "tsb"
            accA = fin[:, 0:R * 66].rearrange("p (r c) -> p r c", c=66)
            if isd:
                accB = fin[:, 4 * 66:(4 + R) * 66].rearrange("p (r c) -> p r c", c=66)
                P.op("dve", lambda e: e.reciprocal(out=rcp[:, 0:R], in_=accA[:, :, 64]), reads=[nt], writes=["rcpa"])
                P.op("dve", lambda e: e.reciprocal(out=rcp[:, 4:4 + R], in_=accB[:, :, 64]), reads=[nt], writes=["rcpb"])
                P.op("dve", lambda e: e.tensor_tensor(out=oa[:, 0:R, :], in0=accA[:, :, 0:64],
                                                      in1=rcp[:, 0:R].unsqueeze(2).to_broadcast([128, R, 64]), op=ALU.mult),
                     reads=[nt, "rcpa"], writes=["oa"])
                P.op("dve", lambda e: e.tensor_tensor(out=ob[:, 0:R, :], in0=accB[:, :, 0:64],
                                                      in1=rcp[:, 4:4 + R].unsqueeze(2).to_broadcast([128, R, 64]), op=ALU.mult),
                     reads=[nt, "rcpb"], writes=["ob"])
                P.op("dve", lambda e: e.scalar_tensor_tensor(out=dd[:, 0:R, :], in0=ob[:, 0:R, :], scalar=neglam, in1=oa[:, 0:R, :],
                                                             op0=ALU.mult, op1=ALU.add), reads=["oa", "ob", "neglam"], writes=["dd"])
                P.op("pool", lambda e: e.tensor_tensor(out=sqd[:, 0:R, :], in0=dd[:, 0:R, :], in1=dd[:, 0:R, :], op=ALU.mult),
                     reads=["dd"], writes=["sqd"])
                P.op("dve", lambda e: e.tensor_reduce(out=ssq4[:, 0:R], in_=sqd[:, 0:R, :], axis=AX.X, op=ALU.add),
                     reads=["sqd"], writes=["ssq4"])
                rms_rstd(ssq4[:, 0:R], "ssq4", rs4[:, 0:R], "rs4", 64 * EPS)
                P.op("dve", lambda e: e.tensor_tensor(out=dd[:, 0:R, :], in0=dd[:, 0:R, :],
                                                      in1=rs4[:, 0:R].unsqueeze(2).to_broadcast([128, R, 64]), op=ALU.mult),
                     reads=["dd", "rs4"], writes=["dd"])
                P.op("pool", lambda e: e.tensor_tensor(out=C[:, t0:t0 + R, h * 64:(h + 1) * 64], in0=dd[:, 0:R, :], in1=Gsub[:, 0:R, :],
                                                       op=ALU.mult), reads=["dd", "Gsub"], writes=[cn])
            else:
                P.op("dve", lambda e: e.reciprocal(out=rcp[:, 0:R], in_=accA[:, :, 64]), reads=[nt], writes=["rcpa"])
                P.op("dve", lambda e: e.tensor_tensor(out=C[:, t0:t0 + R, 512 + h * 64:512 + (h + 1) * 64], in0=accA[:, :, 0:64],
                                                      in1=rcp[:, 0:R].unsqueeze(2).to_broadcast([128, R, 64]), op=ALU.mult),
                     reads=[nt, "rcpa"], writes=[cn])

        issue_loads(0)
        for u in range(len(units)):
            if u + 1 < len(units):
                issue_loads(u + 1)
            hooks = gating_stages(u + 1) if (u + 1 < len(units) and units[u + 1][0] == "m") else []
            attention(u, hooks)

        CTs = [CT, AR.alloc([8, 128], BF16)]

        def c_pre(tl):
            b2 = tl % 2
            lt_ = 16 + tl if tl < 16 else 15
            cn = "C_g%d" % (tl // 4 * 4 if tl < 16 else 16)
            P.dma("sp", "C_x%d" % b2, lambda e: e.dma_start(out=xr[b2], in_=xin[lt_ * 128:(lt_ + 1) * 128, :]), writes=["C_xr%d" % b2])
            transpose8(C[:, tl, :], cn, CTs[b2], "C_CT%d" % b2, 6 + b2)

        def c_main(tl):
            b2 = tl % 2
            for hh in range(2):
                bank = 2 * b2 + hh
                for k in range(8):
                    P.op("pe", lambda e, k=k, hh=hh, bank=bank: e.matmul(PS[bank], lhsT=CTs[b2][:, k, :], rhs=Wo[:, k, hh * 512:(hh + 1) * 512],
                                                                        start=(k == 0), stop=(k == 7)), reads=["C_CT%d" % b2, wo_names[k]], writes=[PSN[bank]])
                P.op("dve", lambda e, hh=hh, bank=bank: e.tensor_tensor(out=x1[b2][:, hh * 512:(hh + 1) * 512], in0=PS[bank],
                                                                       in1=xr[b2][:, hh * 512:(hh + 1) * 512], op=ALU.add),
                     reads=[PSN[bank], "C_xr%d" % b2], writes=["C_x1%d_%d" % (b2, hh)])
            P.dma("sp", "C_s%d" % b2, lambda e: e.dma_start(out=X1[tl * 128:(tl + 1) * 128, :], in_=x1[b2]),
                  reads=["C_x1%d_0" % b2, "C_x1%d_1" % b2])

        c_pre(0)
        for tl in range(NQT):
            if tl + 1 < NQT:
                c_pre(tl + 1)
            c_main(tl)

    phase_BC()
    barrier()
    if stop_after == "BC":
        P.emit()
        return nc

    def phase_FFN(Xin, Xout, li, ntiles):
        reset_arena()
        gbc = AR.alloc([D], F32)
        xs = [AR.alloc([D], F32) for _ in range(3)]
        xr2 = [AR.alloc([D], F32) for _ in range(2)]
        sq = AR.alloc([D], BF16)
        hbs = [AR.alloc([D], BF16) for _ in range(2)]
        hT = AR.alloc([8, 1152], BF16)
        wo = AR.alloc([22, D], BF16)
        wi = [AR.alloc([8, 256], BF16) for _ in range(3)]
        AT = AR.alloc([22, 1152], BF16)
        sg = [AR.alloc([512], BF16) for _ in range(2)]
        yb = [AR.alloc([D], F32) for _ in range(2)]
        ssqs = [AR.alloc([1], F32) for _ in range(2)]
        rss = [AR.alloc([1], F32) for _ in range(2)]
        pf = "F%d" % li
        load_gain_bc(gbc, pf + "_gbc", ffn_g[li:li + 1, :], pf + "_g", 32.0)
        win_v = f_w_in[li].rearrange("(k p) n -> p k n", p=128)
        wout_v = f_w_out[li].rearrange("(j p) n -> p j n", p=128)
        half = (ntiles + 1) // 2
        passes = [list(range(0, half)), list(range(half, ntiles))]
        cnt = {"pro": 0, "y": 0, "g": 0}

        def load_wi(j):
            b3 = j % 3
            P.dma_group("pool", pf + "_wi%d" % b3, [
                lambda e: e.dma_start(out=wi[b3][:, :, 0:128], in_=win_v[:, :, j * 128:(j + 1) * 128]),
                lambda e: e.dma_start(out=wi[b3][:, :, 128:256], in_=win_v[:, :, DFF + j * 128:DFF + (j + 1) * 128])],
                writes=[pf + "_wi%dg" % b3, pf + "_wi%du" % b3])

        def pro_a(i, tl):
            c = cnt["pro"]
            cnt["pro"] += 1
            x3, h2 = c % 3, c % 2
            xn, hbn, pfx = pf + "_xs%d" % x3, pf + "_hb%d" % h2, pf + "n%d" % h2
            P.dma("sp", pf + "_x%d" % x3, lambda e: e.dma_start(out=xs[x3], in_=Xin[tl * 128:(tl + 1) * 128, :]), writes=[xn])
            norm_tile(xs[x3], xn, gbc, pf + "_gbc", hbs[h2], hbn, sq, pf + "_sq", ssqs[h2], rss[h2], pfx)
            return (hbs[h2], hbn)

        def pro_b(i, hbinfo):
            transpose8(hbinfo[0], hbinfo[1], hT[:, :, i * 128:(i + 1) * 128], pf + "_hT%d" % i, 0)

        for j in range(3):
            load_wi(j)
        for j0 in (0, 11):
            P.dma("pool", pf + "_wo%d" % j0, lambda e, j0=j0: e.dma_start(out=wo[:, j0:j0 + 11, :], in_=wout_v[:, j0:j0 + 11, :]),
                  writes=[pf + "_wo%d" % j0])
        prev_ = None
        for i, tl in enumerate(passes[0]):
            info_ = pro_a(i, tl)
            if prev_ is not None:
                pro_b(*prev_)
            prev_ = (i, info_)
        pro_b(*prev_)
        for pi, tiles in enumerate(passes):
            ntok = len(tiles) * 128
            nsub = (ntok + 511) // 512
            sbw = ((ntok // 128 + nsub - 1) // nsub) * 128
            for j in range(22):
                b3 = j % 3
                if j >= 3:
                    load_wi(j)
                for sb, s0 in enumerate(range(0, ntok, sbw)):
                    n = min(sbw, ntok - s0)
                    gb = cnt["g"] % 2
                    cnt["g"] += 1
                    hnames = [pf + "_hT%d" % i for i in range(s0 // 128, (s0 + n) // 128)]
                    for (col, bank, wn) in ((0, gb, "g"), (128, 2 + gb, "u")):
                        for k in range(8):
                            P.op("pe", lambda e, k=k, col=col, bank=bank, b3=b3, s0=s0, n=n: e.matmul(
                                PS[bank][:, 0:n], lhsT=wi[b3][:, k, col:col + 128], rhs=hT[:, k, s0:s0 + n], start=(k == 0), stop=(k == 7)),
                                reads=hnames + [pf + "_wi%d%s" % (b3, wn)], writes=[PSN[bank]])
                    P.op("act", lambda e, gb=gb, n=n: e.activation(out=sg[gb][:, 0:n], in_=PS[gb][:, 0:n], func=AF.Silu),
                         reads=[PSN[gb]], writes=[pf + "_sg%d" % gb])
                    P.op("dve", lambda e, gb=gb, n=n, j=j, s0=s0: e.tensor_tensor(out=AT[:, j, s0:s0 + n], in0=PS[2 + gb][:, 0:n],
                                                                                in1=sg[gb][:, 0:n], op=ALU.mult),
                         reads=[PSN[2 + gb], pf + "_sg%d" % gb], writes=[pf + "_AT%d_%d" % (sb, j % 2)])
            nxt = passes[pi + 1] if pi + 1 < len(passes) else []
            if nxt:
                for j in range(3):
                    load_wi(j)
            for i, tl in enumerate(tiles):
                hbinfo = pro_a(i, nxt[i]) if i < len(nxt) else None
                yi = cnt["y"] % 2
                cnt["y"] += 1
                P.dma("sp", pf + "_xr%d" % yi, lambda e, tl=tl, yi=yi: e.dma_start(out=xr2[yi], in_=Xin[tl * 128:(tl + 1) * 128, :]),
                      writes=[pf + "_xr%d" % yi])
                for hh in range(2):
                    bank = 4 + (2 * yi + hh)
                    if bank == 7:
                        bank = 3 if False else 7
                    for j in range(22):
                        P.op("pe", lambda e, j=j, i=i, hh=hh, bank=bank: e.matmul(PS[bank], lhsT=AT[:, j, i * 128:(i + 1) * 128],
                                                                               rhs=wo[:, j, hh * 512:(hh + 1) * 512],
                                                                               start=(j == 0), stop=(j == 21)),
                             reads=[pf + "_AT%d_0" % (i * 128 // sbw), pf + "_AT%d_1" % (i * 128 // sbw), pf + "_wo%d" % (0 if j < 11 else 11)],
                             writes=[PSN[bank]])
                    P.op("dve", lambda e, hh=hh, bank=bank, yi=yi: e.tensor_tensor(out=yb[yi][:, hh * 512:(hh + 1) * 512], in0=PS[bank],
                                                                                 in1=xr2[yi][:, hh * 512:(hh + 1) * 512], op=ALU.add),
                         reads=[PSN[bank], pf + "_xr%d" % yi], writes=[pf + "_y%d_%d" % (yi, hh)])
                P.dma("sp", pf + "_ys%d" % yi, lambda e, tl=tl, yi=yi: e.dma_start(out=Xout[tl * 128:(tl + 1) * 128, :], in_=yb[yi]),
                      reads=[pf + "_y%d_0" % yi, pf + "_y%d_1" % yi])
                if hbinfo is not None:
                    pro_b(i, hbinfo)

    phase_FFN(X1, X2, 0, NQT)
    barrier()
    if stop_after == "F0":
        P.emit()
        return nc

    def layer_norm_free(src, src_n, dst, dst_n, g_bc, g_n, b_bc, b_n, tmp, tmp_n, sm, pfx, eng2="pool"):
        P.op("dve", lambda e: e.tensor_reduce(out=sm[:, 0:1], in_=src, axis=AX.X, op=ALU.add), reads=[src_n], writes=[pfx + "_s1"])
        P.op("dve", lambda e: e.tensor_scalar(out=sm[:, 1:2], in0=sm[:, 0:1], scalar1=-1.0 / 512.0, scalar2=None, op0=ALU.mult),
             reads=[pfx + "_s1"], writes=[pfx + "_nm"])
        P.op("dve", lambda e: e.tensor_scalar(out=tmp, in0=src, scalar1=sm[:, 1:2], scalar2=None, op0=ALU.add),
             reads=[src_n, pfx + "_nm"], writes=[tmp_n])
        P.op("act", lambda e: e.activation(out=src, in_=tmp, func=AF.Square, accum_out=sm[:, 2:3]), reads=[tmp_n], writes=[src_n, pfx + "_ss"])
        P.op("act", lambda e: e.activation(out=sm[:, 3:4], in_=sm[:, 2:3], func=AF.Ln, bias=float(EPS), scale=1.0 / 512.0),
             reads=[pfx + "_ss"], writes=[pfx + "_rs"])
        P.op("act", lambda e: e.activation(out=sm[:, 3:4], in_=sm[:, 3:4], func=AF.Exp, scale=-0.5), reads=[pfx + "_rs"], writes=[pfx + "_rs"])
        P.op("dve", lambda e: e.scalar_tensor_tensor(out=tmp, in0=tmp, scalar=sm[:, 3:4], in1=g_bc, op0=ALU.mult, op1=ALU.mult),
             reads=[tmp_n, pfx + "_rs", g_n], writes=[tmp_n])
        P.op(eng2, lambda e: e.tensor_tensor(out=dst, in0=tmp, in1=b_bc, op=ALU.add), reads=[tmp_n, b_n], writes=[dst_n])

    def phase_D():
        reset_arena()
        Wi1 = AR.alloc([8, 2048], BF16)
        Wo1 = AR.alloc([8, D], BF16)
        Dg = AR.alloc([31, 4, 128], BF16)
        wsf = AR.alloc([8, 128], F32)
        wsT = AR.alloc([8, 128], BF16)
        cbuf = AR.alloc([4, 2080], BF16)
        gbc = AR.alloc([D], F32)
        brow = AR.alloc([1024], F32)
        glgb = AR.alloc([1024], F32)
        cvv = AR.alloc([1536], F32)
        bsT = AR.alloc([8], F32)
        obc = AR.alloc([8], F32)
        cw = AR.alloc([4, 31], F32)
        xt = [AR.alloc([D], F32) for _ in range(3)]
        sq = AR.alloc([D], BF16)
        NZ = 2
        S_ = []
        for z in range(NZ):
            S_.append(dict(
                hb=AR.alloc([D], BF16), hT=AR.alloc([8, 128], BF16), sig=AR.alloc([4, 128], F32), ctmp=AR.alloc([128], F32),
                t0u=AR.alloc([512], F32), t0v=AR.alloc([512], F32), w1u=AR.alloc([512], F32), w1v=AR.alloc([512], F32),
                w2=AR.alloc([512], F32), w3=AR.alloc([512], F32), gvn=AR.alloc([512], BF16), CC=AR.alloc([D], BF16),
                CCT=AR.alloc([8, 128], BF16), sm=AR.alloc([8], F32), sm2=AR.alloc([8], F32), ssq=AR.alloc([1], F32), rs=AR.alloc([1], F32)))
        yb = [AR.alloc([D], F32) for _ in range(2)]

        wi1v = o_w_in.rearrange("(k p) n -> p k n", p=128)
        for cb_ in (2, 3, 0, 1):
            P.dma("pool", "D_wi_c%d" % cb_, lambda e, cb_=cb_: e.dma_start(out=Wi1[:, :, cb_ * 512:(cb_ + 1) * 512],
                                                                          in_=wi1v[:, :, cb_ * 512:(cb_ + 1) * 512]), writes=["D_Wi_c%d" % cb_])
        load_w_bf16(Wo1, "D_Wo", o_w_out, "D_wo", nsplit=2)
        wo_n = ["D_Wo_%d" % (k // 4 * 4) for k in range(8)]
        load_gain_bc(gbc, "D_gbc", attn_g[1:2, :], "D_g", 32.0)
        P.dma_group("sp", "D_c", [
            lambda e: e.dma_start(out=brow, in_=o_b_row[:, 0:1024].partition_broadcast(128)),
            lambda e: e.dma_start(out=glgb, in_=gl_gb.partition_broadcast(128)),
            lambda e: e.dma_start(out=cvv, in_=cvec.partition_broadcast(128)),
            lambda e: e.dma_start(out=bsT, in_=bsT_in),
            lambda e: e.dma_start(out=obc, in_=o_b_col),
            lambda e: e.dma_start(out=cw, in_=cwT),
            lambda e: e.dma_start(out=wsf, in_=wsT_in)],
            writes=["D_brow", "D_glgb", "D_cvv", "D_bsT", "D_obc", "D_cw", "D_wsf"])
        P.op("pool", lambda e: e.affine_select(out=wsf, in_=wsf, pattern=[[0, 8], [1, 128]], compare_op=ALU.is_ge, fill=0.0,
                                               base=0, channel_multiplier=-1), reads=["D_wsf"], writes=["D_wsf"])
        P.op("pool", lambda e: e.tensor_copy(out=wsT, in_=wsf), reads=["D_wsf"], writes=["D_wsT"])
        for c in range(4):
            P.op("dve" if c % 2 == 0 else "pool", lambda e, c=c: e.tensor_tensor(
                out=Dg[:, :, c, :], in0=ident.unsqueeze(1).to_broadcast([128, 31, 128]),
                in1=cw[:, c, :].unsqueeze(2).to_broadcast([128, 31, 128]), op=ALU.mult), reads=["ident", "D_cw"], writes=["D_Dg%d" % c])

        def gelu(ps, psn, bias_bc, t0, t0n, w1, w1n, dst, dstn):
            P.op("dve", lambda e: e.tensor_tensor(out=t0, in0=ps, in1=bias_bc, op=ALU.add), reads=[psn, "D_brow"], writes=[t0n])
            P.op("act", lambda e: e.activation(out=w1, in_=t0, func=AF.Square), reads=[t0n], writes=[w1n])
            P.op("dve", lambda e: e.tensor_scalar(out=w1, in0=w1, scalar1=0.044715, scalar2=1.0, op0=ALU.mult, op1=ALU.add),
                 reads=[w1n], writes=[w1n])
            P.op("pool", lambda e: e.tensor_tensor(out=w1, in0=w1, in1=t0, op=ALU.mult), reads=[w1n, t0n], writes=[w1n])
            P.op("act", lambda e: e.activation(out=w1, in_=w1, func=AF.Sigmoid, scale=1.5957691216057308), reads=[w1n], writes=[w1n])
            P.op("pool", lambda e: e.tensor_tensor(out=dst, in0=w1, in1=t0, op=ALU.mult), reads=[w1n, t0n], writes=[dstn])

        def tile_stages(it, tl):
            b2 = it % 3
            z = it % NZ
            B = S_[z]
            N_ = lambda nme: "D_%s%d" % (nme, z)
            hb, hT, sig, ctmp, t0u, t0v, w1u, w1v, w2, w3, gvn, CC, CCT, sm, sm2 = (B[k] for k in (
                "hb", "hT", "sig", "ctmp", "t0u", "t0v", "w1u", "w1v", "w2", "w3", "gvn", "CC", "CCT", "sm", "sm2"))
            halo = tl == 16
            xn = "D_xt%d" % b2
            yi = it % 2

            def f1():
                P.dma("sp", "D_x%d" % b2, lambda e: e.dma_start(out=xt[b2], in_=X2[tl * 128:(tl + 1) * 128, :]), writes=[xn])

            def f2():
                P.op("act", lambda e: e.activation(out=sq, in_=xt[b2], func=AF.Square, accum_out=B["ssq"]), reads=[xn], writes=["D_sq", N_("n") + "_ssq"])
                rms_rstd(B["ssq"], N_("n") + "_ssq", B["rs"], N_("n") + "_rs", D * EPS)

            def f3():
                P.op("dve", lambda e: e.scalar_tensor_tensor(out=hb, in0=xt[b2], scalar=B["rs"], in1=gbc, op0=ALU.mult, op1=ALU.mult),
                     reads=[xn, N_("n") + "_rs", "D_gbc"], writes=[N_("hb")])

            def f4():
                transpose8(hb, N_("hb"), hT, N_("hT"), 0)

            def f5():
                for (base, bank) in ((1024, 1), (1536, 2)):
                    for c in range(4):
                        for k in range(8):
                            P.op("pe", lambda e, base=base, bank=bank, c=c, k=k: e.matmul(
                                PS[bank][:, c * 128:(c + 1) * 128], lhsT=Wi1[:, k, base + c * 128:base + (c + 1) * 128], rhs=hT[:, k, :],
                                start=(k == 0), stop=(k == 7)), reads=[N_("hT"), "D_Wi_c%d" % (base // 512)], writes=[PSN[bank]])

            def f6():
                for (base, bank) in ((0, 3), (512, 4)):
                    for k in range(8):
                        P.op("pe", lambda e, base=base, bank=bank, k=k: e.matmul(PS[bank], lhsT=hT[:, k, :], rhs=Wi1[:, k, base:base + 512],
                                                                              start=(k == 0), stop=(k == 7)),
                             reads=[N_("hT"), "D_Wi_c%d" % (base // 512)], writes=[PSN[bank]])

            def f7():
                for c in range(4):
                    P.op("act", lambda e, c=c: e.activation(out=sig[:, c, :], in_=PS[2][:, c * 128:(c + 1) * 128], func=AF.Sigmoid,
                                                            bias=obc[:, 4 + c:5 + c]), reads=[PSN[2], "D_obc"], writes=[N_("sig") + "_%d" % c])
                    if halo:
                        P.op("dve", lambda e, c=c: e.scalar_tensor_tensor(out=ctmp, in0=PS[1][:, c * 128:(c + 1) * 128], scalar=obc[:, c:c + 1],
                                                                          in1=sig[:, c, :], op0=ALU.add, op1=ALU.mult),
                             reads=[PSN[1], "D_obc", N_("sig") + "_%d" % c], writes=[N_("ctmp")])
                        P.op("dve", lambda e, c=c: e.tensor_scalar(out=cbuf[:, c, 0:32], in0=ctmp[:, 96:128], scalar1=flg[:, 1:2], scalar2=None,
                                                                   op0=ALU.mult), reads=[N_("ctmp"), "flg"], writes=["D_cb_h%d" % c])
                    else:
                        P.op("dve", lambda e, c=c: e.scalar_tensor_tensor(out=cbuf[:, c, 32 + tl * 128:32 + (tl + 1) * 128],
                                                                          in0=PS[1][:, c * 128:(c + 1) * 128], scalar=obc[:, c:c + 1],
                                                                          in1=sig[:, c, :], op0=ALU.add, op1=ALU.mult),
                             reads=[PSN[1], "D_obc", N_("sig") + "_%d" % c], writes=["D_cb_%d_%d" % (tl, c)])
            if halo:
                return [f1, f2, f3, f4, f5, f7], []

            def f8():
                gelu(PS[3], PSN[3], brow[:, 0:512], t0u, N_("t0u"), w1u, N_("w1u"), w2, N_("w2"))

            def f9():
                gelu(PS[4], PSN[4], brow[:, 512:1024], t0v, N_("t0v"), w1v, N_("w1v"), w3, N_("w3"))

            def f10():
                layer_norm_free(w3, N_("w3"), gvn, N_("gvn"), glgb[:, 0:512], "D_glgb", glgb[:, 512:1024], "D_glgb", t0v, N_("t0v"), sm, N_("ln1"))

            def f11():
                for g in range(8):
                    P.op("pe", lambda e, g=g: e.matmul(PS[0][:, g * 64:(g + 1) * 64], lhsT=wsT[:, g, :], rhs=gvn[:, g * 64:(g + 1) * 64],
                                                       start=True, stop=True), reads=["D_wsT", N_("gvn")], writes=[PSN[0]])
                P.op("dve", lambda e: e.tensor_tensor(out=t0u.rearrange("p (g d) -> p g d", d=64), in0=PS[0].rearrange("p (g d) -> p g d", d=64),
                                                      in1=bsT[:, 0:8].unsqueeze(2).to_broadcast([128, 8, 64]), op=ALU.add),
                     reads=[PSN[0], "D_bsT"], writes=[N_("t0u")])
                P.op("pool", lambda e: e.tensor_tensor(out=CC[:, 0:512], in0=w2, in1=t0u, op=ALU.mult), reads=[N_("w2"), N_("t0u")], writes=[N_("CCa")])

            def s1():
                for c in range(4):
                    rn = ["D_cb_%d_%d" % (tl, c), ("D_cb_%d_%d" % (tl - 1, c)) if tl > 0 else ("D_cb_h%d" % c)]
                    for j in range(31):
                        o = tl * 128 + 2 + j
                        P.op("pe", lambda e, c=c, j=j, o=o: e.matmul(PS[5][:, c * 128:(c + 1) * 128], lhsT=cbuf[:, c, o:o + 128], rhs=Dg[:, j, c, :],
                                                                  start=(j == 0), stop=(j == 30)), reads=rn + ["D_Dg%d" % c], writes=[PSN[5]])
                P.op("dve", lambda e: e.tensor_tensor(out=w3, in0=PS[5], in1=cvv[:, 0:512], op=ALU.add), reads=[PSN[5], "D_cvv"], writes=[N_("w3")])

            def s2():
                layer_norm_free(w3, N_("w3"), w2, N_("w2"), cvv[:, 512:1024], "D_cvv", cvv[:, 1024:1536], "D_cvv", t0v, N_("t0v"), sm2, N_("ln2"))
                P.op("act", lambda e: e.activation(out=CC[:, 512:1024], in_=w2, func=AF.Silu), reads=[N_("w2")], writes=[N_("CCb")])

            def s3():
                psb = PS[6].bitcast(BF16)
                for k in range(8):
                    P.op("pe", lambda e, k=k: e.transpose(out=psb[:, k * 128:(k + 1) * 128], in_=CC[:, k * 128:(k + 1) * 128], identity=ident),
                         reads=[N_("CCa"), N_("CCb"), "ident"], writes=[PSN[6]])
                P.op("act", lambda e: e.activation(out=CCT, in_=psb[:, 0:1024].rearrange("p (a b) -> p a b", b=128), func=AF.Copy),
                     reads=[PSN[6]], writes=[N_("CCT")])

            def s4():
                for hh in range(2):
                    bank = 7 if hh == 0 else 5
                    for k in range(8):
                        P.op("pe", lambda e, k=k, hh=hh, bank=bank: e.matmul(PS[bank], lhsT=CCT[:, k, :], rhs=Wo1[:, k, hh * 512:(hh + 1) * 512],
                                                                            start=(k == 0), stop=(k == 7)), reads=[N_("CCT"), wo_n[k]], writes=[PSN[bank]])
                    P.op("dve", lambda e, hh=hh, bank=bank: e.tensor_tensor(out=yb[yi][:, hh * 512:(hh + 1) * 512], in0=PS[bank],
                                                                           in1=xt[b2][:, hh * 512:(hh + 1) * 512], op=ALU.add),
                         reads=[PSN[bank], xn], writes=["D_y%d_%d" % (yi, hh)])
                P.dma("sp", "D_ys%d" % yi, lambda e: e.dma_start(out=X3[tl * 128:(tl + 1) * 128, :], in_=yb[yi]),
                      reads=["D_y%d_0" % yi, "D_y%d_1" % yi])
            return [f1, f2, f3, f4, f5, f6, f7, f8, f9, f10, f11], [s1, s2, s3, s4]

        def coalesce(lst):
            out_, run = [], []
            for it_ in lst:
                if it_[0] == "op" and it_[1] == "pe":
                    run.append(it_)
                else:
                    if run:
                        out_.append(("macro", run))
                        run = []
                    out_.append(it_)
            if run:
                out_.append(("macro", run))
            return out_

        def round_robin(chains):
            chains = [c for c in chains if c]
            idx = [0] * len(chains)
            while True:
                progressed = False
                for ci, c in enumerate(chains):
                    if idx[ci] < len(c):
                        P.replay(c[idx[ci]])
                        idx[ci] += 1
                        progressed = True
                if not progressed:
                    break

        order = [16] + list(range(16))
        stages = {}

        def get(it):
            if it not in stages and 0 <= it < len(order):
                stages[it] = tile_stages(it, order[it])
            return stages.get(it)

        F0_, _ = get(0)
        for f in F0_[0:4]:
            f()
        for it, tl in enumerate(order):
            F, Sn = get(it)
            halo = tl == 16
            F[4]()
            if not halo:
                F[5]()
            chains = []
            fc = F[5] if halo else F[6]
            chains.append(coalesce(P.capture(fc)))
            if not halo:
                chains.append(coalesce(P.capture(F[7])))
                chains.append(coalesce(P.capture(lambda: (F[8](), F[9](), F[10]()))))
            if it >= 1 and stages[it - 1][1]:
                Sp = stages[it - 1][1]
                chains.append(coalesce(P.capture(lambda: [st() for st in Sp])))
            nxt = get(it + 1)
            if nxt is not None:
                Fn = nxt[0]
                chains.append(coalesce(P.capture(lambda: [f() for f in Fn[0:4]])))
            round_robin(chains)
        lastS = stages[len(order) - 1][1]
        for st in lastS:
            st()

    phase_D()
    barrier()
    if stop_after == "D":
        P.emit()
        return nc

    phase_FFN(X3, out, 1, 16)
    barrier()
    P.emit()
    return nc


def _bf16(a):
    import ml_dtypes
    return np.asarray(a, np.float32).astype(ml_dtypes.bfloat16)


def _rope_tables(dim, pos):
    inv = 1.0 / (10000.0 ** (np.arange(0, dim, 2, dtype=np.float32) / dim))
    ang = pos.astype(np.float32)[:, None] * inv[None, :]
    return np.cos(ang).astype(np.float32), np.sin(ang).astype(np.float32)


def host_inputs(inp, core):
    b, hf = core // 2, core % 2
    x = np.asarray(inp["x"], np.float32)
    xin = np.concatenate([x[b, 0:2048], x[b, hf * 2048:(hf + 1) * 2048]], 0)
    pos = np.concatenate([np.arange(2048), hf * 2048 + np.arange(2048)])
    c64, s64 = _rope_tables(64, pos)
    c32, s32 = _rope_tables(32, pos)
    flags = np.zeros((128, 20), np.float32)
    flags[:, 0] = 0.0 if hf == 1 else -30000.0
    flags[:, 1] = 1.0 if hf == 1 else 0.0
    flags[:, 4:12] = 0.0 if hf == 1 else -1e30
    onehot = np.zeros((16, 4096), np.float32)
    for n in range(16):
        onehot[n, n * 256:(n + 1) * 256] = 1.0
    g = lambda k: np.asarray(inp[k], np.float32)
    gains = np.concatenate([g("diff_q_norm_g")[0], g("diff_k_norm_g")[0], g("diff_subln_g")[0], g("moba_q_norm_g")[0],
                            g("moba_k_norm_g")[0], np.zeros(64, np.float32)])[None, :]
    lams = np.concatenate([g("diff_lambda_q1")[0], g("diff_lambda_k1")[0], g("diff_lambda_q2")[0], g("diff_lambda_k2")[0]])[None, :]
    ob = g("odd_b_in")[0]
    m = {
        "xin": xin, "cs64": np.concatenate([c64, s64], 1), "cs32": np.concatenate([c32, s32], 1),
        "flags": flags, "onehot": _bf16(onehot),
        "even_w_in": g("even_w_in")[0], "even_w_out": g("even_w_out")[0],
        "ffn_w_in": g("ffn_w_in"), "ffn_w_out": g("ffn_w_out"),
        "odd_w_in": g("odd_w_in")[0], "odd_w_out": g("odd_w_out")[0],
        "attn_norm_g": g("attn_norm_g"), "ffn_norm_g": g("ffn_norm_g"),
        "gains": gains.astype(np.float32), "lams": lams.astype(np.float32),
        "odd_b_row": ob[None, :].copy(),
        "odd_b_col": np.ascontiguousarray(ob[1024:2048].reshape(8, 128).T),
        "gmlp_ln_gb": np.concatenate([g("gmlp_ln_g")[0], g("gmlp_ln_b")[0]])[None, :],
        "gmlp_wsT": np.ascontiguousarray(g("gmlp_w_s")[0].transpose(2, 0, 1)),
        "gmlp_bsT": np.ascontiguousarray(g("gmlp_b_s")[0].T),
        "conv_wT": np.ascontiguousarray(g("conv_w")[0].T.reshape(4, 128, 31).transpose(1, 0, 2)),
        "conv_vecs": np.concatenate([g("conv_b")[0], g("conv_ln_g")[0], g("conv_ln_b")[0]])[None, :],
    }
    return {k: np.ascontiguousarray(v) for k, v in m.items()}


def kernel(**inputs):
    in_maps = [host_inputs(inputs, c) for c in range(8)]
    nc = build_program()
    res = run_bass_kernel_spmd(nc, in_maps, core_ids=list(range(8)))
    out = np.zeros((4, 4096, 1024), np.float32)
    for c in range(8):
        b, hf = c // 2, c % 2
        out[b, hf * 2048:(hf + 1) * 2048] = np.asarray(res.results[c]["out"], np.float32)
    return out
```
